# Optimizing a Trainium2 kernel written in Bass

```python
import math
import jax
import jax.numpy as jnp
from jax import lax
import numpy as np

D_MODEL = 2048
BATCH = 2
SEQ = 4096
DEPTH = 2

CTX_LEN = 256
GRID_W = 64
HEAD_DIM = 128
CHUNK = 64
GLA_HEADS = 4
GLA_DK = 64
GLA_DV = 128
GLA_RANK = 16
GLA_GATE_NORM = 16.0
GDN_HEADS = 4
GDN_DK = 128
GDN_DV = 128
GDN_CONV = 5
GDN_CONV_CH = 2 * GDN_HEADS * GDN_DK + GDN_HEADS * GDN_DV
ATTN_HEADS = 8
ATTN_KV_HEADS = 2
ATTN_GROUP = ATTN_HEADS // ATTN_KV_HEADS
ROPE_THETA = 10000.0
Q_BLOCK = 128
N_EXPERTS = 16
EC_FACTOR = 2
EXPERT_FF = 2048
N_MOD = 6
DEEPNORM_ALPHA = (2 * DEPTH) ** 0.25
DEEPNORM_BETA = (8 * DEPTH) ** -0.25
MIX_WIDTH = GLA_HEADS * GLA_DV + GDN_HEADS * GDN_DV + ATTN_HEADS * HEAD_DIM
IN_SPLITS = (
    GLA_HEADS * GLA_DK, GLA_HEADS * GLA_DK, GLA_HEADS * GLA_DV, GLA_HEADS * GLA_DV, 2 * GLA_RANK,
    GDN_HEADS * GDN_DK, GDN_HEADS * GDN_DK, GDN_HEADS * GDN_DV, GDN_HEADS * GDN_DV, 2 * GDN_HEADS, 2 * GDN_HEADS,
    ATTN_HEADS * HEAD_DIM, ATTN_KV_HEADS * HEAD_DIM, ATTN_KV_HEADS * HEAD_DIM,
)
IN_WIDTH = sum(IN_SPLITS)

kernel_name = 'hybrid_gla_gdn_gqa_ec_moe_diffusion_block'


def _layer_norm(x, gain, bias, eps=1e-5):
    xf = x.astype(jnp.float32)
    mu = jnp.mean(xf, axis=-1, keepdims=True)
    var = jnp.mean(jnp.square(xf - mu), axis=-1, keepdims=True)
    return ((xf - mu) * lax.rsqrt(var + eps) * gain + bias).astype(x.dtype)


def _rms_norm(x, gain, eps=1e-6):
    xf = x.astype(jnp.float32)
    return xf * lax.rsqrt(jnp.mean(xf * xf, axis=-1, keepdims=True) + eps) * gain


def _l2norm(x, eps=1e-6):
    return x * lax.rsqrt(jnp.sum(x * x, axis=-1, keepdims=True) + eps)


def _modulate(x, shift, scale):
    return x * (1.0 + scale) + shift


def _split_cols(p):
    points = np.cumsum(np.array(IN_SPLITS))[:-1].tolist()
    return jnp.split(p, points, axis=-1)


def _heads(a, n_heads):
    b, t, _ = a.shape
    return a.reshape(b, t, n_heads, -1).transpose(0, 2, 1, 3)


def _depthwise_conv(x, w):
    k = w.shape[0]
    return lax.conv_general_dilated(
        x, w.astype(x.dtype)[:, None, :], window_strides=(1,), padding=[(k // 2, k // 2)],
        dimension_numbers=('NWC', 'WIO', 'NWC'), feature_group_count=x.shape[-1])


def _axial_rope_tables(n_tokens):
    rows = n_tokens // GRID_W
    row = jnp.broadcast_to(jnp.arange(rows)[:, None], (rows, GRID_W)).reshape(-1).astype(jnp.float32)
    col = jnp.broadcast_to(jnp.arange(GRID_W)[None, :], (rows, GRID_W)).reshape(-1).astype(jnp.float32)
    half = HEAD_DIM // 2
    inv = ROPE_THETA ** (-jnp.arange(0, half, 2, dtype=jnp.float32) / half)
    ang = jnp.concatenate([row[:, None] * inv, col[:, None] * inv], axis=-1)
    return jnp.cos(ang), jnp.sin(ang)


def _apply_axial_rope(x, cos, sin):
    n = x.shape[1]
    quarter = HEAD_DIM // 4
    bshape = (1, n) + (1,) * (x.ndim - 3) + (2, quarter)
    c = cos.reshape(bshape)
    s = sin.reshape(bshape)
    xr = x.reshape(x.shape[:-1] + (2, 2, quarter))
    x1 = xr[..., 0, :]
    x2 = xr[..., 1, :]
    return jnp.stack([x1 * c - x2 * s, x2 * c + x1 * s], axis=-2).reshape(x.shape)


def _gla_chunked(q, k, v, log_a, s0, with_out):
    b_, h_, t_, dk = q.shape
    dv = v.shape[-1]
    n = t_ // CHUNK
    q = q.reshape(b_, h_, n, CHUNK, dk)
    k = k.reshape(b_, h_, n, CHUNK, dk)
    v = v.reshape(b_, h_, n, CHUNK, dv)
    cum = jnp.cumsum(log_a.reshape(b_, h_, n, CHUNK, dk), axis=3)
    cum_last = cum[:, :, :, -1]
    kv = jnp.einsum('bhnck,bhncv->bhnkv', k * jnp.exp(cum_last[:, :, :, None] - cum), v)
    d_last = jnp.exp(cum_last)
    seq = lambda a: jnp.moveaxis(a, 2, 0)
    if with_out:
        q_dec = q * jnp.exp(cum)
        lower = jnp.tril(jnp.ones((CHUNK, CHUNK), dtype=bool))
        a_intra = jnp.where(lower, jnp.einsum('bhnik,bhnjk->bhnij', q_dec, k * jnp.exp(-cum)), 0.0)
        o_intra = jnp.einsum('bhnij,bhnjv->bhniv', a_intra, v)
        xs = (seq(kv), seq(d_last), seq(q_dec))
    else:
        xs = (seq(kv), seq(d_last))

    def step(state, inp):
        nxt = inp[1][..., None] * state + inp[0]
        if not with_out:
            return nxt, None
        return nxt, jnp.einsum('bhck,bhkv->bhcv', inp[2], state)

    state, o_inter = lax.scan(step, s0, xs)
    if not with_out:
        return None, state
    o = o_intra + jnp.moveaxis(o_inter, 0, 2)
    return o.reshape(b_, h_, t_, dv), state


def _gdn_chunked(q, k, v, g, beta, s0, with_out):
    b_, h_, t_, dk = q.shape
    dv = v.shape[-1]
    n = t_ // CHUNK
    q = q.reshape(b_, h_, n, CHUNK, dk)
    k = k.reshape(b_, h_, n, CHUNK, dk)
    v = v.reshape(b_, h_, n, CHUNK, dv)
    g = jnp.cumsum(g.reshape(b_, h_, n, CHUNK), axis=-1)
    beta = beta.reshape(b_, h_, n, CHUNK)
    k_beta = k * beta[..., None]
    v_beta = v * beta[..., None]
    lower = jnp.tril(jnp.ones((CHUNK, CHUNK), dtype=bool))
    strict = jnp.tril(jnp.ones((CHUNK, CHUNK), dtype=bool), -1)
    gamma = jnp.where(lower, jnp.exp(jnp.where(lower, g[..., :, None] - g[..., None, :], 0.0)), 0.0)
    l_mat = jnp.where(strict, jnp.einsum('bhnid,bhnjd->bhnij', k_beta, k) * gamma, 0.0)
    eye = jnp.eye(CHUNK, dtype=l_mat.dtype)
    t_mat = lax.linalg.triangular_solve(l_mat + eye, jnp.broadcast_to(eye, l_mat.shape),
                                        left_side=True, lower=True, unit_diagonal=True)
    u = jnp.einsum('bhnij,bhnjv->bhniv', t_mat, v_beta)
    w = jnp.einsum('bhnij,bhnjk->bhnik', t_mat, k_beta * jnp.exp(g)[..., None])
    g_last = g[..., -1]
    k_dec = k * jnp.exp(g_last[..., None] - g)[..., None]
    d_last = jnp.exp(g_last)
    seq = lambda a: jnp.moveaxis(a, 2, 0)
    if with_out:
        q_dec = q * jnp.exp(g)[..., None]
        a_intra = jnp.where(lower, jnp.einsum('bhnik,bhnjk->bhnij', q, k) * gamma, 0.0)
        xs = tuple(seq(a) for a in (u, w, k_dec, d_last, q_dec, a_intra))
    else:
        xs = tuple(seq(a) for a in (u, w, k_dec, d_last))

    def step(state, inp):
        u_c, w_c, k_c, d_c = inp[:4]
        v_new = u_c - jnp.einsum('bhck,bhkv->bhcv', w_c, state)
        nxt = d_c[..., None, None] * state + jnp.einsum('bhck,bhcv->bhkv', k_c, v_new)
        if not with_out:
            return nxt, None
        q_c, a_c = inp[4:]
        o = jnp.einsum('bhck,bhkv->bhcv', q_c, state) + jnp.einsum('bhij,bhjv->bhiv', a_c, v_new)
        return nxt, o

    state, o = lax.scan(step, s0, xs)
    if not with_out:
        return None, state
    return jnp.moveaxis(o, 0, 2).reshape(b_, h_, t_, dv), state


def _bidirectional(chunk_fn, ctx_in, lat_in, s0, with_ctx_out):
    o_ctx, o_lat = None, None
    for direction in range(2):
        tf = (lambda a: a) if direction == 0 else (lambda a: jnp.flip(a, axis=2))
        oc, s_ctx = chunk_fn(*[tf(a) for a in ctx_in[direction]], s0, with_ctx_out)
        ol, _ = chunk_fn(*[tf(a) for a in lat_in[direction]], s_ctx, True)
        ol = tf(ol)
        o_lat = ol if o_lat is None else o_lat + ol
        if with_ctx_out:
            oc = tf(oc)
            o_ctx = oc if o_ctx is None else o_ctx + oc
    return o_ctx, o_lat


def _gla_mixer(parts_ctx, parts_lat, w_up, b_up, norm_gain, with_ctx_out):
    def prep(parts):
        q, k, v, g, r = parts
        b_, t_, _ = q.shape
        q = _heads(q, GLA_HEADS).astype(jnp.float32) * GLA_DK ** -0.5
        k = _heads(k, GLA_HEADS).astype(jnp.float32)
        v = _heads(v, GLA_HEADS).astype(jnp.float32)
        r = r.reshape(b_, t_, 2, GLA_RANK)
        logit = jnp.einsum('btzr,zrk->zbtk', r, w_up) + b_up[:, None, None, :]
        log_a = jax.nn.log_sigmoid(logit.astype(jnp.float32)) / GLA_GATE_NORM
        log_a = log_a.reshape(2, b_, t_, GLA_HEADS, GLA_DK).transpose(0, 1, 3, 2, 4)
        return [(q, k, v, log_a[0]), (q, k, v, log_a[1])], g

    def finish(o, g):
        b_, _, t_, _ = o.shape
        o = _rms_norm(o.transpose(0, 2, 1, 3), norm_gain).reshape(b_, t_, GLA_HEADS * GLA_DV)
        return o * jax.nn.silu(g.astype(jnp.float32))

    ctx_in, g_ctx = prep(parts_ctx)
    lat_in, g_lat = prep(parts_lat)
    s0 = jnp.zeros((parts_lat[0].shape[0], GLA_HEADS, GLA_DK, GLA_DV), jnp.float32)
    o_ctx, o_lat = _bidirectional(_gla_chunked, ctx_in, lat_in, s0, with_ctx_out)
    return (finish(o_ctx, g_ctx) if with_ctx_out else None), finish(o_lat, g_lat)


def _gdn_mixer(parts_ctx, parts_lat, conv_w, a_log, dt_bias, norm_gain, with_ctx_out):
    def prep(parts):
        q, k, v, z, b, a = parts
        b_, t_, _ = q.shape
        qkv = jax.nn.silu(_depthwise_conv(jnp.concatenate([q, k, v], axis=-1), conv_w))
        q, k, v = jnp.split(qkv, [GDN_HEADS * GDN_DK, 2 * GDN_HEADS * GDN_DK], axis=-1)
        q = _l2norm(_heads(q, GDN_HEADS).astype(jnp.float32)) * GDN_DK ** -0.5
        k = _l2norm(_heads(k, GDN_HEADS).astype(jnp.float32))
        v = _heads(v, GDN_HEADS).astype(jnp.float32)
        beta = jax.nn.sigmoid(b.astype(jnp.float32)).reshape(b_, t_, 2, GDN_HEADS).transpose(2, 0, 3, 1)
        a = a.astype(jnp.float32).reshape(b_, t_, 2, GDN_HEADS).transpose(2, 0, 3, 1)
        g = -jnp.exp(a_log)[:, None, :, None] * jax.nn.softplus(a + dt_bias[:, None, :, None])
        return [(q, k, v, g[0], beta[0]), (q, k, v, g[1], beta[1])], z

    def finish(o, z):
        b_, _, t_, _ = o.shape
        o = _rms_norm(o.transpose(0, 2, 1, 3), norm_gain)
        o = o * jax.nn.silu(z.astype(jnp.float32).reshape(b_, t_, GDN_HEADS, GDN_DV))
        return o.reshape(b_, t_, GDN_HEADS * GDN_DV)

    ctx_in, z_ctx = prep(parts_ctx)
    lat_in, z_lat = prep(parts_lat)
    s0 = jnp.zeros((parts_lat[0].shape[0], GDN_HEADS, GDN_DK, GDN_DV), jnp.float32)
    o_ctx, o_lat = _bidirectional(_gdn_chunked, ctx_in, lat_in, s0, with_ctx_out)
    return (finish(o_ctx, z_ctx) if with_ctx_out else None), finish(o_lat, z_lat)


def _attend(q, k, v):
    s = jnp.einsum('bqhgd,bkhd->bhgqk', q, k).astype(jnp.float32) * HEAD_DIM ** -0.5
    p = jax.nn.softmax(s, axis=-1).astype(v.dtype)
    return jnp.einsum('bhgqk,bkhd->bqhgd', p, v)


def _gqa_mixer(parts_ctx, parts_lat, qk_gain, cos, sin, with_ctx_out):
    def prep(parts, rotary):
        q, k, v = parts
        b_, t_, _ = q.shape
        q = _rms_norm(q.reshape(b_, t_, ATTN_KV_HEADS, ATTN_GROUP, HEAD_DIM), qk_gain[0])
        k = _rms_norm(k.reshape(b_, t_, ATTN_KV_HEADS, HEAD_DIM), qk_gain[1])
        v = v.reshape(b_, t_, ATTN_KV_HEADS, HEAD_DIM).astype(jnp.float32)
        if rotary:
            q = _apply_axial_rope(q, cos, sin)
            k = _apply_axial_rope(k, cos, sin)
        return q, k, v

    q_c, k_c, v_c = prep(parts_ctx, False)
    q_l, k_l, v_l = prep(parts_lat, True)
    k_all = jnp.concatenate([k_c, k_l], axis=1)
    v_all = jnp.concatenate([v_c, v_l], axis=1)
    b_, n_, _, _, _ = q_l.shape
    nb = n_ // Q_BLOCK
    q_blocks = q_l.reshape(b_, nb, Q_BLOCK, ATTN_KV_HEADS, ATTN_GROUP, HEAD_DIM).transpose(1, 0, 2, 3, 4, 5)
    o_lat = lax.map(lambda qb: _attend(qb, k_all, v_all), q_blocks)
    o_lat = o_lat.transpose(1, 0, 2, 3, 4, 5).reshape(b_, n_, ATTN_HEADS * HEAD_DIM)
    o_ctx = None
    if with_ctx_out:
        o_ctx = _attend(q_c, k_c, v_c).reshape(b_, q_c.shape[1], ATTN_HEADS * HEAD_DIM)
    return o_ctx, o_lat


def _expert_choice_ffn(h, router, w1, w3, w2):
    b_, t_, _ = h.shape
    cap = EC_FACTOR * t_ // N_EXPERTS
    logits = jnp.einsum('btd,de->bet', h, router).astype(jnp.float32)
    aff = jax.nn.softmax(logits, axis=1)
    gate, idx = lax.top_k(aff, cap)
    bidx = jnp.arange(b_)[:, None, None]
    xs = h[bidx, idx]
    a = jnp.einsum('becd,edf->becf', xs, w1)
    u = jnp.einsum('becd,edf->becf', xs, w3)
    y = jnp.einsum('becf,efd->becd', jax.nn.silu(a) * u, w2) * gate[..., None].astype(h.dtype)
    return jnp.zeros_like(h).at[bidx, idx].add(y)


def setup_inputs(seed: int = 0) -> dict:
    key = jax.random.key(seed)
    ks = jax.random.split(key, 24)
    f32 = jnp.float32
    d = D_MODEL

    def nrm(k, shape, scale):
        return jax.random.normal(k, shape, f32) * scale

    dt = jnp.exp(jax.random.uniform(ks[10], (DEPTH, 2, GDN_HEADS), f32, math.log(1e-3), math.log(1e-1)))
    return {
        'x': nrm(ks[0], (BATCH, SEQ, d), 1.0),
        'c': nrm(ks[1], (BATCH, d), 1.0),
        'ctx': nrm(ks[2], (BATCH, CTX_LEN, d), 1.0),
        'c_ctx': nrm(ks[3], (d,), 1.0),
        'w_ada': nrm(ks[4], (DEPTH, d, N_MOD * d), 0.5 * d ** -0.5),
        'b_ada': nrm(ks[5], (DEPTH, N_MOD * d), 0.02),
        'w_in': nrm(ks[6], (DEPTH, d, IN_WIDTH), d ** -0.5),
        'w_out': nrm(ks[7], (DEPTH, MIX_WIDTH, d), DEEPNORM_BETA * MIX_WIDTH ** -0.5),
        'gla_w_up': nrm(ks[8], (DEPTH, 2, GLA_RANK, GLA_HEADS * GLA_DK), GLA_RANK ** -0.5),
        'gla_b_up': nrm(ks[9], (DEPTH, 2, GLA_HEADS * GLA_DK), 0.1),
        'gla_norm': 1.0 + nrm(ks[11], (DEPTH, GLA_DV), 0.02),
        'gdn_conv': nrm(ks[12], (DEPTH, GDN_CONV, GDN_CONV_CH), GDN_CONV ** -0.5),
        'gdn_a_log': jnp.log(jax.random.uniform(ks[13], (DEPTH, 2, GDN_HEADS), f32, 1.0, 16.0)),
        'gdn_dt_bias': dt + jnp.log(-jnp.expm1(-dt)),
        'gdn_norm': 1.0 + nrm(ks[14], (DEPTH, GDN_DV), 0.02),
        'attn_qk_norm': 1.0 + nrm(ks[15], (DEPTH, 2, HEAD_DIM), 0.02),
        'ln_gain': 1.0 + nrm(ks[16], (DEPTH, 2, d), 0.02),
        'ln_bias': nrm(ks[17], (DEPTH, 2, d), 0.02),
        'router': nrm(ks[18], (DEPTH, d, N_EXPERTS), d ** -0.5),
        'w1': nrm(ks[19], (DEPTH, N_EXPERTS, d, EXPERT_FF), d ** -0.5),
        'w3': nrm(ks[20], (DEPTH, N_EXPERTS, d, EXPERT_FF), d ** -0.5),
        'w2': nrm(ks[21], (DEPTH, N_EXPERTS, EXPERT_FF, d), DEEPNORM_BETA * EXPERT_FF ** -0.5),
    }


def reference(x, c, ctx, c_ctx, w_ada, b_ada, w_in, w_out, gla_w_up, gla_b_up, gla_norm,
              gdn_conv, gdn_a_log, gdn_dt_bias, gdn_norm, attn_qk_norm, ln_gain, ln_bias,
              router, w1, w3, w2):
    bsz, n_lat, d = x.shape
    cos, sin = _axial_rope_tables(n_lat)
    x_lat, x_ctx = x, ctx
    for l in range(DEPTH):
        last = l == DEPTH - 1
        mod_lat = (jax.nn.silu(c) @ w_ada[l] + b_ada[l]).reshape(bsz, N_MOD, 1, d)
        mod_ctx = (jax.nn.silu(c_ctx) @ w_ada[l] + b_ada[l]).reshape(N_MOD, 1, 1, d)

        p_lat = _split_cols(_modulate(x_lat, mod_lat[:, 0], mod_lat[:, 1]) @ w_in[l])
        p_ctx = _split_cols(_modulate(x_ctx, mod_ctx[0], mod_ctx[1]) @ w_in[l])
        gla_c, gla_l = _gla_mixer(p_ctx[0:5], p_lat[0:5], gla_w_up[l], gla_b_up[l], gla_norm[l], not last)
        gdn_c, gdn_l = _gdn_mixer(p_ctx[5:11], p_lat[5:11], gdn_conv[l], gdn_a_log[l], gdn_dt_bias[l],
                                  gdn_norm[l], not last)
        att_c, att_l = _gqa_mixer(p_ctx[11:14], p_lat[11:14], attn_qk_norm[l], cos, sin, not last)
        y_lat = jnp.concatenate([gla_l, gdn_l, att_l], axis=-1).astype(x_lat.dtype) @ w_out[l]
        x_lat = _layer_norm(DEEPNORM_ALPHA * x_lat + mod_lat[:, 2] * y_lat, ln_gain[l, 0], ln_bias[l, 0])
        if not last:
            y_ctx = jnp.concatenate([gla_c, gdn_c, att_c], axis=-1).astype(x_ctx.dtype) @ w_out[l]
            x_ctx = _layer_norm(DEEPNORM_ALPHA * x_ctx + mod_ctx[2] * y_ctx, ln_gain[l, 0], ln_bias[l, 0])

        f_lat = _expert_choice_ffn(_modulate(x_lat, mod_lat[:, 3], mod_lat[:, 4]), router[l], w1[l], w3[l], w2[l])
        x_lat = _layer_norm(DEEPNORM_ALPHA * x_lat + mod_lat[:, 5] * f_lat, ln_gain[l, 1], ln_bias[l, 1])
        if not last:
            f_ctx = _expert_choice_ffn(_modulate(x_ctx, mod_ctx[3], mod_ctx[4]), router[l], w1[l], w3[l], w2[l])
            x_ctx = _layer_norm(DEEPNORM_ALPHA * x_ctx + mod_ctx[5] * f_ctx, ln_gain[l, 1], ln_bias[l, 1])
    return x_lat
```

```python
import os
import time
import numpy as np
import concourse.bass as bass
import concourse.mybir as mybir
from concourse.bass_utils import run_bass_kernel_spmd

F32 = mybir.dt.float32
U32 = mybir.dt.uint32
I32 = mybir.dt.int32
AF = mybir.ActivationFunctionType
ALU = mybir.AluOpType
AX = mybir.AxisListType

NCORES = 8
D = 2048
B = 2
SEQ = 4096
CTX = 256
DEPTH = 2
NE = 16
FF = 2048
IN_W = 5168
ALPHA = (2 * DEPTH) ** 0.25


class Tok:
    __slots__ = ("w", "r", "excl")

    def __init__(self, excl=False):
        self.w = None
        self.r = []
        self.excl = excl


class Prog:
    CE = ("act", "pe", "dve", "pool")
    DQ = ("sp", "act", "pool")
    ND = 6

    def __init__(self):
        self.nc = bass.Bass("TRN2", target_bir_lowering=False)
        nc = self.nc
        self.q = {e: [] for e in ("sp", "act", "pe", "dve", "pool")}
        self.csem = {e: nc.alloc_semaphore("c_" + e) for e in self.CE}
        self.ccnt = {e: 0 for e in self.CE}
        self.dsem = {e: [nc.alloc_semaphore("d_%s%d" % (e, i)) for i in range(self.ND)] for e in self.DQ}
        self.dcnt = {e: [0] * self.ND for e in self.DQ}
        self.drr = {e: 0 for e in self.DQ}
        self.waited = {e: {} for e in self.q}
        self.out_deps = []
        self.nm = 0

    def name(self, p):
        self.nm += 1
        return "%s_%d" % (p, self.nm)

    def sb(self, shape, dt=F32, name="sb"):
        return self.nc.alloc_sbuf_tensor(self.name(name), list(shape), dt)

    def ps(self, shape, dt=F32, name="ps"):
        return self.nc.alloc_psum_tensor(self.name(name), list(shape), dt)

    def din(self, name, shape, dt=F32):
        return self.nc.dram_tensor(name, list(shape), dt, kind="ExternalInput").ap()

    def dout(self, name, shape, dt=F32):
        return self.nc.dram_tensor(name, list(shape), dt, kind="ExternalOutput").ap()

    def dscratch(self, name, shape, dt=F32):
        return self.nc.dram_tensor(name, list(shape), dt, kind="Internal").ap()

    def _deps(self, eng, reads, writes, extra=()):
        need = {}

        def add(dep):
            if dep is None:
                return
            s, v = dep
            if need.get(s, 0) < v:
                need[s] = v

        own = self.csem.get(eng)
        for t in reads:
            add(t.w)
            if t.excl:
                for r in t.r:
                    if r[0] is not own:
                        add(r)
        for t in writes:
            add(t.w)
            for r in t.r:
                add(r)
        for d in extra:
            add(d)
        if eng == "pe":
            need.pop(self.csem["pe"], None)
        out = []
        wd = self.waited[eng]
        for s, v in need.items():
            if wd.get(s, 0) >= v:
                continue
            wd[s] = v
            out.append((s, v))
        return out

    def _mark(self, reads, writes, done):
        for t in reads:
            t.r.append(done)
        for t in writes:
            t.w = done
            t.r = []

    def op(self, eng, fn, reads=(), writes=()):
        waits = self._deps(eng, reads, writes)
        self.ccnt[eng] += 1
        done = (self.csem[eng], self.ccnt[eng])
        self.q[eng].append((waits, fn, self.csem[eng], 1))
        self._mark(reads, writes, done)
        return done

    def dma(self, eng, fn, reads=(), writes=(), is_out=False):
        k = self.drr[eng]
        self.drr[eng] = (k + 1) % self.ND
        sem = self.dsem[eng][k]
        prev = (sem, self.dcnt[eng][k]) if self.dcnt[eng][k] else None
        waits = self._deps(eng, reads, writes, extra=(prev,) if prev else ())
        self.dcnt[eng][k] += 16
        done = (sem, self.dcnt[eng][k])
        self.q[eng].append((waits, fn, sem, 16))
        self._mark(reads, writes, done)
        if is_out:
            self.out_deps.append(done)
        return done

    def finish(self):
        nc = self.nc
        fin = {}
        for s, v in self.out_deps:
            fin[s] = max(fin.get(s, 0), v)
        q = self.q
        engmap = {"sp": "sync", "act": "scalar", "pe": "tensor", "dve": "vector", "pool": "gpsimd"}
        with nc.Block() as block:
            for e, bn in engmap.items():
                def body(eng, e=e):
                    for waits, fn, sem, inc in q[e]:
                        for s, v in waits:
                            eng.wait_ge(s, v)
                        fn(eng).then_inc(sem, inc)
                    if e == "pool":
                        for s, v in fin.items():
                            eng.wait_ge(s, v)
                getattr(block, bn)(body)
        return nc


def run_prog(P, in_maps):
    t0 = time.time()
    nc = P.finish()
    t1 = time.time()
    res = run_bass_kernel_spmd(nc, in_maps, core_ids=list(range(NCORES)))
    if os.environ.get("KDBG"):
        nb = sum(v.nbytes for m in in_maps for v in m.values())
        print("[run_prog] build %.1fs run %.1fs in %.0fMB" % (t1 - t0, time.time() - t1, nb / 1e6), flush=True)
    return res.results


def fm(a):
    T, C = a.shape
    return np.ascontiguousarray(a.T.reshape(C // 128, 128, T).transpose(1, 0, 2))


NT_B = 1088


def build_inproj():
    P = Prog()
    nc = P.nc
    KC = D // 128
    xT_d = P.din("xT", [128, KC, NT_B])
    mod_d = P.din("modv", [128, KC, 4])
    w_d = P.din("w", [D, IN_W]).rearrange("(k p) c -> p k c", p=128)
    p_d = P.dout("p", [NT_B, IN_W])

    xT = P.sb([128, KC, NT_B], name="xT")
    modv = P.sb([128, KC, 4], name="modv")
    t_x = [Tok() for _ in range(KC)]
    t_mod = Tok()
    for k in range(KC):
        P.dma("sp" if k % 2 == 0 else "act", lambda e, k=k: e.dma_start(out=xT[:, k, :], in_=xT_d[:, k, :]), writes=[t_x[k]])
    P.dma("sp", lambda e: e.dma_start(out=modv[:], in_=mod_d), writes=[t_mod])
    P.op("dve", lambda e: e.tensor_scalar_add(out=modv[:, :, 1:2], in0=modv[:, :, 1:2], scalar1=1.0), reads=[t_mod], writes=[t_mod])
    P.op("dve", lambda e: e.tensor_scalar_add(out=modv[:, :, 3:4], in0=modv[:, :, 3:4], scalar1=1.0), reads=[t_mod], writes=[t_mod])
    for k in range(KC):
        P.op("dve", lambda e, k=k: e.tensor_scalar(out=xT[:, k, 0:1024], in0=xT[:, k, 0:1024], scalar1=modv[:, k, 1:2],
                                                   scalar2=modv[:, k, 0:1], op0=ALU.mult, op1=ALU.add),
             reads=[t_mod, t_x[k]], writes=[t_x[k]])
        P.op("dve", lambda e, k=k: e.tensor_scalar(out=xT[:, k, 1024:NT_B], in0=xT[:, k, 1024:NT_B], scalar1=modv[:, k, 3:4],
                                                   scalar2=modv[:, k, 2:3], op0=ALU.mult, op1=ALU.add),
             reads=[t_mod, t_x[k]], writes=[t_x[k]])
    NW = 2
    wt = [P.sb([128, KC, 512], name="wt") for _ in range(NW)]
    t_w = [Tok() for _ in range(NW)]
    NPS = 4
    pst = [P.ps([128, 512], name="pp") for _ in range(NPS)]
    t_ps = [Tok() for _ in range(NPS)]
    ot = [P.sb([128, 512], name="ot") for _ in range(NPS)]
    t_ot = [Tok() for _ in range(NPS)]
    tiles = [(i * 128, 128) for i in range(8)] + [(1024, 64)]
    cgs = [(c, min(512, IN_W - c)) for c in range(0, IN_W, 512)]
    it = 0
    for ci, (c0, cw) in enumerate(cgs):
        wb = ci % NW
        for half in range(2):
            ks = slice(half * 8, half * 8 + 8)
            P.dma("sp" if half == 0 else "act",
                  lambda e, wb=wb, ks=ks, c0=c0, cw=cw: e.dma_start(out=wt[wb][:, ks, 0:cw], in_=w_d[:, ks, c0:c0 + cw]),
                  writes=[t_w[wb]])
        for (t0, m) in tiles:
            pb = it % NPS
            it += 1
            for k in range(KC):
                P.op("pe", lambda e, pb=pb, k=k, t0=t0, m=m, wb=wb, cw=cw: e.matmul(
                    pst[pb][0:m, 0:cw], lhsT=xT[:, k, t0:t0 + m], rhs=wt[wb][:, k, 0:cw], start=(k == 0), stop=(k == KC - 1)),
                    reads=[t_x[k], t_w[wb]], writes=[t_ps[pb]])
            ev = "act" if pb % 2 == 0 else "dve"
            if ev == "act":
                P.op("act", lambda e, pb=pb, m=m, cw=cw: e.copy(out=ot[pb][0:m, 0:cw], in_=pst[pb][0:m, 0:cw]),
                     reads=[t_ps[pb]], writes=[t_ot[pb]])
            else:
                P.op("dve", lambda e, pb=pb, m=m, cw=cw: e.tensor_copy(out=ot[pb][0:m, 0:cw], in_=pst[pb][0:m, 0:cw]),
                     reads=[t_ps[pb]], writes=[t_ot[pb]])
            P.dma("pool", lambda e, pb=pb, t0=t0, m=m, c0=c0, cw=cw: e.dma_start(out=p_d[t0:t0 + m, c0:c0 + cw], in_=ot[pb][0:m, 0:cw]),
                  reads=[t_ot[pb]], is_out=True)
    return P


def stage_inproj(x_lat, x_ctx, mod_lat, mod_ctx, w_in_l):
    P = build_inproj()
    in_maps = []
    for c in range(NCORES):
        b, q = divmod(c, 4)
        xs = np.concatenate([x_lat[b, q * 1024:(q + 1) * 1024], x_ctx[b, q * 64:(q + 1) * 64]], axis=0)
        mv = np.stack([mod_lat[b, 0], mod_lat[b, 1], mod_ctx[0], mod_ctx[1]], axis=-1)
        mv = np.ascontiguousarray(mv.reshape(D // 128, 128, 4).transpose(1, 0, 2))
        in_maps.append({"xT": fm(xs), "modv": mv, "w": np.ascontiguousarray(w_in_l)})
    res = run_prog(P, in_maps)
    p_lat = np.empty((B, SEQ, IN_W), np.float32)
    p_ctx = np.empty((B, CTX, IN_W), np.float32)
    for c in range(NCORES):
        b, q = divmod(c, 4)
        p_lat[b, q * 1024:(q + 1) * 1024] = res[c]["p"][:1024]
        p_ctx[b, q * 64:(q + 1) * 64] = res[c]["p"][1024:]
    return p_lat, p_ctx


MODW = 6 * D // NCORES


def build_mod():
    P = Prog()
    KC = D // 128
    cv_d = P.din("cv", [128, KC, 3])
    wa_d = P.din("wa", [DEPTH, D, MODW]).rearrange("l (k p) c -> l p k c", p=128)
    ba_d = P.din("ba", [DEPTH, 1, MODW])
    mod_d = P.dout("mod", [DEPTH, 3, MODW])
    cv = P.sb([128, KC, 3], name="cv")
    ones = P.sb([1, 4], name="ones")
    ba = P.sb([1, DEPTH, MODW], name="ba")
    t_cv, t_ones, t_ba = Tok(), Tok(), Tok()
    P.dma("sp", lambda e: e.dma_start(out=cv[:], in_=cv_d), writes=[t_cv])
    for l in range(DEPTH):
        P.dma("sp", lambda e, l=l: e.dma_start(out=ba[:, l, :], in_=ba_d[l]), writes=[t_ba])
    P.op("dve", lambda e: e.memset(ones[:], 1.0), writes=[t_ones])
    P.op("act", lambda e: e.activation(out=cv[:], in_=cv[:], func=AF.Silu), reads=[t_cv], writes=[t_cv])
    wt = [P.sb([128, KC, 512], name="wa") for _ in range(2)]
    t_w = [Tok(), Tok()]
    pst = [P.ps([128, 512], name="pm") for _ in range(2)]
    t_ps = [Tok(), Tok()]
    ot = [P.sb([4, 512], name="om") for _ in range(2)]
    t_ot = [Tok(), Tok()]
    it = 0
    for l in range(DEPTH):
        for c0 in range(0, MODW, 512):
            bi = it % 2
            it += 1
            for half in range(2):
                ks = slice(half * 8, half * 8 + 8)
                P.dma("sp" if half == 0 else "act",
                      lambda e, bi=bi, ks=ks, c0=c0, l=l: e.dma_start(out=wt[bi][:, ks, :], in_=wa_d[l, :, ks, c0:c0 + 512]),
                      writes=[t_w[bi]])
            for k in range(KC):
                P.op("pe", lambda e, bi=bi, k=k: e.matmul(pst[bi][0:3, :], lhsT=cv[:, k, :], rhs=wt[bi][:, k, :], start=(k == 0), stop=False),
                     reads=[t_cv, t_w[bi]], writes=[t_ps[bi]])
            P.op("pe", lambda e, bi=bi, l=l, c0=c0: e.matmul(pst[bi][0:3, :], lhsT=ones[0:1, 0:3], rhs=ba[0:1, l, c0:c0 + 512], start=False, stop=True),
                 reads=[t_ones, t_ba], writes=[t_ps[bi]])
            P.op("dve", lambda e, bi=bi: e.tensor_copy(out=ot[bi][0:3, :], in_=pst[bi][0:3, :]), reads=[t_ps[bi]], writes=[t_ot[bi]])
            P.dma("pool", lambda e, bi=bi, l=l, c0=c0: e.dma_start(out=mod_d[l, :, c0:c0 + 512], in_=ot[bi][0:3, :]), reads=[t_ot[bi]], is_out=True)
    return P


def stage_mod(c, c_ctx, w_ada, b_ada):
    P = build_mod()
    vec = np.concatenate([c, c_ctx[None]], axis=0)
    cv = np.ascontiguousarray(vec.T.reshape(D // 128, 128, 3).transpose(1, 0, 2))
    in_maps = []
    for ci in range(NCORES):
        cs = slice(ci * MODW, (ci + 1) * MODW)
        in_maps.append({"cv": cv, "wa": np.ascontiguousarray(w_ada[:, :, cs]), "ba": np.ascontiguousarray(b_ada[:, None, cs])})
    res = run_prog(P, in_maps)
    mod = np.concatenate([res[ci]["mod"] for ci in range(NCORES)], axis=-1)
    mod = mod.reshape(DEPTH, 3, 6, D)
    return np.ascontiguousarray(mod[:, 0:2]), np.ascontiguousarray(mod[:, 2])


def ln_tile(P, z, t_z, m, gain, bias, t_c, st, t_st, outt, t_out):
    s1, mu, ss, rstd = st[:, 0:1], st[:, 1:2], st[:, 2:3], st[:, 3:4]
    P.op("dve", lambda e: e.reduce_sum(out=s1[0:m], in_=z[0:m, :], axis=AX.X), reads=[t_z], writes=[t_st])
    P.op("dve", lambda e: e.tensor_scalar_mul(out=mu[0:m], in0=s1[0:m], scalar1=1.0 / D), reads=[t_st], writes=[t_st])
    P.op("dve", lambda e: e.tensor_scalar(out=z[0:m, :], in0=z[0:m, :], scalar1=mu[0:m], scalar2=None, op0=ALU.subtract),
         reads=[t_st, t_z], writes=[t_z])
    P.op("act", lambda e: e.activation(out=outt[0:m, :], in_=z[0:m, :], func=AF.Square, accum_out=ss[0:m]),
         reads=[t_z], writes=[t_out, t_st])
    P.op("dve", lambda e: e.tensor_scalar(out=ss[0:m], in0=ss[0:m], scalar1=1.0 / D, scalar2=1e-5, op0=ALU.mult, op1=ALU.add),
         reads=[t_st], writes=[t_st])
    P.op("act", lambda e: e.activation(out=ss[0:m], in_=ss[0:m], func=AF.Sqrt), reads=[t_st], writes=[t_st])
    P.op("dve", lambda e: e.reciprocal(out=rstd[0:m], in_=ss[0:m]), reads=[t_st], writes=[t_st])
    P.op("dve", lambda e: e.scalar_tensor_tensor(out=outt[0:m, :], in0=z[0:m, :], scalar=rstd[0:m], in1=gain[0:m, :],
                                                 op0=ALU.mult, op1=ALU.mult), reads=[t_st, t_z, t_c], writes=[t_out])
    P.op("dve", lambda e: e.tensor_add(out=outt[0:m, :], in0=outt[0:m, :], in1=bias[0:m, :]), reads=[t_c, t_out], writes=[t_out])


def build_outproj(with_proj=True, nparts=1):
    P = Prog()
    KC = D // 128
    NT = NT_B
    x_d = P.din("x", [NT, D])
    cst_d = P.din("cst", [4, 128, D])
    if with_proj:
        mT_d = P.din("mT", [128, KC, NT])
        w_d = P.din("w", [D, D]).rearrange("(k p) c -> p k c", p=128)
    else:
        y_d = P.din("y", [nparts, NT, D])
    o_d = P.dout("o", [NT, D])
    cst = P.sb([128, 4, D], name="cst")
    t_c = Tok()
    for i in range(4):
        P.dma("sp", lambda e, i=i: e.dma_start(out=cst[:, i, :], in_=cst_d[i]), writes=[t_c])
    if with_proj:
        w = P.sb([128, KC, D], name="w")
        t_w = Tok()
        for k in range(KC):
            P.dma("sp" if k % 2 == 0 else "act", lambda e, k=k: e.dma_start(out=w[:, k, :], in_=w_d[:, k, :]), writes=[t_w])
        mT = [P.sb([128, KC, 128], name="mT") for _ in range(2)]
        t_m = [Tok(), Tok()]
        pst = [P.ps([128, 512], name="po") for _ in range(4)]
        t_ps = [Tok() for _ in range(4)]
    else:
        yt = [P.sb([128, D], name="yt") for _ in range(2)]
        t_y = [Tok(), Tok()]
    xt = [P.sb([128, D], name="xt") for _ in range(2)]
    t_x = [Tok(), Tok()]
    zt = P.sb([128, D], name="zt")
    t_z = Tok()
    st = P.sb([128, 4], name="st")
    t_st = Tok()
    tiles = [(i * 128, 128) for i in range(8)] + [(1024, 64)]
    for ti, (t0, m) in enumerate(tiles):
        bi = ti % 2
        gi = 0 if ti < 8 else 1
        P.dma("sp", lambda e, bi=bi, t0=t0, m=m: e.dma_start(out=xt[bi][0:m, :], in_=x_d[t0:t0 + m, :]), writes=[t_x[bi]])
        if with_proj:
            P.dma("act", lambda e, bi=bi, t0=t0, m=m: e.dma_start(out=mT[bi][:, :, 0:m], in_=mT_d[:, :, t0:t0 + m]), writes=[t_m[bi]])
            for cg in range(4):
                for k in range(KC):
                    P.op("pe", lambda e, bi=bi, cg=cg, k=k, m=m: e.matmul(pst[cg][0:m, :], lhsT=mT[bi][:, k, 0:m], rhs=w[:, k, cg * 512:(cg + 1) * 512],
                                                                      start=(k == 0), stop=(k == KC - 1)),
                         reads=[t_m[bi], t_w], writes=[t_ps[cg]])
                P.op("dve", lambda e, cg=cg, m=m, gi=gi: e.tensor_mul(out=zt[0:m, cg * 512:(cg + 1) * 512], in0=pst[cg][0:m, :],
                                                                     in1=cst[0:m, gi, cg * 512:(cg + 1) * 512]),
                     reads=[t_ps[cg], t_c], writes=[t_z])
        else:
            for pi in range(nparts):
                P.dma("act", lambda e, bi=bi, t0=t0, m=m, pi=pi: e.dma_start(out=yt[bi][0:m, :], in_=y_d[pi, t0:t0 + m, :]), writes=[t_y[bi]])
                if pi == 0:
                    P.op("dve", lambda e, bi=bi, m=m: e.tensor_copy(out=zt[0:m, :], in_=yt[bi][0:m, :]), reads=[t_y[bi]], writes=[t_z])
                else:
                    P.op("dve", lambda e, bi=bi, m=m: e.tensor_add(out=zt[0:m, :], in0=zt[0:m, :], in1=yt[bi][0:m, :]), reads=[t_y[bi], t_z], writes=[t_z])
            P.op("dve", lambda e, m=m, gi=gi: e.tensor_mul(out=zt[0:m, :], in0=zt[0:m, :], in1=cst[0:m, gi, :]), reads=[t_c, t_z], writes=[t_z])
        P.op("dve", lambda e, bi=bi, m=m: e.scalar_tensor_tensor(out=zt[0:m, :], in0=xt[bi][0:m, :], scalar=float(ALPHA), in1=zt[0:m, :],
                                                                 op0=ALU.mult, op1=ALU.add), reads=[t_x[bi], t_z], writes=[t_z])
        ln_tile(P, zt, t_z, m, cst[:, 2, :], cst[:, 3, :], t_c, st, t_st, xt[bi], t_x[bi])
        P.dma("pool", lambda e, bi=bi, t0=t0, m=m: e.dma_start(out=o_d[t0:t0 + m, :], in_=xt[bi][0:m, :]), reads=[t_x[bi]], writes=[t_x[bi]], is_out=True)
    return P


def tok_shard(lat, ctx, c):
    b, q = divmod(c, 4)
    return np.concatenate([lat[b, q * 1024:(q + 1) * 1024], ctx[b, q * 64:(q + 1) * 64]], axis=0)


def tok_unshard(res, key, width):
    lat = np.empty((B, SEQ, width), np.float32)
    ctx = np.empty((B, CTX, width), np.float32)
    for c in range(NCORES):
        b, q = divmod(c, 4)
        lat[b, q * 1024:(q + 1) * 1024] = res[c][key][:1024]
        ctx[b, q * 64:(q + 1) * 64] = res[c][key][1024:]
    return lat, ctx


def bc(v):
    return np.ascontiguousarray(np.broadcast_to(v[None, :], (128, v.shape[0])))


def stage_outproj(mix_lat, mix_ctx, x_lat, x_ctx, gate_lat, gate_ctx, gain, bias, w_out_l):
    P = build_outproj(True)
    in_maps = []
    for c in range(NCORES):
        b = c // 4
        cst = np.stack([bc(gate_lat[b]), bc(gate_ctx), bc(gain), bc(bias)], axis=0)
        in_maps.append({"x": tok_shard(x_lat, x_ctx, c), "cst": cst, "mT": fm(tok_shard(mix_lat, mix_ctx, c)),
                        "w": np.ascontiguousarray(w_out_l)})
    res = run_prog(P, in_maps)
    return tok_unshard(res, "o", D)


NTOK = CTX + SEQ
NTT = NTOK // 128
IDENT = np.eye(128, dtype=np.float32)


def rms_rstd(P, x_ap, m, width, eps, junk, t_junk, st, t_st, reads):
    P.op("act", lambda e: e.activation(out=junk[0:m, 0:width], in_=x_ap, func=AF.Square, accum_out=st[0:m, 0:1]),
         reads=reads, writes=[t_junk, t_st])
    P.op("dve", lambda e: e.tensor_scalar(out=st[0:m, 0:1], in0=st[0:m, 0:1], scalar1=1.0 / width, scalar2=eps, op0=ALU.mult, op1=ALU.add),
         reads=[t_st], writes=[t_st])
    P.op("act", lambda e: e.activation(out=st[0:m, 0:1], in_=st[0:m, 0:1], func=AF.Sqrt), reads=[t_st], writes=[t_st])
    P.op("dve", lambda e: e.reciprocal(out=st[0:m, 1:2], in_=st[0:m, 0:1]), reads=[t_st], writes=[t_st])


def build_attn():
    P = Prog()
    q_d = P.din("q", [NTOK, 256])
    k_d = P.din("k", [NTOK, 128])
    v_d = P.din("v", [NTOK, 128])
    g_d = P.din("g", [2, 128, 128])
    cs_d = P.din("cs", [2, SEQ, 128])
    id_d = P.din("ident", [128, 128])
    o_d = P.dout("o", [NTOK, 256])

    ident = P.sb([128, 128], name="ident")
    gq = P.sb([128, 2, 128], name="gq")
    t_id, t_g = Tok(), Tok()
    P.dma("sp", lambda e: e.dma_start(out=ident[:], in_=id_d), writes=[t_id])
    for i in range(2):
        P.dma("sp", lambda e, i=i: e.dma_start(out=gq[:, i, :], in_=g_d[i]), writes=[t_g])
    qT = P.sb([128, 2, NTOK], name="qT")
    kT = P.sb([128, NTOK], name="kT")
    va = P.sb([128, NTT, 129], name="va")
    t_qT, t_kT, t_va = Tok(), Tok(), Tok()
    P.op("pool", lambda e: e.memset(va[:, :, 128:129], 1.0), writes=[t_va])
    for half in range(2):
        hs = slice(half * 17, half * 17 + 17)
        P.dma("act", lambda e, hs=hs, half=half: e.dma_start(out=va[:, hs, 0:128],
              in_=v_d[half * 17 * 128:(half + 1) * 17 * 128, :].rearrange("(n p) d -> p n d", p=128)), writes=[t_va])
    NB = 2
    xin = [P.sb([128, 384], name="xin") for _ in range(NB)]
    t_xin = [Tok() for _ in range(NB)]
    cst = [P.sb([128, 2, 128], name="cs") for _ in range(NB)]
    t_cs = [Tok() for _ in range(NB)]
    junk = P.sb([128, 128], name="junk")
    t_junk = Tok()
    st = P.sb([128, 2], name="st")
    t_st = Tok()
    xn = P.sb([128, 128], name="xn")
    t_xn = Tok()
    t1 = P.sb([128, 128], name="t1")
    t_t1 = Tok()
    psT = [P.ps([128, 128], name="psT") for _ in range(2)]
    t_psT = [Tok(), Tok()]
    ip = 0
    for tt in range(NTT):
        bi = tt % NB
        t0 = tt * 128
        lat = tt >= 2
        P.dma("sp", lambda e, bi=bi, t0=t0: e.dma_start(out=xin[bi][:, 0:256], in_=q_d[t0:t0 + 128, :]), writes=[t_xin[bi]])
        P.dma("sp", lambda e, bi=bi, t0=t0: e.dma_start(out=xin[bi][:, 256:384], in_=k_d[t0:t0 + 128, :]), writes=[t_xin[bi]])
        if lat:
            for i in range(2):
                P.dma("act", lambda e, bi=bi, t0=t0, i=i: e.dma_start(out=cst[bi][:, i, :], in_=cs_d[i, t0 - CTX:t0 - CTX + 128, :]), writes=[t_cs[bi]])
        for s in range(3):
            xs = xin[bi][:, s * 128:(s + 1) * 128]
            gi = 0 if s < 2 else 1
            rms_rstd(P, xs, 128, 128, 1e-6, junk, t_junk, st, t_st, [t_xin[bi]])
            P.op("dve", lambda e, xs=xs, gi=gi: e.scalar_tensor_tensor(out=xn[:], in0=xs, scalar=st[:, 1:2], in1=gq[:, gi, :], op0=ALU.mult, op1=ALU.mult),
                 reads=[t_xin[bi], t_st, t_g], writes=[t_xn])
            src = xn
            if lat:
                P.op("dve", lambda e, bi=bi: e.tensor_mul(out=t1[:], in0=xn[:], in1=cst[bi][:, 0, :]), reads=[t_xn, t_cs[bi]], writes=[t_t1])
                for a in range(2):
                    for h in range(2):
                        o0 = a * 64 + h * 32
                        i0 = a * 64 + (1 - h) * 32
                        P.op("pool", lambda e, bi=bi, o0=o0, i0=i0: e.tensor_mul(out=junk[:, o0:o0 + 32], in0=xn[:, i0:i0 + 32], in1=cst[bi][:, 1, o0:o0 + 32]),
                             reads=[t_xn, t_cs[bi]], writes=[t_junk])
                P.op("dve", lambda e: e.tensor_add(out=t1[:], in0=t1[:], in1=junk[:]), reads=[t_junk, t_t1], writes=[t_t1])
                src = t1
            pb = ip % 2
            ip += 1
            P.op("pe", lambda e, pb=pb, src=src: e.transpose(out=psT[pb][:], in_=src[:], identity=ident[:]),
                 reads=[t_id, t_t1 if lat else t_xn], writes=[t_psT[pb]])
            if s < 2:
                P.op("act", lambda e, pb=pb, s=s, t0=t0: e.copy(out=qT[:, s, t0:t0 + 128], in_=psT[pb][:]), reads=[t_psT[pb]], writes=[t_qT])
            else:
                P.op("act", lambda e, pb=pb, t0=t0: e.copy(out=kT[:, t0:t0 + 128], in_=psT[pb][:]), reads=[t_psT[pb]], writes=[t_kT])
    NS = 2
    pss = [P.ps([128, 512], name="pss") for _ in range(NS)]
    t_pss = [Tok() for _ in range(NS)]
    NPT = 3
    pt = [P.sb([128, 512], name="pt") for _ in range(NPT)]
    t_pt = [Tok() for _ in range(NPT)]
    acc = [P.ps([128, 512], name="acc") for _ in range(4)]
    t_acc = [Tok() for _ in range(4)]
    ob = [P.sb([128, 128], name="ob") for _ in range(2)]
    t_ob = [Tok(), Tok()]
    rs = P.sb([128, 1], name="rs")
    t_rs = Tok()
    blocks = [(0, 256, 0, 2)] + [(CTX + i * 512, 512, 0, NTT) for i in range(SEQ // 512)]
    isc = 0
    ipt = 0
    iob = 0
    scale = 128 ** -0.5
    for h in range(2):
        for (q0, qw, kt0, kt1) in blocks:
            nq = qw // 128
            for kt in range(kt0, kt1):
                sb_ = isc % NS
                isc += 1
                P.op("pe", lambda e, sb_=sb_, kt=kt, h=h, q0=q0, qw=qw: e.matmul(pss[sb_][:, 0:qw], lhsT=kT[:, kt * 128:(kt + 1) * 128],
                                                                              rhs=qT[:, h, q0:q0 + qw], start=True, stop=True),
                     reads=[t_kT, t_qT], writes=[t_pss[sb_]])
                pb = ipt % NPT
                ipt += 1
                P.op("act", lambda e, sb_=sb_, pb=pb, qw=qw: e.activation(out=pt[pb][:, 0:qw], in_=pss[sb_][:, 0:qw], func=AF.Exp, scale=scale),
                     reads=[t_pss[sb_]], writes=[t_pt[pb]])
                for qi in range(nq):
                    P.op("pe", lambda e, pb=pb, qi=qi, kt=kt, kt0=kt0, kt1=kt1: e.matmul(acc[qi][:, 0:129], lhsT=pt[pb][:, qi * 128:(qi + 1) * 128],
                                                                                      rhs=va[:, kt, :], start=(kt == kt0), stop=(kt == kt1 - 1)),
                         reads=[t_pt[pb], t_va], writes=[t_acc[qi]])
            for qi in range(nq):
                P.op("dve", lambda e, qi=qi: e.reciprocal(out=rs[:], in_=acc[qi][:, 128:129]), reads=[t_acc[qi]], writes=[t_rs])
                oi = iob % 2
                iob += 1
                P.op("dve", lambda e, qi=qi, oi=oi: e.tensor_scalar(out=ob[oi][:], in0=acc[qi][:, 0:128], scalar1=rs[:], scalar2=None, op0=ALU.mult),
                     reads=[t_acc[qi], t_rs], writes=[t_ob[oi]])
                P.dma("pool", lambda e, oi=oi, q0=q0, qi=qi, h=h: e.dma_start(out=o_d[q0 + qi * 128:q0 + (qi + 1) * 128, h * 128:(h + 1) * 128], in_=ob[oi][:]),
                      reads=[t_ob[oi]], writes=[t_ob[oi]], is_out=True)
    return P


def rope_tables():
    rows = SEQ // 64
    row = np.repeat(np.arange(rows), 64).astype(np.float32)
    col = np.tile(np.arange(64), rows).astype(np.float32)
    inv = (10000.0 ** (-np.arange(0, 64, 2, dtype=np.float32) / 64)).astype(np.float32)
    ang = np.concatenate([row[:, None] * inv, col[:, None] * inv], axis=-1)
    cos, sin = np.cos(ang).astype(np.float32), np.sin(ang).astype(np.float32)
    C = np.empty((SEQ, 128), np.float32)
    S = np.empty((SEQ, 128), np.float32)
    for a in range(2):
        c_, s_ = cos[:, a * 32:(a + 1) * 32], sin[:, a * 32:(a + 1) * 32]
        C[:, a * 64:a * 64 + 32] = c_
        C[:, a * 64 + 32:a * 64 + 64] = c_
        S[:, a * 64:a * 64 + 32] = -s_
        S[:, a * 64 + 32:a * 64 + 64] = s_
    return np.stack([C, S], axis=0)


def stage_attn(p_lat, p_ctx, qk_gain_l):
    P = build_attn()
    cs = rope_tables()
    g = np.stack([bc(qk_gain_l[0]), bc(qk_gain_l[1])], axis=0)
    in_maps = []
    for c in range(NCORES):
        b, j = divmod(c, 4)
        pa = np.concatenate([p_ctx[b], p_lat[b]], axis=0)
        kv = j // 2
        in_maps.append({"q": np.ascontiguousarray(pa[:, 3632 + 256 * j:3632 + 256 * (j + 1)]),
                        "k": np.ascontiguousarray(pa[:, 4656 + 128 * kv:4656 + 128 * (kv + 1)]),
                        "v": np.ascontiguousarray(pa[:, 4912 + 128 * kv:4912 + 128 * (kv + 1)]),
                        "g": g, "cs": cs, "ident": IDENT})
    res = run_prog(P, in_maps)
    att_lat = np.empty((B, SEQ, 1024), np.float32)
    att_ctx = np.empty((B, CTX, 1024), np.float32)
    for c in range(NCORES):
        b, j = divmod(c, 4)
        att_ctx[b, :, 256 * j:256 * (j + 1)] = res[c]["o"][:CTX]
        att_lat[b, :, 256 * j:256 * (j + 1)] = res[c]["o"][CTX:]
    return att_lat, att_ctx


TRI_INC = np.triu(np.ones((128, 128), np.float32))
TRI_SUFEX = np.tril(np.ones((128, 128), np.float32), -1)
ANTI = np.ascontiguousarray(np.eye(128, dtype=np.float32)[::-1])


def orig_tile(tt):
    return (1 - tt) if tt < 2 else (35 - tt)


def build_gla():
    P = Prog()
    qT_d = P.din("qT", [2, 64, NTOK])
    kT_d = P.din("kT", [2, 64, NTOK])
    k_d = P.din("k", [2, NTOK, 64])
    v_d = P.din("v", [2, NTOK, 128])
    rT_d = P.din("rT", [2, 16, NTOK])
    w_d = P.din("w", [2, 17, 64])
    g_d = P.din("g", [NTOK, 128])
    gn_d = P.din("gn", [128, 128])
    c_d = P.din("cm", [4, 128, 128])
    o_d = P.dout("o", [NTOK, 128])

    cm = P.sb([128, 4, 128], name="cm")
    gn = P.sb([128, 128], name="gn")
    t_cm, t_gn = Tok(), Tok()
    for i in range(4):
        P.dma("sp", lambda e, i=i: e.dma_start(out=cm[:, i, :], in_=c_d[i]), writes=[t_cm])
    P.dma("sp", lambda e: e.dma_start(out=gn[:], in_=gn_d), writes=[t_gn])
    tri, sufx, anti = cm[:, 0, :], cm[:, 1, :], cm[:, 2, :]
    g = P.sb([128, NTT, 128], name="g")
    t_g = Tok()
    P.dma("act", lambda e: e.dma_start(out=g[:], in_=g_d.rearrange("(n p) d -> p n d", p=128)), writes=[t_g])
    P.op("act", lambda e: e.activation(out=g[:], in_=g[:], func=AF.Silu), reads=[t_g], writes=[t_g])
    ofw = P.sb([128, NTT, 128], name="ofw")
    t_ofw = Tok()
    qT = P.sb([64, NTOK], name="qT")
    kT = P.sb([64, NTOK], name="kT")
    rT = P.sb([17, NTOK], name="rT")
    kk = P.sb([128, NTT, 64], name="kk")
    vv = P.sb([128, NTT, 128], name="vv")
    wa = P.sb([17, 64], name="wa")
    t_in = Tok()
    P.op("dve", lambda e: e.memset(rT[:], 1.0), writes=[t_in])
    S = P.sb([64, 128], name="S")
    t_S = Tok()
    la = P.sb([128, 64], name="la")
    t_la = Tok()
    cs = P.sb([64, 128], name="cs")
    t_cs = Tok()
    sc = P.sb([64, 4], name="sc")
    t_sc = Tok()
    e1 = P.sb([64, 128], name="e1")
    e2 = P.sb([64, 128], name="e2")
    e3 = P.sb([64, 128], name="e3")
    t_e = Tok()
    k4 = P.sb([128, 64], name="k4")
    t_k4 = Tok()
    AT = P.sb([128, 128], name="AT")
    t_AT = Tok()
    ob = P.sb([128, 128], name="ob")
    t_ob = Tok()
    os_ = P.sb([128, 128], name="os")
    t_os = Tok()
    junk = P.sb([128, 128], name="junk")
    t_junk = Tok()
    st = P.sb([128, 2], name="st")
    t_st = Tok()
    ps_la = P.ps([128, 64], name="ps_la")
    ps_cT = P.ps([64, 128], name="ps_cT")
    ps_sf = P.ps([128, 64], name="ps_sf")
    ps_A = P.ps([128, 128], name="ps_A")
    ps_o = P.ps([128, 128], name="ps_o")
    ps_S = P.ps([64, 128], name="ps_S")
    ps_J = P.ps([128, 128], name="ps_J")
    t_pla, t_pcT, t_psf, t_pA, t_po, t_pS, t_pJ = [Tok() for _ in range(7)]
    for z in range(2):
        P.dma("sp", lambda e, z=z: e.dma_start(out=qT[:], in_=qT_d[z]), writes=[t_in])
        P.dma("act", lambda e, z=z: e.dma_start(out=kT[:], in_=kT_d[z]), writes=[t_in])
        P.dma("sp", lambda e, z=z: e.dma_start(out=rT[0:16, :], in_=rT_d[z]), writes=[t_in])
        P.dma("act", lambda e, z=z: e.dma_start(out=kk[:], in_=k_d[z].rearrange("(n p) d -> p n d", p=128)), writes=[t_in])
        P.dma("sp", lambda e, z=z: e.dma_start(out=vv[:], in_=v_d[z].rearrange("(n p) d -> p n d", p=128)), writes=[t_in])
        P.dma("act", lambda e, z=z: e.dma_start(out=wa[:], in_=w_d[z]), writes=[t_in])
        P.op("dve", lambda e: e.memset(S[:], 0.0), writes=[t_S])
        for tt in range(NTT):
            ts_ = slice(tt * 128, (tt + 1) * 128)
            P.op("pe", lambda e, ts_=ts_: e.matmul(ps_la[:], lhsT=rT[0:17, ts_], rhs=wa[0:17, :], start=True, stop=True), reads=[t_in], writes=[t_pla])
            P.op("act", lambda e: e.activation(out=la[:], in_=ps_la[:], func=AF.Exp, scale=-1.0), reads=[t_pla], writes=[t_la])
            P.op("dve", lambda e: e.tensor_scalar_add(out=la[:], in0=la[:], scalar1=1.0), reads=[t_la], writes=[t_la])
            P.op("act", lambda e: e.activation(out=la[:], in_=la[:], func=AF.Ln), reads=[t_la], writes=[t_la])
            P.op("dve", lambda e: e.tensor_scalar_mul(out=la[:], in0=la[:], scalar1=-1.0 / 16.0), reads=[t_la], writes=[t_la])
            P.op("pe", lambda e: e.matmul(ps_cT[:], lhsT=la[:], rhs=tri, start=True, stop=True), reads=[t_la, t_cm], writes=[t_pcT])
            P.op("pe", lambda e: e.matmul(ps_sf[:], lhsT=sufx, rhs=la[:], start=True, stop=True), reads=[t_la, t_cm], writes=[t_psf])
            P.op("act", lambda e: e.copy(out=cs[:], in_=ps_cT[:]), reads=[t_pcT], writes=[t_cs])
            P.op("dve", lambda e: e.tensor_scalar_mul(out=sc[:, 0:1], in0=cs[:, 63:64], scalar1=-1.0), reads=[t_cs], writes=[t_sc])
            P.op("act", lambda e: e.activation(out=e1[:], in_=cs[:], func=AF.Exp, bias=sc[:, 0:1], scale=1.0), reads=[t_cs, t_sc], writes=[t_e])
            P.op("act", lambda e: e.activation(out=e2[:], in_=cs[:], func=AF.Exp, bias=cs[:, 63:64], scale=-1.0), reads=[t_cs], writes=[t_e])
            P.op("act", lambda e: e.activation(out=e3[:], in_=cs[:], func=AF.Exp), reads=[t_cs], writes=[t_e])
            P.op("act", lambda e: e.activation(out=sc[:, 1:2], in_=cs[:, 127:128], func=AF.Exp), reads=[t_cs], writes=[t_sc])
            P.op("act", lambda e: e.activation(out=k4[:], in_=ps_sf[:], func=AF.Exp), reads=[t_psf], writes=[t_k4])
            P.op("dve", lambda e, ts_=ts_: e.scalar_tensor_tensor(out=e1[:], in0=qT[:, ts_], scalar=0.125, in1=e1[:], op0=ALU.mult, op1=ALU.mult),
                 reads=[t_in, t_e], writes=[t_e])
            P.op("dve", lambda e, ts_=ts_: e.tensor_mul(out=e2[:], in0=kT[:, ts_], in1=e2[:]), reads=[t_in, t_e], writes=[t_e])
            P.op("dve", lambda e, ts_=ts_: e.scalar_tensor_tensor(out=e3[:], in0=qT[:, ts_], scalar=0.125, in1=e3[:], op0=ALU.mult, op1=ALU.mult),
                 reads=[t_in, t_e], writes=[t_e])
            P.op("dve", lambda e, tt=tt: e.tensor_mul(out=k4[:], in0=kk[:, tt, :], in1=k4[:]), reads=[t_in, t_k4], writes=[t_k4])
            P.op("pe", lambda e: e.matmul(ps_A[:], lhsT=e2[:], rhs=e1[:], start=True, stop=True), reads=[t_e], writes=[t_pA])
            P.op("dve", lambda e: e.tensor_mul(out=AT[:], in0=ps_A[:], in1=tri), reads=[t_pA, t_cm], writes=[t_AT])
            P.op("pe", lambda e, tt=tt: e.matmul(ps_o[:], lhsT=AT[:], rhs=vv[:, tt, :], start=True, stop=False), reads=[t_AT, t_in], writes=[t_po])
            P.op("pe", lambda e: e.matmul(ps_o[:], lhsT=e3[:], rhs=S[:], start=False, stop=True), reads=[t_e, t_S], writes=[t_po])
            P.op("pe", lambda e, tt=tt: e.matmul(ps_S[:], lhsT=k4[:], rhs=vv[:, tt, :], start=True, stop=True), reads=[t_k4, t_in], writes=[t_pS])
            P.op("dve", lambda e: e.scalar_tensor_tensor(out=S[:], in0=S[:], scalar=sc[:, 1:2], in1=ps_S[:], op0=ALU.mult, op1=ALU.add),
                 reads=[t_sc, t_pS, t_S], writes=[t_S])
            if z == 0:
                P.op("act", lambda e, tt=tt: e.copy(out=ofw[:, tt, :], in_=ps_o[:]), reads=[t_po], writes=[t_ofw])
            else:
                oi = orig_tile(tt)
                P.op("act", lambda e: e.copy(out=ob[:], in_=ps_o[:]), reads=[t_po], writes=[t_ob])
                P.op("pe", lambda e: e.matmul(ps_J[:], lhsT=anti, rhs=ob[:], start=True, stop=True), reads=[t_ob, t_cm], writes=[t_pJ])
                P.op("dve", lambda e, oi=oi: e.tensor_add(out=os_[:], in0=ofw[:, oi, :], in1=ps_J[:]), reads=[t_pJ, t_ofw], writes=[t_os])
                rms_rstd(P, os_[:], 128, 128, 1e-6, junk, t_junk, st, t_st, [t_os])
                P.op("dve", lambda e: e.scalar_tensor_tensor(out=os_[:], in0=os_[:], scalar=st[:, 1:2], in1=gn[:], op0=ALU.mult, op1=ALU.mult),
                     reads=[t_st, t_gn, t_os], writes=[t_os])
                P.op("dve", lambda e, oi=oi: e.tensor_mul(out=os_[:], in0=os_[:], in1=g[:, oi, :]), reads=[t_g, t_os], writes=[t_os])
                P.dma("pool", lambda e, oi=oi: e.dma_start(out=o_d[oi * 128:(oi + 1) * 128, :], in_=os_[:]), reads=[t_os], writes=[t_os], is_out=True)
    return P


def flipseg(a):
    return np.concatenate([a[:CTX][::-1], a[CTX:][::-1]], axis=0)


CMATS = np.stack([TRI_INC, TRI_SUFEX, ANTI, IDENT], axis=0)


def stage_gla(p_lat, p_ctx, w_up_l, b_up_l, norm_l):
    P = build_gla()
    in_maps = []
    for c in range(NCORES):
        b, h = divmod(c, 4)
        pa = np.concatenate([p_ctx[b], p_lat[b]], axis=0)
        q = pa[:, h * 64:(h + 1) * 64]
        k = pa[:, 256 + h * 64:256 + (h + 1) * 64]
        v = pa[:, 512 + h * 128:512 + (h + 1) * 128]
        g = pa[:, 1024 + h * 128:1024 + (h + 1) * 128]
        m = {"g": np.ascontiguousarray(g), "gn": bc(norm_l), "cm": CMATS}
        qs, ks, vs, rs, ws = [], [], [], [], []
        for z in range(2):
            f = (lambda a: a) if z == 0 else flipseg
            r = pa[:, 1536 + z * 16:1536 + (z + 1) * 16]
            qs.append(f(q).T)
            ks.append(f(k))
            vs.append(f(v))
            rs.append(f(r).T)
            ws.append(np.concatenate([w_up_l[z][:, h * 64:(h + 1) * 64], b_up_l[z][None, h * 64:(h + 1) * 64]], axis=0))
        m["qT"] = np.ascontiguousarray(np.stack(qs))
        m["kT"] = np.ascontiguousarray(np.stack([a.T for a in ks]))
        m["k"] = np.ascontiguousarray(np.stack(ks))
        m["v"] = np.ascontiguousarray(np.stack(vs))
        m["rT"] = np.ascontiguousarray(np.stack(rs))
        m["w"] = np.ascontiguousarray(np.stack(ws))
        in_maps.append(m)
    res = run_prog(P, in_maps)
    o_lat = np.empty((B, SEQ, 512), np.float32)
    o_ctx = np.empty((B, CTX, 512), np.float32)
    for c in range(NCORES):
        b, h = divmod(c, 4)
        o_ctx[b, :, 128 * h:128 * (h + 1)] = res[c]["o"][:CTX]
        o_lat[b, :, 128 * h:128 * (h + 1)] = res[c]["o"][CTX:]
    return o_lat, o_ctx


UPP = TRI_INC
LOW = np.ascontiguousarray(TRI_INC.T)
GDN_CM = np.stack([UPP, LOW, UPP - IDENT, LOW - IDENT, IDENT, np.ones((128, 128), np.float32)], axis=0)


def build_gdn(dbg=None):
    P = Prog()
    x_d = P.din("xT", [3, 128, NTOK])
    cw_d = P.din("cw", [128, 3, 5])
    zg_d = P.din("zg", [NTOK, 128])
    ba_d = P.din("ba", [2, 2, 128, NTT])
    sc_d = P.din("sc", [128, 2, 2])
    gn_d = P.din("gn", [128, 128])
    c_d = P.din("cm", [6, 128, 128])
    o_d = P.dout("o", [NTOK, 128])

    cm = P.sb([128, 6, 128], name="cm")
    t_cm = Tok()
    for i in range(6):
        P.dma("sp", lambda e, i=i: e.dma_start(out=cm[:, i, :], in_=c_d[i]), writes=[t_cm])
    ident, ones = cm[:, 4, :], cm[:, 5, :]
    gn = P.sb([128, 128], name="gn")
    cw = P.sb([128, 3, 5], name="cw")
    scal = P.sb([128, 2, 2], name="scal")
    t_gn, t_cw, t_scal = Tok(), Tok(), Tok()
    P.dma("sp", lambda e: e.dma_start(out=gn[:], in_=gn_d), writes=[t_gn])
    P.dma("sp", lambda e: e.dma_start(out=cw[:], in_=cw_d), writes=[t_cw])
    P.dma("sp", lambda e: e.dma_start(out=scal[:], in_=sc_d), writes=[t_scal])
    zg = P.sb([128, NTT, 128], name="zg")
    t_zg = Tok()
    P.dma("act", lambda e: e.dma_start(out=zg[:], in_=zg_d.rearrange("(n p) d -> p n d", p=128)), writes=[t_zg])
    P.op("act", lambda e: e.activation(out=zg[:], in_=zg[:], func=AF.Silu), reads=[t_zg], writes=[t_zg])

    raw = P.sb([128, NTOK], name="raw")
    t_raw = Tok()
    fmx = [P.sb([128, NTOK], name="fmx") for _ in range(3)]
    t_fm = [Tok() for _ in range(3)]
    segs = [(0, CTX), (CTX, NTOK)]
    for s in range(3):
        P.dma("sp", lambda e, s=s: e.dma_start(out=raw[:], in_=x_d[s]), writes=[t_raw])
        acc = fmx[s]
        for (s0, s1) in segs:
            P.op("dve", lambda e, s=s, s0=s0, s1=s1, acc=acc: e.tensor_scalar(out=acc[:, s0:s1], in0=raw[:, s0:s1], scalar1=cw[:, s, 2:3], scalar2=None, op0=ALU.mult),
                 reads=[t_raw, t_cw], writes=[t_fm[s]])
            for j in (0, 1, 3, 4):
                sh = j - 2
                lo = max(s0, s0 - sh)
                hi = min(s1, s1 - sh)
                P.op("dve", lambda e, s=s, j=j, lo=lo, hi=hi, sh=sh, acc=acc: e.scalar_tensor_tensor(
                    out=acc[:, lo:hi], in0=raw[:, lo + sh:hi + sh], scalar=cw[:, s, j:j + 1], in1=acc[:, lo:hi], op0=ALU.mult, op1=ALU.add),
                    reads=[t_raw, t_cw, t_fm[s]], writes=[t_fm[s]])
        P.op("act", lambda e, acc=acc: e.activation(out=acc[:], in_=acc[:], func=AF.Silu), reads=[t_fm[s]], writes=[t_fm[s]])
    bk1 = P.ps([128, 512], name="bk1")
    t_bk1 = Tok(True)
    sq = P.sb([128, 512], name="sq")
    t_sq = Tok()
    blocks = [(0, 256)] + [(CTX + i * 512, 512) for i in range(SEQ // 512)]
    for s in range(2):
        mul = (128 ** -0.5) if s == 0 else 1.0
        for (b0, bw) in blocks:
            P.op("act", lambda e, s=s, b0=b0, bw=bw: e.activation(out=sq[:, 0:bw], in_=fmx[s][:, b0:b0 + bw], func=AF.Square), reads=[t_fm[s]], writes=[t_sq])
            P.op("pe", lambda e, bw=bw: e.matmul(bk1[:, 0:bw], lhsT=ones, rhs=sq[:, 0:bw], start=True, stop=True), reads=[t_sq, t_cm], writes=[t_bk1])
            P.op("dve", lambda e, bw=bw: e.tensor_scalar_add(out=sq[:, 0:bw], in0=bk1[:, 0:bw], scalar1=1e-6), reads=[t_bk1], writes=[t_sq])
            P.op("act", lambda e, bw=bw: e.activation(out=sq[:, 0:bw], in_=sq[:, 0:bw], func=AF.Sqrt), reads=[t_sq], writes=[t_sq])
            P.op("dve", lambda e, bw=bw: e.reciprocal(out=sq[:, 0:bw], in_=sq[:, 0:bw]), reads=[t_sq], writes=[t_sq])
            P.op("dve", lambda e, s=s, b0=b0, bw=bw, mul=mul: e.scalar_tensor_tensor(out=fmx[s][:, b0:b0 + bw], in0=fmx[s][:, b0:b0 + bw], scalar=float(mul), in1=sq[:, 0:bw],
                                                                                   op0=ALU.mult, op1=ALU.mult), reads=[t_sq, t_fm[s]], writes=[t_fm[s]])
    qnT, knT, vT = fmx
    t_qnT, t_knT, t_vT = t_fm
    kn = P.sb([128, NTT, 128], name="kn")
    vt = P.sb([128, NTT, 128], name="vt")
    t_kn, t_vt = Tok(), Tok()
    for tt in range(NTT):
        ts_ = slice(tt * 128, (tt + 1) * 128)
        P.op("pe", lambda e, ts_=ts_: e.transpose(out=bk1[:, 0:128], in_=knT[:, ts_], identity=ident), reads=[t_knT, t_cm], writes=[t_bk1])
        P.op("pe", lambda e, ts_=ts_: e.transpose(out=bk1[:, 128:256], in_=vT[:, ts_], identity=ident), reads=[t_vT, t_cm], writes=[t_bk1])
        P.op("act", lambda e, tt=tt: e.copy(out=kn[:, tt, :], in_=bk1[:, 0:128]), reads=[t_bk1], writes=[t_kn])
        P.op("dve", lambda e, tt=tt: e.tensor_copy(out=vt[:, tt, :], in_=bk1[:, 128:256]), reads=[t_bk1], writes=[t_vt])

    if dbg == "prep":
        for tt in range(NTT):
            P.dma("pool", lambda e, tt=tt: e.dma_start(out=o_d[tt * 128:(tt + 1) * 128, :], in_=kn[:, tt, :]), reads=[t_kn], is_out=True)
        return P
    ofw = P.sb([128, NTT, 128], name="ofw")
    t_ofw = Tok()
    bl = P.sb([128, 2, NTT], name="bl")
    t_bl = Tok()
    nea = P.sb([128, 1], name="nea")
    t_nea = Tok()
    S = P.sb([128, 128], name="S")
    t_S = Tok()
    bkA = bk1
    bkB = P.ps([128, 512], name="bkB")
    bkC = P.ps([128, 512], name="bkC")
    bkD = P.ps([128, 512], name="bkD")
    bkE = P.ps([128, 512], name="bkE")
    bkF = P.ps([128, 512], name="bkF")
    bkG = P.ps([128, 512], name="bkG")
    t_pg = t_G = t_bR = t_bk1
    t_KK = t_QK = Tok(True)
    t_sqL, t_sqLT, t_pX = Tok(True), Tok(True), Tok(True)
    t_pwT = t_pvn = Tok(True)
    t_po = t_pS = Tok(True)
    pg, pG, pbR = bkA[:, 0:1], bkA[:, 128:256], bkA[:, 256:384]
    pKK, pQK = bkB[:, 0:128], bkB[:, 128:256]
    psqL, psqLT = bkC[:, 0:128], bkD[:, 0:128]
    pX = bkE[:, 0:256]
    pwT, pvn = bkF[:, 0:128], bkF[:, 128:256]
    po, pS = bkG[:, 0:128], bkG[:, 128:256]

    def sbt(n, w=128):
        return P.sb([128, w], name=n), Tok()
    lgB, t_lgB = sbt("lgB")
    btB, t_btB = sbt("btB")
    col, t_col = sbt("col", 8)
    GT, t_GT = sbt("GT")
    Gm, t_Gm = sbt("Gm")
    EgR, t_EgR = sbt("EgR")
    L0, t_L0 = sbt("L0")
    gdn_pow = [sbt("pw%d" % i) for i in range(12)]
    LT1, t_LT1 = sbt("LT1")
    AqT, t_AqT = sbt("AqT")
    X, t_X = sbt("X", 256)
    wT, t_wT = sbt("wT")
    vn, t_vn = sbt("vn")
    qd, t_qd = sbt("qd")
    kd, t_kd = sbt("kd")
    os_, t_os = sbt("os")
    junk, t_junk = sbt("junk")
    st, t_st = sbt("st", 2)

    for z in range(2):
        C, CT, Cs, CTs = (cm[:, 0, :], cm[:, 1, :], cm[:, 2, :], cm[:, 3, :]) if z == 0 else (cm[:, 1, :], cm[:, 0, :], cm[:, 3, :], cm[:, 2, :])
        last = 127 if z == 0 else 0
        for i in range(2):
            P.dma("sp", lambda e, z=z, i=i: e.dma_start(out=bl[:, i, :], in_=ba_d[z, i]), writes=[t_bl])
        P.op("act", lambda e: e.activation(out=bl[:, 0, :], in_=bl[:, 0, :], func=AF.Sigmoid), reads=[t_bl], writes=[t_bl])
        P.op("act", lambda e, z=z: e.activation(out=bl[:, 1, :], in_=bl[:, 1, :], func=AF.Exp, bias=scal[:, z, 1:2], scale=1.0), reads=[t_bl, t_scal], writes=[t_bl])
        P.op("dve", lambda e: e.tensor_scalar_add(out=bl[:, 1, :], in0=bl[:, 1, :], scalar1=1.0), reads=[t_bl], writes=[t_bl])
        P.op("act", lambda e: e.activation(out=bl[:, 1, :], in_=bl[:, 1, :], func=AF.Ln), reads=[t_bl], writes=[t_bl])
        P.op("act", lambda e, z=z: e.activation(out=nea[:], in_=scal[:, z, 0:1], func=AF.Exp), reads=[t_scal], writes=[t_nea])
        P.op("dve", lambda e: e.tensor_scalar_mul(out=nea[:], in0=nea[:], scalar1=-1.0), reads=[t_nea], writes=[t_nea])
        P.op("dve", lambda e: e.tensor_scalar(out=bl[:, 1, :], in0=bl[:, 1, :], scalar1=nea[:], scalar2=None, op0=ALU.mult), reads=[t_bl, t_nea], writes=[t_bl])
        P.op("dve", lambda e: e.memset(S[:], 0.0), writes=[t_S])
        order = list(range(NTT)) if z == 0 else [1, 0] + list(range(NTT - 1, 1, -1))
        if dbg is not None and dbg.startswith("main"):
            order = order[:int(dbg[4:])]
        for tt in order:
            ts_ = slice(tt * 128, (tt + 1) * 128)
            beta = bl[:, 0, tt:tt + 1]
            lg = bl[:, 1, tt:tt + 1]
            P.op("dve", lambda e, lg=lg: e.tensor_scalar(out=lgB[:], in0=ones, scalar1=lg, scalar2=None, op0=ALU.mult), reads=[t_bl, t_cm], writes=[t_lgB])
            P.op("dve", lambda e, beta=beta: e.tensor_scalar(out=btB[:], in0=ones, scalar1=beta, scalar2=None, op0=ALU.mult), reads=[t_bl, t_cm], writes=[t_btB])
            P.op("pe", lambda e, C=C: e.matmul(bkA[:, 0:2], lhsT=C, rhs=lgB[:, 0:2], start=True, stop=True), reads=[t_lgB, t_cm], writes=[t_pg])
            P.op("pe", lambda e, C=C: e.matmul(pG, lhsT=lgB[:], rhs=C, start=True, stop=True), reads=[t_lgB, t_cm], writes=[t_G])
            P.op("pe", lambda e: e.matmul(pbR, lhsT=btB[:], rhs=ident, start=True, stop=True), reads=[t_btB, t_cm], writes=[t_bR])
            P.op("pe", lambda e, ts_=ts_: e.matmul(pKK, lhsT=knT[:, ts_], rhs=knT[:, ts_], start=True, stop=True), reads=[t_knT], writes=[t_KK])
            P.op("pe", lambda e, ts_=ts_: e.matmul(pQK, lhsT=knT[:, ts_], rhs=qnT[:, ts_], start=True, stop=True), reads=[t_knT, t_qnT], writes=[t_QK])
            P.op("act", lambda e: e.copy(out=col[:, 0:1], in_=pg), reads=[t_pg], writes=[t_col])
            P.op("dve", lambda e: e.tensor_scalar(out=GT[:], in0=pG, scalar1=col[:, 0:1], scalar2=0.0, op0=ALU.subtract, op1=ALU.min), reads=[t_G, t_col], writes=[t_GT])
            P.op("act", lambda e: e.activation(out=GT[:], in_=GT[:], func=AF.Exp), reads=[t_GT], writes=[t_GT])
            P.op("dve", lambda e: e.tensor_scalar(out=Gm[:], in0=pG, scalar1=col[:, 0:1], scalar2=0.0, op0=ALU.subtract, op1=ALU.max), reads=[t_G, t_col], writes=[t_Gm])
            P.op("act", lambda e: e.activation(out=Gm[:], in_=Gm[:], func=AF.Exp, scale=-1.0), reads=[t_Gm], writes=[t_Gm])
            P.op("act", lambda e: e.activation(out=EgR[:], in_=pG, func=AF.Exp), reads=[t_G], writes=[t_EgR])
            P.op("act", lambda e: e.activation(out=col[:, 4:5], in_=col[:, 0:1], func=AF.Exp), reads=[t_col], writes=[t_col])
            P.op("dve", lambda e, beta=beta: e.tensor_mul(out=col[:, 1:2], in0=col[:, 4:5], in1=beta), reads=[t_col, t_bl], writes=[t_col])
            P.op("dve", lambda e, last=last: e.tensor_sub(out=col[:, 5:6], in0=pG[:, last:last + 1], in1=col[:, 0:1]), reads=[t_G, t_col], writes=[t_col])
            P.op("act", lambda e: e.activation(out=col[:, 2:3], in_=col[:, 5:6], func=AF.Exp), reads=[t_col], writes=[t_col])
            P.op("act", lambda e, last=last: e.activation(out=col[:, 3:4], in_=pG[:, last:last + 1], func=AF.Exp), reads=[t_G], writes=[t_col])
            P.op("dve", lambda e, Cs=Cs: e.tensor_mul(out=LT1[:], in0=GT[:], in1=Cs), reads=[t_GT, t_cm], writes=[t_LT1])
            P.op("dve", lambda e: e.tensor_mul(out=LT1[:], in0=LT1[:], in1=pKK), reads=[t_KK, t_LT1], writes=[t_LT1])
            P.op("dve", lambda e: e.tensor_mul(out=LT1[:], in0=LT1[:], in1=pbR), reads=[t_bR, t_LT1], writes=[t_LT1])
            P.op("dve", lambda e, CTs=CTs: e.tensor_mul(out=L0[:], in0=Gm[:], in1=CTs), reads=[t_Gm, t_cm], writes=[t_L0])
            P.op("dve", lambda e, beta=beta: e.scalar_tensor_tensor(out=L0[:], in0=L0[:], scalar=beta, in1=pKK, op0=ALU.mult, op1=ALU.mult), reads=[t_KK, t_bl, t_L0], writes=[t_L0])
            P.op("dve", lambda e, C=C: e.tensor_mul(out=AqT[:], in0=GT[:], in1=C), reads=[t_GT, t_cm], writes=[t_AqT])
            P.op("dve", lambda e: e.tensor_mul(out=AqT[:], in0=AqT[:], in1=pQK), reads=[t_QK, t_AqT], writes=[t_AqT])
            pw = [(L0, t_L0, LT1, t_LT1)]
            cur_L, cur_tL, cur_LT, cur_tLT = L0, t_L0, LT1, t_LT1
            for pi in range(6):
                nL, t_nL = gdn_pow[2 * pi]
                nLT, t_nLT = gdn_pow[2 * pi + 1]
                P.op("pe", lambda e, cur_L=cur_L, cur_LT=cur_LT: e.matmul(psqL, lhsT=cur_LT[:], rhs=cur_L[:], start=True, stop=True), reads=[cur_tL, cur_tLT], writes=[t_sqL])
                P.op("pe", lambda e, cur_L=cur_L, cur_LT=cur_LT: e.matmul(psqLT, lhsT=cur_L[:], rhs=cur_LT[:], start=True, stop=True), reads=[cur_tL, cur_tLT], writes=[t_sqLT])
                P.op("act", lambda e, nL=nL: e.copy(out=nL[:], in_=psqL), reads=[t_sqL], writes=[t_nL])
                P.op("dve", lambda e, nLT=nLT: e.tensor_copy(out=nLT[:], in_=psqLT), reads=[t_sqLT], writes=[t_nLT])
                pw.append((nL, t_nL, nLT, t_nLT))
                cur_L, cur_tL, cur_LT, cur_tLT = nL, t_nL, nLT, t_nLT
            P.op("dve", lambda e, tt=tt, beta=beta: e.tensor_scalar(out=X[:, 0:128], in0=vt[:, tt, :], scalar1=beta, scalar2=None, op0=ALU.mult), reads=[t_vt, t_bl], writes=[t_X])
            P.op("dve", lambda e, tt=tt: e.tensor_scalar(out=X[:, 128:256], in0=kn[:, tt, :], scalar1=col[:, 1:2], scalar2=None, op0=ALU.mult), reads=[t_kn, t_col], writes=[t_X])
            for pi in range(6, -1, -1):
                _, _, pLT, t_pLT = pw[pi]
                P.op("pe", lambda e, pLT=pLT: e.matmul(pX, lhsT=pLT[:], rhs=X[:], start=True, stop=True), reads=[t_pLT, t_X], writes=[t_pX])
                if pi > 0:
                    P.op("dve", lambda e: e.tensor_add(out=X[:], in0=X[:], in1=pX), reads=[t_pX, t_X], writes=[t_X])
                else:
                    P.op("dve", lambda e: e.tensor_sub(out=X[:], in0=X[:], in1=pX), reads=[t_pX, t_X], writes=[t_X])
            P.op("pe", lambda e: e.transpose(out=pwT, in_=X[:, 128:256], identity=ident), reads=[t_X, t_cm], writes=[t_pwT])
            P.op("act", lambda e: e.copy(out=wT[:], in_=pwT), reads=[t_pwT], writes=[t_wT])
            P.op("pe", lambda e: e.matmul(pvn, lhsT=wT[:], rhs=S[:], start=True, stop=True), reads=[t_wT, t_S], writes=[t_pvn])
            P.op("dve", lambda e: e.tensor_sub(out=vn[:], in0=X[:, 0:128], in1=pvn), reads=[t_pvn, t_X], writes=[t_vn])
            P.op("dve", lambda e, ts_=ts_: e.tensor_mul(out=qd[:], in0=qnT[:, ts_], in1=EgR[:]), reads=[t_qnT, t_EgR], writes=[t_qd])
            P.op("pe", lambda e: e.matmul(po, lhsT=qd[:], rhs=S[:], start=True, stop=False), reads=[t_qd, t_S], writes=[t_po])
            P.op("pe", lambda e: e.matmul(po, lhsT=AqT[:], rhs=vn[:], start=False, stop=True), reads=[t_AqT, t_vn], writes=[t_po])
            P.op("dve", lambda e, tt=tt: e.tensor_scalar(out=kd[:], in0=kn[:, tt, :], scalar1=col[:, 2:3], scalar2=None, op0=ALU.mult), reads=[t_kn, t_col], writes=[t_kd])
            P.op("pe", lambda e: e.matmul(pS, lhsT=kd[:], rhs=vn[:], start=True, stop=True), reads=[t_kd, t_vn], writes=[t_pS])
            P.op("dve", lambda e: e.scalar_tensor_tensor(out=S[:], in0=S[:], scalar=col[:, 3:4], in1=pS, op0=ALU.mult, op1=ALU.add), reads=[t_col, t_pS, t_S], writes=[t_S])
            if z == 0:
                P.op("act", lambda e, tt=tt: e.copy(out=ofw[:, tt, :], in_=po), reads=[t_po], writes=[t_ofw])
            else:
                P.op("dve", lambda e, tt=tt: e.tensor_add(out=os_[:], in0=ofw[:, tt, :], in1=po), reads=[t_po, t_ofw], writes=[t_os])
                rms_rstd(P, os_[:], 128, 128, 1e-6, junk, t_junk, st, t_st, [t_os])
                P.op("dve", lambda e: e.scalar_tensor_tensor(out=os_[:], in0=os_[:], scalar=st[:, 1:2], in1=gn[:], op0=ALU.mult, op1=ALU.mult), reads=[t_st, t_gn, t_os], writes=[t_os])
                P.op("dve", lambda e, tt=tt: e.tensor_mul(out=os_[:], in0=os_[:], in1=zg[:, tt, :]), reads=[t_zg, t_os], writes=[t_os])
                P.dma("pool", lambda e, tt=tt: e.dma_start(out=o_d[tt * 128:(tt + 1) * 128, :], in_=os_[:]), reads=[t_os], writes=[t_os], is_out=True)
    return P


def stage_gdn(p_lat, p_ctx, conv_l, a_log_l, dt_bias_l, norm_l, dbg=None):
    P = build_gdn(dbg)
    in_maps = []
    for c in range(NCORES):
        b, h = divmod(c, 4)
        pa = np.concatenate([p_ctx[b], p_lat[b]], axis=0)
        hs = slice(h * 128, (h + 1) * 128)
        xT = np.stack([pa[:, 1568:2080][:, hs].T, pa[:, 2080:2592][:, hs].T, pa[:, 2592:3104][:, hs].T], axis=0)
        cw = np.stack([conv_l[:, s * 512 + h * 128:s * 512 + (h + 1) * 128].T for s in range(3)], axis=1)
        ba = np.empty((2, 2, 128, NTT), np.float32)
        sc = np.empty((128, 2, 2), np.float32)
        for z in range(2):
            ba[z, 0] = pa[:, 3616 + z * 4 + h].reshape(NTT, 128).T
            ba[z, 1] = pa[:, 3624 + z * 4 + h].reshape(NTT, 128).T
            sc[:, z, 0] = a_log_l[z, h]
            sc[:, z, 1] = dt_bias_l[z, h]
        in_maps.append({"xT": np.ascontiguousarray(xT), "cw": np.ascontiguousarray(cw), "zg": np.ascontiguousarray(pa[:, 3104:3616][:, hs]),
                        "ba": ba, "sc": sc, "gn": bc(norm_l), "cm": GDN_CM})
    res = run_prog(P, in_maps)
    o_lat = np.empty((B, SEQ, 512), np.float32)
    o_ctx = np.empty((B, CTX, 512), np.float32)
    for c in range(NCORES):
        b, h = divmod(c, 4)
        o_ctx[b, :, 128 * h:128 * (h + 1)] = res[c]["o"][:CTX]
        o_lat[b, :, 128 * h:128 * (h + 1)] = res[c]["o"][CTX:]
    return o_lat, o_ctx


def build_router():
    P = Prog()
    KC = D // 128
    xT_d = P.din("xT", [128, KC, NT_B])
    mod_d = P.din("modv", [128, KC, 4])
    r_d = P.din("rw", [D, NE]).rearrange("(k p) c -> p k c", p=128)
    hT_d = P.dout("hT", [128, KC, NT_B])
    a_d = P.dout("aff", [NT_B, NE])
    xT = P.sb([128, KC, NT_B], name="xT")
    modv = P.sb([128, KC, 4], name="modv")
    rw = P.sb([128, KC, NE], name="rw")
    t_x = [Tok() for _ in range(KC)]
    t_mod, t_rw = Tok(), Tok()
    for k in range(KC):
        P.dma("sp" if k % 2 == 0 else "act", lambda e, k=k: e.dma_start(out=xT[:, k, :], in_=xT_d[:, k, :]), writes=[t_x[k]])
    P.dma("sp", lambda e: e.dma_start(out=modv[:], in_=mod_d), writes=[t_mod])
    P.dma("sp", lambda e: e.dma_start(out=rw[:], in_=r_d), writes=[t_rw])
    P.op("dve", lambda e: e.tensor_scalar_add(out=modv[:, :, 1:2], in0=modv[:, :, 1:2], scalar1=1.0), reads=[t_mod], writes=[t_mod])
    P.op("dve", lambda e: e.tensor_scalar_add(out=modv[:, :, 3:4], in0=modv[:, :, 3:4], scalar1=1.0), reads=[t_mod], writes=[t_mod])
    for k in range(KC):
        P.op("dve", lambda e, k=k: e.tensor_scalar(out=xT[:, k, 0:1024], in0=xT[:, k, 0:1024], scalar1=modv[:, k, 1:2],
                                                   scalar2=modv[:, k, 0:1], op0=ALU.mult, op1=ALU.add),
             reads=[t_mod, t_x[k]], writes=[t_x[k]])
        P.op("dve", lambda e, k=k: e.tensor_scalar(out=xT[:, k, 1024:NT_B], in0=xT[:, k, 1024:NT_B], scalar1=modv[:, k, 3:4],
                                                   scalar2=modv[:, k, 2:3], op0=ALU.mult, op1=ALU.add),
             reads=[t_mod, t_x[k]], writes=[t_x[k]])
        P.dma("pool", lambda e, k=k: e.dma_start(out=hT_d[:, k, :], in_=xT[:, k, :]), reads=[t_x[k]], is_out=True)
    pl = [P.ps([128, NE], name="pl") for _ in range(2)]
    t_pl = [Tok(True), Tok(True)]
    ex = [P.sb([128, NE], name="ex") for _ in range(2)]
    t_ex = [Tok(), Tok()]
    st = P.sb([128, 4], name="st")
    t_st = Tok()
    tiles = [(i * 128, 128) for i in range(8)] + [(1024, 64)]
    for ti, (t0, m) in enumerate(tiles):
        bi = ti % 2
        for k in range(KC):
            P.op("pe", lambda e, bi=bi, k=k, t0=t0, m=m: e.matmul(pl[bi][0:m, :], lhsT=xT[:, k, t0:t0 + m], rhs=rw[:, k, :], start=(k == 0), stop=(k == KC - 1)),
                 reads=[t_x[k], t_rw], writes=[t_pl[bi]])
        P.op("dve", lambda e, bi=bi, m=m: e.reduce_max(out=st[0:m, 0:1], in_=pl[bi][0:m, :], axis=AX.X), reads=[t_pl[bi]], writes=[t_st])
        P.op("dve", lambda e, m=m: e.tensor_scalar_mul(out=st[0:m, 1:2], in0=st[0:m, 0:1], scalar1=-1.0), reads=[t_st], writes=[t_st])
        P.op("act", lambda e, bi=bi, m=m: e.activation(out=ex[bi][0:m, :], in_=pl[bi][0:m, :], func=AF.Exp, bias=st[0:m, 1:2], scale=1.0, accum_out=st[0:m, 2:3]),
             reads=[t_pl[bi], t_st], writes=[t_ex[bi], t_st])
        P.op("dve", lambda e, m=m: e.reciprocal(out=st[0:m, 3:4], in_=st[0:m, 2:3]), reads=[t_st], writes=[t_st])
        P.op("dve", lambda e, bi=bi, m=m: e.tensor_scalar(out=ex[bi][0:m, :], in0=ex[bi][0:m, :], scalar1=st[0:m, 3:4], scalar2=None, op0=ALU.mult),
             reads=[t_st, t_ex[bi]], writes=[t_ex[bi]])
        P.dma("pool", lambda e, bi=bi, t0=t0, m=m: e.dma_start(out=a_d[t0:t0 + m, :], in_=ex[bi][0:m, :]), reads=[t_ex[bi]], writes=[t_ex[bi]], is_out=True)
    return P


def unfm(hT):
    p, kc, T = hT.shape
    return np.ascontiguousarray(hT.transpose(2, 1, 0).reshape(T, kc * p))


def stage_router(x_lat, x_ctx, mod_lat, mod_ctx, router_l):
    P = build_router()
    in_maps = []
    for c in range(NCORES):
        b = c // 4
        mv = np.stack([mod_lat[b, 3], mod_lat[b, 4], mod_ctx[3], mod_ctx[4]], axis=-1)
        mv = np.ascontiguousarray(mv.reshape(D // 128, 128, 4).transpose(1, 0, 2))
        in_maps.append({"xT": fm(tok_shard(x_lat, x_ctx, c)), "modv": mv, "rw": np.ascontiguousarray(router_l)})
    res = run_prog(P, in_maps)
    res2 = [{"h": unfm(r["hT"]), "aff": r["aff"]} for r in res]
    h_lat, h_ctx = tok_unshard(res2, "h", D)
    a_lat, a_ctx = tok_unshard(res2, "aff", NE)
    return h_lat, h_ctx, a_lat, a_ctx


NBIS = 30
H_ROWS = B * SEQ + B * CTX


def build_select():
    P = Prog()
    a_d = P.din("A", [8, SEQ])
    cc_d = P.din("cc", [8, 1])
    tvc_d = P.din("tvc", [128, 32])
    io_d = P.din("iota", [128, 512])
    id_d = P.din("ident", [128, 128])
    idx_d = P.dout("idx", [128, 8, 4], I32)
    gate_d = P.dout("gate", [128, 8, 4])
    h_d = P.din("h", [H_ROWS, D])
    xsc_d = P.dout("xsc", [4, 544, D])
    A = P.sb([8, SEQ], name="A")
    M = P.sb([8, SEQ], name="M")
    Cm = P.sb([8, SEQ], name="Cm")
    onesr = P.sb([8, SEQ], name="onesr")
    cc = P.sb([8, 1], name="cc")
    tvc = P.sb([128, 32], name="tvc")
    iota = P.sb([128, 512], name="iota")
    ident = P.sb([128, 128], name="ident")
    t_A, t_M, t_Cm, t_on, t_cc, t_tvc, t_io, t_id = [Tok() for _ in range(8)]
    P.dma("sp", lambda e: e.dma_start(out=A[:], in_=a_d), writes=[t_A])
    P.dma("sp", lambda e: e.dma_start(out=cc[:], in_=cc_d), writes=[t_cc])
    P.dma("act", lambda e: e.dma_start(out=tvc[:], in_=tvc_d), writes=[t_tvc])
    P.dma("act", lambda e: e.dma_start(out=iota[:], in_=io_d), writes=[t_io])
    P.dma("act", lambda e: e.dma_start(out=ident[:], in_=id_d), writes=[t_id])
    P.op("pool", lambda e: e.memset(onesr[:], 1.0), writes=[t_on])
    bs = P.sb([8, 4], name="bs")
    t_bs = Tok()
    P.op("dve", lambda e: e.memset(bs[:], 0.0), writes=[t_bs])
    for k in range(1, NBIS + 1):
        w = 2.0 ** (-k)
        P.op("dve", lambda e, w=w: e.tensor_scalar_add(out=bs[:, 1:2], in0=bs[:, 0:1], scalar1=w), reads=[t_bs], writes=[t_bs])
        P.op("dve", lambda e: e.tensor_scalar(out=M[:], in0=A[:], scalar1=bs[:, 1:2], scalar2=None, op0=ALU.is_ge, op1=ALU.add, accum_out=bs[:, 2:3]),
             reads=[t_A, t_bs], writes=[t_M, t_bs])
        P.op("dve", lambda e: e.tensor_tensor(out=bs[:, 3:4], in0=bs[:, 2:3], in1=cc[:], op=ALU.is_ge), reads=[t_bs, t_cc], writes=[t_bs])
        P.op("dve", lambda e, w=w: e.scalar_tensor_tensor(out=bs[:, 0:1], in0=bs[:, 3:4], scalar=w, in1=bs[:, 0:1], op0=ALU.mult, op1=ALU.add),
             reads=[t_bs], writes=[t_bs])
    P.op("dve", lambda e: e.tensor_scalar(out=M[:], in0=A[:], scalar1=bs[:, 0:1], scalar2=None, op0=ALU.is_ge), reads=[t_A, t_bs], writes=[t_M])
    P.op("dve", lambda e: e.tensor_tensor_scan(out=Cm[:], data0=onesr[:], data1=M[:], initial=0.0, op0=ALU.mult, op1=ALU.add),
         reads=[t_on, t_M], writes=[t_Cm])
    P.op("dve", lambda e: e.tensor_sub(out=Cm[:], in0=Cm[:], in1=M[:]), reads=[t_M, t_Cm], writes=[t_Cm])
    T3 = P.sb([128, 32, 24], name="T3")
    t_T3 = Tok()
    pT = [P.ps([128, 32], name="pT") for _ in range(2)]
    t_pT = [Tok(True), Tok(True)]
    for j in range(32):
        bi = j % 2
        ts_ = slice(j * 128, (j + 1) * 128)
        for i, (src, tk) in enumerate(((A, t_A), (M, t_M), (Cm, t_Cm))):
            P.op("pe", lambda e, bi=bi, i=i, src=src, ts_=ts_: e.transpose(out=pT[bi][:, i * 8:(i + 1) * 8], in_=src[0:8, ts_], identity=ident[0:8, 0:8]),
                 reads=[tk, t_id], writes=[t_pT[bi]])
        P.op("dve" if bi == 0 else "act", (lambda e, bi=bi, j=j: e.tensor_copy(out=T3[:, j, :], in_=pT[bi][:, 0:24])) if bi == 0 else
             (lambda e, bi=bi, j=j: e.copy(out=T3[:, j, :], in_=pT[bi][:, 0:24])), reads=[t_pT[bi]], writes=[t_T3])
    TV = P.sb([128, 32, 8, 2], name="TV")
    t_TV = Tok()
    for r in range(8):
        P.op("dve", lambda e, r=r: e.tensor_copy(out=TV[:, :, r, 0], in_=tvc[:]), reads=[t_tvc], writes=[t_TV])
    P.op("dve", lambda e: e.tensor_copy(out=TV[:, :, :, 1], in_=T3[:, :, 0:8]), reads=[t_T3], writes=[t_TV])
    Pm = P.sb([128, 32, 512], name="Pm")
    t_Pm = Tok()
    pi_ = P.ps([128, 64], name="pi")
    t_pi = Tok(True)
    res_i = P.sb([128, 8, 4], I32, name="res_i")
    res_f = P.sb([128, 8, 4], name="res_f")
    res_g = P.sb([128, 8, 4], name="res_g")
    t_res = Tok()
    P.op("dve", lambda e: e.memset(res_f[:], 0.0), writes=[t_res])
    P.op("dve", lambda e: e.memset(res_g[:], 0.0), writes=[t_res])
    for r in range(8):
        lat = r < 4
        C = 512 if lat else 32
        nj = 32 if lat else 2
        b = r % 2
        base = float(b * SEQ) if lat else float(B * SEQ + b * CTX)
        for j in range(nj):
            P.op("dve", lambda e, j=j, r=r, C=C: e.tensor_scalar(out=Pm[:, j, 0:C], in0=iota[:, 0:C], scalar1=T3[:, j, 16 + r:17 + r],
                                                                 scalar2=T3[:, j, 8 + r:9 + r], op0=ALU.is_equal, op1=ALU.mult),
                 reads=[t_io, t_T3], writes=[t_Pm])
        for sc in range(4 if lat else 1):
            msz = 128 if lat else 32
            for j in range(nj):
                P.op("pe", lambda e, r=r, sc=sc, j=j, msz=msz, nj=nj: e.matmul(pi_[0:msz, (r * 4 + sc) * 2:(r * 4 + sc) * 2 + 2],
                                                                            lhsT=Pm[:, j, sc * 128:sc * 128 + msz], rhs=TV[:, j, r, :],
                                                                            start=(j == 0), stop=(j == nj - 1)),
                     reads=[t_Pm, t_TV], writes=[t_pi])
            P.op("dve", lambda e, r=r, sc=sc, msz=msz, base=base: e.tensor_scalar_add(out=res_f[0:msz, r, sc:sc + 1],
                                                                                   in0=pi_[0:msz, (r * 4 + sc) * 2:(r * 4 + sc) * 2 + 1], scalar1=base),
                 reads=[t_pi], writes=[t_res])
            P.op("dve", lambda e, r=r, sc=sc, msz=msz: e.tensor_copy(out=res_g[0:msz, r, sc:sc + 1], in_=pi_[0:msz, (r * 4 + sc) * 2 + 1:(r * 4 + sc) * 2 + 2]),
                 reads=[t_pi], writes=[t_res])
    P.op("dve", lambda e: e.tensor_copy(out=res_i[:], in_=res_f[:]), reads=[t_res], writes=[t_res])
    P.dma("pool", lambda e: e.dma_start(out=idx_d, in_=res_i[:]), reads=[t_res], is_out=True)
    P.dma("pool", lambda e: e.dma_start(out=gate_d, in_=res_g[:]), reads=[t_res], is_out=True)
    xs = [P.sb([128, D], name="xs") for _ in range(2)]
    t_xs = [Tok(), Tok()]
    ixs = 0
    for el in range(2):
        for b in range(B):
            pas = el * 2 + b
            for (r, sc, s0, m) in [(el * 2 + b, sc, sc * 128, 128) for sc in range(4)] + [(4 + el * 2 + b, 0, 512, 32)]:
                xb = ixs % 2
                ixs += 1
                P.dma("pool", lambda e, xb=xb, r=r, sc=sc, m=m: e.indirect_dma_start(
                    out=xs[xb][0:m, :], out_offset=None, in_=h_d[:, :], in_offset=bass.IndirectOffsetOnAxis(ap=res_i[0:m, r, sc:sc + 1], axis=0)),
                    reads=[t_res], writes=[t_xs[xb]])
                P.dma("sp", lambda e, xb=xb, pas=pas, s0=s0, m=m: e.dma_start(out=xsc_d[pas, s0:s0 + m, :], in_=xs[xb][0:m, :]),
                      reads=[t_xs[xb]], writes=[t_xs[xb]], is_out=True)
    return P


TVC = (np.arange(32)[None, :] * 128 + np.arange(128)[:, None]).astype(np.float32)
IOTA512 = np.ascontiguousarray(np.broadcast_to(np.arange(512, dtype=np.float32)[None, :], (128, 512)))
CCOL = np.array([512] * 4 + [32] * 4, np.float32)[:, None]


def stage_select(a_lat, a_ctx, h_lat, h_ctx):
    P = build_select()
    h_all = np.ascontiguousarray(np.concatenate([h_lat.reshape(B * SEQ, D), h_ctx.reshape(B * CTX, D)], axis=0))
    in_maps = []
    for c in range(NCORES):
        A = np.full((8, SEQ), -1.0, np.float32)
        for el in range(2):
            for b in range(B):
                A[el * 2 + b] = a_lat[b, :, 2 * c + el]
                A[4 + el * 2 + b, :CTX] = a_ctx[b, :, 2 * c + el]
        in_maps.append({"A": A, "cc": CCOL, "tvc": TVC, "iota": IOTA512, "ident": IDENT, "h": h_all})
    res = run_prog(P, in_maps)
    return [(r["idx"], r["gate"], r["xsc"]) for r in res]


def build_expert(els=(0, 1), bs=(0, 1)):
    P = Prog()
    KC = D // 128
    h_d = P.din("xsc", [4, 544, D])
    idx_d = P.din("idx", [128, 8, 4], I32)
    gate_d = P.din("gate", [128, 8, 4])
    w1_d = P.din("w1", [len(els), D, FF])
    w3_d = P.din("w3", [len(els), D, FF])
    w2_d = P.din("w2", [len(els), FF, D])
    id_d = P.din("ident", [128, 128])
    f_d = [P.dout("f%d" % dc, [H_ROWS, 512]) for dc in range(4)]
    ident = P.sb([128, 128], name="ident")
    idx = P.sb([128, 8, 4], I32, name="idx")
    gate = P.sb([128, 8, 4], name="gate")
    t_id, t_idx, t_gate = Tok(), Tok(), Tok()
    P.dma("sp", lambda e: e.dma_start(out=ident[:], in_=id_d), writes=[t_id])
    P.dma("sp", lambda e: e.dma_start(out=idx[:], in_=idx_d), writes=[t_idx])
    P.dma("sp", lambda e: e.dma_start(out=gate[:], in_=gate_d), writes=[t_gate])
    zt = P.sb([128, 2048], name="zt")
    t_z = Tok()
    t_f = Tok()
    P.op("dve", lambda e: e.memset(zt[:], 0.0), writes=[t_z])
    for dc in range(4):
        for r0 in range(0, H_ROWS, 512):
            P.dma("sp" if (r0 // 512) % 2 == 0 else "act",
                  lambda e, dc=dc, r0=r0: e.dma_start(out=f_d[dc][r0:r0 + 512, :].rearrange("(p n) c -> p n c", p=128), in_=zt[:].rearrange("p (n c) -> p n c", n=4)),
                  reads=[t_z], writes=[t_f], is_out=True)
    NSL = 544
    xs = [P.sb([128, D], name="xs") for _ in range(2)]
    t_xs = [Tok(), Tok()]
    xsT = P.sb([128, KC, NSL], name="xsT")
    t_xsT = Tok()
    hT = P.sb([128, KC, NSL], name="hT")
    t_hT = Tok()
    wa = [P.sb([128, KC, 128], name="w1c") for _ in range(2)]
    wu = [P.sb([128, KC, 128], name="w3c") for _ in range(2)]
    t_wa = [Tok(), Tok()]
    t_wu = [Tok(), Tok()]
    w2c = [P.sb([128, KC, 512], name="w2c") for _ in range(2)]
    t_w2 = [Tok(), Tok()]
    tmp = [P.sb([128, 512], name="tmp") for _ in range(2)]
    t_tmp = [Tok(), Tok()]
    yb = [P.sb([128, 512], name="yb") for _ in range(2)]
    t_yb = [Tok(), Tok()]
    pT = [P.ps([128, 512], name="pT") for _ in range(2)]
    t_pT = [Tok(True), Tok(True)]
    pa = [P.ps([128, 512], name="pa") for _ in range(2)]
    t_pa = [Tok(True), Tok(True)]
    pu = [P.ps([128, 512], name="pu") for _ in range(2)]
    t_pu = [Tok(True), Tok(True)]
    py = [P.ps([128, 512], name="py") for _ in range(2)]
    t_py = [Tok(True), Tok(True)]
    ixs = 0
    ipt = 0
    iw = 0
    iw2 = 0
    iau = 0
    iy = 0
    for eli, el in enumerate(els):
        for b in bs:
            chunks = [(el * 2 + b, sc, sc * 128, 128) for sc in range(4)] + [(4 + el * 2 + b, 0, 512, 32)]
            groups = [(0, 512), (512, 32)]
            for (r, sc, s0, m) in chunks:
                xb = ixs % 2
                ixs += 1
                P.dma("sp", lambda e, xb=xb, el=el, b=b, s0=s0, m=m: e.dma_start(out=xs[xb][0:m, :], in_=h_d[el * 2 + b, s0:s0 + m, :]),
                      writes=[t_xs[xb]])
                for k4 in range(KC // 4):
                    pb = ipt % 2
                    ipt += 1
                    for kk in range(4):
                        k = k4 * 4 + kk
                        P.op("pe", lambda e, pb=pb, kk=kk, xb=xb, k=k, m=m: e.transpose(out=pT[pb][:, kk * 128:kk * 128 + m], in_=xs[xb][0:m, k * 128:(k + 1) * 128],
                                                                                   identity=ident[0:m, 0:m]),
                             reads=[t_xs[xb], t_id], writes=[t_pT[pb]])
                    if pb == 0:
                        P.op("act", lambda e, pb=pb, k4=k4, s0=s0, m=m: e.copy(out=xsT[:, k4 * 4:k4 * 4 + 4, s0:s0 + m],
                                                                             in_=pT[pb][:].rearrange("p (a c) -> p a c", a=4)[:, :, 0:m]),
                             reads=[t_pT[pb]], writes=[t_xsT])
                    else:
                        P.op("dve", lambda e, pb=pb, k4=k4, s0=s0, m=m: e.tensor_copy(out=xsT[:, k4 * 4:k4 * 4 + 4, s0:s0 + m],
                                                                                    in_=pT[pb][:].rearrange("p (a c) -> p a c", a=4)[:, :, 0:m]),
                             reads=[t_pT[pb]], writes=[t_xsT])
            for fc in range(KC):
                wb = iw % 2
                iw += 1
                P.dma("sp", lambda e, wb=wb, eli=eli, fc=fc: e.dma_start(out=wa[wb][:], in_=w1_d[eli, :, fc * 128:(fc + 1) * 128].rearrange("(k p) c -> p k c", p=128)),
                      writes=[t_wa[wb]])
                P.dma("act", lambda e, wb=wb, eli=eli, fc=fc: e.dma_start(out=wu[wb][:], in_=w3_d[eli, :, fc * 128:(fc + 1) * 128].rearrange("(k p) c -> p k c", p=128)),
                      writes=[t_wu[wb]])
                for (g0, gw) in groups:
                    ab = iau % 2
                    iau += 1
                    for k in range(KC):
                        P.op("pe", lambda e, ab=ab, wb=wb, k=k, g0=g0, gw=gw: e.matmul(pa[ab][:, 0:gw], lhsT=wa[wb][:, k, :], rhs=xsT[:, k, g0:g0 + gw],
                                                                                    start=(k == 0), stop=(k == KC - 1)),
                             reads=[t_wa[wb], t_xsT], writes=[t_pa[ab]])
                    for k in range(KC):
                        P.op("pe", lambda e, ab=ab, wb=wb, k=k, g0=g0, gw=gw: e.matmul(pu[ab][:, 0:gw], lhsT=wu[wb][:, k, :], rhs=xsT[:, k, g0:g0 + gw],
                                                                                    start=(k == 0), stop=(k == KC - 1)),
                             reads=[t_wu[wb], t_xsT], writes=[t_pu[ab]])
                    P.op("act", lambda e, ab=ab, gw=gw: e.activation(out=tmp[ab][:, 0:gw], in_=pa[ab][:, 0:gw], func=AF.Silu), reads=[t_pa[ab]], writes=[t_tmp[ab]])
                    P.op("dve", lambda e, ab=ab, fc=fc, g0=g0, gw=gw: e.tensor_mul(out=hT[:, fc, g0:g0 + gw], in0=tmp[ab][:, 0:gw], in1=pu[ab][:, 0:gw]),
                         reads=[t_tmp[ab], t_pu[ab]], writes=[t_hT])
            for dc in range(4):
                w2b = iw2 % 2
                iw2 += 1
                for half in range(2):
                    ks = slice(half * 8, half * 8 + 8)
                    P.dma("sp" if half == 0 else "act", lambda e, w2b=w2b, eli=eli, dc=dc, ks=ks: e.dma_start(
                        out=w2c[w2b][:, ks, :], in_=w2_d[eli, :, dc * 512:(dc + 1) * 512].rearrange("(k p) c -> p k c", p=128)[:, ks, :]), writes=[t_w2[w2b]])
                for (r, sc, s0, m) in chunks:
                    yi = iy % 2
                    iy += 1
                    for fc in range(KC):
                        P.op("pe", lambda e, yi=yi, fc=fc, s0=s0, m=m, w2b=w2b: e.matmul(py[yi][0:m, :], lhsT=hT[:, fc, s0:s0 + m], rhs=w2c[w2b][:, fc, :],
                                                                                      start=(fc == 0), stop=(fc == KC - 1)),
                             reads=[t_hT, t_w2[w2b]], writes=[t_py[yi]])
                    P.op("dve", lambda e, yi=yi, r=r, sc=sc, m=m: e.tensor_scalar(out=yb[yi][0:m, :], in0=py[yi][0:m, :], scalar1=gate[0:m, r, sc:sc + 1], scalar2=None, op0=ALU.mult),
                         reads=[t_py[yi], t_gate], writes=[t_yb[yi]])
                    P.dma("pool", lambda e, yi=yi, dc=dc, r=r, sc=sc, m=m: e.indirect_dma_start(
                        out=f_d[dc][:, :], out_offset=bass.IndirectOffsetOnAxis(ap=idx[0:m, r, sc:sc + 1], axis=0), in_=yb[yi][0:m, :], in_offset=None, compute_op=ALU.add),
                        reads=[t_idx, t_yb[yi]], writes=[t_f, t_yb[yi]], is_out=True)
    return P


def stage_expert(sel, w1_l, w3_l, w2_l, els=(0, 1), bs=(0, 1)):
    P = build_expert(els, bs)
    in_maps = []
    for c in range(NCORES):
        in_maps.append({"xsc": sel[c][2], "idx": sel[c][0], "gate": sel[c][1], "w1": np.ascontiguousarray(w1_l[[2 * c + e for e in els]]),
                        "w3": np.ascontiguousarray(w3_l[[2 * c + e for e in els]]), "w2": np.ascontiguousarray(w2_l[[2 * c + e for e in els]]), "ident": IDENT})
    res = run_prog(P, in_maps)
    return [np.stack([r["f%d" % dc] for dc in range(4)]) for r in res]


def stage_final(fparts, x_lat, x_ctx, gate_lat, gate_ctx, gain, bias):
    P = build_outproj(False, NCORES)
    in_maps = []
    for c in range(NCORES):
        b, q = divmod(c, 4)
        ys = []
        for fp in fparts:
            lat = fp[:, b * SEQ + q * 1024:b * SEQ + (q + 1) * 1024, :]
            ctx = fp[:, B * SEQ + b * CTX + q * 64:B * SEQ + b * CTX + (q + 1) * 64, :]
            y = np.concatenate([lat, ctx], axis=1)
            ys.append(y.transpose(1, 0, 2).reshape(NT_B, D))
        cst = np.stack([bc(gate_lat[b]), bc(gate_ctx), bc(gain), bc(bias)], axis=0)
        in_maps.append({"x": tok_shard(x_lat, x_ctx, c), "cst": cst, "y": np.ascontiguousarray(np.stack(ys))})
    res = run_prog(P, in_maps)
    return tok_unshard(res, "o", D)


def kernel(x, c, ctx, c_ctx, w_ada, b_ada, w_in, w_out, gla_w_up, gla_b_up, gla_norm,
           gdn_conv, gdn_a_log, gdn_dt_bias, gdn_norm, attn_qk_norm, ln_gain, ln_bias,
           router, w1, w3, w2):
    f = lambda a: np.asarray(a, dtype=np.float32)
    x, c, ctx, c_ctx = f(x), f(c), f(ctx), f(c_ctx)
    w_ada, b_ada, w_in, w_out = f(w_ada), f(b_ada), f(w_in), f(w_out)
    gla_w_up, gla_b_up, gla_norm = f(gla_w_up), f(gla_b_up), f(gla_norm)
    gdn_conv, gdn_a_log, gdn_dt_bias, gdn_norm = f(gdn_conv), f(gdn_a_log), f(gdn_dt_bias), f(gdn_norm)
    attn_qk_norm, ln_gain, ln_bias, router = f(attn_qk_norm), f(ln_gain), f(ln_bias), f(router)
    w1, w3, w2 = f(w1), f(w3), f(w2)
    mod_lat, mod_ctx = stage_mod(c, c_ctx, w_ada, b_ada)
    x_lat, x_ctx = x, ctx
    for l in range(DEPTH):
        p_lat, p_ctx = stage_inproj(x_lat, x_ctx, mod_lat[l], mod_ctx[l], w_in[l])
        gla_l, gla_c = stage_gla(p_lat, p_ctx, gla_w_up[l], gla_b_up[l], gla_norm[l])
        gdn_l, gdn_c = stage_gdn(p_lat, p_ctx, gdn_conv[l], gdn_a_log[l], gdn_dt_bias[l], gdn_norm[l])
        att_l, att_c = stage_attn(p_lat, p_ctx, attn_qk_norm[l])
        mix_l = np.concatenate([gla_l, gdn_l, att_l], axis=-1)
        mix_c = np.concatenate([gla_c, gdn_c, att_c], axis=-1)
        x_lat, x_ctx = stage_outproj(mix_l, mix_c, x_lat, x_ctx, mod_lat[l][:, 2], mod_ctx[l][2], ln_gain[l, 0], ln_bias[l, 0], w_out[l])
        h_lat, h_ctx, a_lat, a_ctx = stage_router(x_lat, x_ctx, mod_lat[l], mod_ctx[l], router[l])
        sel = stage_select(a_lat, a_ctx, h_lat, h_ctx)
        fparts = stage_expert(sel, w1[l], w3[l], w2[l])
        x_lat, x_ctx = stage_final(fparts, x_lat, x_ctx, mod_lat[l][:, 5], mod_ctx[l][5], ln_gain[l, 1], ln_bias[l, 1])
    return np.ascontiguousarray(x_lat, dtype=np.float32)
```

```python
import os
import time
import numpy as np
import concourse.bass as bass
import concourse.mybir as mybir
from concourse.bass_utils import run_bass_kernel_spmd

F32 = mybir.dt.float32
BF16 = mybir.dt.bfloat16
U32 = mybir.dt.uint32
I32 = mybir.dt.int32
AF = mybir.ActivationFunctionType
ALU = mybir.AluOpType
AX = mybir.AxisListType

NCORES = 8
D = 2048
B = 2
SEQ = 4096
CTX = 256
DEPTH = 2
NE = 16
FF = 2048
IN_W = 5168
ALPHA = (2 * DEPTH) ** 0.25


class Tok:
    __slots__ = ("w", "r", "excl")

    def __init__(self, excl=False):
        self.w = None
        self.r = []
        self.excl = excl


class Prog:
    CE = ("act", "pe", "dve", "pool")
    DQ = ("sp", "act", "pool")
    ND = 6

    def __init__(self):
        self.nc = bass.Bass("TRN2", target_bir_lowering=False)
        nc = self.nc
        self.q = {e: [] for e in ("sp", "act", "pe", "dve", "pool")}
        self.csem = {e: nc.alloc_semaphore("c_" + e) for e in self.CE}
        self.ccnt = {e: 0 for e in self.CE}
        self.dsem = {e: [nc.alloc_semaphore("d_%s%d" % (e, i)) for i in range(self.ND)] for e in self.DQ}
        self.dcnt = {e: [0] * self.ND for e in self.DQ}
        self.drr = {e: 0 for e in self.DQ}
        self.waited = {e: {} for e in self.q}
        self.out_deps = []
        self.nm = 0

    def name(self, p):
        self.nm += 1
        return "%s_%d" % (p, self.nm)

    def sb(self, shape, dt=F32, name="sb"):
        return self.nc.alloc_sbuf_tensor(self.name(name), list(shape), dt)

    def ps(self, shape, dt=F32, name="ps"):
        return self.nc.alloc_psum_tensor(self.name(name), list(shape), dt)

    def din(self, name, shape, dt=F32):
        return self.nc.dram_tensor(name, list(shape), dt, kind="ExternalInput").ap()

    def dout(self, name, shape, dt=F32):
        return self.nc.dram_tensor(name, list(shape), dt, kind="ExternalOutput").ap()

    def dscratch(self, name, shape, dt=F32):
        return self.nc.dram_tensor(name, list(shape), dt, kind="Internal").ap()

    def _deps(self, eng, reads, writes, extra=()):
        need = {}

        def add(dep):
            if dep is None:
                return
            s, v = dep
            if need.get(s, 0) < v:
                need[s] = v

        own = self.csem.get(eng)
        for t in reads:
            add(t.w)
            if t.excl:
                for r in t.r:
                    if r[0] is not own:
                        add(r)
        for t in writes:
            add(t.w)
            for r in t.r:
                add(r)
        for d in extra:
            add(d)
        if eng == "pe":
            need.pop(self.csem["pe"], None)
        out = []
        wd = self.waited[eng]
        for s, v in need.items():
            if wd.get(s, 0) >= v:
                continue
            wd[s] = v
            out.append((s, v))
        return out

    def _mark(self, reads, writes, done):
        for t in reads:
            t.r.append(done)
        for t in writes:
            t.w = done
            t.r = []

    def op(self, eng, fn, reads=(), writes=()):
        waits = self._deps(eng, reads, writes)
        self.ccnt[eng] += 1
        done = (self.csem[eng], self.ccnt[eng])
        self.q[eng].append((waits, fn, self.csem[eng], 1))
        self._mark(reads, writes, done)
        return done

    def dma(self, eng, fn, reads=(), writes=(), is_out=False):
        k = self.drr[eng]
        self.drr[eng] = (k + 1) % self.ND
        sem = self.dsem[eng][k]
        prev = (sem, self.dcnt[eng][k]) if self.dcnt[eng][k] else None
        waits = self._deps(eng, reads, writes, extra=(prev,) if prev else ())
        self.dcnt[eng][k] += 16
        done = (sem, self.dcnt[eng][k])
        self.q[eng].append((waits, fn, sem, 16))
        self._mark(reads, writes, done)
        if is_out:
            self.out_deps.append(done)
        return done

    def finish(self):
        nc = self.nc
        fin = {}
        for s, v in self.out_deps:
            fin[s] = max(fin.get(s, 0), v)
        q = self.q
        engmap = {"sp": "sync", "act": "scalar", "pe": "tensor", "dve": "vector", "pool": "gpsimd"}
        with nc.Block() as block:
            for e, bn in engmap.items():
                def body(eng, e=e):
                    for waits, fn, sem, inc in q[e]:
                        for s, v in waits:
                            eng.wait_ge(s, v)
                        fn(eng).then_inc(sem, inc)
                    if e == "pool":
                        for s, v in fin.items():
                            eng.wait_ge(s, v)
                getattr(block, bn)(body)
        return nc


def run_prog(P, in_maps):
    t0 = time.time()
    nc = P.finish()
    t1 = time.time()
    res = run_bass_kernel_spmd(nc, in_maps, core_ids=list(range(NCORES)))
    if os.environ.get("KDBG"):
        nb = sum(v.nbytes for m in in_maps for v in m.values())
        print("[run_prog] build %.1fs run %.1fs in %.0fMB" % (t1 - t0, time.time() - t1, nb / 1e6), flush=True)
    return res.results


def fm(a):
    T, C = a.shape
    return np.ascontiguousarray(a.T.reshape(C // 128, 128, T).transpose(1, 0, 2))


NT_B = 1088


def build_inproj():
    P = Prog()
    nc = P.nc
    KC = D // 128
    xT_d = P.din("xT", [128, KC, NT_B])
    mod_d = P.din("modv", [128, KC, 4])
    w_d = P.din("w", [D, IN_W]).rearrange("(k p) c -> p k c", p=128)
    p_d = P.dout("p", [NT_B, IN_W])

    xT = P.sb([128, KC, NT_B], name="xT")
    xb = P.sb([128, KC, NT_B], BF16, name="xb")
    modv = P.sb([128, KC, 4], name="modv")
    t_x = [Tok() for _ in range(KC)]
    t_mod = Tok()
    for k in range(KC):
        P.dma("sp" if k % 2 == 0 else "act", lambda e, k=k: e.dma_start(out=xT[:, k, :], in_=xT_d[:, k, :]), writes=[t_x[k]])
    P.dma("sp", lambda e: e.dma_start(out=modv[:], in_=mod_d), writes=[t_mod])
    P.op("dve", lambda e: e.tensor_scalar_add(out=modv[:, :, 1:2], in0=modv[:, :, 1:2], scalar1=1.0), reads=[t_mod], writes=[t_mod])
    P.op("dve", lambda e: e.tensor_scalar_add(out=modv[:, :, 3:4], in0=modv[:, :, 3:4], scalar1=1.0), reads=[t_mod], writes=[t_mod])
    for k in range(KC):
        P.op("dve", lambda e, k=k: e.tensor_scalar(out=xb[:, k, 0:1024], in0=xT[:, k, 0:1024], scalar1=modv[:, k, 1:2],
                                                   scalar2=modv[:, k, 0:1], op0=ALU.mult, op1=ALU.add),
             reads=[t_mod, t_x[k]], writes=[t_x[k]])
        P.op("dve", lambda e, k=k: e.tensor_scalar(out=xb[:, k, 1024:NT_B], in0=xT[:, k, 1024:NT_B], scalar1=modv[:, k, 3:4],
                                                   scalar2=modv[:, k, 2:3], op0=ALU.mult, op1=ALU.add),
             reads=[t_mod, t_x[k]], writes=[t_x[k]])
    NW = 3
    wt = [P.sb([128, KC, 512], BF16, name="wt") for _ in range(NW)]
    t_w = [Tok() for _ in range(NW)]
    NPS = 4
    pst = [P.ps([128, 512], name="pp") for _ in range(NPS)]
    t_ps = [Tok() for _ in range(NPS)]
    ot = [P.sb([128, 512], name="ot") for _ in range(NPS)]
    t_ot = [Tok() for _ in range(NPS)]
    tiles = [(i * 128, 128) for i in range(8)] + [(1024, 64)]
    cgs = [(c, min(512, IN_W - c)) for c in range(0, IN_W, 512)]
    it = 0
    for ci, (c0, cw) in enumerate(cgs):
        wb = ci % NW
        for half in range(2):
            ks = slice(half * 8, half * 8 + 8)
            P.dma("pool",
                  lambda e, wb=wb, ks=ks, c0=c0, cw=cw: e.dma_start(out=wt[wb][:, ks, 0:cw], in_=w_d[:, ks, c0:c0 + cw]),
                  writes=[t_w[wb]])
        for (t0, m) in tiles:
            pb = it % NPS
            it += 1
            for k in range(KC):
                P.op("pe", lambda e, pb=pb, k=k, t0=t0, m=m, wb=wb, cw=cw: e.matmul(
                    pst[pb][0:m, 0:cw], lhsT=xb[:, k, t0:t0 + m], rhs=wt[wb][:, k, 0:cw], start=(k == 0), stop=(k == KC - 1)),
                    reads=[t_x[k], t_w[wb]], writes=[t_ps[pb]])
            ev = "act" if pb % 2 == 0 else "dve"
            if ev == "act":
                P.op("act", lambda e, pb=pb, m=m, cw=cw: e.copy(out=ot[pb][0:m, 0:cw], in_=pst[pb][0:m, 0:cw]),
                     reads=[t_ps[pb]], writes=[t_ot[pb]])
            else:
                P.op("dve", lambda e, pb=pb, m=m, cw=cw: e.tensor_copy(out=ot[pb][0:m, 0:cw], in_=pst[pb][0:m, 0:cw]),
                     reads=[t_ps[pb]], writes=[t_ot[pb]])
            P.dma("sp" if pb % 2 == 0 else "act", lambda e, pb=pb, t0=t0, m=m, c0=c0, cw=cw: e.dma_start(out=p_d[t0:t0 + m, c0:c0 + cw], in_=ot[pb][0:m, 0:cw]),
                  reads=[t_ot[pb]], is_out=True)
    return P


def stage_inproj(x_lat, x_ctx, mod_lat, mod_ctx, w_in_l):
    P = build_inproj()
    in_maps = []
    for c in range(NCORES):
        b, q = divmod(c, 4)
        xs = np.concatenate([x_lat[b, q * 1024:(q + 1) * 1024], x_ctx[b, q * 64:(q + 1) * 64]], axis=0)
        mv = np.stack([mod_lat[b, 0], mod_lat[b, 1], mod_ctx[0], mod_ctx[1]], axis=-1)
        mv = np.ascontiguousarray(mv.reshape(D // 128, 128, 4).transpose(1, 0, 2))
        in_maps.append({"xT": fm(xs), "modv": mv, "w": np.ascontiguousarray(w_in_l)})
    res = run_prog(P, in_maps)
    p_lat = np.empty((B, SEQ, IN_W), np.float32)
    p_ctx = np.empty((B, CTX, IN_W), np.float32)
    for c in range(NCORES):
        b, q = divmod(c, 4)
        p_lat[b, q * 1024:(q + 1) * 1024] = res[c]["p"][:1024]
        p_ctx[b, q * 64:(q + 1) * 64] = res[c]["p"][1024:]
    return p_lat, p_ctx


MODW = 6 * D // NCORES


def build_mod():
    P = Prog()
    KC = D // 128
    cv_d = P.din("cv", [128, KC, 3])
    wa_d = P.din("wa", [DEPTH, D, MODW]).rearrange("l (k p) c -> l p k c", p=128)
    ba_d = P.din("ba", [DEPTH, 1, MODW])
    mod_d = P.dout("mod", [DEPTH, 3, MODW])
    cv = P.sb([128, KC, 3], name="cv")
    ones = P.sb([1, 4], name="ones")
    ba = P.sb([1, DEPTH, MODW], name="ba")
    t_cv, t_ones, t_ba = Tok(), Tok(), Tok()
    P.dma("sp", lambda e: e.dma_start(out=cv[:], in_=cv_d), writes=[t_cv])
    for l in range(DEPTH):
        P.dma("sp", lambda e, l=l: e.dma_start(out=ba[:, l, :], in_=ba_d[l]), writes=[t_ba])
    P.op("dve", lambda e: e.memset(ones[:], 1.0), writes=[t_ones])
    P.op("act", lambda e: e.activation(out=cv[:], in_=cv[:], func=AF.Silu), reads=[t_cv], writes=[t_cv])
    wt = [P.sb([128, KC, 512], name="wa") for _ in range(2)]
    t_w = [Tok(), Tok()]
    pst = [P.ps([128, 512], name="pm") for _ in range(2)]
    t_ps = [Tok(), Tok()]
    ot = [P.sb([4, 512], name="om") for _ in range(2)]
    t_ot = [Tok(), Tok()]
    it = 0
    for l in range(DEPTH):
        for c0 in range(0, MODW, 512):
            bi = it % 2
            it += 1
            for half in range(2):
                ks = slice(half * 8, half * 8 + 8)
                P.dma("sp" if half == 0 else "act",
                      lambda e, bi=bi, ks=ks, c0=c0, l=l: e.dma_start(out=wt[bi][:, ks, :], in_=wa_d[l, :, ks, c0:c0 + 512]),
                      writes=[t_w[bi]])
            for k in range(KC):
                P.op("pe", lambda e, bi=bi, k=k: e.matmul(pst[bi][0:3, :], lhsT=cv[:, k, :], rhs=wt[bi][:, k, :], start=(k == 0), stop=False),
                     reads=[t_cv, t_w[bi]], writes=[t_ps[bi]])
            P.op("pe", lambda e, bi=bi, l=l, c0=c0: e.matmul(pst[bi][0:3, :], lhsT=ones[0:1, 0:3], rhs=ba[0:1, l, c0:c0 + 512], start=False, stop=True),
                 reads=[t_ones, t_ba], writes=[t_ps[bi]])
            P.op("dve", lambda e, bi=bi: e.tensor_copy(out=ot[bi][0:3, :], in_=pst[bi][0:3, :]), reads=[t_ps[bi]], writes=[t_ot[bi]])
            P.dma("pool", lambda e, bi=bi, l=l, c0=c0: e.dma_start(out=mod_d[l, :, c0:c0 + 512], in_=ot[bi][0:3, :]), reads=[t_ot[bi]], is_out=True)
    return P


def stage_mod(c, c_ctx, w_ada, b_ada):
    P = build_mod()
    vec = np.concatenate([c, c_ctx[None]], axis=0)
    cv = np.ascontiguousarray(vec.T.reshape(D // 128, 128, 3).transpose(1, 0, 2))
    in_maps = []
    for ci in range(NCORES):
        cs = slice(ci * MODW, (ci + 1) * MODW)
        in_maps.append({"cv": cv, "wa": np.ascontiguousarray(w_ada[:, :, cs]), "ba": np.ascontiguousarray(b_ada[:, None, cs])})
    res = run_prog(P, in_maps)
    mod = np.concatenate([res[ci]["mod"] for ci in range(NCORES)], axis=-1)
    mod = mod.reshape(DEPTH, 3, 6, D)
    return np.ascontiguousarray(mod[:, 0:2]), np.ascontiguousarray(mod[:, 2])


def ln_tile(P, z, t_z, m, gain, bias, t_c, st, t_st, outt, t_out):
    s1, mu, ss, rstd = st[:, 0:1], st[:, 1:2], st[:, 2:3], st[:, 3:4]
    P.op("dve", lambda e: e.reduce_sum(out=s1[0:m], in_=z[0:m, :], axis=AX.X), reads=[t_z], writes=[t_st])
    P.op("dve", lambda e: e.tensor_scalar_mul(out=mu[0:m], in0=s1[0:m], scalar1=1.0 / D), reads=[t_st], writes=[t_st])
    P.op("dve", lambda e: e.tensor_scalar(out=z[0:m, :], in0=z[0:m, :], scalar1=mu[0:m], scalar2=None, op0=ALU.subtract),
         reads=[t_st, t_z], writes=[t_z])
    P.op("act", lambda e: e.activation(out=outt[0:m, :], in_=z[0:m, :], func=AF.Square, accum_out=ss[0:m]),
         reads=[t_z], writes=[t_out, t_st])
    P.op("dve", lambda e: e.tensor_scalar(out=ss[0:m], in0=ss[0:m], scalar1=1.0 / D, scalar2=1e-5, op0=ALU.mult, op1=ALU.add),
         reads=[t_st], writes=[t_st])
    P.op("act", lambda e: e.activation(out=ss[0:m], in_=ss[0:m], func=AF.Sqrt), reads=[t_st], writes=[t_st])
    P.op("dve", lambda e: e.reciprocal(out=rstd[0:m], in_=ss[0:m]), reads=[t_st], writes=[t_st])
    P.op("dve", lambda e: e.scalar_tensor_tensor(out=outt[0:m, :], in0=z[0:m, :], scalar=rstd[0:m], in1=gain[0:m, :],
                                                 op0=ALU.mult, op1=ALU.mult), reads=[t_st, t_z, t_c], writes=[t_out])
    P.op("dve", lambda e: e.tensor_add(out=outt[0:m, :], in0=outt[0:m, :], in1=bias[0:m, :]), reads=[t_c, t_out], writes=[t_out])


def build_outproj(with_proj=True, nparts=1):
    P = Prog()
    KC = D // 128
    NT = NT_B
    x_d = P.din("x", [NT, D])
    cst_d = P.din("cst", [4, 128, D])
    if with_proj:
        mT_d = P.din("mT", [128, KC, NT])
        w_d = P.din("w", [D, D]).rearrange("(k p) c -> p k c", p=128)
    else:
        y_d = P.din("y", [nparts, NT, D])
    o_d = P.dout("o", [NT, D])
    cst = P.sb([128, 4, D], name="cst")
    t_c = Tok()
    for i in range(4):
        P.dma("sp", lambda e, i=i: e.dma_start(out=cst[:, i, :], in_=cst_d[i]), writes=[t_c])
    if with_proj:
        w = P.sb([128, KC, D], BF16, name="w")
        t_w = Tok()
        for k in range(KC):
            P.dma("pool", lambda e, k=k: e.dma_start(out=w[:, k, :], in_=w_d[:, k, :]), writes=[t_w])
        mT = [P.sb([128, KC, 128], BF16, name="mT") for _ in range(2)]
        t_m = [Tok(), Tok()]
        pst = [P.ps([128, 512], name="po") for _ in range(4)]
        t_ps = [Tok() for _ in range(4)]
    else:
        yt = [P.sb([128, D], name="yt") for _ in range(2)]
        t_y = [Tok(), Tok()]
    xt = [P.sb([128, D], name="xt") for _ in range(2)]
    t_x = [Tok(), Tok()]
    zt = P.sb([128, D], name="zt")
    t_z = Tok()
    st = P.sb([128, 4], name="st")
    t_st = Tok()
    tiles = [(i * 128, 128) for i in range(8)] + [(1024, 64)]
    for ti, (t0, m) in enumerate(tiles):
        bi = ti % 2
        gi = 0 if ti < 8 else 1
        P.dma("sp", lambda e, bi=bi, t0=t0, m=m: e.dma_start(out=xt[bi][0:m, :], in_=x_d[t0:t0 + m, :]), writes=[t_x[bi]])
        if with_proj:
            P.dma("pool", lambda e, bi=bi, t0=t0, m=m: e.dma_start(out=mT[bi][:, :, 0:m], in_=mT_d[:, :, t0:t0 + m]), writes=[t_m[bi]])
            for cg in range(4):
                for k in range(KC):
                    P.op("pe", lambda e, bi=bi, cg=cg, k=k, m=m: e.matmul(pst[cg][0:m, :], lhsT=mT[bi][:, k, 0:m], rhs=w[:, k, cg * 512:(cg + 1) * 512],
                                                                      start=(k == 0), stop=(k == KC - 1)),
                         reads=[t_m[bi], t_w], writes=[t_ps[cg]])
                P.op("dve", lambda e, cg=cg, m=m, gi=gi: e.tensor_mul(out=zt[0:m, cg * 512:(cg + 1) * 512], in0=pst[cg][0:m, :],
                                                                     in1=cst[0:m, gi, cg * 512:(cg + 1) * 512]),
                     reads=[t_ps[cg], t_c], writes=[t_z])
        else:
            for pi in range(nparts):
                P.dma("act", lambda e, bi=bi, t0=t0, m=m, pi=pi: e.dma_start(out=yt[bi][0:m, :], in_=y_d[pi, t0:t0 + m, :]), writes=[t_y[bi]])
                if pi == 0:
                    P.op("dve", lambda e, bi=bi, m=m: e.tensor_copy(out=zt[0:m, :], in_=yt[bi][0:m, :]), reads=[t_y[bi]], writes=[t_z])
                else:
                    P.op("dve", lambda e, bi=bi, m=m: e.tensor_add(out=zt[0:m, :], in0=zt[0:m, :], in1=yt[bi][0:m, :]), reads=[t_y[bi], t_z], writes=[t_z])
            P.op("dve", lambda e, m=m, gi=gi: e.tensor_mul(out=zt[0:m, :], in0=zt[0:m, :], in1=cst[0:m, gi, :]), reads=[t_c, t_z], writes=[t_z])
        P.op("dve", lambda e, bi=bi, m=m: e.scalar_tensor_tensor(out=zt[0:m, :], in0=xt[bi][0:m, :], scalar=float(ALPHA), in1=zt[0:m, :],
                                                                 op0=ALU.mult, op1=ALU.add), reads=[t_x[bi], t_z], writes=[t_z])
        ln_tile(P, zt, t_z, m, cst[:, 2, :], cst[:, 3, :], t_c, st, t_st, xt[bi], t_x[bi])
        P.dma("act", lambda e, bi=bi, t0=t0, m=m: e.dma_start(out=o_d[t0:t0 + m, :], in_=xt[bi][0:m, :]), reads=[t_x[bi]], writes=[t_x[bi]], is_out=True)
    return P


def tok_shard(lat, ctx, c):
    b, q = divmod(c, 4)
    return np.concatenate([lat[b, q * 1024:(q + 1) * 1024], ctx[b, q * 64:(q + 1) * 64]], axis=0)


def tok_unshard(res, key, width):
    lat = np.empty((B, SEQ, width), np.float32)
    ctx = np.empty((B, CTX, width), np.float32)
    for c in range(NCORES):
        b, q = divmod(c, 4)
        lat[b, q * 1024:(q + 1) * 1024] = res[c][key][:1024]
        ctx[b, q * 64:(q + 1) * 64] = res[c][key][1024:]
    return lat, ctx


def bc(v):
    return np.ascontiguousarray(np.broadcast_to(v[None, :], (128, v.shape[0])))


def stage_outproj(mix_lat, mix_ctx, x_lat, x_ctx, gate_lat, gate_ctx, gain, bias, w_out_l):
    P = build_outproj(True)
    in_maps = []
    for c in range(NCORES):
        b = c // 4
        cst = np.stack([bc(gate_lat[b]), bc(gate_ctx), bc(gain), bc(bias)], axis=0)
        in_maps.append({"x": tok_shard(x_lat, x_ctx, c), "cst": cst, "mT": fm(tok_shard(mix_lat, mix_ctx, c)),
                        "w": np.ascontiguousarray(w_out_l)})
    res = run_prog(P, in_maps)
    return tok_unshard(res, "o", D)


NTOK = CTX + SEQ
NTT = NTOK // 128
IDENT = np.eye(128, dtype=np.float32)


def rms_rstd(P, x_ap, m, width, eps, junk, t_junk, st, t_st, reads):
    P.op("act", lambda e: e.activation(out=junk[0:m, 0:width], in_=x_ap, func=AF.Square, accum_out=st[0:m, 0:1]),
         reads=reads, writes=[t_junk, t_st])
    P.op("dve", lambda e: e.tensor_scalar(out=st[0:m, 0:1], in0=st[0:m, 0:1], scalar1=1.0 / width, scalar2=eps, op0=ALU.mult, op1=ALU.add),
         reads=[t_st], writes=[t_st])
    P.op("act", lambda e: e.activation(out=st[0:m, 0:1], in_=st[0:m, 0:1], func=AF.Sqrt), reads=[t_st], writes=[t_st])
    P.op("dve", lambda e: e.reciprocal(out=st[0:m, 1:2], in_=st[0:m, 0:1]), reads=[t_st], writes=[t_st])


def build_attn():
    P = Prog()
    q_d = P.din("q", [NTOK, 256])
    k_d = P.din("k", [NTOK, 128])
    v_d = P.din("v", [NTOK, 128])
    g_d = P.din("g", [2, 128, 128])
    cs_d = P.din("cs", [2, SEQ, 128])
    id_d = P.din("ident", [128, 128])
    o_d = P.dout("o", [NTOK, 256])

    ident = P.sb([128, 128], name="ident")
    gq = P.sb([128, 2, 128], name="gq")
    t_id, t_g = Tok(), Tok()
    P.dma("sp", lambda e: e.dma_start(out=ident[:], in_=id_d), writes=[t_id])
    for i in range(2):
        P.dma("sp", lambda e, i=i: e.dma_start(out=gq[:, i, :], in_=g_d[i]), writes=[t_g])
    qT = P.sb([128, 2, NTOK], BF16, name="qT")
    kT = P.sb([128, NTOK], BF16, name="kT")
    va = P.sb([128, NTT, 129], BF16, name="va")
    t_qT, t_kT, t_va = Tok(), Tok(), Tok()
    P.op("pool", lambda e: e.memset(va[:, :, 128:129], 1.0), writes=[t_va])
    for half in range(2):
        hs = slice(half * 17, half * 17 + 17)
        P.dma("pool", lambda e, hs=hs, half=half: e.dma_start(out=va[:, hs, 0:128],
              in_=v_d[half * 17 * 128:(half + 1) * 17 * 128, :].rearrange("(n p) d -> p n d", p=128)), writes=[t_va])
    NB = 2
    xin = [P.sb([128, 384], name="xin") for _ in range(NB)]
    t_xin = [Tok() for _ in range(NB)]
    cst = [P.sb([128, 2, 128], name="cs") for _ in range(NB)]
    t_cs = [Tok() for _ in range(NB)]
    junk = P.sb([128, 128], name="junk")
    t_junk = Tok()
    st = P.sb([128, 2], name="st")
    t_st = Tok()
    xn = P.sb([128, 128], name="xn")
    t_xn = Tok()
    t1 = P.sb([128, 128], name="t1")
    t_t1 = Tok()
    psT = [P.ps([128, 128], name="psT") for _ in range(2)]
    t_psT = [Tok(), Tok()]
    ip = 0
    for tt in range(NTT):
        bi = tt % NB
        t0 = tt * 128
        lat = tt >= 2
        P.dma("sp", lambda e, bi=bi, t0=t0: e.dma_start(out=xin[bi][:, 0:256], in_=q_d[t0:t0 + 128, :]), writes=[t_xin[bi]])
        P.dma("sp", lambda e, bi=bi, t0=t0: e.dma_start(out=xin[bi][:, 256:384], in_=k_d[t0:t0 + 128, :]), writes=[t_xin[bi]])
        if lat:
            for i in range(2):
                P.dma("act", lambda e, bi=bi, t0=t0, i=i: e.dma_start(out=cst[bi][:, i, :], in_=cs_d[i, t0 - CTX:t0 - CTX + 128, :]), writes=[t_cs[bi]])
        for s in range(3):
            xs = xin[bi][:, s * 128:(s + 1) * 128]
            gi = 0 if s < 2 else 1
            rms_rstd(P, xs, 128, 128, 1e-6, junk, t_junk, st, t_st, [t_xin[bi]])
            P.op("dve", lambda e, xs=xs, gi=gi: e.scalar_tensor_tensor(out=xn[:], in0=xs, scalar=st[:, 1:2], in1=gq[:, gi, :], op0=ALU.mult, op1=ALU.mult),
                 reads=[t_xin[bi], t_st, t_g], writes=[t_xn])
            src = xn
            if lat:
                P.op("dve", lambda e, bi=bi: e.tensor_mul(out=t1[:], in0=xn[:], in1=cst[bi][:, 0, :]), reads=[t_xn, t_cs[bi]], writes=[t_t1])
                for a in range(2):
                    for h in range(2):
                        o0 = a * 64 + h * 32
                        i0 = a * 64 + (1 - h) * 32
                        P.op("pool", lambda e, bi=bi, o0=o0, i0=i0: e.tensor_mul(out=junk[:, o0:o0 + 32], in0=xn[:, i0:i0 + 32], in1=cst[bi][:, 1, o0:o0 + 32]),
                             reads=[t_xn, t_cs[bi]], writes=[t_junk])
                P.op("dve", lambda e: e.tensor_add(out=t1[:], in0=t1[:], in1=junk[:]), reads=[t_junk, t_t1], writes=[t_t1])
                src = t1
            pb = ip % 2
            ip += 1
            P.op("pe", lambda e, pb=pb, src=src: e.transpose(out=psT[pb][:], in_=src[:], identity=ident[:]),
                 reads=[t_id, t_t1 if lat else t_xn], writes=[t_psT[pb]])
            if s < 2:
                P.op("act", lambda e, pb=pb, s=s, t0=t0: e.copy(out=qT[:, s, t0:t0 + 128], in_=psT[pb][:]), reads=[t_psT[pb]], writes=[t_qT])
            else:
                P.op("act", lambda e, pb=pb, t0=t0: e.copy(out=kT[:, t0:t0 + 128], in_=psT[pb][:]), reads=[t_psT[pb]], writes=[t_kT])
    NS = 2
    pss = [P.ps([128, 512], name="pss") for _ in range(NS)]
    t_pss = [Tok() for _ in range(NS)]
    NPT = 3
    pt = [P.sb([128, 512], BF16, name="pt") for _ in range(NPT)]
    t_pt = [Tok() for _ in range(NPT)]
    acc = [P.ps([128, 512], name="acc") for _ in range(4)]
    t_acc = [Tok() for _ in range(4)]
    ob = [P.sb([128, 128], name="ob") for _ in range(2)]
    t_ob = [Tok(), Tok()]
    rs = P.sb([128, 1], name="rs")
    t_rs = Tok()
    blocks = [(0, 256, 0, 2)] + [(CTX + i * 512, 512, 0, NTT) for i in range(SEQ // 512)]
    isc = 0
    ipt = 0
    iob = 0
    scale = 128 ** -0.5
    for h in range(2):
        for (q0, qw, kt0, kt1) in blocks:
            nq = qw // 128
            for kt in range(kt0, kt1):
                sb_ = isc % NS
                isc += 1
                P.op("pe", lambda e, sb_=sb_, kt=kt, h=h, q0=q0, qw=qw: e.matmul(pss[sb_][:, 0:qw], lhsT=kT[:, kt * 128:(kt + 1) * 128],
                                                                              rhs=qT[:, h, q0:q0 + qw], start=True, stop=True),
                     reads=[t_kT, t_qT], writes=[t_pss[sb_]])
                pb = ipt % NPT
                ipt += 1
                P.op("act", lambda e, sb_=sb_, pb=pb, qw=qw: e.activation(out=pt[pb][:, 0:qw], in_=pss[sb_][:, 0:qw], func=AF.Exp, scale=scale),
                     reads=[t_pss[sb_]], writes=[t_pt[pb]])
                for qi in range(nq):
                    P.op("pe", lambda e, pb=pb, qi=qi, kt=kt, kt0=kt0, kt1=kt1: e.matmul(acc[qi][:, 0:129], lhsT=pt[pb][:, qi * 128:(qi + 1) * 128],
                                                                                      rhs=va[:, kt, :], start=(kt == kt0), stop=(kt == kt1 - 1)),
                         reads=[t_pt[pb], t_va], writes=[t_acc[qi]])
            for qi in range(nq):
                P.op("dve", lambda e, qi=qi: e.reciprocal(out=rs[:], in_=acc[qi][:, 128:129]), reads=[t_acc[qi]], writes=[t_rs])
                oi = iob % 2
                iob += 1
                P.op("dve", lambda e, qi=qi, oi=oi: e.tensor_scalar(out=ob[oi][:], in0=acc[qi][:, 0:128], scalar1=rs[:], scalar2=None, op0=ALU.mult),
                     reads=[t_acc[qi], t_rs], writes=[t_ob[oi]])
                P.dma("pool", lambda e, oi=oi, q0=q0, qi=qi, h=h: e.dma_start(out=o_d[q0 + qi * 128:q0 + (qi + 1) * 128, h * 128:(h + 1) * 128], in_=ob[oi][:]),
                      reads=[t_ob[oi]], writes=[t_ob[oi]], is_out=True)
    return P


def rope_tables():
    rows = SEQ // 64
    row = np.repeat(np.arange(rows), 64).astype(np.float32)
    col = np.tile(np.arange(64), rows).astype(np.float32)
    inv = (10000.0 ** (-np.arange(0, 64, 2, dtype=np.float32) / 64)).astype(np.float32)
    ang = np.concatenate([row[:, None] * inv, col[:, None] * inv], axis=-1)
    cos, sin = np.cos(ang).astype(np.float32), np.sin(ang).astype(np.float32)
    C = np.empty((SEQ, 128), np.float32)
    S = np.empty((SEQ, 128), np.float32)
    for a in range(2):
        c_, s_ = cos[:, a * 32:(a + 1) * 32], sin[:, a * 32:(a + 1) * 32]
        C[:, a * 64:a * 64 + 32] = c_
        C[:, a * 64 + 32:a * 64 + 64] = c_
        S[:, a * 64:a * 64 + 32] = -s_
        S[:, a * 64 + 32:a * 64 + 64] = s_
    return np.stack([C, S], axis=0)


def stage_attn(p_lat, p_ctx, qk_gain_l):
    P = build_attn()
    cs = rope_tables()
    g = np.stack([bc(qk_gain_l[0]), bc(qk_gain_l[1])], axis=0)
    in_maps = []
    for c in range(NCORES):
        b, j = divmod(c, 4)
        pa = np.concatenate([p_ctx[b], p_lat[b]], axis=0)
        kv = j // 2
        in_maps.append({"q": np.ascontiguousarray(pa[:, 3632 + 256 * j:3632 + 256 * (j + 1)]),
                        "k": np.ascontiguousarray(pa[:, 4656 + 128 * kv:4656 + 128 * (kv + 1)]),
                        "v": np.ascontiguousarray(pa[:, 4912 + 128 * kv:4912 + 128 * (kv + 1)]),
                        "g": g, "cs": cs, "ident": IDENT})
    res = run_prog(P, in_maps)
    att_lat = np.empty((B, SEQ, 1024), np.float32)
    att_ctx = np.empty((B, CTX, 1024), np.float32)
    for c in range(NCORES):
        b, j = divmod(c, 4)
        att_ctx[b, :, 256 * j:256 * (j + 1)] = res[c]["o"][:CTX]
        att_lat[b, :, 256 * j:256 * (j + 1)] = res[c]["o"][CTX:]
    return att_lat, att_ctx


TRI_INC = np.triu(np.ones((128, 128), np.float32))
TRI_SUFEX = np.tril(np.ones((128, 128), np.float32), -1)
ANTI = np.ascontiguousarray(np.eye(128, dtype=np.float32)[::-1])


def orig_tile(tt):
    return (1 - tt) if tt < 2 else (35 - tt)


def build_gla():
    P = Prog()
    qT_d = P.din("qT", [2, 64, NTOK])
    kT_d = P.din("kT", [2, 64, NTOK])
    k_d = P.din("k", [2, NTOK, 64])
    v_d = P.din("v", [2, NTOK, 128])
    rT_d = P.din("rT", [2, 16, NTOK])
    w_d = P.din("w", [2, 17, 64])
    g_d = P.din("g", [NTOK, 128])
    gn_d = P.din("gn", [128, 128])
    c_d = P.din("cm", [4, 128, 128])
    o_d = P.dout("o", [NTOK, 128])

    cm = P.sb([128, 4, 128], name="cm")
    gn = P.sb([128, 128], name="gn")
    t_cm, t_gn = Tok(), Tok()
    for i in range(4):
        P.dma("sp", lambda e, i=i: e.dma_start(out=cm[:, i, :], in_=c_d[i]), writes=[t_cm])
    P.dma("sp", lambda e: e.dma_start(out=gn[:], in_=gn_d), writes=[t_gn])
    tri, sufx, anti = cm[:, 0, :], cm[:, 1, :], cm[:, 2, :]
    g = P.sb([128, NTT, 128], name="g")
    t_g = Tok()
    P.dma("act", lambda e: e.dma_start(out=g[:], in_=g_d.rearrange("(n p) d -> p n d", p=128)), writes=[t_g])
    P.op("act", lambda e: e.activation(out=g[:], in_=g[:], func=AF.Silu), reads=[t_g], writes=[t_g])
    ofw = P.sb([128, NTT, 128], name="ofw")
    t_ofw = Tok()
    qT = P.sb([64, NTOK], name="qT")
    kT = P.sb([64, NTOK], name="kT")
    rT = P.sb([17, NTOK], name="rT")
    kk = P.sb([128, NTT, 64], name="kk")
    vv = P.sb([128, NTT, 128], name="vv")
    wa = P.sb([17, 64], name="wa")
    t_in = Tok()
    P.op("dve", lambda e: e.memset(rT[:], 1.0), writes=[t_in])
    S = P.sb([64, 128], name="S")
    t_S = Tok()
    la = P.sb([128, 64], name="la")
    t_la = Tok()
    cs = P.sb([64, 128], name="cs")
    t_cs = Tok()
    sc = P.sb([64, 4], name="sc")
    t_sc = Tok()
    e1 = P.sb([64, 128], name="e1")
    e2 = P.sb([64, 128], name="e2")
    e3 = P.sb([64, 128], name="e3")
    t_e = Tok()
    k4 = P.sb([128, 64], name="k4")
    t_k4 = Tok()
    AT = P.sb([128, 128], name="AT")
    t_AT = Tok()
    ob = P.sb([128, 128], name="ob")
    t_ob = Tok()
    os_ = P.sb([128, 128], name="os")
    t_os = Tok()
    junk = P.sb([128, 128], name="junk")
    t_junk = Tok()
    st = P.sb([128, 2], name="st")
    t_st = Tok()
    ps_la = P.ps([128, 64], name="ps_la")
    ps_cT = P.ps([64, 128], name="ps_cT")
    ps_sf = P.ps([128, 64], name="ps_sf")
    ps_A = P.ps([128, 128], name="ps_A")
    ps_o = P.ps([128, 128], name="ps_o")
    ps_S = P.ps([64, 128], name="ps_S")
    ps_J = P.ps([128, 128], name="ps_J")
    t_pla, t_pcT, t_psf, t_pA, t_po, t_pS, t_pJ = [Tok() for _ in range(7)]
    for z in range(2):
        P.dma("sp", lambda e, z=z: e.dma_start(out=qT[:], in_=qT_d[z]), writes=[t_in])
        P.dma("act", lambda e, z=z: e.dma_start(out=kT[:], in_=kT_d[z]), writes=[t_in])
        P.dma("sp", lambda e, z=z: e.dma_start(out=rT[0:16, :], in_=rT_d[z]), writes=[t_in])
        P.dma("act", lambda e, z=z: e.dma_start(out=kk[:], in_=k_d[z].rearrange("(n p) d -> p n d", p=128)), writes=[t_in])
        P.dma("sp", lambda e, z=z: e.dma_start(out=vv[:], in_=v_d[z].rearrange("(n p) d -> p n d", p=128)), writes=[t_in])
        P.dma("act", lambda e, z=z: e.dma_start(out=wa[:], in_=w_d[z]), writes=[t_in])
        P.op("dve", lambda e: e.memset(S[:], 0.0), writes=[t_S])
        for tt in range(NTT):
            ts_ = slice(tt * 128, (tt + 1) * 128)
            P.op("pe", lambda e, ts_=ts_: e.matmul(ps_la[:], lhsT=rT[0:17, ts_], rhs=wa[0:17, :], start=True, stop=True), reads=[t_in], writes=[t_pla])
            P.op("act", lambda e: e.activation(out=la[:], in_=ps_la[:], func=AF.Exp, scale=-1.0), reads=[t_pla], writes=[t_la])
            P.op("dve", lambda e: e.tensor_scalar_add(out=la[:], in0=la[:], scalar1=1.0), reads=[t_la], writes=[t_la])
            P.op("act", lambda e: e.activation(out=la[:], in_=la[:], func=AF.Ln), reads=[t_la], writes=[t_la])
            P.op("dve", lambda e: e.tensor_scalar_mul(out=la[:], in0=la[:], scalar1=-1.0 / 16.0), reads=[t_la], writes=[t_la])
            P.op("pe", lambda e: e.matmul(ps_cT[:], lhsT=la[:], rhs=tri, start=True, stop=True), reads=[t_la, t_cm], writes=[t_pcT])
            P.op("pe", lambda e: e.matmul(ps_sf[:], lhsT=sufx, rhs=la[:], start=True, stop=True), reads=[t_la, t_cm], writes=[t_psf])
            P.op("act", lambda e: e.copy(out=cs[:], in_=ps_cT[:]), reads=[t_pcT], writes=[t_cs])
            P.op("dve", lambda e: e.tensor_scalar_mul(out=sc[:, 0:1], in0=cs[:, 63:64], scalar1=-1.0), reads=[t_cs], writes=[t_sc])
            P.op("act", lambda e: e.activation(out=e1[:], in_=cs[:], func=AF.Exp, bias=sc[:, 0:1], scale=1.0), reads=[t_cs, t_sc], writes=[t_e])
            P.op("act", lambda e: e.activation(out=e2[:], in_=cs[:], func=AF.Exp, bias=cs[:, 63:64], scale=-1.0), reads=[t_cs], writes=[t_e])
            P.op("act", lambda e: e.activation(out=e3[:], in_=cs[:], func=AF.Exp), reads=[t_cs], writes=[t_e])
            P.op("act", lambda e: e.activation(out=sc[:, 1:2], in_=cs[:, 127:128], func=AF.Exp), reads=[t_cs], writes=[t_sc])
            P.op("act", lambda e: e.activation(out=k4[:], in_=ps_sf[:], func=AF.Exp), reads=[t_psf], writes=[t_k4])
            P.op("dve", lambda e, ts_=ts_: e.scalar_tensor_tensor(out=e1[:], in0=qT[:, ts_], scalar=0.125, in1=e1[:], op0=ALU.mult, op1=ALU.mult),
                 reads=[t_in, t_e], writes=[t_e])
            P.op("dve", lambda e, ts_=ts_: e.tensor_mul(out=e2[:], in0=kT[:, ts_], in1=e2[:]), reads=[t_in, t_e], writes=[t_e])
            P.op("dve", lambda e, ts_=ts_: e.scalar_tensor_tensor(out=e3[:], in0=qT[:, ts_], scalar=0.125, in1=e3[:], op0=ALU.mult, op1=ALU.mult),
                 reads=[t_in, t_e], writes=[t_e])
            P.op("dve", lambda e, tt=tt: e.tensor_mul(out=k4[:], in0=kk[:, tt, :], in1=k4[:]), reads=[t_in, t_k4], writes=[t_k4])
            P.op("pe", lambda e: e.matmul(ps_A[:], lhsT=e2[:], rhs=e1[:], start=True, stop=True), reads=[t_e], writes=[t_pA])
            P.op("dve", lambda e: e.tensor_mul(out=AT[:], in0=ps_A[:], in1=tri), reads=[t_pA, t_cm], writes=[t_AT])
            P.op("pe", lambda e, tt=tt: e.matmul(ps_o[:], lhsT=AT[:], rhs=vv[:, tt, :], start=True, stop=False), reads=[t_AT, t_in], writes=[t_po])
            P.op("pe", lambda e: e.matmul(ps_o[:], lhsT=e3[:], rhs=S[:], start=False, stop=True), reads=[t_e, t_S], writes=[t_po])
            P.op("pe", lambda e, tt=tt: e.matmul(ps_S[:], lhsT=k4[:], rhs=vv[:, tt, :], start=True, stop=True), reads=[t_k4, t_in], writes=[t_pS])
            P.op("dve", lambda e: e.scalar_tensor_tensor(out=S[:], in0=S[:], scalar=sc[:, 1:2], in1=ps_S[:], op0=ALU.mult, op1=ALU.add),
                 reads=[t_sc, t_pS, t_S], writes=[t_S])
            if z == 0:
                P.op("act", lambda e, tt=tt: e.copy(out=ofw[:, tt, :], in_=ps_o[:]), reads=[t_po], writes=[t_ofw])
            else:
                oi = orig_tile(tt)
                P.op("act", lambda e: e.copy(out=ob[:], in_=ps_o[:]), reads=[t_po], writes=[t_ob])
                P.op("pe", lambda e: e.matmul(ps_J[:], lhsT=anti, rhs=ob[:], start=True, stop=True), reads=[t_ob, t_cm], writes=[t_pJ])
                P.op("dve", lambda e, oi=oi: e.tensor_add(out=os_[:], in0=ofw[:, oi, :], in1=ps_J[:]), reads=[t_pJ, t_ofw], writes=[t_os])
                rms_rstd(P, os_[:], 128, 128, 1e-6, junk, t_junk, st, t_st, [t_os])
                P.op("dve", lambda e: e.scalar_tensor_tensor(out=os_[:], in0=os_[:], scalar=st[:, 1:2], in1=gn[:], op0=ALU.mult, op1=ALU.mult),
                     reads=[t_st, t_gn, t_os], writes=[t_os])
                P.op("dve", lambda e, oi=oi: e.tensor_mul(out=os_[:], in0=os_[:], in1=g[:, oi, :]), reads=[t_g, t_os], writes=[t_os])
                P.dma("pool", lambda e, oi=oi: e.dma_start(out=o_d[oi * 128:(oi + 1) * 128, :], in_=os_[:]), reads=[t_os], writes=[t_os], is_out=True)
    return P


def flipseg(a):
    return np.concatenate([a[:CTX][::-1], a[CTX:][::-1]], axis=0)


CMATS = np.stack([TRI_INC, TRI_SUFEX, ANTI, IDENT], axis=0)


def stage_gla(p_lat, p_ctx, w_up_l, b_up_l, norm_l):
    P = build_gla()
    in_maps = []
    for c in range(NCORES):
        b, h = divmod(c, 4)
        pa = np.concatenate([p_ctx[b], p_lat[b]], axis=0)
        q = pa[:, h * 64:(h + 1) * 64]
        k = pa[:, 256 + h * 64:256 + (h + 1) * 64]
        v = pa[:, 512 + h * 128:512 + (h + 1) * 128]
        g = pa[:, 1024 + h * 128:1024 + (h + 1) * 128]
        m = {"g": np.ascontiguousarray(g), "gn": bc(norm_l), "cm": CMATS}
        qs, ks, vs, rs, ws = [], [], [], [], []
        for z in range(2):
            f = (lambda a: a) if z == 0 else flipseg
            r = pa[:, 1536 + z * 16:1536 + (z + 1) * 16]
            qs.append(f(q).T)
            ks.append(f(k))
            vs.append(f(v))
            rs.append(f(r).T)
            ws.append(np.concatenate([w_up_l[z][:, h * 64:(h + 1) * 64], b_up_l[z][None, h * 64:(h + 1) * 64]], axis=0))
        m["qT"] = np.ascontiguousarray(np.stack(qs))
        m["kT"] = np.ascontiguousarray(np.stack([a.T for a in ks]))
        m["k"] = np.ascontiguousarray(np.stack(ks))
        m["v"] = np.ascontiguousarray(np.stack(vs))
        m["rT"] = np.ascontiguousarray(np.stack(rs))
        m["w"] = np.ascontiguousarray(np.stack(ws))
        in_maps.append(m)
    res = run_prog(P, in_maps)
    o_lat = np.empty((B, SEQ, 512), np.float32)
    o_ctx = np.empty((B, CTX, 512), np.float32)
    for c in range(NCORES):
        b, h = divmod(c, 4)
        o_ctx[b, :, 128 * h:128 * (h + 1)] = res[c]["o"][:CTX]
        o_lat[b, :, 128 * h:128 * (h + 1)] = res[c]["o"][CTX:]
    return o_lat, o_ctx


UPP = TRI_INC
LOW = np.ascontiguousarray(TRI_INC.T)
GDN_CM = np.stack([UPP, LOW, UPP - IDENT, LOW - IDENT, IDENT, np.ones((128, 128), np.float32)], axis=0)


def build_gdn(dbg=None):
    P = Prog()
    x_d = P.din("xT", [3, 128, NTOK])
    cw_d = P.din("cw", [128, 3, 5])
    zg_d = P.din("zg", [NTOK, 128])
    ba_d = P.din("ba", [2, 2, 128, NTT])
    sc_d = P.din("sc", [128, 2, 2])
    gn_d = P.din("gn", [128, 128])
    c_d = P.din("cm", [6, 128, 128])
    o_d = P.dout("o", [NTOK, 128])

    cm = P.sb([128, 6, 128], name="cm")
    t_cm = Tok()
    for i in range(6):
        P.dma("sp", lambda e, i=i: e.dma_start(out=cm[:, i, :], in_=c_d[i]), writes=[t_cm])
    ident, ones = cm[:, 4, :], cm[:, 5, :]
    gn = P.sb([128, 128], name="gn")
    cw = P.sb([128, 3, 5], name="cw")
    scal = P.sb([128, 2, 2], name="scal")
    t_gn, t_cw, t_scal = Tok(), Tok(), Tok()
    P.dma("sp", lambda e: e.dma_start(out=gn[:], in_=gn_d), writes=[t_gn])
    P.dma("sp", lambda e: e.dma_start(out=cw[:], in_=cw_d), writes=[t_cw])
    P.dma("sp", lambda e: e.dma_start(out=scal[:], in_=sc_d), writes=[t_scal])
    zg = P.sb([128, NTT, 128], name="zg")
    t_zg = Tok()
    P.dma("act", lambda e: e.dma_start(out=zg[:], in_=zg_d.rearrange("(n p) d -> p n d", p=128)), writes=[t_zg])
    P.op("act", lambda e: e.activation(out=zg[:], in_=zg[:], func=AF.Silu), reads=[t_zg], writes=[t_zg])

    raw = P.sb([128, NTOK], name="raw")
    t_raw = Tok()
    fmx = [P.sb([128, NTOK], name="fmx") for _ in range(3)]
    t_fm = [Tok() for _ in range(3)]
    segs = [(0, CTX), (CTX, NTOK)]
    for s in range(3):
        P.dma("sp", lambda e, s=s: e.dma_start(out=raw[:], in_=x_d[s]), writes=[t_raw])
        acc = fmx[s]
        for (s0, s1) in segs:
            P.op("dve", lambda e, s=s, s0=s0, s1=s1, acc=acc: e.tensor_scalar(out=acc[:, s0:s1], in0=raw[:, s0:s1], scalar1=cw[:, s, 2:3], scalar2=None, op0=ALU.mult),
                 reads=[t_raw, t_cw], writes=[t_fm[s]])
            for j in (0, 1, 3, 4):
                sh = j - 2
                lo = max(s0, s0 - sh)
                hi = min(s1, s1 - sh)
                P.op("dve", lambda e, s=s, j=j, lo=lo, hi=hi, sh=sh, acc=acc: e.scalar_tensor_tensor(
                    out=acc[:, lo:hi], in0=raw[:, lo + sh:hi + sh], scalar=cw[:, s, j:j + 1], in1=acc[:, lo:hi], op0=ALU.mult, op1=ALU.add),
                    reads=[t_raw, t_cw, t_fm[s]], writes=[t_fm[s]])
        P.op("act", lambda e, acc=acc: e.activation(out=acc[:], in_=acc[:], func=AF.Silu), reads=[t_fm[s]], writes=[t_fm[s]])
    bk1 = P.ps([128, 512], name="bk1")
    t_bk1 = Tok(True)
    sq = P.sb([128, 512], name="sq")
    t_sq = Tok()
    blocks = [(0, 256)] + [(CTX + i * 512, 512) for i in range(SEQ // 512)]
    for s in range(2):
        mul = (128 ** -0.5) if s == 0 else 1.0
        for (b0, bw) in blocks:
            P.op("act", lambda e, s=s, b0=b0, bw=bw: e.activation(out=sq[:, 0:bw], in_=fmx[s][:, b0:b0 + bw], func=AF.Square), reads=[t_fm[s]], writes=[t_sq])
            P.op("pe", lambda e, bw=bw: e.matmul(bk1[:, 0:bw], lhsT=ones, rhs=sq[:, 0:bw], start=True, stop=True), reads=[t_sq, t_cm], writes=[t_bk1])
            P.op("dve", lambda e, bw=bw: e.tensor_scalar_add(out=sq[:, 0:bw], in0=bk1[:, 0:bw], scalar1=1e-6), reads=[t_bk1], writes=[t_sq])
            P.op("act", lambda e, bw=bw: e.activation(out=sq[:, 0:bw], in_=sq[:, 0:bw], func=AF.Sqrt), reads=[t_sq], writes=[t_sq])
            P.op("dve", lambda e, bw=bw: e.reciprocal(out=sq[:, 0:bw], in_=sq[:, 0:bw]), reads=[t_sq], writes=[t_sq])
            P.op("dve", lambda e, s=s, b0=b0, bw=bw, mul=mul: e.scalar_tensor_tensor(out=fmx[s][:, b0:b0 + bw], in0=fmx[s][:, b0:b0 + bw], scalar=float(mul), in1=sq[:, 0:bw],
                                                                                   op0=ALU.mult, op1=ALU.mult), reads=[t_sq, t_fm[s]], writes=[t_fm[s]])
    qnT, knT, vT = fmx
    t_qnT, t_knT, t_vT = t_fm
    kn = P.sb([128, NTT, 128], name="kn")
    vt = P.sb([128, NTT, 128], name="vt")
    t_kn, t_vt = Tok(), Tok()
    for tt in range(NTT):
        ts_ = slice(tt * 128, (tt + 1) * 128)
        P.op("pe", lambda e, ts_=ts_: e.transpose(out=bk1[:, 0:128], in_=knT[:, ts_], identity=ident), reads=[t_knT, t_cm], writes=[t_bk1])
        P.op("pe", lambda e, ts_=ts_: e.transpose(out=bk1[:, 128:256], in_=vT[:, ts_], identity=ident), reads=[t_vT, t_cm], writes=[t_bk1])
        P.op("act", lambda e, tt=tt: e.copy(out=kn[:, tt, :], in_=bk1[:, 0:128]), reads=[t_bk1], writes=[t_kn])
        P.op("dve", lambda e, tt=tt: e.tensor_copy(out=vt[:, tt, :], in_=bk1[:, 128:256]), reads=[t_bk1], writes=[t_vt])

    if dbg == "prep":
        for tt in range(NTT):
            P.dma("pool", lambda e, tt=tt: e.dma_start(out=o_d[tt * 128:(tt + 1) * 128, :], in_=kn[:, tt, :]), reads=[t_kn], is_out=True)
        return P
    ofw = P.sb([128, NTT, 128], name="ofw")
    t_ofw = Tok()
    bl = P.sb([128, 2, NTT], name="bl")
    t_bl = Tok()
    nea = P.sb([128, 1], name="nea")
    t_nea = Tok()
    S = P.sb([128, 128], name="S")
    t_S = Tok()
    bkA = bk1
    bkB = P.ps([128, 512], name="bkB")
    bkC = P.ps([128, 512], name="bkC")
    bkD = P.ps([128, 512], name="bkD")
    bkE = P.ps([128, 512], name="bkE")
    bkF = P.ps([128, 512], name="bkF")
    bkG = P.ps([128, 512], name="bkG")
    t_pg = t_G = t_bR = t_bk1
    t_KK = t_QK = Tok(True)
    t_sqL, t_sqLT, t_pX = Tok(True), Tok(True), Tok(True)
    t_pwT = t_pvn = Tok(True)
    t_po = t_pS = Tok(True)
    pg, pG, pbR = bkA[:, 0:1], bkA[:, 128:256], bkA[:, 256:384]
    pKK, pQK = bkB[:, 0:128], bkB[:, 128:256]
    psqL, psqLT = bkC[:, 0:128], bkD[:, 0:128]
    pX = bkE[:, 0:256]
    pwT, pvn = bkF[:, 0:128], bkF[:, 128:256]
    po, pS = bkG[:, 0:128], bkG[:, 128:256]

    def sbt(n, w=128):
        return P.sb([128, w], name=n), Tok()
    lgB, t_lgB = sbt("lgB")
    btB, t_btB = sbt("btB")
    col, t_col = sbt("col", 8)
    GT, t_GT = sbt("GT")
    Gm, t_Gm = sbt("Gm")
    EgR, t_EgR = sbt("EgR")
    L0, t_L0 = sbt("L0")
    gdn_pow = [sbt("pw%d" % i) for i in range(12)]
    LT1, t_LT1 = sbt("LT1")
    AqT, t_AqT = sbt("AqT")
    X, t_X = sbt("X", 256)
    wT, t_wT = sbt("wT")
    vn, t_vn = sbt("vn")
    qd, t_qd = sbt("qd")
    kd, t_kd = sbt("kd")
    os_, t_os = sbt("os")
    junk, t_junk = sbt("junk")
    st, t_st = sbt("st", 2)

    for z in range(2):
        C, CT, Cs, CTs = (cm[:, 0, :], cm[:, 1, :], cm[:, 2, :], cm[:, 3, :]) if z == 0 else (cm[:, 1, :], cm[:, 0, :], cm[:, 3, :], cm[:, 2, :])
        last = 127 if z == 0 else 0
        for i in range(2):
            P.dma("sp", lambda e, z=z, i=i: e.dma_start(out=bl[:, i, :], in_=ba_d[z, i]), writes=[t_bl])
        P.op("act", lambda e: e.activation(out=bl[:, 0, :], in_=bl[:, 0, :], func=AF.Sigmoid), reads=[t_bl], writes=[t_bl])
        P.op("act", lambda e, z=z: e.activation(out=bl[:, 1, :], in_=bl[:, 1, :], func=AF.Exp, bias=scal[:, z, 1:2], scale=1.0), reads=[t_bl, t_scal], writes=[t_bl])
        P.op("dve", lambda e: e.tensor_scalar_add(out=bl[:, 1, :], in0=bl[:, 1, :], scalar1=1.0), reads=[t_bl], writes=[t_bl])
        P.op("act", lambda e: e.activation(out=bl[:, 1, :], in_=bl[:, 1, :], func=AF.Ln), reads=[t_bl], writes=[t_bl])
        P.op("act", lambda e, z=z: e.activation(out=nea[:], in_=scal[:, z, 0:1], func=AF.Exp), reads=[t_scal], writes=[t_nea])
        P.op("dve", lambda e: e.tensor_scalar_mul(out=nea[:], in0=nea[:], scalar1=-1.0), reads=[t_nea], writes=[t_nea])
        P.op("dve", lambda e: e.tensor_scalar(out=bl[:, 1, :], in0=bl[:, 1, :], scalar1=nea[:], scalar2=None, op0=ALU.mult), reads=[t_bl, t_nea], writes=[t_bl])
        P.op("dve", lambda e: e.memset(S[:], 0.0), writes=[t_S])
        order = list(range(NTT)) if z == 0 else [1, 0] + list(range(NTT - 1, 1, -1))
        if dbg is not None and dbg.startswith("main"):
            order = order[:int(dbg[4:])]
        for tt in order:
            ts_ = slice(tt * 128, (tt + 1) * 128)
            beta = bl[:, 0, tt:tt + 1]
            lg = bl[:, 1, tt:tt + 1]
            P.op("dve", lambda e, lg=lg: e.tensor_scalar(out=lgB[:], in0=ones, scalar1=lg, scalar2=None, op0=ALU.mult), reads=[t_bl, t_cm], writes=[t_lgB])
            P.op("dve", lambda e, beta=beta: e.tensor_scalar(out=btB[:], in0=ones, scalar1=beta, scalar2=None, op0=ALU.mult), reads=[t_bl, t_cm], writes=[t_btB])
            P.op("pe", lambda e, C=C: e.matmul(bkA[:, 0:2], lhsT=C, rhs=lgB[:, 0:2], start=True, stop=True), reads=[t_lgB, t_cm], writes=[t_pg])
            P.op("pe", lambda e, C=C: e.matmul(pG, lhsT=lgB[:], rhs=C, start=True, stop=True), reads=[t_lgB, t_cm], writes=[t_G])
            P.op("pe", lambda e: e.matmul(pbR, lhsT=btB[:], rhs=ident, start=True, stop=True), reads=[t_btB, t_cm], writes=[t_bR])
            P.op("pe", lambda e, ts_=ts_: e.matmul(pKK, lhsT=knT[:, ts_], rhs=knT[:, ts_], start=True, stop=True), reads=[t_knT], writes=[t_KK])
            P.op("pe", lambda e, ts_=ts_: e.matmul(pQK, lhsT=knT[:, ts_], rhs=qnT[:, ts_], start=True, stop=True), reads=[t_knT, t_qnT], writes=[t_QK])
            P.op("act", lambda e: e.copy(out=col[:, 0:1], in_=pg), reads=[t_pg], writes=[t_col])
            P.op("dve", lambda e: e.tensor_scalar(out=GT[:], in0=pG, scalar1=col[:, 0:1], scalar2=0.0, op0=ALU.subtract, op1=ALU.min), reads=[t_G, t_col], writes=[t_GT])
            P.op("act", lambda e: e.activation(out=GT[:], in_=GT[:], func=AF.Exp), reads=[t_GT], writes=[t_GT])
            P.op("dve", lambda e: e.tensor_scalar(out=Gm[:], in0=pG, scalar1=col[:, 0:1], scalar2=0.0, op0=ALU.subtract, op1=ALU.max), reads=[t_G, t_col], writes=[t_Gm])
            P.op("act", lambda e: e.activation(out=Gm[:], in_=Gm[:], func=AF.Exp, scale=-1.0), reads=[t_Gm], writes=[t_Gm])
            P.op("act", lambda e: e.activation(out=EgR[:], in_=pG, func=AF.Exp), reads=[t_G], writes=[t_EgR])
            P.op("act", lambda e: e.activation(out=col[:, 4:5], in_=col[:, 0:1], func=AF.Exp), reads=[t_col], writes=[t_col])
            P.op("dve", lambda e, beta=beta: e.tensor_mul(out=col[:, 1:2], in0=col[:, 4:5], in1=beta), reads=[t_col, t_bl], writes=[t_col])
            P.op("dve", lambda e, last=last: e.tensor_sub(out=col[:, 5:6], in0=pG[:, last:last + 1], in1=col[:, 0:1]), reads=[t_G, t_col], writes=[t_col])
            P.op("act", lambda e: e.activation(out=col[:, 2:3], in_=col[:, 5:6], func=AF.Exp), reads=[t_col], writes=[t_col])
            P.op("act", lambda e, last=last: e.activation(out=col[:, 3:4], in_=pG[:, last:last + 1], func=AF.Exp), reads=[t_G], writes=[t_col])
            P.op("dve", lambda e, Cs=Cs: e.tensor_mul(out=LT1[:], in0=GT[:], in1=Cs), reads=[t_GT, t_cm], writes=[t_LT1])
            P.op("dve", lambda e: e.tensor_mul(out=LT1[:], in0=LT1[:], in1=pKK), reads=[t_KK, t_LT1], writes=[t_LT1])
            P.op("dve", lambda e: e.tensor_mul(out=LT1[:], in0=LT1[:], in1=pbR), reads=[t_bR, t_LT1], writes=[t_LT1])
            P.op("dve", lambda e, CTs=CTs: e.tensor_mul(out=L0[:], in0=Gm[:], in1=CTs), reads=[t_Gm, t_cm], writes=[t_L0])
            P.op("dve", lambda e, beta=beta: e.scalar_tensor_tensor(out=L0[:], in0=L0[:], scalar=beta, in1=pKK, op0=ALU.mult, op1=ALU.mult), reads=[t_KK, t_bl, t_L0], writes=[t_L0])
            P.op("dve", lambda e, C=C: e.tensor_mul(out=AqT[:], in0=GT[:], in1=C), reads=[t_GT, t_cm], writes=[t_AqT])
            P.op("dve", lambda e: e.tensor_mul(out=AqT[:], in0=AqT[:], in1=pQK), reads=[t_QK, t_AqT], writes=[t_AqT])
            pw = [(L0, t_L0, LT1, t_LT1)]
            cur_L, cur_tL, cur_LT, cur_tLT = L0, t_L0, LT1, t_LT1
            for pi in range(6):
                nL, t_nL = gdn_pow[2 * pi]
                nLT, t_nLT = gdn_pow[2 * pi + 1]
                P.op("pe", lambda e, cur_L=cur_L, cur_LT=cur_LT: e.matmul(psqL, lhsT=cur_LT[:], rhs=cur_L[:], start=True, stop=True), reads=[cur_tL, cur_tLT], writes=[t_sqL])
                P.op("pe", lambda e, cur_L=cur_L, cur_LT=cur_LT: e.matmul(psqLT, lhsT=cur_L[:], rhs=cur_LT[:], start=True, stop=True), reads=[cur_tL, cur_tLT], writes=[t_sqLT])
                P.op("act", lambda e, nL=nL: e.copy(out=nL[:], in_=psqL), reads=[t_sqL], writes=[t_nL])
                P.op("dve", lambda e, nLT=nLT: e.tensor_copy(out=nLT[:], in_=psqLT), reads=[t_sqLT], writes=[t_nLT])
                pw.append((nL, t_nL, nLT, t_nLT))
                cur_L, cur_tL, cur_LT, cur_tLT = nL, t_nL, nLT, t_nLT
            P.op("dve", lambda e, tt=tt, beta=beta: e.tensor_scalar(out=X[:, 0:128], in0=vt[:, tt, :], scalar1=beta, scalar2=None, op0=ALU.mult), reads=[t_vt, t_bl], writes=[t_X])
            P.op("dve", lambda e, tt=tt: e.tensor_scalar(out=X[:, 128:256], in0=kn[:, tt, :], scalar1=col[:, 1:2], scalar2=None, op0=ALU.mult), reads=[t_kn, t_col], writes=[t_X])
            for pi in range(6, -1, -1):
                _, _, pLT, t_pLT = pw[pi]
                P.op("pe", lambda e, pLT=pLT: e.matmul(pX, lhsT=pLT[:], rhs=X[:], start=True, stop=True), reads=[t_pLT, t_X], writes=[t_pX])
                if pi > 0:
                    P.op("dve", lambda e: e.tensor_add(out=X[:], in0=X[:], in1=pX), reads=[t_pX, t_X], writes=[t_X])
                else:
                    P.op("dve", lambda e: e.tensor_sub(out=X[:], in0=X[:], in1=pX), reads=[t_pX, t_X], writes=[t_X])
            P.op("pe", lambda e: e.transpose(out=pwT, in_=X[:, 128:256], identity=ident), reads=[t_X, t_cm], writes=[t_pwT])
            P.op("act", lambda e: e.copy(out=wT[:], in_=pwT), reads=[t_pwT], writes=[t_wT])
            P.op("pe", lambda e: e.matmul(pvn, lhsT=wT[:], rhs=S[:], start=True, stop=True), reads=[t_wT, t_S], writes=[t_pvn])
            P.op("dve", lambda e: e.tensor_sub(out=vn[:], in0=X[:, 0:128], in1=pvn), reads=[t_pvn, t_X], writes=[t_vn])
            P.op("dve", lambda e, ts_=ts_: e.tensor_mul(out=qd[:], in0=qnT[:, ts_], in1=EgR[:]), reads=[t_qnT, t_EgR], writes=[t_qd])
            P.op("pe", lambda e: e.matmul(po, lhsT=qd[:], rhs=S[:], start=True, stop=False), reads=[t_qd, t_S], writes=[t_po])
            P.op("pe", lambda e: e.matmul(po, lhsT=AqT[:], rhs=vn[:], start=False, stop=True), reads=[t_AqT, t_vn], writes=[t_po])
            P.op("dve", lambda e, tt=tt: e.tensor_scalar(out=kd[:], in0=kn[:, tt, :], scalar1=col[:, 2:3], scalar2=None, op0=ALU.mult), reads=[t_kn, t_col], writes=[t_kd])
            P.op("pe", lambda e: e.matmul(pS, lhsT=kd[:], rhs=vn[:], start=True, stop=True), reads=[t_kd, t_vn], writes=[t_pS])
            P.op("dve", lambda e: e.scalar_tensor_tensor(out=S[:], in0=S[:], scalar=col[:, 3:4], in1=pS, op0=ALU.mult, op1=ALU.add), reads=[t_col, t_pS, t_S], writes=[t_S])
            if z == 0:
                P.op("act", lambda e, tt=tt: e.copy(out=ofw[:, tt, :], in_=po), reads=[t_po], writes=[t_ofw])
            else:
                P.op("dve", lambda e, tt=tt: e.tensor_add(out=os_[:], in0=ofw[:, tt, :], in1=po), reads=[t_po, t_ofw], writes=[t_os])
                rms_rstd(P, os_[:], 128, 128, 1e-6, junk, t_junk, st, t_st, [t_os])
                P.op("dve", lambda e: e.scalar_tensor_tensor(out=os_[:], in0=os_[:], scalar=st[:, 1:2], in1=gn[:], op0=ALU.mult, op1=ALU.mult), reads=[t_st, t_gn, t_os], writes=[t_os])
                P.op("dve", lambda e, tt=tt: e.tensor_mul(out=os_[:], in0=os_[:], in1=zg[:, tt, :]), reads=[t_zg, t_os], writes=[t_os])
                P.dma("pool", lambda e, tt=tt: e.dma_start(out=o_d[tt * 128:(tt + 1) * 128, :], in_=os_[:]), reads=[t_os], writes=[t_os], is_out=True)
    return P


def stage_gdn(p_lat, p_ctx, conv_l, a_log_l, dt_bias_l, norm_l, dbg=None):
    P = build_gdn(dbg)
    in_maps = []
    for c in range(NCORES):
        b, h = divmod(c, 4)
        pa = np.concatenate([p_ctx[b], p_lat[b]], axis=0)
        hs = slice(h * 128, (h + 1) * 128)
        xT = np.stack([pa[:, 1568:2080][:, hs].T, pa[:, 2080:2592][:, hs].T, pa[:, 2592:3104][:, hs].T], axis=0)
        cw = np.stack([conv_l[:, s * 512 + h * 128:s * 512 + (h + 1) * 128].T for s in range(3)], axis=1)
        ba = np.empty((2, 2, 128, NTT), np.float32)
        sc = np.empty((128, 2, 2), np.float32)
        for z in range(2):
            ba[z, 0] = pa[:, 3616 + z * 4 + h].reshape(NTT, 128).T
            ba[z, 1] = pa[:, 3624 + z * 4 + h].reshape(NTT, 128).T
            sc[:, z, 0] = a_log_l[z, h]
            sc[:, z, 1] = dt_bias_l[z, h]
        in_maps.append({"xT": np.ascontiguousarray(xT), "cw": np.ascontiguousarray(cw), "zg": np.ascontiguousarray(pa[:, 3104:3616][:, hs]),
                        "ba": ba, "sc": sc, "gn": bc(norm_l), "cm": GDN_CM})
    res = run_prog(P, in_maps)
    o_lat = np.empty((B, SEQ, 512), np.float32)
    o_ctx = np.empty((B, CTX, 512), np.float32)
    for c in range(NCORES):
        b, h = divmod(c, 4)
        o_ctx[b, :, 128 * h:128 * (h + 1)] = res[c]["o"][:CTX]
        o_lat[b, :, 128 * h:128 * (h + 1)] = res[c]["o"][CTX:]
    return o_lat, o_ctx


def build_router():
    P = Prog()
    KC = D // 128
    xT_d = P.din("xT", [128, KC, NT_B])
    mod_d = P.din("modv", [128, KC, 4])
    r_d = P.din("rw", [D, NE]).rearrange("(k p) c -> p k c", p=128)
    hT_d = P.dout("hT", [128, KC, NT_B])
    a_d = P.dout("aff", [NT_B, NE])
    xT = P.sb([128, KC, NT_B], name="xT")
    modv = P.sb([128, KC, 4], name="modv")
    rw = P.sb([128, KC, NE], name="rw")
    t_x = [Tok() for _ in range(KC)]
    t_mod, t_rw = Tok(), Tok()
    for k in range(KC):
        P.dma("sp" if k % 2 == 0 else "act", lambda e, k=k: e.dma_start(out=xT[:, k, :], in_=xT_d[:, k, :]), writes=[t_x[k]])
    P.dma("sp", lambda e: e.dma_start(out=modv[:], in_=mod_d), writes=[t_mod])
    P.dma("sp", lambda e: e.dma_start(out=rw[:], in_=r_d), writes=[t_rw])
    P.op("dve", lambda e: e.tensor_scalar_add(out=modv[:, :, 1:2], in0=modv[:, :, 1:2], scalar1=1.0), reads=[t_mod], writes=[t_mod])
    P.op("dve", lambda e: e.tensor_scalar_add(out=modv[:, :, 3:4], in0=modv[:, :, 3:4], scalar1=1.0), reads=[t_mod], writes=[t_mod])
    for k in range(KC):
        P.op("dve", lambda e, k=k: e.tensor_scalar(out=xT[:, k, 0:1024], in0=xT[:, k, 0:1024], scalar1=modv[:, k, 1:2],
                                                   scalar2=modv[:, k, 0:1], op0=ALU.mult, op1=ALU.add),
             reads=[t_mod, t_x[k]], writes=[t_x[k]])
        P.op("dve", lambda e, k=k: e.tensor_scalar(out=xT[:, k, 1024:NT_B], in0=xT[:, k, 1024:NT_B], scalar1=modv[:, k, 3:4],
                                                   scalar2=modv[:, k, 2:3], op0=ALU.mult, op1=ALU.add),
             reads=[t_mod, t_x[k]], writes=[t_x[k]])
        P.dma("pool", lambda e, k=k: e.dma_start(out=hT_d[:, k, :], in_=xT[:, k, :]), reads=[t_x[k]], is_out=True)
    pl = [P.ps([128, NE], name="pl") for _ in range(2)]
    t_pl = [Tok(True), Tok(True)]
    ex = [P.sb([128, NE], name="ex") for _ in range(2)]
    t_ex = [Tok(), Tok()]
    st = P.sb([128, 4], name="st")
    t_st = Tok()
    tiles = [(i * 128, 128) for i in range(8)] + [(1024, 64)]
    for ti, (t0, m) in enumerate(tiles):
        bi = ti % 2
        for k in range(KC):
            P.op("pe", lambda e, bi=bi, k=k, t0=t0, m=m: e.matmul(pl[bi][0:m, :], lhsT=xT[:, k, t0:t0 + m], rhs=rw[:, k, :], start=(k == 0), stop=(k == KC - 1)),
                 reads=[t_x[k], t_rw], writes=[t_pl[bi]])
        P.op("dve", lambda e, bi=bi, m=m: e.reduce_max(out=st[0:m, 0:1], in_=pl[bi][0:m, :], axis=AX.X), reads=[t_pl[bi]], writes=[t_st])
        P.op("dve", lambda e, m=m: e.tensor_scalar_mul(out=st[0:m, 1:2], in0=st[0:m, 0:1], scalar1=-1.0), reads=[t_st], writes=[t_st])
        P.op("act", lambda e, bi=bi, m=m: e.activation(out=ex[bi][0:m, :], in_=pl[bi][0:m, :], func=AF.Exp, bias=st[0:m, 1:2], scale=1.0, accum_out=st[0:m, 2:3]),
             reads=[t_pl[bi], t_st], writes=[t_ex[bi], t_st])
        P.op("dve", lambda e, m=m: e.reciprocal(out=st[0:m, 3:4], in_=st[0:m, 2:3]), reads=[t_st], writes=[t_st])
        P.op("dve", lambda e, bi=bi, m=m: e.tensor_scalar(out=ex[bi][0:m, :], in0=ex[bi][0:m, :], scalar1=st[0:m, 3:4], scalar2=None, op0=ALU.mult),
             reads=[t_st, t_ex[bi]], writes=[t_ex[bi]])
        P.dma("pool", lambda e, bi=bi, t0=t0, m=m: e.dma_start(out=a_d[t0:t0 + m, :], in_=ex[bi][0:m, :]), reads=[t_ex[bi]], writes=[t_ex[bi]], is_out=True)
    return P


def unfm(hT):
    p, kc, T = hT.shape
    return np.ascontiguousarray(hT.transpose(2, 1, 0).reshape(T, kc * p))


def stage_router(x_lat, x_ctx, mod_lat, mod_ctx, router_l):
    P = build_router()
    in_maps = []
    for c in range(NCORES):
        b = c // 4
        mv = np.stack([mod_lat[b, 3], mod_lat[b, 4], mod_ctx[3], mod_ctx[4]], axis=-1)
        mv = np.ascontiguousarray(mv.reshape(D // 128, 128, 4).transpose(1, 0, 2))
        in_maps.append({"xT": fm(tok_shard(x_lat, x_ctx, c)), "modv": mv, "rw": np.ascontiguousarray(router_l)})
    res = run_prog(P, in_maps)
    res2 = [{"h": unfm(r["hT"]), "aff": r["aff"]} for r in res]
    h_lat, h_ctx = tok_unshard(res2, "h", D)
    a_lat, a_ctx = tok_unshard(res2, "aff", NE)
    return h_lat, h_ctx, a_lat, a_ctx


NBIS = 30
H_ROWS = B * SEQ + B * CTX


def build_select():
    P = Prog()
    a_d = P.din("A", [8, SEQ])
    cc_d = P.din("cc", [8, 1])
    tvc_d = P.din("tvc", [128, 32])
    io_d = P.din("iota", [128, 512])
    id_d = P.din("ident", [128, 128])
    idx_d = P.dout("idx", [128, 8, 4], I32)
    gate_d = P.dout("gate", [128, 8, 4])
    h_d = P.din("h", [H_ROWS, D])
    xsc_d = P.dout("xsc", [4, 544, D])
    A = P.sb([8, SEQ], name="A")
    M = P.sb([8, SEQ], name="M")
    Cm = P.sb([8, SEQ], name="Cm")
    onesr = P.sb([8, SEQ], name="onesr")
    cc = P.sb([8, 1], name="cc")
    tvc = P.sb([128, 32], name="tvc")
    iota = P.sb([128, 512], name="iota")
    ident = P.sb([128, 128], name="ident")
    t_A, t_M, t_Cm, t_on, t_cc, t_tvc, t_io, t_id = [Tok() for _ in range(8)]
    P.dma("sp", lambda e: e.dma_start(out=A[:], in_=a_d), writes=[t_A])
    P.dma("sp", lambda e: e.dma_start(out=cc[:], in_=cc_d), writes=[t_cc])
    P.dma("act", lambda e: e.dma_start(out=tvc[:], in_=tvc_d), writes=[t_tvc])
    P.dma("act", lambda e: e.dma_start(out=iota[:], in_=io_d), writes=[t_io])
    P.dma("act", lambda e: e.dma_start(out=ident[:], in_=id_d), writes=[t_id])
    P.op("pool", lambda e: e.memset(onesr[:], 1.0), writes=[t_on])
    bs = P.sb([8, 4], name="bs")
    t_bs = Tok()
    P.op("dve", lambda e: e.memset(bs[:], 0.0), writes=[t_bs])
    for k in range(1, NBIS + 1):
        w = 2.0 ** (-k)
        P.op("dve", lambda e, w=w: e.tensor_scalar_add(out=bs[:, 1:2], in0=bs[:, 0:1], scalar1=w), reads=[t_bs], writes=[t_bs])
        P.op("dve", lambda e: e.tensor_scalar(out=M[:], in0=A[:], scalar1=bs[:, 1:2], scalar2=None, op0=ALU.is_ge, op1=ALU.add, accum_out=bs[:, 2:3]),
             reads=[t_A, t_bs], writes=[t_M, t_bs])
        P.op("dve", lambda e: e.tensor_tensor(out=bs[:, 3:4], in0=bs[:, 2:3], in1=cc[:], op=ALU.is_ge), reads=[t_bs, t_cc], writes=[t_bs])
        P.op("dve", lambda e, w=w: e.scalar_tensor_tensor(out=bs[:, 0:1], in0=bs[:, 3:4], scalar=w, in1=bs[:, 0:1], op0=ALU.mult, op1=ALU.add),
             reads=[t_bs], writes=[t_bs])
    P.op("dve", lambda e: e.tensor_scalar(out=M[:], in0=A[:], scalar1=bs[:, 0:1], scalar2=None, op0=ALU.is_ge), reads=[t_A, t_bs], writes=[t_M])
    P.op("dve", lambda e: e.tensor_tensor_scan(out=Cm[:], data0=onesr[:], data1=M[:], initial=0.0, op0=ALU.mult, op1=ALU.add),
         reads=[t_on, t_M], writes=[t_Cm])
    P.op("dve", lambda e: e.tensor_sub(out=Cm[:], in0=Cm[:], in1=M[:]), reads=[t_M, t_Cm], writes=[t_Cm])
    T3 = P.sb([128, 32, 24], name="T3")
    t_T3 = Tok()
    pT = [P.ps([128, 32], name="pT") for _ in range(2)]
    t_pT = [Tok(True), Tok(True)]
    for j in range(32):
        bi = j % 2
        ts_ = slice(j * 128, (j + 1) * 128)
        for i, (src, tk) in enumerate(((A, t_A), (M, t_M), (Cm, t_Cm))):
            P.op("pe", lambda e, bi=bi, i=i, src=src, ts_=ts_: e.transpose(out=pT[bi][:, i * 8:(i + 1) * 8], in_=src[0:8, ts_], identity=ident[0:8, 0:8]),
                 reads=[tk, t_id], writes=[t_pT[bi]])
        P.op("dve" if bi == 0 else "act", (lambda e, bi=bi, j=j: e.tensor_copy(out=T3[:, j, :], in_=pT[bi][:, 0:24])) if bi == 0 else
             (lambda e, bi=bi, j=j: e.copy(out=T3[:, j, :], in_=pT[bi][:, 0:24])), reads=[t_pT[bi]], writes=[t_T3])
    TV = P.sb([128, 32, 8, 2], name="TV")
    t_TV = Tok()
    for r in range(8):
        P.op("dve", lambda e, r=r: e.tensor_copy(out=TV[:, :, r, 0], in_=tvc[:]), reads=[t_tvc], writes=[t_TV])
    P.op("dve", lambda e: e.tensor_copy(out=TV[:, :, :, 1], in_=T3[:, :, 0:8]), reads=[t_T3], writes=[t_TV])
    Pm = P.sb([128, 32, 512], name="Pm")
    t_Pm = Tok()
    pi_ = P.ps([128, 64], name="pi")
    t_pi = Tok(True)
    res_i = P.sb([128, 8, 4], I32, name="res_i")
    res_f = P.sb([128, 8, 4], name="res_f")
    res_g = P.sb([128, 8, 4], name="res_g")
    t_res = Tok()
    P.op("dve", lambda e: e.memset(res_f[:], 0.0), writes=[t_res])
    P.op("dve", lambda e: e.memset(res_g[:], 0.0), writes=[t_res])
    for r in range(8):
        lat = r < 4
        C = 512 if lat else 32
        nj = 32 if lat else 2
        b = r % 2
        base = float(b * SEQ) if lat else float(B * SEQ + b * CTX)
        for j in range(nj):
            P.op("dve", lambda e, j=j, r=r, C=C: e.tensor_scalar(out=Pm[:, j, 0:C], in0=iota[:, 0:C], scalar1=T3[:, j, 16 + r:17 + r],
                                                                 scalar2=T3[:, j, 8 + r:9 + r], op0=ALU.is_equal, op1=ALU.mult),
                 reads=[t_io, t_T3], writes=[t_Pm])
        for sc in range(4 if lat else 1):
            msz = 128 if lat else 32
            for j in range(nj):
                P.op("pe", lambda e, r=r, sc=sc, j=j, msz=msz, nj=nj: e.matmul(pi_[0:msz, (r * 4 + sc) * 2:(r * 4 + sc) * 2 + 2],
                                                                            lhsT=Pm[:, j, sc * 128:sc * 128 + msz], rhs=TV[:, j, r, :],
                                                                            start=(j == 0), stop=(j == nj - 1)),
                     reads=[t_Pm, t_TV], writes=[t_pi])
            P.op("dve", lambda e, r=r, sc=sc, msz=msz, base=base: e.tensor_scalar_add(out=res_f[0:msz, r, sc:sc + 1],
                                                                                   in0=pi_[0:msz, (r * 4 + sc) * 2:(r * 4 + sc) * 2 + 1], scalar1=base),
                 reads=[t_pi], writes=[t_res])
            P.op("dve", lambda e, r=r, sc=sc, msz=msz: e.tensor_copy(out=res_g[0:msz, r, sc:sc + 1], in_=pi_[0:msz, (r * 4 + sc) * 2 + 1:(r * 4 + sc) * 2 + 2]),
                 reads=[t_pi], writes=[t_res])
    P.op("dve", lambda e: e.tensor_copy(out=res_i[:], in_=res_f[:]), reads=[t_res], writes=[t_res])
    P.dma("pool", lambda e: e.dma_start(out=idx_d, in_=res_i[:]), reads=[t_res], is_out=True)
    P.dma("pool", lambda e: e.dma_start(out=gate_d, in_=res_g[:]), reads=[t_res], is_out=True)
    xs = [P.sb([128, D], name="xs") for _ in range(2)]
    t_xs = [Tok(), Tok()]
    ixs = 0
    for el in range(2):
        for b in range(B):
            pas = el * 2 + b
            for (r, sc, s0, m) in [(el * 2 + b, sc, sc * 128, 128) for sc in range(4)] + [(4 + el * 2 + b, 0, 512, 32)]:
                xb = ixs % 2
                ixs += 1
                P.dma("pool", lambda e, xb=xb, r=r, sc=sc, m=m: e.indirect_dma_start(
                    out=xs[xb][0:m, :], out_offset=None, in_=h_d[:, :], in_offset=bass.IndirectOffsetOnAxis(ap=res_i[0:m, r, sc:sc + 1], axis=0)),
                    reads=[t_res], writes=[t_xs[xb]])
                P.dma("sp", lambda e, xb=xb, pas=pas, s0=s0, m=m: e.dma_start(out=xsc_d[pas, s0:s0 + m, :], in_=xs[xb][0:m, :]),
                      reads=[t_xs[xb]], writes=[t_xs[xb]], is_out=True)
    return P


TVC = (np.arange(32)[None, :] * 128 + np.arange(128)[:, None]).astype(np.float32)
IOTA512 = np.ascontiguousarray(np.broadcast_to(np.arange(512, dtype=np.float32)[None, :], (128, 512)))
CCOL = np.array([512] * 4 + [32] * 4, np.float32)[:, None]


def stage_select(a_lat, a_ctx, h_lat, h_ctx):
    P = build_select()
    h_all = np.ascontiguousarray(np.concatenate([h_lat.reshape(B * SEQ, D), h_ctx.reshape(B * CTX, D)], axis=0))
    in_maps = []
    for c in range(NCORES):
        A = np.full((8, SEQ), -1.0, np.float32)
        for el in range(2):
            for b in range(B):
                A[el * 2 + b] = a_lat[b, :, 2 * c + el]
                A[4 + el * 2 + b, :CTX] = a_ctx[b, :, 2 * c + el]
        in_maps.append({"A": A, "cc": CCOL, "tvc": TVC, "iota": IOTA512, "ident": IDENT, "h": h_all})
    res = run_prog(P, in_maps)
    return [(r["idx"], r["gate"], r["xsc"]) for r in res]


def build_expert(els=(0, 1), bs=(0, 1)):
    P = Prog()
    KC = D // 128
    h_d = P.din("xsc", [4, 544, D])
    idx_d = P.din("idx", [128, 8, 4], I32)
    gate_d = P.din("gate", [128, 8, 4])
    w1_d = P.din("w1", [len(els), D, FF])
    w3_d = P.din("w3", [len(els), D, FF])
    w2_d = P.din("w2", [len(els), FF, D])
    id_d = P.din("ident", [128, 128])
    f_d = [P.dout("f%d" % dc, [H_ROWS, 512]) for dc in range(4)]
    ident = P.sb([128, 128], name="ident")
    idx = P.sb([128, 8, 4], I32, name="idx")
    gate = P.sb([128, 8, 4], name="gate")
    t_id, t_idx, t_gate = Tok(), Tok(), Tok()
    P.dma("sp", lambda e: e.dma_start(out=ident[:], in_=id_d), writes=[t_id])
    P.dma("sp", lambda e: e.dma_start(out=idx[:], in_=idx_d), writes=[t_idx])
    P.dma("sp", lambda e: e.dma_start(out=gate[:], in_=gate_d), writes=[t_gate])
    zt = P.sb([128, 2048], name="zt")
    t_z = Tok()
    t_f = [Tok() for _ in range(4)]
    P.op("dve", lambda e: e.memset(zt[:], 0.0), writes=[t_z])
    for dc in range(4):
        for r0 in range(0, H_ROWS, 512):
            P.dma("sp" if (r0 // 512) % 2 == 0 else "act",
                  lambda e, dc=dc, r0=r0: e.dma_start(out=f_d[dc][r0:r0 + 512, :].rearrange("(p n) c -> p n c", p=128), in_=zt[:].rearrange("p (n c) -> p n c", n=4)),
                  reads=[t_z], writes=[t_f[dc]], is_out=True)
    NSL = 544 * len(bs)
    xs = [P.sb([128, D], name="xs") for _ in range(2)]
    t_xs = [Tok(), Tok()]
    xsT = P.sb([128, KC, NSL], BF16, name="xsT")
    t_xsT = Tok()
    hT = P.sb([128, KC, NSL], BF16, name="hT")
    t_hT = Tok()
    NWB = 3
    wa = [P.sb([128, KC, 128], BF16, name="w1c") for _ in range(NWB)]
    wu = [P.sb([128, KC, 128], BF16, name="w3c") for _ in range(NWB)]
    t_wa = [Tok() for _ in range(NWB)]
    t_wu = [Tok() for _ in range(NWB)]
    w2c = [P.sb([128, KC, 512], BF16, name="w2c") for _ in range(2)]
    t_w2 = [Tok(), Tok()]
    tmp = [P.sb([128, 512], name="tmp") for _ in range(2)]
    t_tmp = [Tok(), Tok()]
    yb = [P.sb([128, 512], name="yb") for _ in range(2)]
    t_yb = [Tok(), Tok()]
    pT = [P.ps([128, 512], name="pT") for _ in range(2)]
    t_pT = [Tok(True), Tok(True)]
    pa = [P.ps([128, 512], name="pa") for _ in range(2)]
    t_pa = [Tok(True), Tok(True)]
    pu = [P.ps([128, 512], name="pu") for _ in range(2)]
    t_pu = [Tok(True), Tok(True)]
    py = [P.ps([128, 512], name="py") for _ in range(2)]
    t_py = [Tok(True), Tok(True)]
    ixs = ipt = iw = iw2 = iau = iy = 0
    for eli, el in enumerate(els):
        chunks = []
        groups = []
        for bi_, b in enumerate(bs):
            o = bi_ * 544
            chunks += [(el * 2 + b, sc, o + sc * 128, 128, sc * 128) for sc in range(4)] + [(4 + el * 2 + b, 0, o + 512, 32, 512)]
            groups += [(o, 512), (o + 512, 32)]
        for (r, sc, s0, m, src0) in chunks:
            xb = ixs % 2
            ixs += 1
            b = r % 2
            P.dma("sp", lambda e, xb=xb, el=el, b=b, src0=src0, m=m: e.dma_start(out=xs[xb][0:m, :], in_=h_d[el * 2 + b, src0:src0 + m, :]),
                  writes=[t_xs[xb]])
            for k4 in range(KC // 4):
                pb = ipt % 2
                ipt += 1
                for kk in range(4):
                    k = k4 * 4 + kk
                    P.op("pe", lambda e, pb=pb, kk=kk, xb=xb, k=k, m=m: e.transpose(out=pT[pb][:, kk * 128:kk * 128 + m], in_=xs[xb][0:m, k * 128:(k + 1) * 128],
                                                                               identity=ident[0:m, 0:m]),
                         reads=[t_xs[xb], t_id], writes=[t_pT[pb]])
                if pb == 0:
                    P.op("act", lambda e, pb=pb, k4=k4, s0=s0, m=m: e.copy(out=xsT[:, k4 * 4:k4 * 4 + 4, s0:s0 + m],
                                                                         in_=pT[pb][:].rearrange("p (a c) -> p a c", a=4)[:, :, 0:m]),
                         reads=[t_pT[pb]], writes=[t_xsT])
                else:
                    P.op("dve", lambda e, pb=pb, k4=k4, s0=s0, m=m: e.tensor_copy(out=xsT[:, k4 * 4:k4 * 4 + 4, s0:s0 + m],
                                                                                in_=pT[pb][:].rearrange("p (a c) -> p a c", a=4)[:, :, 0:m]),
                         reads=[t_pT[pb]], writes=[t_xsT])
        for fc in range(KC):
            wb = iw % NWB
            iw += 1
            P.dma("pool", lambda e, wb=wb, eli=eli, fc=fc: e.dma_start(out=wa[wb][:], in_=w1_d[eli, :, fc * 128:(fc + 1) * 128].rearrange("(k p) c -> p k c", p=128)),
                  writes=[t_wa[wb]])
            P.dma("pool", lambda e, wb=wb, eli=eli, fc=fc: e.dma_start(out=wu[wb][:], in_=w3_d[eli, :, fc * 128:(fc + 1) * 128].rearrange("(k p) c -> p k c", p=128)),
                  writes=[t_wu[wb]])
            for (g0, gw) in groups:
                ab = iau % 2
                iau += 1
                for k in range(KC):
                    P.op("pe", lambda e, ab=ab, wb=wb, k=k, g0=g0, gw=gw: e.matmul(pa[ab][:, 0:gw], lhsT=wa[wb][:, k, :], rhs=xsT[:, k, g0:g0 + gw],
                                                                                start=(k == 0), stop=(k == KC - 1)),
                         reads=[t_wa[wb], t_xsT], writes=[t_pa[ab]])
                for k in range(KC):
                    P.op("pe", lambda e, ab=ab, wb=wb, k=k, g0=g0, gw=gw: e.matmul(pu[ab][:, 0:gw], lhsT=wu[wb][:, k, :], rhs=xsT[:, k, g0:g0 + gw],
                                                                                start=(k == 0), stop=(k == KC - 1)),
                         reads=[t_wu[wb], t_xsT], writes=[t_pu[ab]])
                P.op("act", lambda e, ab=ab, gw=gw: e.activation(out=tmp[ab][:, 0:gw], in_=pa[ab][:, 0:gw], func=AF.Silu), reads=[t_pa[ab]], writes=[t_tmp[ab]])
                P.op("dve", lambda e, ab=ab, fc=fc, g0=g0, gw=gw: e.tensor_mul(out=hT[:, fc, g0:g0 + gw], in0=tmp[ab][:, 0:gw], in1=pu[ab][:, 0:gw]),
                     reads=[t_tmp[ab], t_pu[ab]], writes=[t_hT])
        for dc in range(4):
            w2b = iw2 % 2
            iw2 += 1
            for half in range(2):
                ks = slice(half * 8, half * 8 + 8)
                P.dma("pool", lambda e, w2b=w2b, eli=eli, dc=dc, ks=ks: e.dma_start(
                    out=w2c[w2b][:, ks, :], in_=w2_d[eli, :, dc * 512:(dc + 1) * 512].rearrange("(k p) c -> p k c", p=128)[:, ks, :]), writes=[t_w2[w2b]])
            for (r, sc, s0, m, src0) in chunks:
                yi = iy % 2
                iy += 1
                for fc in range(KC):
                    P.op("pe", lambda e, yi=yi, fc=fc, s0=s0, m=m, w2b=w2b: e.matmul(py[yi][0:m, :], lhsT=hT[:, fc, s0:s0 + m], rhs=w2c[w2b][:, fc, :],
                                                                                  start=(fc == 0), stop=(fc == KC - 1)),
                         reads=[t_hT, t_w2[w2b]], writes=[t_py[yi]])
                P.op("dve", lambda e, yi=yi, r=r, sc=sc, m=m: e.tensor_scalar(out=yb[yi][0:m, :], in0=py[yi][0:m, :], scalar1=gate[0:m, r, sc:sc + 1], scalar2=None, op0=ALU.mult),
                     reads=[t_py[yi], t_gate], writes=[t_yb[yi]])
                P.dma("pool", lambda e, yi=yi, dc=dc, r=r, sc=sc, m=m: e.indirect_dma_start(
                    out=f_d[dc][:, :], out_offset=bass.IndirectOffsetOnAxis(ap=idx[0:m, r, sc:sc + 1], axis=0), in_=yb[yi][0:m, :], in_offset=None, compute_op=ALU.add),
                    reads=[t_idx, t_yb[yi]], writes=[t_f[dc], t_yb[yi]], is_out=True)
    return P


def stage_expert(sel, w1_l, w3_l, w2_l, els=(0, 1), bs=(0, 1)):
    P = build_expert(els, bs)
    in_maps = []
    for c in range(NCORES):
        in_maps.append({"xsc": sel[c][2], "idx": sel[c][0], "gate": sel[c][1], "w1": np.ascontiguousarray(w1_l[[2 * c + e for e in els]]),
                        "w3": np.ascontiguousarray(w3_l[[2 * c + e for e in els]]), "w2": np.ascontiguousarray(w2_l[[2 * c + e for e in els]]), "ident": IDENT})
    res = run_prog(P, in_maps)
    return [np.stack([r["f%d" % dc] for dc in range(4)]) for r in res]


def stage_final(fparts, x_lat, x_ctx, gate_lat, gate_ctx, gain, bias):
    P = build_outproj(False, NCORES)
    in_maps = []
    for c in range(NCORES):
        b, q = divmod(c, 4)
        ys = []
        for fp in fparts:
            lat = fp[:, b * SEQ + q * 1024:b * SEQ + (q + 1) * 1024, :]
            ctx = fp[:, B * SEQ + b * CTX + q * 64:B * SEQ + b * CTX + (q + 1) * 64, :]
            y = np.concatenate([lat, ctx], axis=1)
            ys.append(y.transpose(1, 0, 2).reshape(NT_B, D))
        cst = np.stack([bc(gate_lat[b]), bc(gate_ctx), bc(gain), bc(bias)], axis=0)
        in_maps.append({"x": tok_shard(x_lat, x_ctx, c), "cst": cst, "y": np.ascontiguousarray(np.stack(ys))})
    res = run_prog(P, in_maps)
    return tok_unshard(res, "o", D)


def kernel(x, c, ctx, c_ctx, w_ada, b_ada, w_in, w_out, gla_w_up, gla_b_up, gla_norm,
           gdn_conv, gdn_a_log, gdn_dt_bias, gdn_norm, attn_qk_norm, ln_gain, ln_bias,
           router, w1, w3, w2):
    f = lambda a: np.asarray(a, dtype=np.float32)
    x, c, ctx, c_ctx = f(x), f(c), f(ctx), f(c_ctx)
    w_ada, b_ada, w_in, w_out = f(w_ada), f(b_ada), f(w_in), f(w_out)
    gla_w_up, gla_b_up, gla_norm = f(gla_w_up), f(gla_b_up), f(gla_norm)
    gdn_conv, gdn_a_log, gdn_dt_bias, gdn_norm = f(gdn_conv), f(gdn_a_log), f(gdn_dt_bias), f(gdn_norm)
    attn_qk_norm, ln_gain, ln_bias, router = f(attn_qk_norm), f(ln_gain), f(ln_bias), f(router)
    w1, w3, w2 = f(w1), f(w3), f(w2)
    mod_lat, mod_ctx = stage_mod(c, c_ctx, w_ada, b_ada)
    x_lat, x_ctx = x, ctx
    for l in range(DEPTH):
        p_lat, p_ctx = stage_inproj(x_lat, x_ctx, mod_lat[l], mod_ctx[l], w_in[l])
        gla_l, gla_c = stage_gla(p_lat, p_ctx, gla_w_up[l], gla_b_up[l], gla_norm[l])
        gdn_l, gdn_c = stage_gdn(p_lat, p_ctx, gdn_conv[l], gdn_a_log[l], gdn_dt_bias[l], gdn_norm[l])
        att_l, att_c = stage_attn(p_lat, p_ctx, attn_qk_norm[l])
        mix_l = np.concatenate([gla_l, gdn_l, att_l], axis=-1)
        mix_c = np.concatenate([gla_c, gdn_c, att_c], axis=-1)
        x_lat, x_ctx = stage_outproj(mix_l, mix_c, x_lat, x_ctx, mod_lat[l][:, 2], mod_ctx[l][2], ln_gain[l, 0], ln_bias[l, 0], w_out[l])
        h_lat, h_ctx, a_lat, a_ctx = stage_router(x_lat, x_ctx, mod_lat[l], mod_ctx[l], router[l])
        sel = stage_select(a_lat, a_ctx, h_lat, h_ctx)
        fparts = stage_expert(sel, w1[l], w3[l], w2[l])
        x_lat, x_ctx = stage_final(fparts, x_lat, x_ctx, mod_lat[l][:, 5], mod_ctx[l][5], ln_gain[l, 1], ln_bias[l, 1])
    return np.ascontiguousarray(x_lat, dtype=np.float32)
```

```python
import os
import time
import numpy as np
import concourse.bass as bass
import concourse.mybir as mybir
from concourse.bass_utils import run_bass_kernel_spmd

F32 = mybir.dt.float32
BF16 = mybir.dt.bfloat16
U32 = mybir.dt.uint32
I32 = mybir.dt.int32
AF = mybir.ActivationFunctionType
ALU = mybir.AluOpType
AX = mybir.AxisListType

NCORES = 8
D = 2048
B = 2
SEQ = 4096
CTX = 256
DEPTH = 2
NE = 16
FF = 2048
IN_W = 5168
ALPHA = (2 * DEPTH) ** 0.25


class Tok:
    __slots__ = ("w", "r", "excl")

    def __init__(self, excl=False):
        self.w = None
        self.r = []
        self.excl = excl


class Prog:
    CE = ("act", "pe", "dve", "pool")
    DQ = ("sp", "act", "pool")
    ND = 6

    def __init__(self):
        self.nc = bass.Bass("TRN2", target_bir_lowering=False)
        nc = self.nc
        self.q = {e: [] for e in ("sp", "act", "pe", "dve", "pool")}
        self.csem = {e: nc.alloc_semaphore("c_" + e) for e in self.CE}
        self.ccnt = {e: 0 for e in self.CE}
        self.dsem = {e: [nc.alloc_semaphore("d_%s%d" % (e, i)) for i in range(self.ND)] for e in self.DQ}
        self.dcnt = {e: [0] * self.ND for e in self.DQ}
        self.drr = {e: 0 for e in self.DQ}
        self.waited = {e: {} for e in self.q}
        self.out_deps = []
        self.nm = 0

    def name(self, p):
        self.nm += 1
        return "%s_%d" % (p, self.nm)

    def sb(self, shape, dt=F32, name="sb"):
        return self.nc.alloc_sbuf_tensor(self.name(name), list(shape), dt)

    def ps(self, shape, dt=F32, name="ps"):
        return self.nc.alloc_psum_tensor(self.name(name), list(shape), dt)

    def din(self, name, shape, dt=F32):
        return self.nc.dram_tensor(name, list(shape), dt, kind="ExternalInput").ap()

    def dout(self, name, shape, dt=F32):
        return self.nc.dram_tensor(name, list(shape), dt, kind="ExternalOutput").ap()

    def dscratch(self, name, shape, dt=F32):
        return self.nc.dram_tensor(name, list(shape), dt, kind="Internal").ap()

    def _deps(self, eng, reads, writes, extra=()):
        need = {}

        def add(dep):
            if dep is None:
                return
            s, v = dep
            if need.get(s, 0) < v:
                need[s] = v

        own = self.csem.get(eng)
        for t in reads:
            add(t.w)
            if t.excl:
                for r in t.r:
                    if r[0] is not own:
                        add(r)
        for t in writes:
            add(t.w)
            for r in t.r:
                add(r)
        for d in extra:
            add(d)
        if eng == "pe":
            need.pop(self.csem["pe"], None)
        out = []
        wd = self.waited[eng]
        for s, v in need.items():
            if wd.get(s, 0) >= v:
                continue
            wd[s] = v
            out.append((s, v))
        return out

    def _mark(self, reads, writes, done):
        for t in reads:
            t.r.append(done)
        for t in writes:
            t.w = done
            t.r = []

    def op(self, eng, fn, reads=(), writes=()):
        waits = self._deps(eng, reads, writes)
        self.ccnt[eng] += 1
        done = (self.csem[eng], self.ccnt[eng])
        self.q[eng].append((waits, fn, self.csem[eng], 1))
        self._mark(reads, writes, done)
        return done

    def dma(self, eng, fn, reads=(), writes=(), is_out=False):
        k = self.drr[eng]
        self.drr[eng] = (k + 1) % self.ND
        sem = self.dsem[eng][k]
        prev = (sem, self.dcnt[eng][k]) if self.dcnt[eng][k] else None
        waits = self._deps(eng, reads, writes, extra=(prev,) if prev else ())
        self.dcnt[eng][k] += 16
        done = (sem, self.dcnt[eng][k])
        self.q[eng].append((waits, fn, sem, 16))
        self._mark(reads, writes, done)
        if is_out:
            self.out_deps.append(done)
        return done

    def finish(self):
        nc = self.nc
        fin = {}
        for s, v in self.out_deps:
            fin[s] = max(fin.get(s, 0), v)
        q = self.q
        engmap = {"sp": "sync", "act": "scalar", "pe": "tensor", "dve": "vector", "pool": "gpsimd"}
        with nc.Block() as block:
            for e, bn in engmap.items():
                def body(eng, e=e):
                    for waits, fn, sem, inc in q[e]:
                        for s, v in waits:
                            eng.wait_ge(s, v)
                        fn(eng).then_inc(sem, inc)
                    if e == "pool":
                        for s, v in fin.items():
                            eng.wait_ge(s, v)
                getattr(block, bn)(body)
        return nc


def run_prog(P, in_maps):
    t0 = time.time()
    nc = P.finish()
    t1 = time.time()
    res = run_bass_kernel_spmd(nc, in_maps, core_ids=list(range(NCORES)))
    if os.environ.get("KDBG"):
        nb = sum(v.nbytes for m in in_maps for v in m.values())
        print("[run_prog] build %.1fs run %.1fs in %.0fMB" % (t1 - t0, time.time() - t1, nb / 1e6), flush=True)
    return res.results


def fm(a):
    T, C = a.shape
    return np.ascontiguousarray(a.T.reshape(C // 128, 128, T).transpose(1, 0, 2))


NT_B = 1088


def build_inproj():
    P = Prog()
    nc = P.nc
    KC = D // 128
    xT_d = P.din("xT", [128, KC, NT_B])
    mod_d = P.din("modv", [128, KC, 4])
    w_d = P.din("w", [D, IN_W]).rearrange("(k p) c -> p k c", p=128)
    p_d = P.dout("p", [NT_B, IN_W])

    xT = P.sb([128, KC, NT_B], name="xT")
    xb = P.sb([128, KC, NT_B], BF16, name="xb")
    modv = P.sb([128, KC, 4], name="modv")
    t_x = [Tok() for _ in range(KC)]
    t_mod = Tok()
    for k in range(KC):
        P.dma("sp" if k % 2 == 0 else "act", lambda e, k=k: e.dma_start(out=xT[:, k, :], in_=xT_d[:, k, :]), writes=[t_x[k]])
    P.dma("sp", lambda e: e.dma_start(out=modv[:], in_=mod_d), writes=[t_mod])
    P.op("dve", lambda e: e.tensor_scalar_add(out=modv[:, :, 1:2], in0=modv[:, :, 1:2], scalar1=1.0), reads=[t_mod], writes=[t_mod])
    P.op("dve", lambda e: e.tensor_scalar_add(out=modv[:, :, 3:4], in0=modv[:, :, 3:4], scalar1=1.0), reads=[t_mod], writes=[t_mod])
    for k in range(KC):
        P.op("dve", lambda e, k=k: e.tensor_scalar(out=xb[:, k, 0:1024], in0=xT[:, k, 0:1024], scalar1=modv[:, k, 1:2],
                                                   scalar2=modv[:, k, 0:1], op0=ALU.mult, op1=ALU.add),
             reads=[t_mod, t_x[k]], writes=[t_x[k]])
        P.op("dve", lambda e, k=k: e.tensor_scalar(out=xb[:, k, 1024:NT_B], in0=xT[:, k, 1024:NT_B], scalar1=modv[:, k, 3:4],
                                                   scalar2=modv[:, k, 2:3], op0=ALU.mult, op1=ALU.add),
             reads=[t_mod, t_x[k]], writes=[t_x[k]])
    NW = 3
    wt = [P.sb([128, KC, 512], BF16, name="wt") for _ in range(NW)]
    t_w = [Tok() for _ in range(NW)]
    NPS = 4
    pst = [P.ps([128, 512], name="pp") for _ in range(NPS)]
    t_ps = [Tok() for _ in range(NPS)]
    ot = [P.sb([128, 512], name="ot") for _ in range(NPS)]
    t_ot = [Tok() for _ in range(NPS)]
    tiles = [(i * 128, 128) for i in range(8)] + [(1024, 64)]
    cgs = [(c, min(512, IN_W - c)) for c in range(0, IN_W, 512)]
    it = 0
    for ci, (c0, cw) in enumerate(cgs):
        wb = ci % NW
        for half in range(2):
            ks = slice(half * 8, half * 8 + 8)
            P.dma("pool",
                  lambda e, wb=wb, ks=ks, c0=c0, cw=cw: e.dma_start(out=wt[wb][:, ks, 0:cw], in_=w_d[:, ks, c0:c0 + cw]),
                  writes=[t_w[wb]])
        for (t0, m) in tiles:
            pb = it % NPS
            it += 1
            for k in range(KC):
                P.op("pe", lambda e, pb=pb, k=k, t0=t0, m=m, wb=wb, cw=cw: e.matmul(
                    pst[pb][0:m, 0:cw], lhsT=xb[:, k, t0:t0 + m], rhs=wt[wb][:, k, 0:cw], start=(k == 0), stop=(k == KC - 1)),
                    reads=[t_x[k], t_w[wb]], writes=[t_ps[pb]])
            ev = "act" if pb % 2 == 0 else "dve"
            if ev == "act":
                P.op("act", lambda e, pb=pb, m=m, cw=cw: e.copy(out=ot[pb][0:m, 0:cw], in_=pst[pb][0:m, 0:cw]),
                     reads=[t_ps[pb]], writes=[t_ot[pb]])
            else:
                P.op("dve", lambda e, pb=pb, m=m, cw=cw: e.tensor_copy(out=ot[pb][0:m, 0:cw], in_=pst[pb][0:m, 0:cw]),
                     reads=[t_ps[pb]], writes=[t_ot[pb]])
            P.dma("sp" if pb % 2 == 0 else "act", lambda e, pb=pb, t0=t0, m=m, c0=c0, cw=cw: e.dma_start(out=p_d[t0:t0 + m, c0:c0 + cw], in_=ot[pb][0:m, 0:cw]),
                  reads=[t_ot[pb]], is_out=True)
    return P


def stage_inproj(x_lat, x_ctx, mod_lat, mod_ctx, w_in_l):
    P = build_inproj()
    in_maps = []
    for c in range(NCORES):
        b, q = divmod(c, 4)
        xs = np.concatenate([x_lat[b, q * 1024:(q + 1) * 1024], x_ctx[b, q * 64:(q + 1) * 64]], axis=0)
        mv = np.stack([mod_lat[b, 0], mod_lat[b, 1], mod_ctx[0], mod_ctx[1]], axis=-1)
        mv = np.ascontiguousarray(mv.reshape(D // 128, 128, 4).transpose(1, 0, 2))
        in_maps.append({"xT": fm(xs), "modv": mv, "w": np.ascontiguousarray(w_in_l)})
    res = run_prog(P, in_maps)
    p_lat = np.empty((B, SEQ, IN_W), np.float32)
    p_ctx = np.empty((B, CTX, IN_W), np.float32)
    for c in range(NCORES):
        b, q = divmod(c, 4)
        p_lat[b, q * 1024:(q + 1) * 1024] = res[c]["p"][:1024]
        p_ctx[b, q * 64:(q + 1) * 64] = res[c]["p"][1024:]
    return p_lat, p_ctx


MODW = 6 * D // NCORES


def build_mod():
    P = Prog()
    KC = D // 128
    cv_d = P.din("cv", [128, KC, 3])
    wa_d = P.din("wa", [DEPTH, D, MODW]).rearrange("l (k p) c -> l p k c", p=128)
    ba_d = P.din("ba", [DEPTH, 1, MODW])
    mod_d = P.dout("mod", [DEPTH, 3, MODW])
    cv = P.sb([128, KC, 3], name="cv")
    ones = P.sb([1, 4], name="ones")
    ba = P.sb([1, DEPTH, MODW], name="ba")
    t_cv, t_ones, t_ba = Tok(), Tok(), Tok()
    P.dma("sp", lambda e: e.dma_start(out=cv[:], in_=cv_d), writes=[t_cv])
    for l in range(DEPTH):
        P.dma("sp", lambda e, l=l: e.dma_start(out=ba[:, l, :], in_=ba_d[l]), writes=[t_ba])
    P.op("dve", lambda e: e.memset(ones[:], 1.0), writes=[t_ones])
    P.op("act", lambda e: e.activation(out=cv[:], in_=cv[:], func=AF.Silu), reads=[t_cv], writes=[t_cv])
    wt = [P.sb([128, KC, 512], name="wa") for _ in range(2)]
    t_w = [Tok(), Tok()]
    pst = [P.ps([128, 512], name="pm") for _ in range(2)]
    t_ps = [Tok(), Tok()]
    ot = [P.sb([4, 512], name="om") for _ in range(2)]
    t_ot = [Tok(), Tok()]
    it = 0
    for l in range(DEPTH):
        for c0 in range(0, MODW, 512):
            bi = it % 2
            it += 1
            for half in range(2):
                ks = slice(half * 8, half * 8 + 8)
                P.dma("sp" if half == 0 else "act",
                      lambda e, bi=bi, ks=ks, c0=c0, l=l: e.dma_start(out=wt[bi][:, ks, :], in_=wa_d[l, :, ks, c0:c0 + 512]),
                      writes=[t_w[bi]])
            for k in range(KC):
                P.op("pe", lambda e, bi=bi, k=k: e.matmul(pst[bi][0:3, :], lhsT=cv[:, k, :], rhs=wt[bi][:, k, :], start=(k == 0), stop=False),
                     reads=[t_cv, t_w[bi]], writes=[t_ps[bi]])
            P.op("pe", lambda e, bi=bi, l=l, c0=c0: e.matmul(pst[bi][0:3, :], lhsT=ones[0:1, 0:3], rhs=ba[0:1, l, c0:c0 + 512], start=False, stop=True),
                 reads=[t_ones, t_ba], writes=[t_ps[bi]])
            P.op("dve", lambda e, bi=bi: e.tensor_copy(out=ot[bi][0:3, :], in_=pst[bi][0:3, :]), reads=[t_ps[bi]], writes=[t_ot[bi]])
            P.dma("pool", lambda e, bi=bi, l=l, c0=c0: e.dma_start(out=mod_d[l, :, c0:c0 + 512], in_=ot[bi][0:3, :]), reads=[t_ot[bi]], is_out=True)
    return P


def stage_mod(c, c_ctx, w_ada, b_ada):
    P = build_mod()
    vec = np.concatenate([c, c_ctx[None]], axis=0)
    cv = np.ascontiguousarray(vec.T.reshape(D // 128, 128, 3).transpose(1, 0, 2))
    in_maps = []
    for ci in range(NCORES):
        cs = slice(ci * MODW, (ci + 1) * MODW)
        in_maps.append({"cv": cv, "wa": np.ascontiguousarray(w_ada[:, :, cs]), "ba": np.ascontiguousarray(b_ada[:, None, cs])})
    res = run_prog(P, in_maps)
    mod = np.concatenate([res[ci]["mod"] for ci in range(NCORES)], axis=-1)
    mod = mod.reshape(DEPTH, 3, 6, D)
    return np.ascontiguousarray(mod[:, 0:2]), np.ascontiguousarray(mod[:, 2])


def ln_tile(P, z, t_z, m, gain, bias, t_c, st, t_st, outt, t_out):
    s1, mu, ss, rstd = st[:, 0:1], st[:, 1:2], st[:, 2:3], st[:, 3:4]
    P.op("dve", lambda e: e.reduce_sum(out=s1[0:m], in_=z[0:m, :], axis=AX.X), reads=[t_z], writes=[t_st])
    P.op("dve", lambda e: e.tensor_scalar_mul(out=mu[0:m], in0=s1[0:m], scalar1=1.0 / D), reads=[t_st], writes=[t_st])
    P.op("dve", lambda e: e.tensor_scalar(out=z[0:m, :], in0=z[0:m, :], scalar1=mu[0:m], scalar2=None, op0=ALU.subtract),
         reads=[t_st, t_z], writes=[t_z])
    P.op("act", lambda e: e.activation(out=outt[0:m, :], in_=z[0:m, :], func=AF.Square, accum_out=ss[0:m]),
         reads=[t_z], writes=[t_out, t_st])
    P.op("dve", lambda e: e.tensor_scalar(out=ss[0:m], in0=ss[0:m], scalar1=1.0 / D, scalar2=1e-5, op0=ALU.mult, op1=ALU.add),
         reads=[t_st], writes=[t_st])
    P.op("act", lambda e: e.activation(out=ss[0:m], in_=ss[0:m], func=AF.Sqrt), reads=[t_st], writes=[t_st])
    P.op("dve", lambda e: e.reciprocal(out=rstd[0:m], in_=ss[0:m]), reads=[t_st], writes=[t_st])
    P.op("dve", lambda e: e.scalar_tensor_tensor(out=outt[0:m, :], in0=z[0:m, :], scalar=rstd[0:m], in1=gain[0:m, :],
                                                 op0=ALU.mult, op1=ALU.mult), reads=[t_st, t_z, t_c], writes=[t_out])
    P.op("dve", lambda e: e.tensor_add(out=outt[0:m, :], in0=outt[0:m, :], in1=bias[0:m, :]), reads=[t_c, t_out], writes=[t_out])


def build_outproj(with_proj=True, nparts=1):
    P = Prog()
    KC = D // 128
    NT = NT_B
    x_d = P.din("x", [NT, D])
    cst_d = P.din("cst", [4, 128, D])
    if with_proj:
        mT_d = P.din("mT", [128, KC, NT])
        w_d = P.din("w", [D, D]).rearrange("(k p) c -> p k c", p=128)
    else:
        y_d = P.din("y", [nparts, NT, D])
    o_d = P.dout("o", [NT, D])
    cst = P.sb([128, 4, D], name="cst")
    t_c = Tok()
    for i in range(4):
        P.dma("sp", lambda e, i=i: e.dma_start(out=cst[:, i, :], in_=cst_d[i]), writes=[t_c])
    if with_proj:
        w = P.sb([128, KC, D], BF16, name="w")
        t_w = Tok()
        for k in range(KC):
            P.dma("pool", lambda e, k=k: e.dma_start(out=w[:, k, :], in_=w_d[:, k, :]), writes=[t_w])
        mT = [P.sb([128, KC, 128], BF16, name="mT") for _ in range(2)]
        t_m = [Tok(), Tok()]
        pst = [P.ps([128, 512], name="po") for _ in range(4)]
        t_ps = [Tok() for _ in range(4)]
    else:
        yt = [P.sb([128, D], name="yt") for _ in range(2)]
        t_y = [Tok(), Tok()]
    xt = [P.sb([128, D], name="xt") for _ in range(2)]
    t_x = [Tok(), Tok()]
    zt = P.sb([128, D], name="zt")
    t_z = Tok()
    st = P.sb([128, 4], name="st")
    t_st = Tok()
    tiles = [(i * 128, 128) for i in range(8)] + [(1024, 64)]
    for ti, (t0, m) in enumerate(tiles):
        bi = ti % 2
        gi = 0 if ti < 8 else 1
        P.dma("sp", lambda e, bi=bi, t0=t0, m=m: e.dma_start(out=xt[bi][0:m, :], in_=x_d[t0:t0 + m, :]), writes=[t_x[bi]])
        if with_proj:
            P.dma("pool", lambda e, bi=bi, t0=t0, m=m: e.dma_start(out=mT[bi][:, :, 0:m], in_=mT_d[:, :, t0:t0 + m]), writes=[t_m[bi]])
            for cg in range(4):
                for k in range(KC):
                    P.op("pe", lambda e, bi=bi, cg=cg, k=k, m=m: e.matmul(pst[cg][0:m, :], lhsT=mT[bi][:, k, 0:m], rhs=w[:, k, cg * 512:(cg + 1) * 512],
                                                                      start=(k == 0), stop=(k == KC - 1)),
                         reads=[t_m[bi], t_w], writes=[t_ps[cg]])
                P.op("dve", lambda e, cg=cg, m=m, gi=gi: e.tensor_mul(out=zt[0:m, cg * 512:(cg + 1) * 512], in0=pst[cg][0:m, :],
                                                                     in1=cst[0:m, gi, cg * 512:(cg + 1) * 512]),
                     reads=[t_ps[cg], t_c], writes=[t_z])
        else:
            for pi in range(nparts):
                P.dma("act", lambda e, bi=bi, t0=t0, m=m, pi=pi: e.dma_start(out=yt[bi][0:m, :], in_=y_d[pi, t0:t0 + m, :]), writes=[t_y[bi]])
                if pi == 0:
                    P.op("dve", lambda e, bi=bi, m=m: e.tensor_copy(out=zt[0:m, :], in_=yt[bi][0:m, :]), reads=[t_y[bi]], writes=[t_z])
                else:
                    P.op("dve", lambda e, bi=bi, m=m: e.tensor_add(out=zt[0:m, :], in0=zt[0:m, :], in1=yt[bi][0:m, :]), reads=[t_y[bi], t_z], writes=[t_z])
            P.op("dve", lambda e, m=m, gi=gi: e.tensor_mul(out=zt[0:m, :], in0=zt[0:m, :], in1=cst[0:m, gi, :]), reads=[t_c, t_z], writes=[t_z])
        P.op("dve", lambda e, bi=bi, m=m: e.scalar_tensor_tensor(out=zt[0:m, :], in0=xt[bi][0:m, :], scalar=float(ALPHA), in1=zt[0:m, :],
                                                                 op0=ALU.mult, op1=ALU.add), reads=[t_x[bi], t_z], writes=[t_z])
        ln_tile(P, zt, t_z, m, cst[:, 2, :], cst[:, 3, :], t_c, st, t_st, xt[bi], t_x[bi])
        P.dma("act", lambda e, bi=bi, t0=t0, m=m: e.dma_start(out=o_d[t0:t0 + m, :], in_=xt[bi][0:m, :]), reads=[t_x[bi]], writes=[t_x[bi]], is_out=True)
    return P


def tok_shard(lat, ctx, c):
    b, q = divmod(c, 4)
    return np.concatenate([lat[b, q * 1024:(q + 1) * 1024], ctx[b, q * 64:(q + 1) * 64]], axis=0)


def tok_unshard(res, key, width):
    lat = np.empty((B, SEQ, width), np.float32)
    ctx = np.empty((B, CTX, width), np.float32)
    for c in range(NCORES):
        b, q = divmod(c, 4)
        lat[b, q * 1024:(q + 1) * 1024] = res[c][key][:1024]
        ctx[b, q * 64:(q + 1) * 64] = res[c][key][1024:]
    return lat, ctx


def bc(v):
    return np.ascontiguousarray(np.broadcast_to(v[None, :], (128, v.shape[0])))


def stage_outproj(mix_lat, mix_ctx, x_lat, x_ctx, gate_lat, gate_ctx, gain, bias, w_out_l):
    P = build_outproj(True)
    in_maps = []
    for c in range(NCORES):
        b = c // 4
        cst = np.stack([bc(gate_lat[b]), bc(gate_ctx), bc(gain), bc(bias)], axis=0)
        in_maps.append({"x": tok_shard(x_lat, x_ctx, c), "cst": cst, "mT": fm(tok_shard(mix_lat, mix_ctx, c)),
                        "w": np.ascontiguousarray(w_out_l)})
    res = run_prog(P, in_maps)
    return tok_unshard(res, "o", D)


NTOK = CTX + SEQ
NTT = NTOK // 128
IDENT = np.eye(128, dtype=np.float32)


def rms_rstd(P, x_ap, m, width, eps, junk, t_junk, st, t_st, reads):
    P.op("act", lambda e: e.activation(out=junk[0:m, 0:width], in_=x_ap, func=AF.Square, accum_out=st[0:m, 0:1]),
         reads=reads, writes=[t_junk, t_st])
    P.op("dve", lambda e: e.tensor_scalar(out=st[0:m, 0:1], in0=st[0:m, 0:1], scalar1=1.0 / width, scalar2=eps, op0=ALU.mult, op1=ALU.add),
         reads=[t_st], writes=[t_st])
    P.op("act", lambda e: e.activation(out=st[0:m, 0:1], in_=st[0:m, 0:1], func=AF.Sqrt), reads=[t_st], writes=[t_st])
    P.op("dve", lambda e: e.reciprocal(out=st[0:m, 1:2], in_=st[0:m, 0:1]), reads=[t_st], writes=[t_st])


def build_attn():
    P = Prog()
    q_d = P.din("q", [NTOK, 256])
    k_d = P.din("k", [NTOK, 128])
    v_d = P.din("v", [NTOK, 128])
    g_d = P.din("g", [2, 128, 128])
    cs_d = P.din("cs", [2, SEQ, 128])
    id_d = P.din("ident", [128, 128])
    o_d = P.dout("o", [NTOK, 256])

    ident = P.sb([128, 128], name="ident")
    gq = P.sb([128, 2, 128], name="gq")
    t_id, t_g = Tok(), Tok()
    P.dma("sp", lambda e: e.dma_start(out=ident[:], in_=id_d), writes=[t_id])
    for i in range(2):
        P.dma("sp", lambda e, i=i: e.dma_start(out=gq[:, i, :], in_=g_d[i]), writes=[t_g])
    qT = P.sb([128, 2, NTOK], BF16, name="qT")
    kT = P.sb([128, NTOK], BF16, name="kT")
    va = P.sb([128, NTT, 129], BF16, name="va")
    t_qT, t_kT, t_va = Tok(), Tok(), Tok()
    P.op("pool", lambda e: e.memset(va[:, :, 128:129], 1.0), writes=[t_va])
    for half in range(2):
        hs = slice(half * 17, half * 17 + 17)
        P.dma("pool", lambda e, hs=hs, half=half: e.dma_start(out=va[:, hs, 0:128],
              in_=v_d[half * 17 * 128:(half + 1) * 17 * 128, :].rearrange("(n p) d -> p n d", p=128)), writes=[t_va])
    psT = [P.ps([128, 512], name="psT") for _ in range(2)]
    t_psT = [Tok(True), Tok(True)]

    def prep_lane(li):
        xin = P.sb([128, 3, 128], name="xin")
        cst = P.sb([128, 2, 128], name="cs")
        sq = P.sb([128, 3, 128], name="sq")
        xn = P.sb([128, 3, 128], name="xn")
        t1 = P.sb([128, 3, 128], name="t1")
        t2 = P.sb([128, 3, 128], name="t2")
        st = P.sb([128, 2, 3], name="st")
        t_xin, t_cs, t_sq, t_xn, t_t1, t_t2, t_st = [Tok() for _ in range(7)]
        for tt in range(li, NTT, 2):
            t0 = tt * 128
            lat = tt >= 2
            P.dma("sp", lambda e, t0=t0: e.dma_start(out=xin[:, 0:2, :], in_=q_d[t0:t0 + 128, :].rearrange("p (h d) -> p h d", h=2)), writes=[t_xin])
            P.dma("sp", lambda e, t0=t0: e.dma_start(out=xin[:, 2, :], in_=k_d[t0:t0 + 128, :]), writes=[t_xin])
            if lat:
                for i in range(2):
                    P.dma("act", lambda e, t0=t0, i=i: e.dma_start(out=cst[:, i, :], in_=cs_d[i, t0 - CTX:t0 - CTX + 128, :]), writes=[t_cs])
            yield
            P.op("act", lambda e: e.activation(out=sq[:], in_=xin[:], func=AF.Square), reads=[t_xin], writes=[t_sq])
            yield
            P.op("dve", lambda e: e.reduce_sum(out=st[:, 0, :], in_=sq[:], axis=AX.X), reads=[t_sq], writes=[t_st])
            yield
            P.op("dve", lambda e: e.tensor_scalar(out=st[:, 0, :], in0=st[:, 0, :], scalar1=1.0 / 128, scalar2=1e-6, op0=ALU.mult, op1=ALU.add), reads=[t_st], writes=[t_st])
            yield
            P.op("act", lambda e: e.activation(out=st[:, 0, :], in_=st[:, 0, :], func=AF.Sqrt), reads=[t_st], writes=[t_st])
            yield
            P.op("dve", lambda e: e.reciprocal(out=st[:, 1, :], in_=st[:, 0, :]), reads=[t_st], writes=[t_st])
            yield
            for h in range(3):
                gi = 0 if h < 2 else 1
                P.op("dve", lambda e, h=h, gi=gi: e.scalar_tensor_tensor(out=xn[:, h, :], in0=xin[:, h, :], scalar=st[:, 1, h:h + 1], in1=gq[:, gi, :], op0=ALU.mult, op1=ALU.mult),
                     reads=[t_xin, t_st, t_g], writes=[t_xn])
            yield
            src, t_src = xn, t_xn
            if lat:
                for h in range(3):
                    P.op("pool", lambda e, h=h: e.tensor_mul(out=t1[:, h, :], in0=xn[:, h, :], in1=cst[:, 0, :]), reads=[t_xn, t_cs], writes=[t_t1])
                x5 = xn[:].rearrange("p h (a b f) -> p h a b f", a=2, b=2)
                o5 = t2[:].rearrange("p h (a b f) -> p h a b f", a=2, b=2)
                s4 = cst[:, 1, :].rearrange("p (a b f) -> p a b f", a=2, b=2)
                for h in range(3):
                    for hb in range(2):
                        P.op("dve", lambda e, h=h, hb=hb: e.tensor_mul(out=o5[:, h, :, hb, :], in0=x5[:, h, :, 1 - hb, :], in1=s4[:, :, hb, :]),
                             reads=[t_xn, t_cs], writes=[t_t2])
                yield
                P.op("dve", lambda e: e.tensor_add(out=t1[:], in0=t1[:], in1=t2[:]), reads=[t_t2, t_t1], writes=[t_t1])
                yield
                src, t_src = t1, t_t1
            for h in range(3):
                P.op("pe", lambda e, h=h, src=src: e.transpose(out=psT[li][:, h * 128:(h + 1) * 128], in_=src[:, h, :], identity=ident[:]),
                     reads=[t_id, t_src], writes=[t_psT[li]])
            yield
            P.op("act", lambda e, t0=t0: e.copy(out=qT[:, :, t0:t0 + 128], in_=psT[li][:, 0:256].rearrange("p (h t) -> p h t", h=2)), reads=[t_psT[li]], writes=[t_qT])
            P.op("act", lambda e, t0=t0: e.copy(out=kT[:, t0:t0 + 128], in_=psT[li][:, 256:384]), reads=[t_psT[li]], writes=[t_kT])
            yield

    gens = [prep_lane(0), prep_lane(1)]
    while gens:
        for g in list(gens):
            try:
                next(g)
            except StopIteration:
                gens.remove(g)
    NS = 2
    pss = [P.ps([128, 512], name="pss") for _ in range(NS)]
    t_pss = [Tok() for _ in range(NS)]
    NPT = 3
    pt = [P.sb([128, 512], BF16, name="pt") for _ in range(NPT)]
    t_pt = [Tok() for _ in range(NPT)]
    acc = [P.ps([128, 512], name="acc") for _ in range(4)]
    t_acc = [Tok() for _ in range(4)]
    ob = [P.sb([128, 128], name="ob") for _ in range(2)]
    t_ob = [Tok(), Tok()]
    rs = P.sb([128, 1], name="rs")
    t_rs = Tok()
    blocks = [(0, 256, 0, 2)] + [(CTX + i * 512, 512, 0, NTT) for i in range(SEQ // 512)]
    isc = 0
    ipt = 0
    iob = 0
    scale = 128 ** -0.5
    for h in range(2):
        for (q0, qw, kt0, kt1) in blocks:
            nq = qw // 128
            for kt in range(kt0, kt1):
                sb_ = isc % NS
                isc += 1
                P.op("pe", lambda e, sb_=sb_, kt=kt, h=h, q0=q0, qw=qw: e.matmul(pss[sb_][:, 0:qw], lhsT=kT[:, kt * 128:(kt + 1) * 128],
                                                                              rhs=qT[:, h, q0:q0 + qw], start=True, stop=True),
                     reads=[t_kT, t_qT], writes=[t_pss[sb_]])
                pb = ipt % NPT
                ipt += 1
                P.op("act", lambda e, sb_=sb_, pb=pb, qw=qw: e.activation(out=pt[pb][:, 0:qw], in_=pss[sb_][:, 0:qw], func=AF.Exp, scale=scale),
                     reads=[t_pss[sb_]], writes=[t_pt[pb]])
                for qi in range(nq):
                    P.op("pe", lambda e, pb=pb, qi=qi, kt=kt, kt0=kt0, kt1=kt1: e.matmul(acc[qi][:, 0:129], lhsT=pt[pb][:, qi * 128:(qi + 1) * 128],
                                                                                      rhs=va[:, kt, :], start=(kt == kt0), stop=(kt == kt1 - 1)),
                         reads=[t_pt[pb], t_va], writes=[t_acc[qi]])
            for qi in range(nq):
                P.op("dve", lambda e, qi=qi: e.reciprocal(out=rs[:], in_=acc[qi][:, 128:129]), reads=[t_acc[qi]], writes=[t_rs])
                oi = iob % 2
                iob += 1
                P.op("dve", lambda e, qi=qi, oi=oi: e.tensor_scalar(out=ob[oi][:], in0=acc[qi][:, 0:128], scalar1=rs[:], scalar2=None, op0=ALU.mult),
                     reads=[t_acc[qi], t_rs], writes=[t_ob[oi]])
                P.dma("pool", lambda e, oi=oi, q0=q0, qi=qi, h=h: e.dma_start(out=o_d[q0 + qi * 128:q0 + (qi + 1) * 128, h * 128:(h + 1) * 128], in_=ob[oi][:]),
                      reads=[t_ob[oi]], writes=[t_ob[oi]], is_out=True)
    return P


def rope_tables():
    rows = SEQ // 64
    row = np.repeat(np.arange(rows), 64).astype(np.float32)
    col = np.tile(np.arange(64), rows).astype(np.float32)
    inv = (10000.0 ** (-np.arange(0, 64, 2, dtype=np.float32) / 64)).astype(np.float32)
    ang = np.concatenate([row[:, None] * inv, col[:, None] * inv], axis=-1)
    cos, sin = np.cos(ang).astype(np.float32), np.sin(ang).astype(np.float32)
    C = np.empty((SEQ, 128), np.float32)
    S = np.empty((SEQ, 128), np.float32)
    for a in range(2):
        c_, s_ = cos[:, a * 32:(a + 1) * 32], sin[:, a * 32:(a + 1) * 32]
        C[:, a * 64:a * 64 + 32] = c_
        C[:, a * 64 + 32:a * 64 + 64] = c_
        S[:, a * 64:a * 64 + 32] = -s_
        S[:, a * 64 + 32:a * 64 + 64] = s_
    return np.stack([C, S], axis=0)


def stage_attn(p_lat, p_ctx, qk_gain_l):
    P = build_attn()
    cs = rope_tables()
    g = np.stack([bc(qk_gain_l[0]), bc(qk_gain_l[1])], axis=0)
    in_maps = []
    for c in range(NCORES):
        b, j = divmod(c, 4)
        pa = np.concatenate([p_ctx[b], p_lat[b]], axis=0)
        kv = j // 2
        in_maps.append({"q": np.ascontiguousarray(pa[:, 3632 + 256 * j:3632 + 256 * (j + 1)]),
                        "k": np.ascontiguousarray(pa[:, 4656 + 128 * kv:4656 + 128 * (kv + 1)]),
                        "v": np.ascontiguousarray(pa[:, 4912 + 128 * kv:4912 + 128 * (kv + 1)]),
                        "g": g, "cs": cs, "ident": IDENT})
    res = run_prog(P, in_maps)
    att_lat = np.empty((B, SEQ, 1024), np.float32)
    att_ctx = np.empty((B, CTX, 1024), np.float32)
    for c in range(NCORES):
        b, j = divmod(c, 4)
        att_ctx[b, :, 256 * j:256 * (j + 1)] = res[c]["o"][:CTX]
        att_lat[b, :, 256 * j:256 * (j + 1)] = res[c]["o"][CTX:]
    return att_lat, att_ctx


TRI_INC = np.triu(np.ones((128, 128), np.float32))
TRI_SUFEX = np.tril(np.ones((128, 128), np.float32), -1)
ANTI = np.ascontiguousarray(np.eye(128, dtype=np.float32)[::-1])


def orig_tile(tt):
    return (1 - tt) if tt < 2 else (35 - tt)


def build_gla():
    P = Prog()
    qT_d = P.din("qT", [2, 64, NTOK])
    kT_d = P.din("kT", [2, 64, NTOK])
    k_d = P.din("k", [2, NTOK, 64])
    v_d = P.din("v", [2, NTOK, 128])
    rT_d = P.din("rT", [2, 16, NTOK])
    w_d = P.din("w", [2, 17, 64])
    g_d = P.din("g", [NTOK, 128])
    gn_d = P.din("gn", [128, 128])
    c_d = P.din("cm", [4, 128, 128])
    o_d = P.dout("o", [NTOK, 128])

    cm = P.sb([128, 4, 128], name="cm")
    gn = P.sb([128, 128], name="gn")
    t_cm, t_gn = Tok(), Tok()
    for i in range(4):
        P.dma("sp", lambda e, i=i: e.dma_start(out=cm[:, i, :], in_=c_d[i]), writes=[t_cm])
    P.dma("sp", lambda e: e.dma_start(out=gn[:], in_=gn_d), writes=[t_gn])
    tri, sufx, anti = cm[:, 0, :], cm[:, 1, :], cm[:, 2, :]
    g = P.sb([128, NTT, 128], name="g")
    t_g = Tok()
    P.dma("act", lambda e: e.dma_start(out=g[:], in_=g_d.rearrange("(n p) d -> p n d", p=128)), writes=[t_g])
    P.op("act", lambda e: e.activation(out=g[:], in_=g[:], func=AF.Silu), reads=[t_g], writes=[t_g])
    ofw = P.sb([128, NTT, 128], name="ofw")
    t_ofw = Tok()
    qT = P.sb([64, NTOK], name="qT")
    kT = P.sb([64, NTOK], name="kT")
    rT = P.sb([17, NTOK], name="rT")
    kk = P.sb([128, NTT, 64], name="kk")
    vv = P.sb([128, NTT, 128], name="vv")
    wa = P.sb([17, 64], name="wa")
    t_in = Tok()
    P.op("dve", lambda e: e.memset(rT[:], 1.0), writes=[t_in])
    S = P.sb([64, 128], name="S")
    t_S = Tok()
    la = P.sb([128, 64], name="la")
    t_la = Tok()
    cs = P.sb([64, 128], name="cs")
    t_cs = Tok()
    sc = P.sb([64, 4], name="sc")
    t_sc = Tok()
    e1 = P.sb([64, 128], name="e1")
    e2 = P.sb([64, 128], name="e2")
    e3 = P.sb([64, 128], name="e3")
    t_e = Tok()
    k4 = P.sb([128, 64], name="k4")
    t_k4 = Tok()
    AT = P.sb([128, 128], name="AT")
    t_AT = Tok()
    ob = P.sb([128, 128], name="ob")
    t_ob = Tok()
    os_ = P.sb([128, 128], name="os")
    t_os = Tok()
    junk = P.sb([128, 128], name="junk")
    t_junk = Tok()
    st = P.sb([128, 2], name="st")
    t_st = Tok()
    ps_la = P.ps([128, 64], name="ps_la")
    ps_cT = P.ps([64, 128], name="ps_cT")
    ps_sf = P.ps([128, 64], name="ps_sf")
    ps_A = P.ps([128, 128], name="ps_A")
    ps_o = P.ps([128, 128], name="ps_o")
    ps_S = P.ps([64, 128], name="ps_S")
    ps_J = P.ps([128, 128], name="ps_J")
    t_pla, t_pcT, t_psf, t_pA, t_po, t_pS, t_pJ = [Tok() for _ in range(7)]
    for z in range(2):
        P.dma("sp", lambda e, z=z: e.dma_start(out=qT[:], in_=qT_d[z]), writes=[t_in])
        P.dma("act", lambda e, z=z: e.dma_start(out=kT[:], in_=kT_d[z]), writes=[t_in])
        P.dma("sp", lambda e, z=z: e.dma_start(out=rT[0:16, :], in_=rT_d[z]), writes=[t_in])
        P.dma("act", lambda e, z=z: e.dma_start(out=kk[:], in_=k_d[z].rearrange("(n p) d -> p n d", p=128)), writes=[t_in])
        P.dma("sp", lambda e, z=z: e.dma_start(out=vv[:], in_=v_d[z].rearrange("(n p) d -> p n d", p=128)), writes=[t_in])
        P.dma("act", lambda e, z=z: e.dma_start(out=wa[:], in_=w_d[z]), writes=[t_in])
        P.op("dve", lambda e: e.memset(S[:], 0.0), writes=[t_S])
        for tt in range(NTT):
            ts_ = slice(tt * 128, (tt + 1) * 128)
            P.op("pe", lambda e, ts_=ts_: e.matmul(ps_la[:], lhsT=rT[0:17, ts_], rhs=wa[0:17, :], start=True, stop=True), reads=[t_in], writes=[t_pla])
            P.op("act", lambda e: e.activation(out=la[:], in_=ps_la[:], func=AF.Exp, scale=-1.0), reads=[t_pla], writes=[t_la])
            P.op("dve", lambda e: e.tensor_scalar_add(out=la[:], in0=la[:], scalar1=1.0), reads=[t_la], writes=[t_la])
            P.op("act", lambda e: e.activation(out=la[:], in_=la[:], func=AF.Ln), reads=[t_la], writes=[t_la])
            P.op("dve", lambda e: e.tensor_scalar_mul(out=la[:], in0=la[:], scalar1=-1.0 / 16.0), reads=[t_la], writes=[t_la])
            P.op("pe", lambda e: e.matmul(ps_cT[:], lhsT=la[:], rhs=tri, start=True, stop=True), reads=[t_la, t_cm], writes=[t_pcT])
            P.op("pe", lambda e: e.matmul(ps_sf[:], lhsT=sufx, rhs=la[:], start=True, stop=True), reads=[t_la, t_cm], writes=[t_psf])
            P.op("act", lambda e: e.copy(out=cs[:], in_=ps_cT[:]), reads=[t_pcT], writes=[t_cs])
            P.op("dve", lambda e: e.tensor_scalar_mul(out=sc[:, 0:1], in0=cs[:, 63:64], scalar1=-1.0), reads=[t_cs], writes=[t_sc])
            P.op("act", lambda e: e.activation(out=e1[:], in_=cs[:], func=AF.Exp, bias=sc[:, 0:1], scale=1.0), reads=[t_cs, t_sc], writes=[t_e])
            P.op("act", lambda e: e.activation(out=e2[:], in_=cs[:], func=AF.Exp, bias=cs[:, 63:64], scale=-1.0), reads=[t_cs], writes=[t_e])
            P.op("act", lambda e: e.activation(out=e3[:], in_=cs[:], func=AF.Exp), reads=[t_cs], writes=[t_e])
            P.op("act", lambda e: e.activation(out=sc[:, 1:2], in_=cs[:, 127:128], func=AF.Exp), reads=[t_cs], writes=[t_sc])
            P.op("act", lambda e: e.activation(out=k4[:], in_=ps_sf[:], func=AF.Exp), reads=[t_psf], writes=[t_k4])
            P.op("dve", lambda e, ts_=ts_: e.scalar_tensor_tensor(out=e1[:], in0=qT[:, ts_], scalar=0.125, in1=e1[:], op0=ALU.mult, op1=ALU.mult),
                 reads=[t_in, t_e], writes=[t_e])
            P.op("dve", lambda e, ts_=ts_: e.tensor_mul(out=e2[:], in0=kT[:, ts_], in1=e2[:]), reads=[t_in, t_e], writes=[t_e])
            P.op("dve", lambda e, ts_=ts_: e.scalar_tensor_tensor(out=e3[:], in0=qT[:, ts_], scalar=0.125, in1=e3[:], op0=ALU.mult, op1=ALU.mult),
                 reads=[t_in, t_e], writes=[t_e])
            P.op("dve", lambda e, tt=tt: e.tensor_mul(out=k4[:], in0=kk[:, tt, :], in1=k4[:]), reads=[t_in, t_k4], writes=[t_k4])
            P.op("pe", lambda e: e.matmul(ps_A[:], lhsT=e2[:], rhs=e1[:], start=True, stop=True), reads=[t_e], writes=[t_pA])
            P.op("dve", lambda e: e.tensor_mul(out=AT[:], in0=ps_A[:], in1=tri), reads=[t_pA, t_cm], writes=[t_AT])
            P.op("pe", lambda e, tt=tt: e.matmul(ps_o[:], lhsT=AT[:], rhs=vv[:, tt, :], start=True, stop=False), reads=[t_AT, t_in], writes=[t_po])
            P.op("pe", lambda e: e.matmul(ps_o[:], lhsT=e3[:], rhs=S[:], start=False, stop=True), reads=[t_e, t_S], writes=[t_po])
            P.op("pe", lambda e, tt=tt: e.matmul(ps_S[:], lhsT=k4[:], rhs=vv[:, tt, :], start=True, stop=True), reads=[t_k4, t_in], writes=[t_pS])
            P.op("dve", lambda e: e.scalar_tensor_tensor(out=S[:], in0=S[:], scalar=sc[:, 1:2], in1=ps_S[:], op0=ALU.mult, op1=ALU.add),
                 reads=[t_sc, t_pS, t_S], writes=[t_S])
            if z == 0:
                P.op("act", lambda e, tt=tt: e.copy(out=ofw[:, tt, :], in_=ps_o[:]), reads=[t_po], writes=[t_ofw])
            else:
                oi = orig_tile(tt)
                P.op("act", lambda e: e.copy(out=ob[:], in_=ps_o[:]), reads=[t_po], writes=[t_ob])
                P.op("pe", lambda e: e.matmul(ps_J[:], lhsT=anti, rhs=ob[:], start=True, stop=True), reads=[t_ob, t_cm], writes=[t_pJ])
                P.op("dve", lambda e, oi=oi: e.tensor_add(out=os_[:], in0=ofw[:, oi, :], in1=ps_J[:]), reads=[t_pJ, t_ofw], writes=[t_os])
                rms_rstd(P, os_[:], 128, 128, 1e-6, junk, t_junk, st, t_st, [t_os])
                P.op("dve", lambda e: e.scalar_tensor_tensor(out=os_[:], in0=os_[:], scalar=st[:, 1:2], in1=gn[:], op0=ALU.mult, op1=ALU.mult),
                     reads=[t_st, t_gn, t_os], writes=[t_os])
                P.op("dve", lambda e, oi=oi: e.tensor_mul(out=os_[:], in0=os_[:], in1=g[:, oi, :]), reads=[t_g, t_os], writes=[t_os])
                P.dma("pool", lambda e, oi=oi: e.dma_start(out=o_d[oi * 128:(oi + 1) * 128, :], in_=os_[:]), reads=[t_os], writes=[t_os], is_out=True)
    return P


def flipseg(a):
    return np.concatenate([a[:CTX][::-1], a[CTX:][::-1]], axis=0)


CMATS = np.stack([TRI_INC, TRI_SUFEX, ANTI, IDENT], axis=0)


def stage_gla(p_lat, p_ctx, w_up_l, b_up_l, norm_l):
    P = build_gla()
    in_maps = []
    for c in range(NCORES):
        b, h = divmod(c, 4)
        pa = np.concatenate([p_ctx[b], p_lat[b]], axis=0)
        q = pa[:, h * 64:(h + 1) * 64]
        k = pa[:, 256 + h * 64:256 + (h + 1) * 64]
        v = pa[:, 512 + h * 128:512 + (h + 1) * 128]
        g = pa[:, 1024 + h * 128:1024 + (h + 1) * 128]
        m = {"g": np.ascontiguousarray(g), "gn": bc(norm_l), "cm": CMATS}
        qs, ks, vs, rs, ws = [], [], [], [], []
        for z in range(2):
            f = (lambda a: a) if z == 0 else flipseg
            r = pa[:, 1536 + z * 16:1536 + (z + 1) * 16]
            qs.append(f(q).T)
            ks.append(f(k))
            vs.append(f(v))
            rs.append(f(r).T)
            ws.append(np.concatenate([w_up_l[z][:, h * 64:(h + 1) * 64], b_up_l[z][None, h * 64:(h + 1) * 64]], axis=0))
        m["qT"] = np.ascontiguousarray(np.stack(qs))
        m["kT"] = np.ascontiguousarray(np.stack([a.T for a in ks]))
        m["k"] = np.ascontiguousarray(np.stack(ks))
        m["v"] = np.ascontiguousarray(np.stack(vs))
        m["rT"] = np.ascontiguousarray(np.stack(rs))
        m["w"] = np.ascontiguousarray(np.stack(ws))
        in_maps.append(m)
    res = run_prog(P, in_maps)
    o_lat = np.empty((B, SEQ, 512), np.float32)
    o_ctx = np.empty((B, CTX, 512), np.float32)
    for c in range(NCORES):
        b, h = divmod(c, 4)
        o_ctx[b, :, 128 * h:128 * (h + 1)] = res[c]["o"][:CTX]
        o_lat[b, :, 128 * h:128 * (h + 1)] = res[c]["o"][CTX:]
    return o_lat, o_ctx


UPP = TRI_INC
LOW = np.ascontiguousarray(TRI_INC.T)
GDN_CM = np.stack([UPP, LOW, UPP - IDENT, LOW - IDENT, IDENT, np.ones((128, 128), np.float32)], axis=0)


def build_gdn(dbg=None):
    P = Prog()
    x_d = P.din("xT", [3, 128, NTOK])
    cw_d = P.din("cw", [128, 3, 5])
    zg_d = P.din("zg", [NTOK, 128])
    ba_d = P.din("ba", [2, 2, 128, NTT])
    sc_d = P.din("sc", [128, 2, 2])
    gn_d = P.din("gn", [128, 128])
    c_d = P.din("cm", [6, 128, 128])
    o_d = P.dout("o", [NTOK, 128])

    cm = P.sb([128, 6, 128], name="cm")
    t_cm = Tok()
    for i in range(6):
        P.dma("sp", lambda e, i=i: e.dma_start(out=cm[:, i, :], in_=c_d[i]), writes=[t_cm])
    ident, ones = cm[:, 4, :], cm[:, 5, :]
    gn = P.sb([128, 128], name="gn")
    cw = P.sb([128, 3, 5], name="cw")
    scal = P.sb([128, 2, 2], name="scal")
    t_gn, t_cw, t_scal = Tok(), Tok(), Tok()
    P.dma("sp", lambda e: e.dma_start(out=gn[:], in_=gn_d), writes=[t_gn])
    P.dma("sp", lambda e: e.dma_start(out=cw[:], in_=cw_d), writes=[t_cw])
    P.dma("sp", lambda e: e.dma_start(out=scal[:], in_=sc_d), writes=[t_scal])
    zg = P.sb([128, NTT, 128], name="zg")
    t_zg = Tok()
    P.dma("act", lambda e: e.dma_start(out=zg[:], in_=zg_d.rearrange("(n p) d -> p n d", p=128)), writes=[t_zg])
    P.op("act", lambda e: e.activation(out=zg[:], in_=zg[:], func=AF.Silu), reads=[t_zg], writes=[t_zg])

    raw = P.sb([128, NTOK], name="raw")
    t_raw = Tok()
    fmx = [P.sb([128, NTOK], name="fmx") for _ in range(3)]
    t_fm = [Tok() for _ in range(3)]
    segs = [(0, CTX), (CTX, NTOK)]
    for s in range(3):
        P.dma("sp", lambda e, s=s: e.dma_start(out=raw[:], in_=x_d[s]), writes=[t_raw])
        acc = fmx[s]
        for (s0, s1) in segs:
            P.op("dve", lambda e, s=s, s0=s0, s1=s1, acc=acc: e.tensor_scalar(out=acc[:, s0:s1], in0=raw[:, s0:s1], scalar1=cw[:, s, 2:3], scalar2=None, op0=ALU.mult),
                 reads=[t_raw, t_cw], writes=[t_fm[s]])
            for j in (0, 1, 3, 4):
                sh = j - 2
                lo = max(s0, s0 - sh)
                hi = min(s1, s1 - sh)
                P.op("dve", lambda e, s=s, j=j, lo=lo, hi=hi, sh=sh, acc=acc: e.scalar_tensor_tensor(
                    out=acc[:, lo:hi], in0=raw[:, lo + sh:hi + sh], scalar=cw[:, s, j:j + 1], in1=acc[:, lo:hi], op0=ALU.mult, op1=ALU.add),
                    reads=[t_raw, t_cw, t_fm[s]], writes=[t_fm[s]])
        P.op("act", lambda e, acc=acc: e.activation(out=acc[:], in_=acc[:], func=AF.Silu), reads=[t_fm[s]], writes=[t_fm[s]])
    bk1 = P.ps([128, 512], name="bk1")
    t_bk1 = Tok(True)
    sq = P.sb([128, 512], name="sq")
    t_sq = Tok()
    blocks = [(0, 256)] + [(CTX + i * 512, 512) for i in range(SEQ // 512)]
    for s in range(2):
        mul = (128 ** -0.5) if s == 0 else 1.0
        for (b0, bw) in blocks:
            P.op("act", lambda e, s=s, b0=b0, bw=bw: e.activation(out=sq[:, 0:bw], in_=fmx[s][:, b0:b0 + bw], func=AF.Square), reads=[t_fm[s]], writes=[t_sq])
            P.op("pe", lambda e, bw=bw: e.matmul(bk1[:, 0:bw], lhsT=ones, rhs=sq[:, 0:bw], start=True, stop=True), reads=[t_sq, t_cm], writes=[t_bk1])
            P.op("dve", lambda e, bw=bw: e.tensor_scalar_add(out=sq[:, 0:bw], in0=bk1[:, 0:bw], scalar1=1e-6), reads=[t_bk1], writes=[t_sq])
            P.op("act", lambda e, bw=bw: e.activation(out=sq[:, 0:bw], in_=sq[:, 0:bw], func=AF.Sqrt), reads=[t_sq], writes=[t_sq])
            P.op("dve", lambda e, bw=bw: e.reciprocal(out=sq[:, 0:bw], in_=sq[:, 0:bw]), reads=[t_sq], writes=[t_sq])
            P.op("dve", lambda e, s=s, b0=b0, bw=bw, mul=mul: e.scalar_tensor_tensor(out=fmx[s][:, b0:b0 + bw], in0=fmx[s][:, b0:b0 + bw], scalar=float(mul), in1=sq[:, 0:bw],
                                                                                   op0=ALU.mult, op1=ALU.mult), reads=[t_sq, t_fm[s]], writes=[t_fm[s]])
    qnT, knT, vT = fmx
    t_qnT, t_knT, t_vT = t_fm
    kn = P.sb([128, NTT, 128], name="kn")
    vt = P.sb([128, NTT, 128], name="vt")
    t_kn, t_vt = Tok(), Tok()
    for tt in range(NTT):
        ts_ = slice(tt * 128, (tt + 1) * 128)
        P.op("pe", lambda e, ts_=ts_: e.transpose(out=bk1[:, 0:128], in_=knT[:, ts_], identity=ident), reads=[t_knT, t_cm], writes=[t_bk1])
        P.op("pe", lambda e, ts_=ts_: e.transpose(out=bk1[:, 128:256], in_=vT[:, ts_], identity=ident), reads=[t_vT, t_cm], writes=[t_bk1])
        P.op("act", lambda e, tt=tt: e.copy(out=kn[:, tt, :], in_=bk1[:, 0:128]), reads=[t_bk1], writes=[t_kn])
        P.op("dve", lambda e, tt=tt: e.tensor_copy(out=vt[:, tt, :], in_=bk1[:, 128:256]), reads=[t_bk1], writes=[t_vt])

    if dbg == "prep":
        for tt in range(NTT):
            P.dma("pool", lambda e, tt=tt: e.dma_start(out=o_d[tt * 128:(tt + 1) * 128, :], in_=kn[:, tt, :]), reads=[t_kn], is_out=True)
        return P
    odir = [P.sb([128, NTT, 128], name="odir") for _ in range(2)]
    t_odir = [Tok(), Tok()]

    def sbt(n, w=128):
        return P.sb([128, w], name=n), Tok()

    def make_lane(z):
        L = {}
        L["bl"] = P.sb([128, 2, NTT], name="bl")
        L["t_bl"] = Tok()
        L["nea"], L["t_nea"] = sbt("nea", 1)
        L["S"], L["t_S"] = sbt("S")
        bA = bk1 if z == 0 else P.ps([128, 512], name="bA")
        bB = P.ps([128, 512], name="bB")
        bC = P.ps([128, 512], name="bC")
        bD = P.ps([128, 512], name="bD")
        L["tA"] = t_bk1 if z == 0 else Tok(True)
        L["tB"], L["tC"], L["tD"] = Tok(True), Tok(True), Tok(True)
        L["pG"], L["pbR"], L["pKK"], L["pQK"] = bA[:, 0:128], bA[:, 128:256], bA[:, 256:384], bA[:, 384:512]
        L["psqL"], L["psqLT"], L["pg"] = bB[:, 0:128], bB[:, 128:256], bB[:, 256:258]
        L["pX"], L["pwT"], L["pvn"] = bC[:, 0:256], bC[:, 256:384], bC[:, 384:512]
        L["po"], L["pS"] = bD[:, 0:128], bD[:, 128:256]
        for n in ("lgB", "btB", "GT", "Gm", "EgR", "L0", "LT1", "AqT", "wT", "vn", "qd", "kd"):
            L[n], L["t_" + n] = sbt(n)
        L["col"], L["t_col"] = sbt("col", 8)
        L["X"], L["t_X"] = sbt("X", 256)
        L["pow"] = [sbt("pw%d" % i) for i in range(12)]
        return L

    def lane_gen(z, L):
        C, CT, Cs, CTs = (cm[:, 0, :], cm[:, 1, :], cm[:, 2, :], cm[:, 3, :]) if z == 0 else (cm[:, 1, :], cm[:, 0, :], cm[:, 3, :], cm[:, 2, :])
        last = 127 if z == 0 else 0
        bl, t_bl, nea, t_nea, S, t_S = L["bl"], L["t_bl"], L["nea"], L["t_nea"], L["S"], L["t_S"]
        tA, tB, tC, tD = L["tA"], L["tB"], L["tC"], L["tD"]
        pG, pbR, pKK, pQK, psqL, psqLT, pg = L["pG"], L["pbR"], L["pKK"], L["pQK"], L["psqL"], L["psqLT"], L["pg"]
        pX, pwT, pvn, po, pS = L["pX"], L["pwT"], L["pvn"], L["po"], L["pS"]
        lgB, btB, GT, Gm, EgR, L0, LT1, AqT, wT, vn, qd, kd, col, X = [L[n] for n in ("lgB", "btB", "GT", "Gm", "EgR", "L0", "LT1", "AqT", "wT", "vn", "qd", "kd", "col", "X")]
        t_lgB, t_btB, t_GT, t_Gm, t_EgR, t_L0, t_LT1, t_AqT, t_wT, t_vn, t_qd, t_kd, t_col, t_X = [L["t_" + n] for n in ("lgB", "btB", "GT", "Gm", "EgR", "L0", "LT1", "AqT", "wT", "vn", "qd", "kd", "col", "X")]
        for i in range(2):
            P.dma("sp", lambda e, i=i: e.dma_start(out=bl[:, i, :], in_=ba_d[z, i]), writes=[t_bl])
        P.op("act", lambda e: e.activation(out=bl[:, 0, :], in_=bl[:, 0, :], func=AF.Sigmoid), reads=[t_bl], writes=[t_bl])
        P.op("act", lambda e: e.activation(out=bl[:, 1, :], in_=bl[:, 1, :], func=AF.Exp, bias=scal[:, z, 1:2], scale=1.0), reads=[t_bl, t_scal], writes=[t_bl])
        P.op("dve", lambda e: e.tensor_scalar_add(out=bl[:, 1, :], in0=bl[:, 1, :], scalar1=1.0), reads=[t_bl], writes=[t_bl])
        P.op("act", lambda e: e.activation(out=bl[:, 1, :], in_=bl[:, 1, :], func=AF.Ln), reads=[t_bl], writes=[t_bl])
        P.op("act", lambda e: e.activation(out=nea[:], in_=scal[:, z, 0:1], func=AF.Exp), reads=[t_scal], writes=[t_nea])
        P.op("dve", lambda e: e.tensor_scalar_mul(out=nea[:], in0=nea[:], scalar1=-1.0), reads=[t_nea], writes=[t_nea])
        P.op("dve", lambda e: e.tensor_scalar(out=bl[:, 1, :], in0=bl[:, 1, :], scalar1=nea[:], scalar2=None, op0=ALU.mult), reads=[t_bl, t_nea], writes=[t_bl])
        P.op("dve", lambda e: e.memset(S[:], 0.0), writes=[t_S])
        yield
        order = list(range(NTT)) if z == 0 else [1, 0] + list(range(NTT - 1, 1, -1))
        if dbg is not None and dbg.startswith("main"):
            order = order[:int(dbg[4:])]
        for tt in order:
            ts_ = slice(tt * 128, (tt + 1) * 128)
            beta = bl[:, 0, tt:tt + 1]
            lg = bl[:, 1, tt:tt + 1]
            P.op("dve", lambda e, lg=lg: e.tensor_scalar(out=lgB[:], in0=ones, scalar1=lg, scalar2=None, op0=ALU.mult), reads=[t_bl, t_cm], writes=[t_lgB])
            yield
            P.op("dve", lambda e, beta=beta: e.tensor_scalar(out=btB[:], in0=ones, scalar1=beta, scalar2=None, op0=ALU.mult), reads=[t_bl, t_cm], writes=[t_btB])
            yield
            P.op("pe", lambda e: e.matmul(pg, lhsT=C, rhs=lgB[:, 0:2], start=True, stop=True), reads=[t_lgB, t_cm], writes=[tB])
            P.op("pe", lambda e: e.matmul(pG, lhsT=lgB[:], rhs=C, start=True, stop=True), reads=[t_lgB, t_cm], writes=[tA])
            P.op("pe", lambda e: e.matmul(pbR, lhsT=btB[:], rhs=ident, start=True, stop=True), reads=[t_btB, t_cm], writes=[tA])
            P.op("pe", lambda e, ts_=ts_: e.matmul(pKK, lhsT=knT[:, ts_], rhs=knT[:, ts_], start=True, stop=True), reads=[t_knT], writes=[tA])
            P.op("pe", lambda e, ts_=ts_: e.matmul(pQK, lhsT=knT[:, ts_], rhs=qnT[:, ts_], start=True, stop=True), reads=[t_knT, t_qnT], writes=[tA])
            yield
            P.op("act", lambda e: e.copy(out=col[:, 0:1], in_=pg[:, 0:1]), reads=[tB], writes=[t_col])
            yield
            P.op("dve", lambda e: e.tensor_scalar(out=GT[:], in0=pG, scalar1=col[:, 0:1], scalar2=0.0, op0=ALU.subtract, op1=ALU.min), reads=[tA, t_col], writes=[t_GT])
            yield
            P.op("act", lambda e: e.activation(out=GT[:], in_=GT[:], func=AF.Exp), reads=[t_GT], writes=[t_GT])
            yield
            P.op("dve", lambda e: e.tensor_scalar(out=Gm[:], in0=pG, scalar1=col[:, 0:1], scalar2=0.0, op0=ALU.subtract, op1=ALU.max), reads=[tA, t_col], writes=[t_Gm])
            yield
            P.op("act", lambda e: e.activation(out=Gm[:], in_=Gm[:], func=AF.Exp, scale=-1.0), reads=[t_Gm], writes=[t_Gm])
            yield
            P.op("act", lambda e: e.activation(out=EgR[:], in_=pG, func=AF.Exp), reads=[tA], writes=[t_EgR])
            yield
            P.op("act", lambda e: e.activation(out=col[:, 4:5], in_=col[:, 0:1], func=AF.Exp), reads=[t_col], writes=[t_col])
            yield
            P.op("dve", lambda e, beta=beta: e.tensor_mul(out=col[:, 1:2], in0=col[:, 4:5], in1=beta), reads=[t_col, t_bl], writes=[t_col])
            yield
            P.op("dve", lambda e: e.tensor_sub(out=col[:, 5:6], in0=pG[:, last:last + 1], in1=col[:, 0:1]), reads=[tA, t_col], writes=[t_col])
            yield
            P.op("act", lambda e: e.activation(out=col[:, 2:3], in_=col[:, 5:6], func=AF.Exp), reads=[t_col], writes=[t_col])
            yield
            P.op("act", lambda e: e.activation(out=col[:, 3:4], in_=pG[:, last:last + 1], func=AF.Exp), reads=[tA], writes=[t_col])
            yield
            P.op("dve", lambda e: e.tensor_mul(out=LT1[:], in0=GT[:], in1=Cs), reads=[t_GT, t_cm], writes=[t_LT1])
            yield
            P.op("dve", lambda e: e.tensor_mul(out=LT1[:], in0=LT1[:], in1=pKK), reads=[tA, t_LT1], writes=[t_LT1])
            yield
            P.op("dve", lambda e: e.tensor_mul(out=LT1[:], in0=LT1[:], in1=pbR), reads=[tA, t_LT1], writes=[t_LT1])
            yield
            P.op("dve", lambda e: e.tensor_mul(out=L0[:], in0=Gm[:], in1=CTs), reads=[t_Gm, t_cm], writes=[t_L0])
            yield
            P.op("dve", lambda e, beta=beta: e.scalar_tensor_tensor(out=L0[:], in0=L0[:], scalar=beta, in1=pKK, op0=ALU.mult, op1=ALU.mult), reads=[tA, t_bl, t_L0], writes=[t_L0])
            yield
            P.op("dve", lambda e: e.tensor_mul(out=AqT[:], in0=GT[:], in1=C), reads=[t_GT, t_cm], writes=[t_AqT])
            yield
            P.op("dve", lambda e: e.tensor_mul(out=AqT[:], in0=AqT[:], in1=pQK), reads=[tA, t_AqT], writes=[t_AqT])
            yield
            P.op("dve", lambda e, tt=tt, beta=beta: e.tensor_scalar(out=X[:, 0:128], in0=vt[:, tt, :], scalar1=beta, scalar2=None, op0=ALU.mult), reads=[t_vt, t_bl], writes=[t_X])
            yield
            P.op("dve", lambda e, tt=tt: e.tensor_scalar(out=X[:, 128:256], in0=kn[:, tt, :], scalar1=col[:, 1:2], scalar2=None, op0=ALU.mult), reads=[t_kn, t_col], writes=[t_X])
            yield
            cur_L, cur_tL, cur_LT, cur_tLT = L0, t_L0, LT1, t_LT1
            for pi in range(7):
                if pi < 6:
                    nL, t_nL = L["pow"][2 * pi]
                    nLT, t_nLT = L["pow"][2 * pi + 1]
                    P.op("pe", lambda e, cur_L=cur_L, cur_LT=cur_LT: e.matmul(psqL, lhsT=cur_LT[:], rhs=cur_L[:], start=True, stop=True), reads=[cur_tL, cur_tLT], writes=[tB])
                    P.op("pe", lambda e, cur_L=cur_L, cur_LT=cur_LT: e.matmul(psqLT, lhsT=cur_L[:], rhs=cur_LT[:], start=True, stop=True), reads=[cur_tL, cur_tLT], writes=[tB])
                P.op("pe", lambda e, cur_LT=cur_LT: e.matmul(pX, lhsT=cur_LT[:], rhs=X[:], start=True, stop=True), reads=[cur_tLT, t_X], writes=[tC])
                yield
                if pi < 6:
                    P.op("act", lambda e, nL=nL: e.copy(out=nL[:], in_=psqL), reads=[tB], writes=[t_nL])
                    P.op("act", lambda e, nLT=nLT: e.copy(out=nLT[:], in_=psqLT), reads=[tB], writes=[t_nLT])
                if pi > 0:
                    P.op("dve", lambda e: e.tensor_add(out=X[:], in0=X[:], in1=pX), reads=[tC, t_X], writes=[t_X])
                else:
                    P.op("dve", lambda e: e.tensor_sub(out=X[:], in0=X[:], in1=pX), reads=[tC, t_X], writes=[t_X])
                yield
                if pi < 6:
                    cur_L, cur_tL, cur_LT, cur_tLT = nL, t_nL, nLT, t_nLT
            P.op("pe", lambda e: e.transpose(out=pwT, in_=X[:, 128:256], identity=ident), reads=[t_X, t_cm], writes=[tC])
            yield
            P.op("act", lambda e: e.copy(out=wT[:], in_=pwT), reads=[tC], writes=[t_wT])
            yield
            P.op("pe", lambda e: e.matmul(pvn, lhsT=wT[:], rhs=S[:], start=True, stop=True), reads=[t_wT, t_S], writes=[tC])
            yield
            P.op("dve", lambda e: e.tensor_sub(out=vn[:], in0=X[:, 0:128], in1=pvn), reads=[tC, t_X], writes=[t_vn])
            yield
            P.op("dve", lambda e, ts_=ts_: e.tensor_mul(out=qd[:], in0=qnT[:, ts_], in1=EgR[:]), reads=[t_qnT, t_EgR], writes=[t_qd])
            yield
            P.op("pe", lambda e: e.matmul(po, lhsT=qd[:], rhs=S[:], start=True, stop=False), reads=[t_qd, t_S], writes=[tD])
            P.op("pe", lambda e: e.matmul(po, lhsT=AqT[:], rhs=vn[:], start=False, stop=True), reads=[t_AqT, t_vn], writes=[tD])
            yield
            P.op("dve", lambda e, tt=tt: e.tensor_scalar(out=kd[:], in0=kn[:, tt, :], scalar1=col[:, 2:3], scalar2=None, op0=ALU.mult), reads=[t_kn, t_col], writes=[t_kd])
            yield
            P.op("pe", lambda e: e.matmul(pS, lhsT=kd[:], rhs=vn[:], start=True, stop=True), reads=[t_kd, t_vn], writes=[tD])
            yield
            P.op("act", lambda e, tt=tt: e.copy(out=odir[z][:, tt, :], in_=po), reads=[tD], writes=[t_odir[z]])
            yield
            P.op("dve", lambda e: e.scalar_tensor_tensor(out=S[:], in0=S[:], scalar=col[:, 3:4], in1=pS, op0=ALU.mult, op1=ALU.add), reads=[t_col, tD, t_S], writes=[t_S])
            yield

    gens = [lane_gen(z, make_lane(z)) for z in range(2)]
    while gens:
        for g in list(gens):
            try:
                next(g)
            except StopIteration:
                gens.remove(g)
    rs = P.sb([128, 2, NTT], name="rs")
    t_rs = Tok()
    osum = odir[0]
    P.op("dve", lambda e: e.tensor_add(out=osum[:], in0=odir[0][:], in1=odir[1][:]), reads=[t_odir[1], t_odir[0]], writes=[t_odir[0]])
    sqv = raw[:].rearrange("p (n d) -> p n d", d=128)
    P.op("act", lambda e: e.activation(out=sqv, in_=osum[:], func=AF.Square), reads=[t_odir[0]], writes=[t_raw])
    P.op("dve", lambda e: e.reduce_sum(out=rs[:, 0, :], in_=sqv, axis=AX.X), reads=[t_raw], writes=[t_rs])
    P.op("dve", lambda e: e.tensor_scalar(out=rs[:, 0, :], in0=rs[:, 0, :], scalar1=1.0 / 128, scalar2=1e-6, op0=ALU.mult, op1=ALU.add), reads=[t_rs], writes=[t_rs])
    P.op("act", lambda e: e.activation(out=rs[:, 0, :], in_=rs[:, 0, :], func=AF.Sqrt), reads=[t_rs], writes=[t_rs])
    P.op("dve", lambda e: e.reciprocal(out=rs[:, 1, :], in_=rs[:, 0, :]), reads=[t_rs], writes=[t_rs])
    for tt in range(NTT):
        P.op("dve", lambda e, tt=tt: e.scalar_tensor_tensor(out=osum[:, tt, :], in0=osum[:, tt, :], scalar=rs[:, 1, tt:tt + 1], in1=gn[:], op0=ALU.mult, op1=ALU.mult),
             reads=[t_rs, t_gn, t_odir[0]], writes=[t_odir[0]])
    P.op("dve", lambda e: e.tensor_mul(out=osum[:], in0=osum[:], in1=zg[:]), reads=[t_zg, t_odir[0]], writes=[t_odir[0]])
    for half in range(2):
        hs = slice(half * 17, half * 17 + 17)
        P.dma("sp" if half == 0 else "act", lambda e, hs=hs, half=half: e.dma_start(
            out=o_d[half * 17 * 128:(half + 1) * 17 * 128, :].rearrange("(n p) d -> p n d", p=128), in_=osum[:, hs, :]), reads=[t_odir[0]], is_out=True)
    return P


def stage_gdn(p_lat, p_ctx, conv_l, a_log_l, dt_bias_l, norm_l, dbg=None):
    P = build_gdn(dbg)
    in_maps = []
    for c in range(NCORES):
        b, h = divmod(c, 4)
        pa = np.concatenate([p_ctx[b], p_lat[b]], axis=0)
        hs = slice(h * 128, (h + 1) * 128)
        xT = np.stack([pa[:, 1568:2080][:, hs].T, pa[:, 2080:2592][:, hs].T, pa[:, 2592:3104][:, hs].T], axis=0)
        cw = np.stack([conv_l[:, s * 512 + h * 128:s * 512 + (h + 1) * 128].T for s in range(3)], axis=1)
        ba = np.empty((2, 2, 128, NTT), np.float32)
        sc = np.empty((128, 2, 2), np.float32)
        for z in range(2):
            ba[z, 0] = pa[:, 3616 + z * 4 + h].reshape(NTT, 128).T
            ba[z, 1] = pa[:, 3624 + z * 4 + h].reshape(NTT, 128).T
            sc[:, z, 0] = a_log_l[z, h]
            sc[:, z, 1] = dt_bias_l[z, h]
        in_maps.append({"xT": np.ascontiguousarray(xT), "cw": np.ascontiguousarray(cw), "zg": np.ascontiguousarray(pa[:, 3104:3616][:, hs]),
                        "ba": ba, "sc": sc, "gn": bc(norm_l), "cm": GDN_CM})
    res = run_prog(P, in_maps)
    o_lat = np.empty((B, SEQ, 512), np.float32)
    o_ctx = np.empty((B, CTX, 512), np.float32)
    for c in range(NCORES):
        b, h = divmod(c, 4)
        o_ctx[b, :, 128 * h:128 * (h + 1)] = res[c]["o"][:CTX]
        o_lat[b, :, 128 * h:128 * (h + 1)] = res[c]["o"][CTX:]
    return o_lat, o_ctx


def build_router():
    P = Prog()
    KC = D // 128
    xT_d = P.din("xT", [128, KC, NT_B])
    mod_d = P.din("modv", [128, KC, 4])
    r_d = P.din("rw", [D, NE]).rearrange("(k p) c -> p k c", p=128)
    hT_d = P.dout("hT", [128, KC, NT_B])
    a_d = P.dout("aff", [NT_B, NE])
    xT = P.sb([128, KC, NT_B], name="xT")
    modv = P.sb([128, KC, 4], name="modv")
    rw = P.sb([128, KC, NE], name="rw")
    t_x = [Tok() for _ in range(KC)]
    t_mod, t_rw = Tok(), Tok()
    for k in range(KC):
        P.dma("sp" if k % 2 == 0 else "act", lambda e, k=k: e.dma_start(out=xT[:, k, :], in_=xT_d[:, k, :]), writes=[t_x[k]])
    P.dma("sp", lambda e: e.dma_start(out=modv[:], in_=mod_d), writes=[t_mod])
    P.dma("sp", lambda e: e.dma_start(out=rw[:], in_=r_d), writes=[t_rw])
    P.op("dve", lambda e: e.tensor_scalar_add(out=modv[:, :, 1:2], in0=modv[:, :, 1:2], scalar1=1.0), reads=[t_mod], writes=[t_mod])
    P.op("dve", lambda e: e.tensor_scalar_add(out=modv[:, :, 3:4], in0=modv[:, :, 3:4], scalar1=1.0), reads=[t_mod], writes=[t_mod])
    for k in range(KC):
        P.op("dve", lambda e, k=k: e.tensor_scalar(out=xT[:, k, 0:1024], in0=xT[:, k, 0:1024], scalar1=modv[:, k, 1:2],
                                                   scalar2=modv[:, k, 0:1], op0=ALU.mult, op1=ALU.add),
             reads=[t_mod, t_x[k]], writes=[t_x[k]])
        P.op("dve", lambda e, k=k: e.tensor_scalar(out=xT[:, k, 1024:NT_B], in0=xT[:, k, 1024:NT_B], scalar1=modv[:, k, 3:4],
                                                   scalar2=modv[:, k, 2:3], op0=ALU.mult, op1=ALU.add),
             reads=[t_mod, t_x[k]], writes=[t_x[k]])
        P.dma("pool", lambda e, k=k: e.dma_start(out=hT_d[:, k, :], in_=xT[:, k, :]), reads=[t_x[k]], is_out=True)
    pl = [P.ps([128, NE], name="pl") for _ in range(2)]
    t_pl = [Tok(True), Tok(True)]
    ex = [P.sb([128, NE], name="ex") for _ in range(2)]
    t_ex = [Tok(), Tok()]
    st = P.sb([128, 4], name="st")
    t_st = Tok()
    tiles = [(i * 128, 128) for i in range(8)] + [(1024, 64)]
    for ti, (t0, m) in enumerate(tiles):
        bi = ti % 2
        for k in range(KC):
            P.op("pe", lambda e, bi=bi, k=k, t0=t0, m=m: e.matmul(pl[bi][0:m, :], lhsT=xT[:, k, t0:t0 + m], rhs=rw[:, k, :], start=(k == 0), stop=(k == KC - 1)),
                 reads=[t_x[k], t_rw], writes=[t_pl[bi]])
        P.op("dve", lambda e, bi=bi, m=m: e.reduce_max(out=st[0:m, 0:1], in_=pl[bi][0:m, :], axis=AX.X), reads=[t_pl[bi]], writes=[t_st])
        P.op("dve", lambda e, m=m: e.tensor_scalar_mul(out=st[0:m, 1:2], in0=st[0:m, 0:1], scalar1=-1.0), reads=[t_st], writes=[t_st])
        P.op("act", lambda e, bi=bi, m=m: e.activation(out=ex[bi][0:m, :], in_=pl[bi][0:m, :], func=AF.Exp, bias=st[0:m, 1:2], scale=1.0, accum_out=st[0:m, 2:3]),
             reads=[t_pl[bi], t_st], writes=[t_ex[bi], t_st])
        P.op("dve", lambda e, m=m: e.reciprocal(out=st[0:m, 3:4], in_=st[0:m, 2:3]), reads=[t_st], writes=[t_st])
        P.op("dve", lambda e, bi=bi, m=m: e.tensor_scalar(out=ex[bi][0:m, :], in0=ex[bi][0:m, :], scalar1=st[0:m, 3:4], scalar2=None, op0=ALU.mult),
             reads=[t_st, t_ex[bi]], writes=[t_ex[bi]])
        P.dma("pool", lambda e, bi=bi, t0=t0, m=m: e.dma_start(out=a_d[t0:t0 + m, :], in_=ex[bi][0:m, :]), reads=[t_ex[bi]], writes=[t_ex[bi]], is_out=True)
    return P


def unfm(hT):
    p, kc, T = hT.shape
    return np.ascontiguousarray(hT.transpose(2, 1, 0).reshape(T, kc * p))


def stage_router(x_lat, x_ctx, mod_lat, mod_ctx, router_l):
    P = build_router()
    in_maps = []
    for c in range(NCORES):
        b = c // 4
        mv = np.stack([mod_lat[b, 3], mod_lat[b, 4], mod_ctx[3], mod_ctx[4]], axis=-1)
        mv = np.ascontiguousarray(mv.reshape(D // 128, 128, 4).transpose(1, 0, 2))
        in_maps.append({"xT": fm(tok_shard(x_lat, x_ctx, c)), "modv": mv, "rw": np.ascontiguousarray(router_l)})
    res = run_prog(P, in_maps)
    res2 = [{"h": unfm(r["hT"]), "aff": r["aff"]} for r in res]
    h_lat, h_ctx = tok_unshard(res2, "h", D)
    a_lat, a_ctx = tok_unshard(res2, "aff", NE)
    return h_lat, h_ctx, a_lat, a_ctx


NBIS = 30
H_ROWS = B * SEQ + B * CTX


def build_select():
    P = Prog()
    a_d = P.din("A", [8, SEQ])
    cc_d = P.din("cc", [8, 1])
    tvc_d = P.din("tvc", [128, 32])
    io_d = P.din("iota", [128, 512])
    id_d = P.din("ident", [128, 128])
    idx_d = P.dout("idx", [128, 8, 4], I32)
    gate_d = P.dout("gate", [128, 8, 4])
    h_d = P.din("h", [H_ROWS, D])
    xsc_d = P.dout("xsc", [4, 544, D])
    A = P.sb([8, SEQ], name="A")
    M = P.sb([8, SEQ], name="M")
    Cm = P.sb([8, SEQ], name="Cm")
    onesr = P.sb([8, SEQ], name="onesr")
    cc = P.sb([8, 1], name="cc")
    tvc = P.sb([128, 32], name="tvc")
    iota = P.sb([128, 512], name="iota")
    ident = P.sb([128, 128], name="ident")
    t_A, t_M, t_Cm, t_on, t_cc, t_tvc, t_io, t_id = [Tok() for _ in range(8)]
    P.dma("sp", lambda e: e.dma_start(out=A[:], in_=a_d), writes=[t_A])
    P.dma("sp", lambda e: e.dma_start(out=cc[:], in_=cc_d), writes=[t_cc])
    P.dma("act", lambda e: e.dma_start(out=tvc[:], in_=tvc_d), writes=[t_tvc])
    P.dma("act", lambda e: e.dma_start(out=iota[:], in_=io_d), writes=[t_io])
    P.dma("act", lambda e: e.dma_start(out=ident[:], in_=id_d), writes=[t_id])
    P.op("pool", lambda e: e.memset(onesr[:], 1.0), writes=[t_on])
    bs = P.sb([8, 4], name="bs")
    t_bs = Tok()
    P.op("dve", lambda e: e.memset(bs[:], 0.0), writes=[t_bs])
    for k in range(1, NBIS + 1):
        w = 2.0 ** (-k)
        P.op("dve", lambda e, w=w: e.tensor_scalar_add(out=bs[:, 1:2], in0=bs[:, 0:1], scalar1=w), reads=[t_bs], writes=[t_bs])
        P.op("dve", lambda e: e.tensor_scalar(out=M[:], in0=A[:], scalar1=bs[:, 1:2], scalar2=None, op0=ALU.is_ge, op1=ALU.add, accum_out=bs[:, 2:3]),
             reads=[t_A, t_bs], writes=[t_M, t_bs])
        P.op("dve", lambda e: e.tensor_tensor(out=bs[:, 3:4], in0=bs[:, 2:3], in1=cc[:], op=ALU.is_ge), reads=[t_bs, t_cc], writes=[t_bs])
        P.op("dve", lambda e, w=w: e.scalar_tensor_tensor(out=bs[:, 0:1], in0=bs[:, 3:4], scalar=w, in1=bs[:, 0:1], op0=ALU.mult, op1=ALU.add),
             reads=[t_bs], writes=[t_bs])
    P.op("dve", lambda e: e.tensor_scalar(out=M[:], in0=A[:], scalar1=bs[:, 0:1], scalar2=None, op0=ALU.is_ge), reads=[t_A, t_bs], writes=[t_M])
    P.op("dve", lambda e: e.tensor_tensor_scan(out=Cm[:], data0=onesr[:], data1=M[:], initial=0.0, op0=ALU.mult, op1=ALU.add),
         reads=[t_on, t_M], writes=[t_Cm])
    P.op("dve", lambda e: e.tensor_sub(out=Cm[:], in0=Cm[:], in1=M[:]), reads=[t_M, t_Cm], writes=[t_Cm])
    T3 = P.sb([128, 32, 24], name="T3")
    t_T3 = Tok()
    pT = [P.ps([128, 32], name="pT") for _ in range(2)]
    t_pT = [Tok(True), Tok(True)]
    for j in range(32):
        bi = j % 2
        ts_ = slice(j * 128, (j + 1) * 128)
        for i, (src, tk) in enumerate(((A, t_A), (M, t_M), (Cm, t_Cm))):
            P.op("pe", lambda e, bi=bi, i=i, src=src, ts_=ts_: e.transpose(out=pT[bi][:, i * 8:(i + 1) * 8], in_=src[0:8, ts_], identity=ident[0:8, 0:8]),
                 reads=[tk, t_id], writes=[t_pT[bi]])
        P.op("dve" if bi == 0 else "act", (lambda e, bi=bi, j=j: e.tensor_copy(out=T3[:, j, :], in_=pT[bi][:, 0:24])) if bi == 0 else
             (lambda e, bi=bi, j=j: e.copy(out=T3[:, j, :], in_=pT[bi][:, 0:24])), reads=[t_pT[bi]], writes=[t_T3])
    TV = P.sb([128, 32, 8, 2], name="TV")
    t_TV = Tok()
    for r in range(8):
        P.op("dve", lambda e, r=r: e.tensor_copy(out=TV[:, :, r, 0], in_=tvc[:]), reads=[t_tvc], writes=[t_TV])
    P.op("dve", lambda e: e.tensor_copy(out=TV[:, :, :, 1], in_=T3[:, :, 0:8]), reads=[t_T3], writes=[t_TV])
    Pm = P.sb([128, 32, 512], name="Pm")
    t_Pm = Tok()
    pi_ = P.ps([128, 64], name="pi")
    t_pi = Tok(True)
    res_i = P.sb([128, 8, 4], I32, name="res_i")
    res_f = P.sb([128, 8, 4], name="res_f")
    res_g = P.sb([128, 8, 4], name="res_g")
    t_res = Tok()
    P.op("dve", lambda e: e.memset(res_f[:], 0.0), writes=[t_res])
    P.op("dve", lambda e: e.memset(res_g[:], 0.0), writes=[t_res])
    for r in range(8):
        lat = r < 4
        C = 512 if lat else 32
        nj = 32 if lat else 2
        b = r % 2
        base = float(b * SEQ) if lat else float(B * SEQ + b * CTX)
        for j in range(nj):
            P.op("dve", lambda e, j=j, r=r, C=C: e.tensor_scalar(out=Pm[:, j, 0:C], in0=iota[:, 0:C], scalar1=T3[:, j, 16 + r:17 + r],
                                                                 scalar2=T3[:, j, 8 + r:9 + r], op0=ALU.is_equal, op1=ALU.mult),
                 reads=[t_io, t_T3], writes=[t_Pm])
        for sc in range(4 if lat else 1):
            msz = 128 if lat else 32
            for j in range(nj):
                P.op("pe", lambda e, r=r, sc=sc, j=j, msz=msz, nj=nj: e.matmul(pi_[0:msz, (r * 4 + sc) * 2:(r * 4 + sc) * 2 + 2],
                                                                            lhsT=Pm[:, j, sc * 128:sc * 128 + msz], rhs=TV[:, j, r, :],
                                                                            start=(j == 0), stop=(j == nj - 1)),
                     reads=[t_Pm, t_TV], writes=[t_pi])
            P.op("dve", lambda e, r=r, sc=sc, msz=msz, base=base: e.tensor_scalar_add(out=res_f[0:msz, r, sc:sc + 1],
                                                                                   in0=pi_[0:msz, (r * 4 + sc) * 2:(r * 4 + sc) * 2 + 1], scalar1=base),
                 reads=[t_pi], writes=[t_res])
            P.op("dve", lambda e, r=r, sc=sc, msz=msz: e.tensor_copy(out=res_g[0:msz, r, sc:sc + 1], in_=pi_[0:msz, (r * 4 + sc) * 2 + 1:(r * 4 + sc) * 2 + 2]),
                 reads=[t_pi], writes=[t_res])
    P.op("dve", lambda e: e.tensor_copy(out=res_i[:], in_=res_f[:]), reads=[t_res], writes=[t_res])
    P.dma("pool", lambda e: e.dma_start(out=idx_d, in_=res_i[:]), reads=[t_res], is_out=True)
    P.dma("pool", lambda e: e.dma_start(out=gate_d, in_=res_g[:]), reads=[t_res], is_out=True)
    xs = [P.sb([128, D], name="xs") for _ in range(2)]
    t_xs = [Tok(), Tok()]
    ixs = 0
    for el in range(2):
        for b in range(B):
            pas = el * 2 + b
            for (r, sc, s0, m) in [(el * 2 + b, sc, sc * 128, 128) for sc in range(4)] + [(4 + el * 2 + b, 0, 512, 32)]:
                xb = ixs % 2
                ixs += 1
                P.dma("pool", lambda e, xb=xb, r=r, sc=sc, m=m: e.indirect_dma_start(
                    out=xs[xb][0:m, :], out_offset=None, in_=h_d[:, :], in_offset=bass.IndirectOffsetOnAxis(ap=res_i[0:m, r, sc:sc + 1], axis=0)),
                    reads=[t_res], writes=[t_xs[xb]])
                P.dma("sp", lambda e, xb=xb, pas=pas, s0=s0, m=m: e.dma_start(out=xsc_d[pas, s0:s0 + m, :], in_=xs[xb][0:m, :]),
                      reads=[t_xs[xb]], writes=[t_xs[xb]], is_out=True)
    return P


TVC = (np.arange(32)[None, :] * 128 + np.arange(128)[:, None]).astype(np.float32)
IOTA512 = np.ascontiguousarray(np.broadcast_to(np.arange(512, dtype=np.float32)[None, :], (128, 512)))
CCOL = np.array([512] * 4 + [32] * 4, np.float32)[:, None]


def stage_select(a_lat, a_ctx, h_lat, h_ctx):
    P = build_select()
    h_all = np.ascontiguousarray(np.concatenate([h_lat.reshape(B * SEQ, D), h_ctx.reshape(B * CTX, D)], axis=0))
    in_maps = []
    for c in range(NCORES):
        A = np.full((8, SEQ), -1.0, np.float32)
        for el in range(2):
            for b in range(B):
                A[el * 2 + b] = a_lat[b, :, 2 * c + el]
                A[4 + el * 2 + b, :CTX] = a_ctx[b, :, 2 * c + el]
        in_maps.append({"A": A, "cc": CCOL, "tvc": TVC, "iota": IOTA512, "ident": IDENT, "h": h_all})
    res = run_prog(P, in_maps)
    return [(r["idx"], r["gate"], r["xsc"]) for r in res]


def build_expert(els=(0, 1), bs=(0, 1)):
    P = Prog()
    KC = D // 128
    h_d = P.din("xsc", [4, 544, D])
    idx_d = P.din("idx", [128, 8, 4], I32)
    gate_d = P.din("gate", [128, 8, 4])
    w1_d = P.din("w1", [len(els), D, FF])
    w3_d = P.din("w3", [len(els), D, FF])
    w2_d = P.din("w2", [len(els), FF, D])
    id_d = P.din("ident", [128, 128])
    f_d = [P.dout("f%d" % dc, [H_ROWS, 512]) for dc in range(4)]
    ident = P.sb([128, 128], name="ident")
    idx = P.sb([128, 8, 4], I32, name="idx")
    gate = P.sb([128, 8, 4], name="gate")
    t_id, t_idx, t_gate = Tok(), Tok(), Tok()
    P.dma("sp", lambda e: e.dma_start(out=ident[:], in_=id_d), writes=[t_id])
    P.dma("sp", lambda e: e.dma_start(out=idx[:], in_=idx_d), writes=[t_idx])
    P.dma("sp", lambda e: e.dma_start(out=gate[:], in_=gate_d), writes=[t_gate])
    zt = P.sb([128, 2048], name="zt")
    t_z = Tok()
    t_f = [Tok() for _ in range(4)]
    P.op("dve", lambda e: e.memset(zt[:], 0.0), writes=[t_z])
    for dc in range(4):
        for r0 in range(0, H_ROWS, 512):
            P.dma("sp" if (r0 // 512) % 2 == 0 else "act",
                  lambda e, dc=dc, r0=r0: e.dma_start(out=f_d[dc][r0:r0 + 512, :].rearrange("(p n) c -> p n c", p=128), in_=zt[:].rearrange("p (n c) -> p n c", n=4)),
                  reads=[t_z], writes=[t_f[dc]], is_out=True)
    NSL = 544 * len(bs)
    xs = [P.sb([128, D], name="xs") for _ in range(2)]
    t_xs = [Tok(), Tok()]
    xsT = P.sb([128, KC, NSL], BF16, name="xsT")
    t_xsT = Tok()
    hT = P.sb([128, KC, NSL], BF16, name="hT")
    t_hT = Tok()
    NWB = 3
    wa = [P.sb([128, KC, 128], BF16, name="w1c") for _ in range(NWB)]
    wu = [P.sb([128, KC, 128], BF16, name="w3c") for _ in range(NWB)]
    t_wa = [Tok() for _ in range(NWB)]
    t_wu = [Tok() for _ in range(NWB)]
    w2c = [P.sb([128, KC, 512], BF16, name="w2c") for _ in range(2)]
    t_w2 = [Tok(), Tok()]
    tmp = [P.sb([128, 512], name="tmp") for _ in range(2)]
    t_tmp = [Tok(), Tok()]
    yb = [P.sb([128, 512], name="yb") for _ in range(2)]
    t_yb = [Tok(), Tok()]
    pT = [P.ps([128, 512], name="pT") for _ in range(2)]
    t_pT = [Tok(True), Tok(True)]
    pa = [P.ps([128, 512], name="pa") for _ in range(2)]
    t_pa = [Tok(True), Tok(True)]
    pu = [P.ps([128, 512], name="pu") for _ in range(2)]
    t_pu = [Tok(True), Tok(True)]
    py = [P.ps([128, 512], name="py") for _ in range(2)]
    t_py = [Tok(True), Tok(True)]
    ixs = ipt = iw = iw2 = iau = iy = 0
    for eli, el in enumerate(els):
        chunks = []
        groups = []
        for bi_, b in enumerate(bs):
            o = bi_ * 544
            chunks += [(el * 2 + b, sc, o + sc * 128, 128, sc * 128) for sc in range(4)] + [(4 + el * 2 + b, 0, o + 512, 32, 512)]
            groups += [(o, 512), (o + 512, 32)]
        for (r, sc, s0, m, src0) in chunks:
            xb = ixs % 2
            ixs += 1
            b = r % 2
            P.dma("sp", lambda e, xb=xb, el=el, b=b, src0=src0, m=m: e.dma_start(out=xs[xb][0:m, :], in_=h_d[el * 2 + b, src0:src0 + m, :]),
                  writes=[t_xs[xb]])
            for k4 in range(KC // 4):
                pb = ipt % 2
                ipt += 1
                for kk in range(4):
                    k = k4 * 4 + kk
                    P.op("pe", lambda e, pb=pb, kk=kk, xb=xb, k=k, m=m: e.transpose(out=pT[pb][:, kk * 128:kk * 128 + m], in_=xs[xb][0:m, k * 128:(k + 1) * 128],
                                                                               identity=ident[0:m, 0:m]),
                         reads=[t_xs[xb], t_id], writes=[t_pT[pb]])
                if pb == 0:
                    P.op("act", lambda e, pb=pb, k4=k4, s0=s0, m=m: e.copy(out=xsT[:, k4 * 4:k4 * 4 + 4, s0:s0 + m],
                                                                         in_=pT[pb][:].rearrange("p (a c) -> p a c", a=4)[:, :, 0:m]),
                         reads=[t_pT[pb]], writes=[t_xsT])
                else:
                    P.op("dve", lambda e, pb=pb, k4=k4, s0=s0, m=m: e.tensor_copy(out=xsT[:, k4 * 4:k4 * 4 + 4, s0:s0 + m],
                                                                                in_=pT[pb][:].rearrange("p (a c) -> p a c", a=4)[:, :, 0:m]),
                         reads=[t_pT[pb]], writes=[t_xsT])
        for fc in range(KC):
            wb = iw % NWB
            iw += 1
            P.dma("pool", lambda e, wb=wb, eli=eli, fc=fc: e.dma_start(out=wa[wb][:], in_=w1_d[eli, :, fc * 128:(fc + 1) * 128].rearrange("(k p) c -> p k c", p=128)),
                  writes=[t_wa[wb]])
            P.dma("pool", lambda e, wb=wb, eli=eli, fc=fc: e.dma_start(out=wu[wb][:], in_=w3_d[eli, :, fc * 128:(fc + 1) * 128].rearrange("(k p) c -> p k c", p=128)),
                  writes=[t_wu[wb]])
            for (g0, gw) in groups:
                ab = iau % 2
                iau += 1
                for k in range(KC):
                    P.op("pe", lambda e, ab=ab, wb=wb, k=k, g0=g0, gw=gw: e.matmul(pa[ab][:, 0:gw], lhsT=wa[wb][:, k, :], rhs=xsT[:, k, g0:g0 + gw],
                                                                                start=(k == 0), stop=(k == KC - 1)),
                         reads=[t_wa[wb], t_xsT], writes=[t_pa[ab]])
                for k in range(KC):
                    P.op("pe", lambda e, ab=ab, wb=wb, k=k, g0=g0, gw=gw: e.matmul(pu[ab][:, 0:gw], lhsT=wu[wb][:, k, :], rhs=xsT[:, k, g0:g0 + gw],
                                                                                start=(k == 0), stop=(k == KC - 1)),
                         reads=[t_wu[wb], t_xsT], writes=[t_pu[ab]])
                P.op("act", lambda e, ab=ab, gw=gw: e.activation(out=tmp[ab][:, 0:gw], in_=pa[ab][:, 0:gw], func=AF.Silu), reads=[t_pa[ab]], writes=[t_tmp[ab]])
                P.op("dve", lambda e, ab=ab, fc=fc, g0=g0, gw=gw: e.tensor_mul(out=hT[:, fc, g0:g0 + gw], in0=tmp[ab][:, 0:gw], in1=pu[ab][:, 0:gw]),
                     reads=[t_tmp[ab], t_pu[ab]], writes=[t_hT])
        for dc in range(4):
            w2b = iw2 % 2
            iw2 += 1
            for half in range(2):
                ks = slice(half * 8, half * 8 + 8)
                P.dma("pool", lambda e, w2b=w2b, eli=eli, dc=dc, ks=ks: e.dma_start(
                    out=w2c[w2b][:, ks, :], in_=w2_d[eli, :, dc * 512:(dc + 1) * 512].rearrange("(k p) c -> p k c", p=128)[:, ks, :]), writes=[t_w2[w2b]])
            for (r, sc, s0, m, src0) in chunks:
                yi = iy % 2
                iy += 1
                for fc in range(KC):
                    P.op("pe", lambda e, yi=yi, fc=fc, s0=s0, m=m, w2b=w2b: e.matmul(py[yi][0:m, :], lhsT=hT[:, fc, s0:s0 + m], rhs=w2c[w2b][:, fc, :],
                                                                                  start=(fc == 0), stop=(fc == KC - 1)),
                         reads=[t_hT, t_w2[w2b]], writes=[t_py[yi]])
                P.op("dve", lambda e, yi=yi, r=r, sc=sc, m=m: e.tensor_scalar(out=yb[yi][0:m, :], in0=py[yi][0:m, :], scalar1=gate[0:m, r, sc:sc + 1], scalar2=None, op0=ALU.mult),
                     reads=[t_py[yi], t_gate], writes=[t_yb[yi]])
                P.dma("pool", lambda e, yi=yi, dc=dc, r=r, sc=sc, m=m: e.indirect_dma_start(
                    out=f_d[dc][:, :], out_offset=bass.IndirectOffsetOnAxis(ap=idx[0:m, r, sc:sc + 1], axis=0), in_=yb[yi][0:m, :], in_offset=None, compute_op=ALU.add),
                    reads=[t_idx, t_yb[yi]], writes=[t_f[dc], t_yb[yi]], is_out=True)
    return P


def stage_expert(sel, w1_l, w3_l, w2_l, els=(0, 1), bs=(0, 1)):
    P = build_expert(els, bs)
    in_maps = []
    for c in range(NCORES):
        in_maps.append({"xsc": sel[c][2], "idx": sel[c][0], "gate": sel[c][1], "w1": np.ascontiguousarray(w1_l[[2 * c + e for e in els]]),
                        "w3": np.ascontiguousarray(w3_l[[2 * c + e for e in els]]), "w2": np.ascontiguousarray(w2_l[[2 * c + e for e in els]]), "ident": IDENT})
    res = run_prog(P, in_maps)
    return [np.stack([r["f%d" % dc] for dc in range(4)]) for r in res]


def stage_final(fparts, x_lat, x_ctx, gate_lat, gate_ctx, gain, bias):
    P = build_outproj(False, NCORES)
    in_maps = []
    for c in range(NCORES):
        b, q = divmod(c, 4)
        ys = []
        for fp in fparts:
            lat = fp[:, b * SEQ + q * 1024:b * SEQ + (q + 1) * 1024, :]
            ctx = fp[:, B * SEQ + b * CTX + q * 64:B * SEQ + b * CTX + (q + 1) * 64, :]
            y = np.concatenate([lat, ctx], axis=1)
            ys.append(y.transpose(1, 0, 2).reshape(NT_B, D))
        cst = np.stack([bc(gate_lat[b]), bc(gate_ctx), bc(gain), bc(bias)], axis=0)
        in_maps.append({"x": tok_shard(x_lat, x_ctx, c), "cst": cst, "y": np.ascontiguousarray(np.stack(ys))})
    res = run_prog(P, in_maps)
    return tok_unshard(res, "o", D)


def kernel(x, c, ctx, c_ctx, w_ada, b_ada, w_in, w_out, gla_w_up, gla_b_up, gla_norm,
           gdn_conv, gdn_a_log, gdn_dt_bias, gdn_norm, attn_qk_norm, ln_gain, ln_bias,
           router, w1, w3, w2):
    f = lambda a: np.asarray(a, dtype=np.float32)
    x, c, ctx, c_ctx = f(x), f(c), f(ctx), f(c_ctx)
    w_ada, b_ada, w_in, w_out = f(w_ada), f(b_ada), f(w_in), f(w_out)
    gla_w_up, gla_b_up, gla_norm = f(gla_w_up), f(gla_b_up), f(gla_norm)
    gdn_conv, gdn_a_log, gdn_dt_bias, gdn_norm = f(gdn_conv), f(gdn_a_log), f(gdn_dt_bias), f(gdn_norm)
    attn_qk_norm, ln_gain, ln_bias, router = f(attn_qk_norm), f(ln_gain), f(ln_bias), f(router)
    w1, w3, w2 = f(w1), f(w3), f(w2)
    mod_lat, mod_ctx = stage_mod(c, c_ctx, w_ada, b_ada)
    x_lat, x_ctx = x, ctx
    for l in range(DEPTH):
        p_lat, p_ctx = stage_inproj(x_lat, x_ctx, mod_lat[l], mod_ctx[l], w_in[l])
        gla_l, gla_c = stage_gla(p_lat, p_ctx, gla_w_up[l], gla_b_up[l], gla_norm[l])
        gdn_l, gdn_c = stage_gdn(p_lat, p_ctx, gdn_conv[l], gdn_a_log[l], gdn_dt_bias[l], gdn_norm[l])
        att_l, att_c = stage_attn(p_lat, p_ctx, attn_qk_norm[l])
        mix_l = np.concatenate([gla_l, gdn_l, att_l], axis=-1)
        mix_c = np.concatenate([gla_c, gdn_c, att_c], axis=-1)
        x_lat, x_ctx = stage_outproj(mix_l, mix_c, x_lat, x_ctx, mod_lat[l][:, 2], mod_ctx[l][2], ln_gain[l, 0], ln_bias[l, 0], w_out[l])
        h_lat, h_ctx, a_lat, a_ctx = stage_router(x_lat, x_ctx, mod_lat[l], mod_ctx[l], router[l])
        sel = stage_select(a_lat, a_ctx, h_lat, h_ctx)
        fparts = stage_expert(sel, w1[l], w3[l], w2[l])
        x_lat, x_ctx = stage_final(fparts, x_lat, x_ctx, mod_lat[l][:, 5], mod_ctx[l][5], ln_gain[l, 1], ln_bias[l, 1])
    return np.ascontiguousarray(x_lat, dtype=np.float32)
```

```python
import os
import time
import numpy as np
import concourse.bass as bass
import concourse.mybir as mybir
from concourse.bass_utils import run_bass_kernel_spmd

F32 = mybir.dt.float32
BF16 = mybir.dt.bfloat16
U32 = mybir.dt.uint32
I32 = mybir.dt.int32
AF = mybir.ActivationFunctionType
ALU = mybir.AluOpType
AX = mybir.AxisListType

NCORES = 8
D = 2048
B = 2
SEQ = 4096
CTX = 256
DEPTH = 2
NE = 16
FF = 2048
IN_W = 5168
ALPHA = (2 * DEPTH) ** 0.25


class Tok:
    __slots__ = ("w", "r", "excl")

    def __init__(self, excl=False):
        self.w = None
        self.r = []
        self.excl = excl


class Prog:
    CE = ("act", "pe", "dve", "pool")
    DQ = ("sp", "act", "pool")
    ND = 6

    def __init__(self):
        self.nc = bass.Bass("TRN2", target_bir_lowering=False)
        nc = self.nc
        self.q = {e: [] for e in ("sp", "act", "pe", "dve", "pool")}
        self.csem = {e: nc.alloc_semaphore("c_" + e) for e in self.CE}
        self.ccnt = {e: 0 for e in self.CE}
        self.dsem = {e: [nc.alloc_semaphore("d_%s%d" % (e, i)) for i in range(self.ND)] for e in self.DQ}
        self.dcnt = {e: [0] * self.ND for e in self.DQ}
        self.drr = {e: 0 for e in self.DQ}
        self.waited = {e: {} for e in self.q}
        self.out_deps = []
        self.nm = 0

    def name(self, p):
        self.nm += 1
        return "%s_%d" % (p, self.nm)

    def sb(self, shape, dt=F32, name="sb"):
        return self.nc.alloc_sbuf_tensor(self.name(name), list(shape), dt)

    def ps(self, shape, dt=F32, name="ps"):
        return self.nc.alloc_psum_tensor(self.name(name), list(shape), dt)

    def din(self, name, shape, dt=F32):
        return self.nc.dram_tensor(name, list(shape), dt, kind="ExternalInput").ap()

    def dout(self, name, shape, dt=F32):
        return self.nc.dram_tensor(name, list(shape), dt, kind="ExternalOutput").ap()

    def dscratch(self, name, shape, dt=F32):
        return self.nc.dram_tensor(name, list(shape), dt, kind="Internal").ap()

    def _deps(self, eng, reads, writes, extra=()):
        need = {}

        def add(dep):
            if dep is None:
                return
            s, v = dep
            if need.get(s, 0) < v:
                need[s] = v

        own = self.csem.get(eng)
        for t in reads:
            add(t.w)
            if t.excl:
                for r in t.r:
                    if r[0] is not own:
                        add(r)
        for t in writes:
            add(t.w)
            for r in t.r:
                add(r)
        for d in extra:
            add(d)
        if eng == "pe":
            need.pop(self.csem["pe"], None)
        out = []
        wd = self.waited[eng]
        for s, v in need.items():
            if wd.get(s, 0) >= v:
                continue
            wd[s] = v
            out.append((s, v))
        return out

    def _mark(self, reads, writes, done):
        for t in reads:
            t.r.append(done)
        for t in writes:
            t.w = done
            t.r = []

    def op(self, eng, fn, reads=(), writes=()):
        waits = self._deps(eng, reads, writes)
        self.ccnt[eng] += 1
        done = (self.csem[eng], self.ccnt[eng])
        self.q[eng].append((waits, fn, self.csem[eng], 1))
        self._mark(reads, writes, done)
        return done

    def dma(self, eng, fn, reads=(), writes=(), is_out=False):
        k = self.drr[eng]
        self.drr[eng] = (k + 1) % self.ND
        sem = self.dsem[eng][k]
        prev = (sem, self.dcnt[eng][k]) if self.dcnt[eng][k] else None
        waits = self._deps(eng, reads, writes, extra=(prev,) if prev else ())
        self.dcnt[eng][k] += 16
        done = (sem, self.dcnt[eng][k])
        self.q[eng].append((waits, fn, sem, 16))
        self._mark(reads, writes, done)
        if is_out:
            self.out_deps.append(done)
        return done

    def finish(self):
        nc = self.nc
        fin = {}
        for s, v in self.out_deps:
            fin[s] = max(fin.get(s, 0), v)
        q = self.q
        engmap = {"sp": "sync", "act": "scalar", "pe": "tensor", "dve": "vector", "pool": "gpsimd"}
        with nc.Block() as block:
            for e, bn in engmap.items():
                def body(eng, e=e):
                    for waits, fn, sem, inc in q[e]:
                        for s, v in waits:
                            eng.wait_ge(s, v)
                        fn(eng).then_inc(sem, inc)
                    if e == "pool":
                        for s, v in fin.items():
                            eng.wait_ge(s, v)
                getattr(block, bn)(body)
        return nc


def run_prog(P, in_maps):
    t0 = time.time()
    nc = P.finish()
    t1 = time.time()
    res = run_bass_kernel_spmd(nc, in_maps, core_ids=list(range(NCORES)))
    if os.environ.get("KDBG"):
        nb = sum(v.nbytes for m in in_maps for v in m.values())
        print("[run_prog] build %.1fs run %.1fs in %.0fMB" % (t1 - t0, time.time() - t1, nb / 1e6), flush=True)
    return res.results


def fm(a):
    T, C = a.shape
    return np.ascontiguousarray(a.T.reshape(C // 128, 128, T).transpose(1, 0, 2))


NT_B = 1088


def build_inproj():
    P = Prog()
    nc = P.nc
    KC = D // 128
    xT_d = P.din("xT", [128, KC, NT_B])
    mod_d = P.din("modv", [128, KC, 4])
    w_d = P.din("w", [D, IN_W]).rearrange("(k p) c -> p k c", p=128)
    p_d = P.dout("p", [NT_B, IN_W])

    xT = P.sb([128, KC, NT_B], name="xT")
    xb = P.sb([128, KC, NT_B], BF16, name="xb")
    modv = P.sb([128, KC, 4], name="modv")
    t_x = [Tok() for _ in range(KC)]
    t_mod = Tok()
    for k in range(KC):
        P.dma("sp" if k % 2 == 0 else "act", lambda e, k=k: e.dma_start(out=xT[:, k, :], in_=xT_d[:, k, :]), writes=[t_x[k]])
    P.dma("sp", lambda e: e.dma_start(out=modv[:], in_=mod_d), writes=[t_mod])
    P.op("dve", lambda e: e.tensor_scalar_add(out=modv[:, :, 1:2], in0=modv[:, :, 1:2], scalar1=1.0), reads=[t_mod], writes=[t_mod])
    P.op("dve", lambda e: e.tensor_scalar_add(out=modv[:, :, 3:4], in0=modv[:, :, 3:4], scalar1=1.0), reads=[t_mod], writes=[t_mod])
    for k in range(KC):
        P.op("dve", lambda e, k=k: e.tensor_scalar(out=xb[:, k, 0:1024], in0=xT[:, k, 0:1024], scalar1=modv[:, k, 1:2],
                                                   scalar2=modv[:, k, 0:1], op0=ALU.mult, op1=ALU.add),
             reads=[t_mod, t_x[k]], writes=[t_x[k]])
        P.op("dve", lambda e, k=k: e.tensor_scalar(out=xb[:, k, 1024:NT_B], in0=xT[:, k, 1024:NT_B], scalar1=modv[:, k, 3:4],
                                                   scalar2=modv[:, k, 2:3], op0=ALU.mult, op1=ALU.add),
             reads=[t_mod, t_x[k]], writes=[t_x[k]])
    NW = 3
    wt = [P.sb([128, KC, 512], BF16, name="wt") for _ in range(NW)]
    t_w = [Tok() for _ in range(NW)]
    NPS = 4
    pst = [P.ps([128, 512], name="pp") for _ in range(NPS)]
    t_ps = [Tok() for _ in range(NPS)]
    ot = [P.sb([128, 512], name="ot") for _ in range(NPS)]
    t_ot = [Tok() for _ in range(NPS)]
    tiles = [(i * 128, 128) for i in range(8)] + [(1024, 64)]
    cgs = [(c, min(512, IN_W - c)) for c in range(0, IN_W, 512)]
    it = 0
    for ci, (c0, cw) in enumerate(cgs):
        wb = ci % NW
        for half in range(2):
            ks = slice(half * 8, half * 8 + 8)
            P.dma("pool",
                  lambda e, wb=wb, ks=ks, c0=c0, cw=cw: e.dma_start(out=wt[wb][:, ks, 0:cw], in_=w_d[:, ks, c0:c0 + cw]),
                  writes=[t_w[wb]])
        for (t0, m) in tiles:
            pb = it % NPS
            it += 1
            for k in range(KC):
                P.op("pe", lambda e, pb=pb, k=k, t0=t0, m=m, wb=wb, cw=cw: e.matmul(
                    pst[pb][0:m, 0:cw], lhsT=xb[:, k, t0:t0 + m], rhs=wt[wb][:, k, 0:cw], start=(k == 0), stop=(k == KC - 1)),
                    reads=[t_x[k], t_w[wb]], writes=[t_ps[pb]])
            ev = "act" if pb % 2 == 0 else "dve"
            if ev == "act":
                P.op("act", lambda e, pb=pb, m=m, cw=cw: e.copy(out=ot[pb][0:m, 0:cw], in_=pst[pb][0:m, 0:cw]),
                     reads=[t_ps[pb]], writes=[t_ot[pb]])
            else:
                P.op("dve", lambda e, pb=pb, m=m, cw=cw: e.tensor_copy(out=ot[pb][0:m, 0:cw], in_=pst[pb][0:m, 0:cw]),
                     reads=[t_ps[pb]], writes=[t_ot[pb]])
            P.dma("sp" if pb % 2 == 0 else "act", lambda e, pb=pb, t0=t0, m=m, c0=c0, cw=cw: e.dma_start(out=p_d[t0:t0 + m, c0:c0 + cw], in_=ot[pb][0:m, 0:cw]),
                  reads=[t_ot[pb]], is_out=True)
    return P


def stage_inproj(x_lat, x_ctx, mod_lat, mod_ctx, w_in_l):
    P = build_inproj()
    in_maps = []
    for c in range(NCORES):
        b, q = divmod(c, 4)
        xs = np.concatenate([x_lat[b, q * 1024:(q + 1) * 1024], x_ctx[b, q * 64:(q + 1) * 64]], axis=0)
        mv = np.stack([mod_lat[b, 0], mod_lat[b, 1], mod_ctx[0], mod_ctx[1]], axis=-1)
        mv = np.ascontiguousarray(mv.reshape(D // 128, 128, 4).transpose(1, 0, 2))
        in_maps.append({"xT": fm(xs), "modv": mv, "w": np.ascontiguousarray(w_in_l)})
    res = run_prog(P, in_maps)
    p_lat = np.empty((B, SEQ, IN_W), np.float32)
    p_ctx = np.empty((B, CTX, IN_W), np.float32)
    for c in range(NCORES):
        b, q = divmod(c, 4)
        p_lat[b, q * 1024:(q + 1) * 1024] = res[c]["p"][:1024]
        p_ctx[b, q * 64:(q + 1) * 64] = res[c]["p"][1024:]
    return p_lat, p_ctx


MODW = 6 * D // NCORES


def build_mod():
    P = Prog()
    KC = D // 128
    cv_d = P.din("cv", [128, KC, 3])
    wa_d = P.din("wa", [DEPTH, D, MODW]).rearrange("l (k p) c -> l p k c", p=128)
    ba_d = P.din("ba", [DEPTH, 1, MODW])
    mod_d = P.dout("mod", [DEPTH, 3, MODW])
    cv = P.sb([128, KC, 3], name="cv")
    ones = P.sb([1, 4], name="ones")
    ba = P.sb([1, DEPTH, MODW], name="ba")
    t_cv, t_ones, t_ba = Tok(), Tok(), Tok()
    P.dma("sp", lambda e: e.dma_start(out=cv[:], in_=cv_d), writes=[t_cv])
    for l in range(DEPTH):
        P.dma("sp", lambda e, l=l: e.dma_start(out=ba[:, l, :], in_=ba_d[l]), writes=[t_ba])
    P.op("dve", lambda e: e.memset(ones[:], 1.0), writes=[t_ones])
    P.op("act", lambda e: e.activation(out=cv[:], in_=cv[:], func=AF.Silu), reads=[t_cv], writes=[t_cv])
    wt = [P.sb([128, KC, 512], name="wa") for _ in range(2)]
    t_w = [Tok(), Tok()]
    pst = [P.ps([128, 512], name="pm") for _ in range(2)]
    t_ps = [Tok(), Tok()]
    ot = [P.sb([4, 512], name="om") for _ in range(2)]
    t_ot = [Tok(), Tok()]
    it = 0
    for l in range(DEPTH):
        for c0 in range(0, MODW, 512):
            bi = it % 2
            it += 1
            for half in range(2):
                ks = slice(half * 8, half * 8 + 8)
                P.dma("sp" if half == 0 else "act",
                      lambda e, bi=bi, ks=ks, c0=c0, l=l: e.dma_start(out=wt[bi][:, ks, :], in_=wa_d[l, :, ks, c0:c0 + 512]),
                      writes=[t_w[bi]])
            for k in range(KC):
                P.op("pe", lambda e, bi=bi, k=k: e.matmul(pst[bi][0:3, :], lhsT=cv[:, k, :], rhs=wt[bi][:, k, :], start=(k == 0), stop=False),
                     reads=[t_cv, t_w[bi]], writes=[t_ps[bi]])
            P.op("pe", lambda e, bi=bi, l=l, c0=c0: e.matmul(pst[bi][0:3, :], lhsT=ones[0:1, 0:3], rhs=ba[0:1, l, c0:c0 + 512], start=False, stop=True),
                 reads=[t_ones, t_ba], writes=[t_ps[bi]])
            P.op("dve", lambda e, bi=bi: e.tensor_copy(out=ot[bi][0:3, :], in_=pst[bi][0:3, :]), reads=[t_ps[bi]], writes=[t_ot[bi]])
            P.dma("pool", lambda e, bi=bi, l=l, c0=c0: e.dma_start(out=mod_d[l, :, c0:c0 + 512], in_=ot[bi][0:3, :]), reads=[t_ot[bi]], is_out=True)
    return P


def stage_mod(c, c_ctx, w_ada, b_ada):
    P = build_mod()
    vec = np.concatenate([c, c_ctx[None]], axis=0)
    cv = np.ascontiguousarray(vec.T.reshape(D // 128, 128, 3).transpose(1, 0, 2))
    in_maps = []
    for ci in range(NCORES):
        cs = slice(ci * MODW, (ci + 1) * MODW)
        in_maps.append({"cv": cv, "wa": np.ascontiguousarray(w_ada[:, :, cs]), "ba": np.ascontiguousarray(b_ada[:, None, cs])})
    res = run_prog(P, in_maps)
    mod = np.concatenate([res[ci]["mod"] for ci in range(NCORES)], axis=-1)
    mod = mod.reshape(DEPTH, 3, 6, D)
    return np.ascontiguousarray(mod[:, 0:2]), np.ascontiguousarray(mod[:, 2])


def ln_tile(P, z, t_z, m, gain, bias, t_c, st, t_st, outt, t_out):
    s1, mu, ss, rstd = st[:, 0:1], st[:, 1:2], st[:, 2:3], st[:, 3:4]
    P.op("dve", lambda e: e.reduce_sum(out=s1[0:m], in_=z[0:m, :], axis=AX.X), reads=[t_z], writes=[t_st])
    P.op("dve", lambda e: e.tensor_scalar_mul(out=mu[0:m], in0=s1[0:m], scalar1=1.0 / D), reads=[t_st], writes=[t_st])
    P.op("dve", lambda e: e.tensor_scalar(out=z[0:m, :], in0=z[0:m, :], scalar1=mu[0:m], scalar2=None, op0=ALU.subtract),
         reads=[t_st, t_z], writes=[t_z])
    P.op("act", lambda e: e.activation(out=outt[0:m, :], in_=z[0:m, :], func=AF.Square, accum_out=ss[0:m]),
         reads=[t_z], writes=[t_out, t_st])
    P.op("dve", lambda e: e.tensor_scalar(out=ss[0:m], in0=ss[0:m], scalar1=1.0 / D, scalar2=1e-5, op0=ALU.mult, op1=ALU.add),
         reads=[t_st], writes=[t_st])
    P.op("act", lambda e: e.activation(out=ss[0:m], in_=ss[0:m], func=AF.Sqrt), reads=[t_st], writes=[t_st])
    P.op("dve", lambda e: e.reciprocal(out=rstd[0:m], in_=ss[0:m]), reads=[t_st], writes=[t_st])
    P.op("dve", lambda e: e.scalar_tensor_tensor(out=outt[0:m, :], in0=z[0:m, :], scalar=rstd[0:m], in1=gain[0:m, :],
                                                 op0=ALU.mult, op1=ALU.mult), reads=[t_st, t_z, t_c], writes=[t_out])
    P.op("dve", lambda e: e.tensor_add(out=outt[0:m, :], in0=outt[0:m, :], in1=bias[0:m, :]), reads=[t_c, t_out], writes=[t_out])


def build_outproj(with_proj=True, nparts=1):
    P = Prog()
    KC = D // 128
    NT = NT_B
    x_d = P.din("x", [NT, D])
    cst_d = P.din("cst", [4, 128, D])
    if with_proj:
        mT_d = P.din("mT", [128, KC, NT])
        w_d = P.din("w", [D, D]).rearrange("(k p) c -> p k c", p=128)
    else:
        y_d = P.din("y", [nparts, NT, D])
    o_d = P.dout("o", [NT, D])
    cst = P.sb([128, 4, D], name="cst")
    t_c = Tok()
    for i in range(4):
        P.dma("sp", lambda e, i=i: e.dma_start(out=cst[:, i, :], in_=cst_d[i]), writes=[t_c])
    if with_proj:
        w = P.sb([128, KC, D], BF16, name="w")
        t_w = Tok()
        for k in range(KC):
            P.dma("pool", lambda e, k=k: e.dma_start(out=w[:, k, :], in_=w_d[:, k, :]), writes=[t_w])
        mT = [P.sb([128, KC, 128], BF16, name="mT") for _ in range(2)]
        t_m = [Tok(), Tok()]
        pst = [P.ps([128, 512], name="po") for _ in range(4)]
        t_ps = [Tok() for _ in range(4)]
    else:
        NY = 6
        yt = [P.sb([128, D], name="yt") for _ in range(NY)]
        t_y = [Tok() for _ in range(NY)]
        iy_ = [0]
    xt = [P.sb([128, D], name="xt") for _ in range(2)]
    t_x = [Tok(), Tok()]
    zt = P.sb([128, D], name="zt")
    t_z = Tok()
    st = P.sb([128, 4], name="st")
    t_st = Tok()
    tiles = [(i * 128, 128) for i in range(8)] + [(1024, 64)]
    for ti, (t0, m) in enumerate(tiles):
        bi = ti % 2
        gi = 0 if ti < 8 else 1
        P.dma("sp", lambda e, bi=bi, t0=t0, m=m: e.dma_start(out=xt[bi][0:m, :], in_=x_d[t0:t0 + m, :]), writes=[t_x[bi]])
        if with_proj:
            P.dma("pool", lambda e, bi=bi, t0=t0, m=m: e.dma_start(out=mT[bi][:, :, 0:m], in_=mT_d[:, :, t0:t0 + m]), writes=[t_m[bi]])
            for cg in range(4):
                for k in range(KC):
                    P.op("pe", lambda e, bi=bi, cg=cg, k=k, m=m: e.matmul(pst[cg][0:m, :], lhsT=mT[bi][:, k, 0:m], rhs=w[:, k, cg * 512:(cg + 1) * 512],
                                                                      start=(k == 0), stop=(k == KC - 1)),
                         reads=[t_m[bi], t_w], writes=[t_ps[cg]])
                P.op("dve", lambda e, cg=cg, m=m, gi=gi: e.tensor_mul(out=zt[0:m, cg * 512:(cg + 1) * 512], in0=pst[cg][0:m, :],
                                                                     in1=cst[0:m, gi, cg * 512:(cg + 1) * 512]),
                     reads=[t_ps[cg], t_c], writes=[t_z])
        else:
            for pi in range(nparts):
                yb_ = iy_[0] % NY
                iy_[0] += 1
                P.dma("act" if yb_ % 2 == 0 else "sp", lambda e, yb_=yb_, t0=t0, m=m, pi=pi: e.dma_start(out=yt[yb_][0:m, :], in_=y_d[pi, t0:t0 + m, :]), writes=[t_y[yb_]])
                if pi == 0:
                    P.op("dve", lambda e, yb_=yb_, m=m: e.tensor_copy(out=zt[0:m, :], in_=yt[yb_][0:m, :]), reads=[t_y[yb_]], writes=[t_z])
                else:
                    P.op("dve", lambda e, yb_=yb_, m=m: e.tensor_add(out=zt[0:m, :], in0=zt[0:m, :], in1=yt[yb_][0:m, :]), reads=[t_y[yb_], t_z], writes=[t_z])
            P.op("dve", lambda e, m=m, gi=gi: e.tensor_mul(out=zt[0:m, :], in0=zt[0:m, :], in1=cst[0:m, gi, :]), reads=[t_c, t_z], writes=[t_z])
        P.op("dve", lambda e, bi=bi, m=m: e.scalar_tensor_tensor(out=zt[0:m, :], in0=xt[bi][0:m, :], scalar=float(ALPHA), in1=zt[0:m, :],
                                                                 op0=ALU.mult, op1=ALU.add), reads=[t_x[bi], t_z], writes=[t_z])
        ln_tile(P, zt, t_z, m, cst[:, 2, :], cst[:, 3, :], t_c, st, t_st, xt[bi], t_x[bi])
        P.dma("act", lambda e, bi=bi, t0=t0, m=m: e.dma_start(out=o_d[t0:t0 + m, :], in_=xt[bi][0:m, :]), reads=[t_x[bi]], writes=[t_x[bi]], is_out=True)
    return P


def tok_shard(lat, ctx, c):
    b, q = divmod(c, 4)
    return np.concatenate([lat[b, q * 1024:(q + 1) * 1024], ctx[b, q * 64:(q + 1) * 64]], axis=0)


def tok_unshard(res, key, width):
    lat = np.empty((B, SEQ, width), np.float32)
    ctx = np.empty((B, CTX, width), np.float32)
    for c in range(NCORES):
        b, q = divmod(c, 4)
        lat[b, q * 1024:(q + 1) * 1024] = res[c][key][:1024]
        ctx[b, q * 64:(q + 1) * 64] = res[c][key][1024:]
    return lat, ctx


def bc(v):
    return np.ascontiguousarray(np.broadcast_to(v[None, :], (128, v.shape[0])))


def stage_outproj(mix_lat, mix_ctx, x_lat, x_ctx, gate_lat, gate_ctx, gain, bias, w_out_l):
    P = build_outproj(True)
    in_maps = []
    for c in range(NCORES):
        b = c // 4
        cst = np.stack([bc(gate_lat[b]), bc(gate_ctx), bc(gain), bc(bias)], axis=0)
        in_maps.append({"x": tok_shard(x_lat, x_ctx, c), "cst": cst, "mT": fm(tok_shard(mix_lat, mix_ctx, c)),
                        "w": np.ascontiguousarray(w_out_l)})
    res = run_prog(P, in_maps)
    return tok_unshard(res, "o", D)


NTOK = CTX + SEQ
NTT = NTOK // 128
IDENT = np.eye(128, dtype=np.float32)


def rms_rstd(P, x_ap, m, width, eps, junk, t_junk, st, t_st, reads):
    P.op("act", lambda e: e.activation(out=junk[0:m, 0:width], in_=x_ap, func=AF.Square, accum_out=st[0:m, 0:1]),
         reads=reads, writes=[t_junk, t_st])
    P.op("dve", lambda e: e.tensor_scalar(out=st[0:m, 0:1], in0=st[0:m, 0:1], scalar1=1.0 / width, scalar2=eps, op0=ALU.mult, op1=ALU.add),
         reads=[t_st], writes=[t_st])
    P.op("act", lambda e: e.activation(out=st[0:m, 0:1], in_=st[0:m, 0:1], func=AF.Sqrt), reads=[t_st], writes=[t_st])
    P.op("dve", lambda e: e.reciprocal(out=st[0:m, 1:2], in_=st[0:m, 0:1]), reads=[t_st], writes=[t_st])


def build_attn():
    P = Prog()
    q_d = P.din("q", [NTOK, 256])
    k_d = P.din("k", [NTOK, 128])
    v_d = P.din("v", [NTOK, 128])
    g_d = P.din("g", [2, 128, 128])
    cs_d = P.din("cs", [2, SEQ, 128])
    id_d = P.din("ident", [128, 128])
    o_d = P.dout("o", [NTOK, 256])

    ident = P.sb([128, 128], name="ident")
    gq = P.sb([128, 2, 128], name="gq")
    t_id, t_g = Tok(), Tok()
    P.dma("sp", lambda e: e.dma_start(out=ident[:], in_=id_d), writes=[t_id])
    for i in range(2):
        P.dma("sp", lambda e, i=i: e.dma_start(out=gq[:, i, :], in_=g_d[i]), writes=[t_g])
    qT = P.sb([128, 2, NTOK], BF16, name="qT")
    kT = P.sb([128, NTOK], BF16, name="kT")
    va = P.sb([128, NTT, 129], BF16, name="va")
    t_qT, t_kT, t_va = Tok(), Tok(), Tok()
    P.op("pool", lambda e: e.memset(va[:, :, 128:129], 1.0), writes=[t_va])
    for half in range(2):
        hs = slice(half * 17, half * 17 + 17)
        P.dma("pool", lambda e, hs=hs, half=half: e.dma_start(out=va[:, hs, 0:128],
              in_=v_d[half * 17 * 128:(half + 1) * 17 * 128, :].rearrange("(n p) d -> p n d", p=128)), writes=[t_va])
    psT = [P.ps([128, 512], name="psT") for _ in range(2)]
    t_psT = [Tok(True), Tok(True)]

    def prep_lane(li):
        xin = P.sb([128, 3, 128], name="xin")
        cst = P.sb([128, 2, 128], name="cs")
        sq = P.sb([128, 3, 128], name="sq")
        xn = P.sb([128, 3, 128], name="xn")
        t1 = P.sb([128, 3, 128], name="t1")
        t2 = P.sb([128, 3, 128], name="t2")
        st = P.sb([128, 2, 3], name="st")
        t_xin, t_cs, t_sq, t_xn, t_t1, t_t2, t_st = [Tok() for _ in range(7)]
        for tt in range(li, NTT, 2):
            t0 = tt * 128
            lat = tt >= 2
            P.dma("sp", lambda e, t0=t0: e.dma_start(out=xin[:, 0:2, :], in_=q_d[t0:t0 + 128, :].rearrange("p (h d) -> p h d", h=2)), writes=[t_xin])
            P.dma("sp", lambda e, t0=t0: e.dma_start(out=xin[:, 2, :], in_=k_d[t0:t0 + 128, :]), writes=[t_xin])
            if lat:
                for i in range(2):
                    P.dma("act", lambda e, t0=t0, i=i: e.dma_start(out=cst[:, i, :], in_=cs_d[i, t0 - CTX:t0 - CTX + 128, :]), writes=[t_cs])
            yield
            P.op("act", lambda e: e.activation(out=sq[:], in_=xin[:], func=AF.Square), reads=[t_xin], writes=[t_sq])
            yield
            P.op("dve", lambda e: e.reduce_sum(out=st[:, 0, :], in_=sq[:], axis=AX.X), reads=[t_sq], writes=[t_st])
            yield
            P.op("dve", lambda e: e.tensor_scalar(out=st[:, 0, :], in0=st[:, 0, :], scalar1=1.0 / 128, scalar2=1e-6, op0=ALU.mult, op1=ALU.add), reads=[t_st], writes=[t_st])
            yield
            P.op("act", lambda e: e.activation(out=st[:, 0, :], in_=st[:, 0, :], func=AF.Sqrt), reads=[t_st], writes=[t_st])
            yield
            P.op("dve", lambda e: e.reciprocal(out=st[:, 1, :], in_=st[:, 0, :]), reads=[t_st], writes=[t_st])
            yield
            for h in range(3):
                gi = 0 if h < 2 else 1
                P.op("dve", lambda e, h=h, gi=gi: e.scalar_tensor_tensor(out=xn[:, h, :], in0=xin[:, h, :], scalar=st[:, 1, h:h + 1], in1=gq[:, gi, :], op0=ALU.mult, op1=ALU.mult),
                     reads=[t_xin, t_st, t_g], writes=[t_xn])
            yield
            src, t_src = xn, t_xn
            if lat:
                for h in range(3):
                    P.op("pool", lambda e, h=h: e.tensor_mul(out=t1[:, h, :], in0=xn[:, h, :], in1=cst[:, 0, :]), reads=[t_xn, t_cs], writes=[t_t1])
                x5 = xn[:].rearrange("p h (a b f) -> p h a b f", a=2, b=2)
                o5 = t2[:].rearrange("p h (a b f) -> p h a b f", a=2, b=2)
                s4 = cst[:, 1, :].rearrange("p (a b f) -> p a b f", a=2, b=2)
                for h in range(3):
                    for hb in range(2):
                        P.op("dve", lambda e, h=h, hb=hb: e.tensor_mul(out=o5[:, h, :, hb, :], in0=x5[:, h, :, 1 - hb, :], in1=s4[:, :, hb, :]),
                             reads=[t_xn, t_cs], writes=[t_t2])
                yield
                P.op("dve", lambda e: e.tensor_add(out=t1[:], in0=t1[:], in1=t2[:]), reads=[t_t2, t_t1], writes=[t_t1])
                yield
                src, t_src = t1, t_t1
            for h in range(3):
                P.op("pe", lambda e, h=h, src=src: e.transpose(out=psT[li][:, h * 128:(h + 1) * 128], in_=src[:, h, :], identity=ident[:]),
                     reads=[t_id, t_src], writes=[t_psT[li]])
            yield
            P.op("act", lambda e, t0=t0: e.copy(out=qT[:, :, t0:t0 + 128], in_=psT[li][:, 0:256].rearrange("p (h t) -> p h t", h=2)), reads=[t_psT[li]], writes=[t_qT])
            P.op("act", lambda e, t0=t0: e.copy(out=kT[:, t0:t0 + 128], in_=psT[li][:, 256:384]), reads=[t_psT[li]], writes=[t_kT])
            yield

    gens = [prep_lane(0), prep_lane(1)]
    while gens:
        for g in list(gens):
            try:
                next(g)
            except StopIteration:
                gens.remove(g)
    NS = 2
    pss = [P.ps([128, 512], name="pss") for _ in range(NS)]
    t_pss = [Tok() for _ in range(NS)]
    NPT = 3
    pt = [P.sb([128, 512], BF16, name="pt") for _ in range(NPT)]
    t_pt = [Tok() for _ in range(NPT)]
    acc = [P.ps([128, 512], name="acc") for _ in range(4)]
    t_acc = [Tok() for _ in range(4)]
    ob = [P.sb([128, 128], name="ob") for _ in range(2)]
    t_ob = [Tok(), Tok()]
    rs = P.sb([128, 1], name="rs")
    t_rs = Tok()
    blocks = [(0, 256, 0, 2)] + [(CTX + i * 512, 512, 0, NTT) for i in range(SEQ // 512)]
    scale = 128 ** -0.5
    iters = [(h, q0, qw, kt0, kt1, kt) for h in range(2) for (q0, qw, kt0, kt1) in blocks for kt in range(kt0, kt1)]
    iob = 0

    def emit_qk(i):
        h, q0, qw, kt0, kt1, kt = iters[i]
        sb_ = i % NS
        P.op("pe", lambda e: e.matmul(pss[sb_][:, 0:qw], lhsT=kT[:, kt * 128:(kt + 1) * 128], rhs=qT[:, h, q0:q0 + qw], start=True, stop=True),
             reads=[t_kT, t_qT], writes=[t_pss[sb_]])

    emit_qk(0)
    for i, (h, q0, qw, kt0, kt1, kt) in enumerate(iters):
        nq = qw // 128
        sb_ = i % NS
        pb = i % NPT
        if i + 1 < len(iters):
            emit_qk(i + 1)
        P.op("act", lambda e, sb_=sb_, pb=pb, qw=qw: e.activation(out=pt[pb][:, 0:qw], in_=pss[sb_][:, 0:qw], func=AF.Exp, scale=scale),
             reads=[t_pss[sb_]], writes=[t_pt[pb]])
        for qi in range(nq):
            P.op("pe", lambda e, pb=pb, qi=qi, kt=kt, kt0=kt0, kt1=kt1: e.matmul(acc[qi][:, 0:129], lhsT=pt[pb][:, qi * 128:(qi + 1) * 128],
                                                                              rhs=va[:, kt, :], start=(kt == kt0), stop=(kt == kt1 - 1)),
                 reads=[t_pt[pb], t_va], writes=[t_acc[qi]])
        if kt == kt1 - 1:
            for qi in range(nq):
                P.op("dve", lambda e, qi=qi: e.reciprocal(out=rs[:], in_=acc[qi][:, 128:129]), reads=[t_acc[qi]], writes=[t_rs])
                oi = iob % 2
                iob += 1
                P.op("dve", lambda e, qi=qi, oi=oi: e.tensor_scalar(out=ob[oi][:], in0=acc[qi][:, 0:128], scalar1=rs[:], scalar2=None, op0=ALU.mult),
                     reads=[t_acc[qi], t_rs], writes=[t_ob[oi]])
                P.dma("pool", lambda e, oi=oi, q0=q0, qi=qi, h=h: e.dma_start(out=o_d[q0 + qi * 128:q0 + (qi + 1) * 128, h * 128:(h + 1) * 128], in_=ob[oi][:]),
                      reads=[t_ob[oi]], writes=[t_ob[oi]], is_out=True)
    return P


def rope_tables():
    rows = SEQ // 64
    row = np.repeat(np.arange(rows), 64).astype(np.float32)
    col = np.tile(np.arange(64), rows).astype(np.float32)
    inv = (10000.0 ** (-np.arange(0, 64, 2, dtype=np.float32) / 64)).astype(np.float32)
    ang = np.concatenate([row[:, None] * inv, col[:, None] * inv], axis=-1)
    cos, sin = np.cos(ang).astype(np.float32), np.sin(ang).astype(np.float32)
    C = np.empty((SEQ, 128), np.float32)
    S = np.empty((SEQ, 128), np.float32)
    for a in range(2):
        c_, s_ = cos[:, a * 32:(a + 1) * 32], sin[:, a * 32:(a + 1) * 32]
        C[:, a * 64:a * 64 + 32] = c_
        C[:, a * 64 + 32:a * 64 + 64] = c_
        S[:, a * 64:a * 64 + 32] = -s_
        S[:, a * 64 + 32:a * 64 + 64] = s_
    return np.stack([C, S], axis=0)


def stage_attn(p_lat, p_ctx, qk_gain_l):
    P = build_attn()
    cs = rope_tables()
    g = np.stack([bc(qk_gain_l[0]), bc(qk_gain_l[1])], axis=0)
    in_maps = []
    for c in range(NCORES):
        b, j = divmod(c, 4)
        pa = np.concatenate([p_ctx[b], p_lat[b]], axis=0)
        kv = j // 2
        in_maps.append({"q": np.ascontiguousarray(pa[:, 3632 + 256 * j:3632 + 256 * (j + 1)]),
                        "k": np.ascontiguousarray(pa[:, 4656 + 128 * kv:4656 + 128 * (kv + 1)]),
                        "v": np.ascontiguousarray(pa[:, 4912 + 128 * kv:4912 + 128 * (kv + 1)]),
                        "g": g, "cs": cs, "ident": IDENT})
    res = run_prog(P, in_maps)
    att_lat = np.empty((B, SEQ, 1024), np.float32)
    att_ctx = np.empty((B, CTX, 1024), np.float32)
    for c in range(NCORES):
        b, j = divmod(c, 4)
        att_ctx[b, :, 256 * j:256 * (j + 1)] = res[c]["o"][:CTX]
        att_lat[b, :, 256 * j:256 * (j + 1)] = res[c]["o"][CTX:]
    return att_lat, att_ctx


TRI_INC = np.triu(np.ones((128, 128), np.float32))
TRI_SUFEX = np.tril(np.ones((128, 128), np.float32), -1)
ANTI = np.ascontiguousarray(np.eye(128, dtype=np.float32)[::-1])


def orig_tile(tt):
    return (1 - tt) if tt < 2 else (35 - tt)


def build_gla():
    P = Prog()
    qT_d = P.din("qT", [64, NTOK])
    kT_d = P.din("kT", [64, NTOK])
    k_d = P.din("k", [NTOK, 64])
    v_d = P.din("v", [NTOK, 128])
    rT_d = P.din("rT", [2, 16, NTOK])
    w_d = P.din("w", [2, 17, 64])
    g_d = P.din("g", [NTOK, 128])
    gn_d = P.din("gn", [128, 128])
    c_d = P.din("cm", [4, 128, 128])
    o_d = P.dout("o", [NTOK, 128])

    cm = P.sb([128, 4, 128], name="cm")
    gn = P.sb([128, 128], name="gn")
    t_cm, t_gn = Tok(), Tok()
    for i in range(4):
        P.dma("sp", lambda e, i=i: e.dma_start(out=cm[:, i, :], in_=c_d[i]), writes=[t_cm])
    P.dma("sp", lambda e: e.dma_start(out=gn[:], in_=gn_d), writes=[t_gn])
    g = P.sb([128, NTT, 128], name="g")
    t_g = Tok()
    P.dma("act", lambda e: e.dma_start(out=g[:], in_=g_d.rearrange("(n p) d -> p n d", p=128)), writes=[t_g])
    P.op("act", lambda e: e.activation(out=g[:], in_=g[:], func=AF.Silu), reads=[t_g], writes=[t_g])
    qT = P.sb([64, NTOK], name="qT")
    kT = P.sb([64, NTOK], name="kT")
    kk = P.sb([128, NTT, 64], name="kk")
    vv = P.sb([128, NTT, 128], name="vv")
    t_in = Tok()
    P.dma("sp", lambda e: e.dma_start(out=qT[:], in_=qT_d), writes=[t_in])
    P.dma("act", lambda e: e.dma_start(out=kT[:], in_=kT_d), writes=[t_in])
    P.dma("act", lambda e: e.dma_start(out=kk[:], in_=k_d.rearrange("(n p) d -> p n d", p=128)), writes=[t_in])
    P.dma("sp", lambda e: e.dma_start(out=vv[:], in_=v_d.rearrange("(n p) d -> p n d", p=128)), writes=[t_in])
    odir = [P.sb([128, NTT, 128], name="odir") for _ in range(2)]
    t_odir = [Tok(), Tok()]

    def lane(z):
        C, sufx = (cm[:, 0, :], cm[:, 3, :]) if z == 0 else (cm[:, 1, :], cm[:, 2, :])
        mid, last = (63, 127) if z == 0 else (64, 0)
        rT = P.sb([17, NTOK], name="rT")
        wa = P.sb([17, 64], name="wa")
        t_r = Tok()
        P.op("dve", lambda e: e.memset(rT[:], 1.0), writes=[t_r])
        P.dma("sp", lambda e: e.dma_start(out=rT[0:16, :], in_=rT_d[z]), writes=[t_r])
        P.dma("act", lambda e: e.dma_start(out=wa[:], in_=w_d[z]), writes=[t_r])
        S = P.sb([64, 128], name="S")
        t_S = Tok()
        P.op("dve", lambda e: e.memset(S[:], 0.0), writes=[t_S])
        la = P.sb([128, 64], name="la")
        cs = P.sb([64, 128], name="cs")
        sc = P.sb([64, 4], name="sc")
        e1 = P.sb([64, 128], name="e1")
        e2 = P.sb([64, 128], name="e2")
        e3 = P.sb([64, 128], name="e3")
        k4 = P.sb([128, 64], name="k4")
        AT = P.sb([128, 128], name="AT")
        t_la, t_cs, t_sc, t_e1, t_e2, t_e3, t_k4, t_AT = [Tok() for _ in range(8)]
        b1 = P.ps([128, 512], name="b1")
        b2 = P.ps([128, 512], name="b2")
        b3 = P.ps([128, 512], name="b3")
        t1, t2, t3 = Tok(True), Tok(True), Tok(True)
        ps_la, ps_sf, ps_cT = b1[:, 0:64], b1[:, 64:128], b1[0:64, 128:256]
        ps_A = b2[:, 0:128]
        ps_o, ps_S = b3[:, 0:128], b3[0:64, 128:256]
        yield
        order = list(range(NTT)) if z == 0 else [1, 0] + list(range(NTT - 1, 1, -1))
        for tt in order:
            ts_ = slice(tt * 128, (tt + 1) * 128)
            P.op("pe", lambda e, ts_=ts_: e.matmul(ps_la, lhsT=rT[0:17, ts_], rhs=wa[0:17, :], start=True, stop=True), reads=[t_r], writes=[t1])
            yield
            P.op("act", lambda e: e.activation(out=la[:], in_=ps_la, func=AF.Exp, scale=-1.0), reads=[t1], writes=[t_la])
            yield
            P.op("dve", lambda e: e.tensor_scalar_add(out=la[:], in0=la[:], scalar1=1.0), reads=[t_la], writes=[t_la])
            yield
            P.op("act", lambda e: e.activation(out=la[:], in_=la[:], func=AF.Ln), reads=[t_la], writes=[t_la])
            yield
            P.op("dve", lambda e: e.tensor_scalar_mul(out=la[:], in0=la[:], scalar1=-1.0 / 16.0), reads=[t_la], writes=[t_la])
            yield
            P.op("pe", lambda e: e.matmul(ps_cT, lhsT=la[:], rhs=C, start=True, stop=True), reads=[t_la, t_cm], writes=[t1])
            P.op("pe", lambda e: e.matmul(ps_sf, lhsT=sufx, rhs=la[:], start=True, stop=True), reads=[t_la, t_cm], writes=[t1])
            yield
            P.op("act", lambda e: e.copy(out=cs[:], in_=ps_cT), reads=[t1], writes=[t_cs])
            P.op("act", lambda e: e.activation(out=k4[:], in_=ps_sf, func=AF.Exp), reads=[t1], writes=[t_k4])
            yield
            P.op("dve", lambda e: e.tensor_scalar_mul(out=sc[:, 0:1], in0=cs[:, mid:mid + 1], scalar1=-1.0), reads=[t_cs], writes=[t_sc])
            P.op("dve", lambda e, tt=tt: e.tensor_mul(out=k4[:], in0=kk[:, tt, :], in1=k4[:]), reads=[t_in, t_k4], writes=[t_k4])
            yield
            P.op("act", lambda e: e.activation(out=e1[:], in_=cs[:], func=AF.Exp, bias=sc[:, 0:1], scale=1.0), reads=[t_cs, t_sc], writes=[t_e1])
            P.op("act", lambda e: e.activation(out=e2[:], in_=cs[:], func=AF.Exp, bias=cs[:, mid:mid + 1], scale=-1.0), reads=[t_cs], writes=[t_e2])
            P.op("act", lambda e: e.activation(out=e3[:], in_=cs[:], func=AF.Exp), reads=[t_cs], writes=[t_e3])
            P.op("act", lambda e: e.activation(out=sc[:, 1:2], in_=cs[:, last:last + 1], func=AF.Exp), reads=[t_cs], writes=[t_sc])
            yield
            P.op("dve", lambda e, ts_=ts_: e.scalar_tensor_tensor(out=e1[:], in0=qT[:, ts_], scalar=0.125, in1=e1[:], op0=ALU.mult, op1=ALU.mult), reads=[t_in, t_e1], writes=[t_e1])
            P.op("dve", lambda e, ts_=ts_: e.tensor_mul(out=e2[:], in0=kT[:, ts_], in1=e2[:]), reads=[t_in, t_e2], writes=[t_e2])
            P.op("dve", lambda e, ts_=ts_: e.scalar_tensor_tensor(out=e3[:], in0=qT[:, ts_], scalar=0.125, in1=e3[:], op0=ALU.mult, op1=ALU.mult), reads=[t_in, t_e3], writes=[t_e3])
            yield
            P.op("pe", lambda e: e.matmul(ps_A, lhsT=e2[:], rhs=e1[:], start=True, stop=True), reads=[t_e1, t_e2], writes=[t2])
            yield
            P.op("dve", lambda e: e.tensor_mul(out=AT[:], in0=ps_A, in1=C), reads=[t2, t_cm], writes=[t_AT])
            yield
            P.op("pe", lambda e, tt=tt: e.matmul(ps_o, lhsT=AT[:], rhs=vv[:, tt, :], start=True, stop=False), reads=[t_AT, t_in], writes=[t3])
            P.op("pe", lambda e: e.matmul(ps_o, lhsT=e3[:], rhs=S[:], start=False, stop=True), reads=[t_e3, t_S], writes=[t3])
            P.op("pe", lambda e, tt=tt: e.matmul(ps_S, lhsT=k4[:], rhs=vv[:, tt, :], start=True, stop=True), reads=[t_k4, t_in], writes=[t3])
            yield
            P.op("act", lambda e, tt=tt: e.copy(out=odir[z][:, tt, :], in_=ps_o), reads=[t3], writes=[t_odir[z]])
            yield
            P.op("dve", lambda e: e.scalar_tensor_tensor(out=S[:], in0=S[:], scalar=sc[:, 1:2], in1=ps_S, op0=ALU.mult, op1=ALU.add), reads=[t_sc, t3, t_S], writes=[t_S])
            yield

    gens = [lane(0), lane(1)]
    while gens:
        for gg in list(gens):
            try:
                next(gg)
            except StopIteration:
                gens.remove(gg)
    rs = P.sb([128, 2, NTT], name="rs")
    sq = P.sb([128, NTT, 128], name="sq")
    t_rs, t_sq = Tok(), Tok()
    osum = odir[0]
    P.op("dve", lambda e: e.tensor_add(out=osum[:], in0=odir[0][:], in1=odir[1][:]), reads=[t_odir[1], t_odir[0]], writes=[t_odir[0]])
    P.op("act", lambda e: e.activation(out=sq[:], in_=osum[:], func=AF.Square), reads=[t_odir[0]], writes=[t_sq])
    P.op("dve", lambda e: e.reduce_sum(out=rs[:, 0, :], in_=sq[:], axis=AX.X), reads=[t_sq], writes=[t_rs])
    P.op("dve", lambda e: e.tensor_scalar(out=rs[:, 0, :], in0=rs[:, 0, :], scalar1=1.0 / 128, scalar2=1e-6, op0=ALU.mult, op1=ALU.add), reads=[t_rs], writes=[t_rs])
    P.op("act", lambda e: e.activation(out=rs[:, 0, :], in_=rs[:, 0, :], func=AF.Sqrt), reads=[t_rs], writes=[t_rs])
    P.op("dve", lambda e: e.reciprocal(out=rs[:, 1, :], in_=rs[:, 0, :]), reads=[t_rs], writes=[t_rs])
    for tt in range(NTT):
        P.op("dve", lambda e, tt=tt: e.scalar_tensor_tensor(out=osum[:, tt, :], in0=osum[:, tt, :], scalar=rs[:, 1, tt:tt + 1], in1=gn[:], op0=ALU.mult, op1=ALU.mult),
             reads=[t_rs, t_gn, t_odir[0]], writes=[t_odir[0]])
    P.op("dve", lambda e: e.tensor_mul(out=osum[:], in0=osum[:], in1=g[:]), reads=[t_g, t_odir[0]], writes=[t_odir[0]])
    for half in range(2):
        hs = slice(half * 17, half * 17 + 17)
        P.dma("sp" if half == 0 else "act", lambda e, hs=hs, half=half: e.dma_start(
            out=o_d[half * 17 * 128:(half + 1) * 17 * 128, :].rearrange("(n p) d -> p n d", p=128), in_=osum[:, hs, :]), reads=[t_odir[0]], is_out=True)
    return P


def flipseg(a):
    return np.concatenate([a[:CTX][::-1], a[CTX:][::-1]], axis=0)


CMATS = np.stack([TRI_INC, TRI_SUFEX, ANTI, IDENT], axis=0)


def stage_gla(p_lat, p_ctx, w_up_l, b_up_l, norm_l):
    P = build_gla()
    in_maps = []
    for c in range(NCORES):
        b, h = divmod(c, 4)
        pa = np.concatenate([p_ctx[b], p_lat[b]], axis=0)
        q = pa[:, h * 64:(h + 1) * 64]
        k = pa[:, 256 + h * 64:256 + (h + 1) * 64]
        v = pa[:, 512 + h * 128:512 + (h + 1) * 128]
        g = pa[:, 1024 + h * 128:1024 + (h + 1) * 128]
        rs = [pa[:, 1536 + z * 16:1536 + (z + 1) * 16].T for z in range(2)]
        ws = [np.concatenate([w_up_l[z][:, h * 64:(h + 1) * 64], b_up_l[z][None, h * 64:(h + 1) * 64]], axis=0) for z in range(2)]
        in_maps.append({"g": np.ascontiguousarray(g), "gn": bc(norm_l), "cm": np.ascontiguousarray(GDN_CM[0:4]),
                        "qT": np.ascontiguousarray(q.T), "kT": np.ascontiguousarray(k.T), "k": np.ascontiguousarray(k),
                        "v": np.ascontiguousarray(v), "rT": np.ascontiguousarray(np.stack(rs)), "w": np.ascontiguousarray(np.stack(ws))})
    res = run_prog(P, in_maps)
    o_lat = np.empty((B, SEQ, 512), np.float32)
    o_ctx = np.empty((B, CTX, 512), np.float32)
    for c in range(NCORES):
        b, h = divmod(c, 4)
        o_ctx[b, :, 128 * h:128 * (h + 1)] = res[c]["o"][:CTX]
        o_lat[b, :, 128 * h:128 * (h + 1)] = res[c]["o"][CTX:]
    return o_lat, o_ctx


UPP = TRI_INC
LOW = np.ascontiguousarray(TRI_INC.T)
GDN_CM = np.stack([UPP, LOW, UPP - IDENT, LOW - IDENT, IDENT, np.ones((128, 128), np.float32)], axis=0)


def build_gdn(dbg=None):
    P = Prog()
    x_d = P.din("xT", [3, 128, NTOK])
    cw_d = P.din("cw", [128, 3, 5])
    zg_d = P.din("zg", [NTOK, 128])
    ba_d = P.din("ba", [2, 2, 128, NTT])
    sc_d = P.din("sc", [128, 2, 2])
    gn_d = P.din("gn", [128, 128])
    c_d = P.din("cm", [6, 128, 128])
    o_d = P.dout("o", [NTOK, 128])

    cm = P.sb([128, 6, 128], name="cm")
    t_cm = Tok()
    for i in range(6):
        P.dma("sp", lambda e, i=i: e.dma_start(out=cm[:, i, :], in_=c_d[i]), writes=[t_cm])
    ident, ones = cm[:, 4, :], cm[:, 5, :]
    gn = P.sb([128, 128], name="gn")
    cw = P.sb([128, 3, 5], name="cw")
    scal = P.sb([128, 2, 2], name="scal")
    t_gn, t_cw, t_scal = Tok(), Tok(), Tok()
    P.dma("sp", lambda e: e.dma_start(out=gn[:], in_=gn_d), writes=[t_gn])
    P.dma("sp", lambda e: e.dma_start(out=cw[:], in_=cw_d), writes=[t_cw])
    P.dma("sp", lambda e: e.dma_start(out=scal[:], in_=sc_d), writes=[t_scal])
    zg = P.sb([128, NTT, 128], name="zg")
    t_zg = Tok()
    P.dma("act", lambda e: e.dma_start(out=zg[:], in_=zg_d.rearrange("(n p) d -> p n d", p=128)), writes=[t_zg])
    P.op("act", lambda e: e.activation(out=zg[:], in_=zg[:], func=AF.Silu), reads=[t_zg], writes=[t_zg])

    raw = P.sb([128, NTOK], name="raw")
    t_raw = Tok()
    fmx = [P.sb([128, NTOK], name="fmx") for _ in range(3)]
    t_fm = [Tok() for _ in range(3)]
    segs = [(0, CTX), (CTX, NTOK)]
    for s in range(3):
        P.dma("sp", lambda e, s=s: e.dma_start(out=raw[:], in_=x_d[s]), writes=[t_raw])
        acc = fmx[s]
        for (s0, s1) in segs:
            P.op("dve", lambda e, s=s, s0=s0, s1=s1, acc=acc: e.tensor_scalar(out=acc[:, s0:s1], in0=raw[:, s0:s1], scalar1=cw[:, s, 2:3], scalar2=None, op0=ALU.mult),
                 reads=[t_raw, t_cw], writes=[t_fm[s]])
            for j in (0, 1, 3, 4):
                sh = j - 2
                lo = max(s0, s0 - sh)
                hi = min(s1, s1 - sh)
                P.op("dve", lambda e, s=s, j=j, lo=lo, hi=hi, sh=sh, acc=acc: e.scalar_tensor_tensor(
                    out=acc[:, lo:hi], in0=raw[:, lo + sh:hi + sh], scalar=cw[:, s, j:j + 1], in1=acc[:, lo:hi], op0=ALU.mult, op1=ALU.add),
                    reads=[t_raw, t_cw, t_fm[s]], writes=[t_fm[s]])
        P.op("act", lambda e, acc=acc: e.activation(out=acc[:], in_=acc[:], func=AF.Silu), reads=[t_fm[s]], writes=[t_fm[s]])
    bk1 = P.ps([128, 512], name="bk1")
    t_bk1 = Tok(True)
    sq = P.sb([128, 512], name="sq")
    t_sq = Tok()
    blocks = [(0, 256)] + [(CTX + i * 512, 512) for i in range(SEQ // 512)]
    for s in range(2):
        mul = (128 ** -0.5) if s == 0 else 1.0
        for (b0, bw) in blocks:
            P.op("act", lambda e, s=s, b0=b0, bw=bw: e.activation(out=sq[:, 0:bw], in_=fmx[s][:, b0:b0 + bw], func=AF.Square), reads=[t_fm[s]], writes=[t_sq])
            P.op("pe", lambda e, bw=bw: e.matmul(bk1[:, 0:bw], lhsT=ones, rhs=sq[:, 0:bw], start=True, stop=True), reads=[t_sq, t_cm], writes=[t_bk1])
            P.op("dve", lambda e, bw=bw: e.tensor_scalar_add(out=sq[:, 0:bw], in0=bk1[:, 0:bw], scalar1=1e-6), reads=[t_bk1], writes=[t_sq])
            P.op("act", lambda e, bw=bw: e.activation(out=sq[:, 0:bw], in_=sq[:, 0:bw], func=AF.Sqrt), reads=[t_sq], writes=[t_sq])
            P.op("dve", lambda e, bw=bw: e.reciprocal(out=sq[:, 0:bw], in_=sq[:, 0:bw]), reads=[t_sq], writes=[t_sq])
            P.op("dve", lambda e, s=s, b0=b0, bw=bw, mul=mul: e.scalar_tensor_tensor(out=fmx[s][:, b0:b0 + bw], in0=fmx[s][:, b0:b0 + bw], scalar=float(mul), in1=sq[:, 0:bw],
                                                                                   op0=ALU.mult, op1=ALU.mult), reads=[t_sq, t_fm[s]], writes=[t_fm[s]])
    qnT, knT, vT = fmx
    t_qnT, t_knT, t_vT = t_fm
    kn = P.sb([128, NTT, 128], name="kn")
    vt = P.sb([128, NTT, 128], name="vt")
    t_kn, t_vt = Tok(), Tok()
    for tt in range(NTT):
        ts_ = slice(tt * 128, (tt + 1) * 128)
        P.op("pe", lambda e, ts_=ts_: e.transpose(out=bk1[:, 0:128], in_=knT[:, ts_], identity=ident), reads=[t_knT, t_cm], writes=[t_bk1])
        P.op("pe", lambda e, ts_=ts_: e.transpose(out=bk1[:, 128:256], in_=vT[:, ts_], identity=ident), reads=[t_vT, t_cm], writes=[t_bk1])
        P.op("act", lambda e, tt=tt: e.copy(out=kn[:, tt, :], in_=bk1[:, 0:128]), reads=[t_bk1], writes=[t_kn])
        P.op("dve", lambda e, tt=tt: e.tensor_copy(out=vt[:, tt, :], in_=bk1[:, 128:256]), reads=[t_bk1], writes=[t_vt])

    if dbg == "prep":
        for tt in range(NTT):
            P.dma("pool", lambda e, tt=tt: e.dma_start(out=o_d[tt * 128:(tt + 1) * 128, :], in_=kn[:, tt, :]), reads=[t_kn], is_out=True)
        return P
    odir = [P.sb([128, NTT, 128], name="odir") for _ in range(2)]
    t_odir = [Tok(), Tok()]

    def sbt(n, w=128):
        return P.sb([128, w], name=n), Tok()

    def make_lane(z):
        L = {}
        L["bl"] = P.sb([128, 2, NTT], name="bl")
        L["t_bl"] = Tok()
        L["nea"], L["t_nea"] = sbt("nea", 1)
        L["S"], L["t_S"] = sbt("S")
        bA = bk1 if z == 0 else P.ps([128, 512], name="bA")
        bB = P.ps([128, 512], name="bB")
        bC = P.ps([128, 512], name="bC")
        bD = P.ps([128, 512], name="bD")
        L["tA"] = t_bk1 if z == 0 else Tok(True)
        L["tB"], L["tC"], L["tD"] = Tok(True), Tok(True), Tok(True)
        L["pG"], L["pbR"], L["pKK"], L["pQK"] = bA[:, 0:128], bA[:, 128:256], bA[:, 256:384], bA[:, 384:512]
        L["psqL"], L["psqLT"], L["pg"] = bB[:, 0:128], bB[:, 128:256], bB[:, 256:258]
        L["pX"], L["pwT"], L["pvn"] = bC[:, 0:256], bC[:, 256:384], bC[:, 384:512]
        L["po"], L["pS"] = bD[:, 0:128], bD[:, 128:256]
        for n in ("lgB", "btB", "GT", "Gm", "EgR", "L0", "LT1", "AqT", "wT", "vn", "qd", "kd"):
            L[n], L["t_" + n] = sbt(n)
        L["col"], L["t_col"] = sbt("col", 8)
        L["X"], L["t_X"] = sbt("X", 256)
        L["pow"] = [sbt("pw%d" % i) for i in range(12)]
        return L

    def lane_gen(z, L):
        C, CT, Cs, CTs = (cm[:, 0, :], cm[:, 1, :], cm[:, 2, :], cm[:, 3, :]) if z == 0 else (cm[:, 1, :], cm[:, 0, :], cm[:, 3, :], cm[:, 2, :])
        last = 127 if z == 0 else 0
        bl, t_bl, nea, t_nea, S, t_S = L["bl"], L["t_bl"], L["nea"], L["t_nea"], L["S"], L["t_S"]
        tA, tB, tC, tD = L["tA"], L["tB"], L["tC"], L["tD"]
        pG, pbR, pKK, pQK, psqL, psqLT, pg = L["pG"], L["pbR"], L["pKK"], L["pQK"], L["psqL"], L["psqLT"], L["pg"]
        pX, pwT, pvn, po, pS = L["pX"], L["pwT"], L["pvn"], L["po"], L["pS"]
        lgB, btB, GT, Gm, EgR, L0, LT1, AqT, wT, vn, qd, kd, col, X = [L[n] for n in ("lgB", "btB", "GT", "Gm", "EgR", "L0", "LT1", "AqT", "wT", "vn", "qd", "kd", "col", "X")]
        t_lgB, t_btB, t_GT, t_Gm, t_EgR, t_L0, t_LT1, t_AqT, t_wT, t_vn, t_qd, t_kd, t_col, t_X = [L["t_" + n] for n in ("lgB", "btB", "GT", "Gm", "EgR", "L0", "LT1", "AqT", "wT", "vn", "qd", "kd", "col", "X")]
        for i in range(2):
            P.dma("sp", lambda e, i=i: e.dma_start(out=bl[:, i, :], in_=ba_d[z, i]), writes=[t_bl])
        P.op("act", lambda e: e.activation(out=bl[:, 0, :], in_=bl[:, 0, :], func=AF.Sigmoid), reads=[t_bl], writes=[t_bl])
        P.op("act", lambda e: e.activation(out=bl[:, 1, :], in_=bl[:, 1, :], func=AF.Exp, bias=scal[:, z, 1:2], scale=1.0), reads=[t_bl, t_scal], writes=[t_bl])
        P.op("dve", lambda e: e.tensor_scalar_add(out=bl[:, 1, :], in0=bl[:, 1, :], scalar1=1.0), reads=[t_bl], writes=[t_bl])
        P.op("act", lambda e: e.activation(out=bl[:, 1, :], in_=bl[:, 1, :], func=AF.Ln), reads=[t_bl], writes=[t_bl])
        P.op("act", lambda e: e.activation(out=nea[:], in_=scal[:, z, 0:1], func=AF.Exp), reads=[t_scal], writes=[t_nea])
        P.op("dve", lambda e: e.tensor_scalar_mul(out=nea[:], in0=nea[:], scalar1=-1.0), reads=[t_nea], writes=[t_nea])
        P.op("dve", lambda e: e.tensor_scalar(out=bl[:, 1, :], in0=bl[:, 1, :], scalar1=nea[:], scalar2=None, op0=ALU.mult), reads=[t_bl, t_nea], writes=[t_bl])
        P.op("dve", lambda e: e.memset(S[:], 0.0), writes=[t_S])
        yield
        order = list(range(NTT)) if z == 0 else [1, 0] + list(range(NTT - 1, 1, -1))
        if dbg is not None and dbg.startswith("main"):
            order = order[:int(dbg[4:])]
        for tt in order:
            ts_ = slice(tt * 128, (tt + 1) * 128)
            beta = bl[:, 0, tt:tt + 1]
            lg = bl[:, 1, tt:tt + 1]
            P.op("dve", lambda e, lg=lg: e.tensor_scalar(out=lgB[:], in0=ones, scalar1=lg, scalar2=None, op0=ALU.mult), reads=[t_bl, t_cm], writes=[t_lgB])
            yield
            P.op("dve", lambda e, beta=beta: e.tensor_scalar(out=btB[:], in0=ones, scalar1=beta, scalar2=None, op0=ALU.mult), reads=[t_bl, t_cm], writes=[t_btB])
            yield
            P.op("pe", lambda e: e.matmul(pg, lhsT=C, rhs=lgB[:, 0:2], start=True, stop=True), reads=[t_lgB, t_cm], writes=[tB])
            P.op("pe", lambda e: e.matmul(pG, lhsT=lgB[:], rhs=C, start=True, stop=True), reads=[t_lgB, t_cm], writes=[tA])
            P.op("pe", lambda e: e.matmul(pbR, lhsT=btB[:], rhs=ident, start=True, stop=True), reads=[t_btB, t_cm], writes=[tA])
            P.op("pe", lambda e, ts_=ts_: e.matmul(pKK, lhsT=knT[:, ts_], rhs=knT[:, ts_], start=True, stop=True), reads=[t_knT], writes=[tA])
            P.op("pe", lambda e, ts_=ts_: e.matmul(pQK, lhsT=knT[:, ts_], rhs=qnT[:, ts_], start=True, stop=True), reads=[t_knT, t_qnT], writes=[tA])
            yield
            P.op("act", lambda e: e.copy(out=col[:, 0:1], in_=pg[:, 0:1]), reads=[tB], writes=[t_col])
            yield
            P.op("dve", lambda e: e.tensor_scalar(out=GT[:], in0=pG, scalar1=col[:, 0:1], scalar2=0.0, op0=ALU.subtract, op1=ALU.min), reads=[tA, t_col], writes=[t_GT])
            yield
            P.op("act", lambda e: e.activation(out=GT[:], in_=GT[:], func=AF.Exp), reads=[t_GT], writes=[t_GT])
            yield
            P.op("dve", lambda e: e.tensor_scalar(out=Gm[:], in0=pG, scalar1=col[:, 0:1], scalar2=0.0, op0=ALU.subtract, op1=ALU.max), reads=[tA, t_col], writes=[t_Gm])
            yield
            P.op("act", lambda e: e.activation(out=Gm[:], in_=Gm[:], func=AF.Exp, scale=-1.0), reads=[t_Gm], writes=[t_Gm])
            yield
            P.op("act", lambda e: e.activation(out=EgR[:], in_=pG, func=AF.Exp), reads=[tA], writes=[t_EgR])
            yield
            P.op("act", lambda e: e.activation(out=col[:, 4:5], in_=col[:, 0:1], func=AF.Exp), reads=[t_col], writes=[t_col])
            yield
            P.op("dve", lambda e, beta=beta: e.tensor_mul(out=col[:, 1:2], in0=col[:, 4:5], in1=beta), reads=[t_col, t_bl], writes=[t_col])
            yield
            P.op("dve", lambda e: e.tensor_sub(out=col[:, 5:6], in0=pG[:, last:last + 1], in1=col[:, 0:1]), reads=[tA, t_col], writes=[t_col])
            yield
            P.op("act", lambda e: e.activation(out=col[:, 2:3], in_=col[:, 5:6], func=AF.Exp), reads=[t_col], writes=[t_col])
            yield
            P.op("act", lambda e: e.activation(out=col[:, 3:4], in_=pG[:, last:last + 1], func=AF.Exp), reads=[tA], writes=[t_col])
            yield
            P.op("dve", lambda e: e.tensor_mul(out=LT1[:], in0=GT[:], in1=Cs), reads=[t_GT, t_cm], writes=[t_LT1])
            yield
            P.op("dve", lambda e: e.tensor_mul(out=LT1[:], in0=LT1[:], in1=pKK), reads=[tA, t_LT1], writes=[t_LT1])
            yield
            P.op("dve", lambda e: e.tensor_mul(out=LT1[:], in0=LT1[:], in1=pbR), reads=[tA, t_LT1], writes=[t_LT1])
            yield
            P.op("dve", lambda e: e.tensor_mul(out=L0[:], in0=Gm[:], in1=CTs), reads=[t_Gm, t_cm], writes=[t_L0])
            yield
            P.op("dve", lambda e, beta=beta: e.scalar_tensor_tensor(out=L0[:], in0=L0[:], scalar=beta, in1=pKK, op0=ALU.mult, op1=ALU.mult), reads=[tA, t_bl, t_L0], writes=[t_L0])
            yield
            P.op("dve", lambda e: e.tensor_mul(out=AqT[:], in0=GT[:], in1=C), reads=[t_GT, t_cm], writes=[t_AqT])
            yield
            P.op("dve", lambda e: e.tensor_mul(out=AqT[:], in0=AqT[:], in1=pQK), reads=[tA, t_AqT], writes=[t_AqT])
            yield
            P.op("dve", lambda e, tt=tt, beta=beta: e.tensor_scalar(out=X[:, 0:128], in0=vt[:, tt, :], scalar1=beta, scalar2=None, op0=ALU.mult), reads=[t_vt, t_bl], writes=[t_X])
            yield
            P.op("dve", lambda e, tt=tt: e.tensor_scalar(out=X[:, 128:256], in0=kn[:, tt, :], scalar1=col[:, 1:2], scalar2=None, op0=ALU.mult), reads=[t_kn, t_col], writes=[t_X])
            yield
            cur_L, cur_tL, cur_LT, cur_tLT = L0, t_L0, LT1, t_LT1
            for pi in range(7):
                if pi < 6:
                    nL, t_nL = L["pow"][2 * pi]
                    nLT, t_nLT = L["pow"][2 * pi + 1]
                    P.op("pe", lambda e, cur_L=cur_L, cur_LT=cur_LT: e.matmul(psqL, lhsT=cur_LT[:], rhs=cur_L[:], start=True, stop=True), reads=[cur_tL, cur_tLT], writes=[tB])
                    P.op("pe", lambda e, cur_L=cur_L, cur_LT=cur_LT: e.matmul(psqLT, lhsT=cur_L[:], rhs=cur_LT[:], start=True, stop=True), reads=[cur_tL, cur_tLT], writes=[tB])
                P.op("pe", lambda e, cur_LT=cur_LT: e.matmul(pX, lhsT=cur_LT[:], rhs=X[:], start=True, stop=True), reads=[cur_tLT, t_X], writes=[tC])
                yield
                if pi < 6:
                    P.op("act", lambda e, nL=nL: e.copy(out=nL[:], in_=psqL), reads=[tB], writes=[t_nL])
                    P.op("act", lambda e, nLT=nLT: e.copy(out=nLT[:], in_=psqLT), reads=[tB], writes=[t_nLT])
                if pi > 0:
                    P.op("dve", lambda e: e.tensor_add(out=X[:], in0=X[:], in1=pX), reads=[tC, t_X], writes=[t_X])
                else:
                    P.op("dve", lambda e: e.tensor_sub(out=X[:], in0=X[:], in1=pX), reads=[tC, t_X], writes=[t_X])
                yield
                if pi < 6:
                    cur_L, cur_tL, cur_LT, cur_tLT = nL, t_nL, nLT, t_nLT
            P.op("pe", lambda e: e.transpose(out=pwT, in_=X[:, 128:256], identity=ident), reads=[t_X, t_cm], writes=[tC])
            yield
            P.op("act", lambda e: e.copy(out=wT[:], in_=pwT), reads=[tC], writes=[t_wT])
            yield
            P.op("pe", lambda e: e.matmul(pvn, lhsT=wT[:], rhs=S[:], start=True, stop=True), reads=[t_wT, t_S], writes=[tC])
            yield
            P.op("dve", lambda e: e.tensor_sub(out=vn[:], in0=X[:, 0:128], in1=pvn), reads=[tC, t_X], writes=[t_vn])
            yield
            P.op("dve", lambda e, ts_=ts_: e.tensor_mul(out=qd[:], in0=qnT[:, ts_], in1=EgR[:]), reads=[t_qnT, t_EgR], writes=[t_qd])
            yield
            P.op("pe", lambda e: e.matmul(po, lhsT=qd[:], rhs=S[:], start=True, stop=False), reads=[t_qd, t_S], writes=[tD])
            P.op("pe", lambda e: e.matmul(po, lhsT=AqT[:], rhs=vn[:], start=False, stop=True), reads=[t_AqT, t_vn], writes=[tD])
            yield
            P.op("dve", lambda e, tt=tt: e.tensor_scalar(out=kd[:], in0=kn[:, tt, :], scalar1=col[:, 2:3], scalar2=None, op0=ALU.mult), reads=[t_kn, t_col], writes=[t_kd])
            yield
            P.op("pe", lambda e: e.matmul(pS, lhsT=kd[:], rhs=vn[:], start=True, stop=True), reads=[t_kd, t_vn], writes=[tD])
            yield
            P.op("act", lambda e, tt=tt: e.copy(out=odir[z][:, tt, :], in_=po), reads=[tD], writes=[t_odir[z]])
            yield
            P.op("dve", lambda e: e.scalar_tensor_tensor(out=S[:], in0=S[:], scalar=col[:, 3:4], in1=pS, op0=ALU.mult, op1=ALU.add), reads=[t_col, tD, t_S], writes=[t_S])
            yield

    gens = [lane_gen(z, make_lane(z)) for z in range(2)]
    while gens:
        for g in list(gens):
            try:
                next(g)
            except StopIteration:
                gens.remove(g)
    rs = P.sb([128, 2, NTT], name="rs")
    t_rs = Tok()
    osum = odir[0]
    P.op("dve", lambda e: e.tensor_add(out=osum[:], in0=odir[0][:], in1=odir[1][:]), reads=[t_odir[1], t_odir[0]], writes=[t_odir[0]])
    sqv = raw[:].rearrange("p (n d) -> p n d", d=128)
    P.op("act", lambda e: e.activation(out=sqv, in_=osum[:], func=AF.Square), reads=[t_odir[0]], writes=[t_raw])
    P.op("dve", lambda e: e.reduce_sum(out=rs[:, 0, :], in_=sqv, axis=AX.X), reads=[t_raw], writes=[t_rs])
    P.op("dve", lambda e: e.tensor_scalar(out=rs[:, 0, :], in0=rs[:, 0, :], scalar1=1.0 / 128, scalar2=1e-6, op0=ALU.mult, op1=ALU.add), reads=[t_rs], writes=[t_rs])
    P.op("act", lambda e: e.activation(out=rs[:, 0, :], in_=rs[:, 0, :], func=AF.Sqrt), reads=[t_rs], writes=[t_rs])
    P.op("dve", lambda e: e.reciprocal(out=rs[:, 1, :], in_=rs[:, 0, :]), reads=[t_rs], writes=[t_rs])
    for tt in range(NTT):
        P.op("dve", lambda e, tt=tt: e.scalar_tensor_tensor(out=osum[:, tt, :], in0=osum[:, tt, :], scalar=rs[:, 1, tt:tt + 1], in1=gn[:], op0=ALU.mult, op1=ALU.mult),
             reads=[t_rs, t_gn, t_odir[0]], writes=[t_odir[0]])
    P.op("dve", lambda e: e.tensor_mul(out=osum[:], in0=osum[:], in1=zg[:]), reads=[t_zg, t_odir[0]], writes=[t_odir[0]])
    for half in range(2):
        hs = slice(half * 17, half * 17 + 17)
        P.dma("sp" if half == 0 else "act", lambda e, hs=hs, half=half: e.dma_start(
            out=o_d[half * 17 * 128:(half + 1) * 17 * 128, :].rearrange("(n p) d -> p n d", p=128), in_=osum[:, hs, :]), reads=[t_odir[0]], is_out=True)
    return P


def stage_gdn(p_lat, p_ctx, conv_l, a_log_l, dt_bias_l, norm_l, dbg=None):
    P = build_gdn(dbg)
    in_maps = []
    for c in range(NCORES):
        b, h = divmod(c, 4)
        pa = np.concatenate([p_ctx[b], p_lat[b]], axis=0)
        hs = slice(h * 128, (h + 1) * 128)
        xT = np.stack([pa[:, 1568:2080][:, hs].T, pa[:, 2080:2592][:, hs].T, pa[:, 2592:3104][:, hs].T], axis=0)
        cw = np.stack([conv_l[:, s * 512 + h * 128:s * 512 + (h + 1) * 128].T for s in range(3)], axis=1)
        ba = np.empty((2, 2, 128, NTT), np.float32)
        sc = np.empty((128, 2, 2), np.float32)
        for z in range(2):
            ba[z, 0] = pa[:, 3616 + z * 4 + h].reshape(NTT, 128).T
            ba[z, 1] = pa[:, 3624 + z * 4 + h].reshape(NTT, 128).T
            sc[:, z, 0] = a_log_l[z, h]
            sc[:, z, 1] = dt_bias_l[z, h]
        in_maps.append({"xT": np.ascontiguousarray(xT), "cw": np.ascontiguousarray(cw), "zg": np.ascontiguousarray(pa[:, 3104:3616][:, hs]),
                        "ba": ba, "sc": sc, "gn": bc(norm_l), "cm": GDN_CM})
    res = run_prog(P, in_maps)
    o_lat = np.empty((B, SEQ, 512), np.float32)
    o_ctx = np.empty((B, CTX, 512), np.float32)
    for c in range(NCORES):
        b, h = divmod(c, 4)
        o_ctx[b, :, 128 * h:128 * (h + 1)] = res[c]["o"][:CTX]
        o_lat[b, :, 128 * h:128 * (h + 1)] = res[c]["o"][CTX:]
    return o_lat, o_ctx


def build_router():
    P = Prog()
    KC = D // 128
    xT_d = P.din("xT", [128, KC, NT_B])
    mod_d = P.din("modv", [128, KC, 4])
    r_d = P.din("rw", [D, NE]).rearrange("(k p) c -> p k c", p=128)
    hT_d = P.dout("hT", [128, KC, NT_B])
    a_d = P.dout("aff", [NT_B, NE])
    xT = P.sb([128, KC, NT_B], name="xT")
    modv = P.sb([128, KC, 4], name="modv")
    rw = P.sb([128, KC, NE], name="rw")
    t_x = [Tok() for _ in range(KC)]
    t_mod, t_rw = Tok(), Tok()
    for k in range(KC):
        P.dma("sp" if k % 2 == 0 else "act", lambda e, k=k: e.dma_start(out=xT[:, k, :], in_=xT_d[:, k, :]), writes=[t_x[k]])
    P.dma("sp", lambda e: e.dma_start(out=modv[:], in_=mod_d), writes=[t_mod])
    P.dma("sp", lambda e: e.dma_start(out=rw[:], in_=r_d), writes=[t_rw])
    P.op("dve", lambda e: e.tensor_scalar_add(out=modv[:, :, 1:2], in0=modv[:, :, 1:2], scalar1=1.0), reads=[t_mod], writes=[t_mod])
    P.op("dve", lambda e: e.tensor_scalar_add(out=modv[:, :, 3:4], in0=modv[:, :, 3:4], scalar1=1.0), reads=[t_mod], writes=[t_mod])
    for k in range(KC):
        P.op("dve", lambda e, k=k: e.tensor_scalar(out=xT[:, k, 0:1024], in0=xT[:, k, 0:1024], scalar1=modv[:, k, 1:2],
                                                   scalar2=modv[:, k, 0:1], op0=ALU.mult, op1=ALU.add),
             reads=[t_mod, t_x[k]], writes=[t_x[k]])
        P.op("dve", lambda e, k=k: e.tensor_scalar(out=xT[:, k, 1024:NT_B], in0=xT[:, k, 1024:NT_B], scalar1=modv[:, k, 3:4],
                                                   scalar2=modv[:, k, 2:3], op0=ALU.mult, op1=ALU.add),
             reads=[t_mod, t_x[k]], writes=[t_x[k]])
        P.dma("pool", lambda e, k=k: e.dma_start(out=hT_d[:, k, :], in_=xT[:, k, :]), reads=[t_x[k]], is_out=True)
    pl = [P.ps([128, NE], name="pl") for _ in range(2)]
    t_pl = [Tok(True), Tok(True)]
    ex = [P.sb([128, NE], name="ex") for _ in range(2)]
    t_ex = [Tok(), Tok()]
    st = P.sb([128, 4], name="st")
    t_st = Tok()
    tiles = [(i * 128, 128) for i in range(8)] + [(1024, 64)]
    for ti, (t0, m) in enumerate(tiles):
        bi = ti % 2
        for k in range(KC):
            P.op("pe", lambda e, bi=bi, k=k, t0=t0, m=m: e.matmul(pl[bi][0:m, :], lhsT=xT[:, k, t0:t0 + m], rhs=rw[:, k, :], start=(k == 0), stop=(k == KC - 1)),
                 reads=[t_x[k], t_rw], writes=[t_pl[bi]])
        P.op("dve", lambda e, bi=bi, m=m: e.reduce_max(out=st[0:m, 0:1], in_=pl[bi][0:m, :], axis=AX.X), reads=[t_pl[bi]], writes=[t_st])
        P.op("dve", lambda e, m=m: e.tensor_scalar_mul(out=st[0:m, 1:2], in0=st[0:m, 0:1], scalar1=-1.0), reads=[t_st], writes=[t_st])
        P.op("act", lambda e, bi=bi, m=m: e.activation(out=ex[bi][0:m, :], in_=pl[bi][0:m, :], func=AF.Exp, bias=st[0:m, 1:2], scale=1.0, accum_out=st[0:m, 2:3]),
             reads=[t_pl[bi], t_st], writes=[t_ex[bi], t_st])
        P.op("dve", lambda e, m=m: e.reciprocal(out=st[0:m, 3:4], in_=st[0:m, 2:3]), reads=[t_st], writes=[t_st])
        P.op("dve", lambda e, bi=bi, m=m: e.tensor_scalar(out=ex[bi][0:m, :], in0=ex[bi][0:m, :], scalar1=st[0:m, 3:4], scalar2=None, op0=ALU.mult),
             reads=[t_st, t_ex[bi]], writes=[t_ex[bi]])
        P.dma("pool", lambda e, bi=bi, t0=t0, m=m: e.dma_start(out=a_d[t0:t0 + m, :], in_=ex[bi][0:m, :]), reads=[t_ex[bi]], writes=[t_ex[bi]], is_out=True)
    return P


def unfm(hT):
    p, kc, T = hT.shape
    return np.ascontiguousarray(hT.transpose(2, 1, 0).reshape(T, kc * p))


def stage_router(x_lat, x_ctx, mod_lat, mod_ctx, router_l):
    P = build_router()
    in_maps = []
    for c in range(NCORES):
        b = c // 4
        mv = np.stack([mod_lat[b, 3], mod_lat[b, 4], mod_ctx[3], mod_ctx[4]], axis=-1)
        mv = np.ascontiguousarray(mv.reshape(D // 128, 128, 4).transpose(1, 0, 2))
        in_maps.append({"xT": fm(tok_shard(x_lat, x_ctx, c)), "modv": mv, "rw": np.ascontiguousarray(router_l)})
    res = run_prog(P, in_maps)
    res2 = [{"h": unfm(r["hT"]), "aff": r["aff"]} for r in res]
    h_lat, h_ctx = tok_unshard(res2, "h", D)
    a_lat, a_ctx = tok_unshard(res2, "aff", NE)
    return h_lat, h_ctx, a_lat, a_ctx


NBIS = 30
H_ROWS = B * SEQ + B * CTX


def build_select():
    P = Prog()
    a_d = P.din("A", [8, SEQ])
    cc_d = P.din("cc", [8, 1])
    tvc_d = P.din("tvc", [128, 32])
    io_d = P.din("iota", [128, 512])
    id_d = P.din("ident", [128, 128])
    idx_d = P.dout("idx", [128, 8, 4], I32)
    gate_d = P.dout("gate", [128, 8, 4])
    h_d = P.din("h", [H_ROWS, D])
    xsc_d = P.dout("xsc", [4, 544, D])
    A = P.sb([8, SEQ], name="A")
    M = P.sb([8, SEQ], name="M")
    Cm = P.sb([8, SEQ], name="Cm")
    onesr = P.sb([8, SEQ], name="onesr")
    cc = P.sb([8, 1], name="cc")
    tvc = P.sb([128, 32], name="tvc")
    iota = P.sb([128, 512], name="iota")
    ident = P.sb([128, 128], name="ident")
    t_A, t_M, t_Cm, t_on, t_cc, t_tvc, t_io, t_id = [Tok() for _ in range(8)]
    P.dma("sp", lambda e: e.dma_start(out=A[:], in_=a_d), writes=[t_A])
    P.dma("sp", lambda e: e.dma_start(out=cc[:], in_=cc_d), writes=[t_cc])
    P.dma("act", lambda e: e.dma_start(out=tvc[:], in_=tvc_d), writes=[t_tvc])
    P.dma("act", lambda e: e.dma_start(out=iota[:], in_=io_d), writes=[t_io])
    P.dma("act", lambda e: e.dma_start(out=ident[:], in_=id_d), writes=[t_id])
    P.op("pool", lambda e: e.memset(onesr[:], 1.0), writes=[t_on])
    bs = P.sb([8, 4], name="bs")
    t_bs = Tok()
    P.op("dve", lambda e: e.memset(bs[:], 0.0), writes=[t_bs])
    for k in range(1, NBIS + 1):
        w = 2.0 ** (-k)
        P.op("dve", lambda e, w=w: e.tensor_scalar_add(out=bs[:, 1:2], in0=bs[:, 0:1], scalar1=w), reads=[t_bs], writes=[t_bs])
        P.op("dve", lambda e: e.tensor_scalar(out=M[:], in0=A[:], scalar1=bs[:, 1:2], scalar2=None, op0=ALU.is_ge, op1=ALU.add, accum_out=bs[:, 2:3]),
             reads=[t_A, t_bs], writes=[t_M, t_bs])
        P.op("dve", lambda e: e.tensor_tensor(out=bs[:, 3:4], in0=bs[:, 2:3], in1=cc[:], op=ALU.is_ge), reads=[t_bs, t_cc], writes=[t_bs])
        P.op("dve", lambda e, w=w: e.scalar_tensor_tensor(out=bs[:, 0:1], in0=bs[:, 3:4], scalar=w, in1=bs[:, 0:1], op0=ALU.mult, op1=ALU.add),
             reads=[t_bs], writes=[t_bs])
    P.op("dve", lambda e: e.tensor_scalar(out=M[:], in0=A[:], scalar1=bs[:, 0:1], scalar2=None, op0=ALU.is_ge), reads=[t_A, t_bs], writes=[t_M])
    P.op("dve", lambda e: e.tensor_tensor_scan(out=Cm[:], data0=onesr[:], data1=M[:], initial=0.0, op0=ALU.mult, op1=ALU.add),
         reads=[t_on, t_M], writes=[t_Cm])
    P.op("dve", lambda e: e.tensor_sub(out=Cm[:], in0=Cm[:], in1=M[:]), reads=[t_M, t_Cm], writes=[t_Cm])
    T3 = P.sb([128, 32, 24], name="T3")
    t_T3 = Tok()
    pT = [P.ps([128, 32], name="pT") for _ in range(2)]
    t_pT = [Tok(True), Tok(True)]
    for j in range(32):
        bi = j % 2
        ts_ = slice(j * 128, (j + 1) * 128)
        for i, (src, tk) in enumerate(((A, t_A), (M, t_M), (Cm, t_Cm))):
            P.op("pe", lambda e, bi=bi, i=i, src=src, ts_=ts_: e.transpose(out=pT[bi][:, i * 8:(i + 1) * 8], in_=src[0:8, ts_], identity=ident[0:8, 0:8]),
                 reads=[tk, t_id], writes=[t_pT[bi]])
        P.op("dve" if bi == 0 else "act", (lambda e, bi=bi, j=j: e.tensor_copy(out=T3[:, j, :], in_=pT[bi][:, 0:24])) if bi == 0 else
             (lambda e, bi=bi, j=j: e.copy(out=T3[:, j, :], in_=pT[bi][:, 0:24])), reads=[t_pT[bi]], writes=[t_T3])
    TV = P.sb([128, 32, 8, 2], name="TV")
    t_TV = Tok()
    for r in range(8):
        P.op("dve", lambda e, r=r: e.tensor_copy(out=TV[:, :, r, 0], in_=tvc[:]), reads=[t_tvc], writes=[t_TV])
    P.op("dve", lambda e: e.tensor_copy(out=TV[:, :, :, 1], in_=T3[:, :, 0:8]), reads=[t_T3], writes=[t_TV])
    Pm = P.sb([128, 32, 512], name="Pm")
    t_Pm = Tok()
    pi_ = P.ps([128, 64], name="pi")
    t_pi = Tok(True)
    res_i = P.sb([128, 8, 4], I32, name="res_i")
    res_f = P.sb([128, 8, 4], name="res_f")
    res_g = P.sb([128, 8, 4], name="res_g")
    t_res = Tok()
    P.op("dve", lambda e: e.memset(res_f[:], 0.0), writes=[t_res])
    P.op("dve", lambda e: e.memset(res_g[:], 0.0), writes=[t_res])
    for r in range(8):
        lat = r < 4
        C = 512 if lat else 32
        nj = 32 if lat else 2
        b = r % 2
        base = float(b * SEQ) if lat else float(B * SEQ + b * CTX)
        for j in range(nj):
            P.op("dve", lambda e, j=j, r=r, C=C: e.tensor_scalar(out=Pm[:, j, 0:C], in0=iota[:, 0:C], scalar1=T3[:, j, 16 + r:17 + r],
                                                                 scalar2=T3[:, j, 8 + r:9 + r], op0=ALU.is_equal, op1=ALU.mult),
                 reads=[t_io, t_T3], writes=[t_Pm])
        for sc in range(4 if lat else 1):
            msz = 128 if lat else 32
            for j in range(nj):
                P.op("pe", lambda e, r=r, sc=sc, j=j, msz=msz, nj=nj: e.matmul(pi_[0:msz, (r * 4 + sc) * 2:(r * 4 + sc) * 2 + 2],
                                                                            lhsT=Pm[:, j, sc * 128:sc * 128 + msz], rhs=TV[:, j, r, :],
                                                                            start=(j == 0), stop=(j == nj - 1)),
                     reads=[t_Pm, t_TV], writes=[t_pi])
            P.op("dve", lambda e, r=r, sc=sc, msz=msz, base=base: e.tensor_scalar_add(out=res_f[0:msz, r, sc:sc + 1],
                                                                                   in0=pi_[0:msz, (r * 4 + sc) * 2:(r * 4 + sc) * 2 + 1], scalar1=base),
                 reads=[t_pi], writes=[t_res])
            P.op("dve", lambda e, r=r, sc=sc, msz=msz: e.tensor_copy(out=res_g[0:msz, r, sc:sc + 1], in_=pi_[0:msz, (r * 4 + sc) * 2 + 1:(r * 4 + sc) * 2 + 2]),
                 reads=[t_pi], writes=[t_res])
    P.op("dve", lambda e: e.tensor_copy(out=res_i[:], in_=res_f[:]), reads=[t_res], writes=[t_res])
    P.dma("pool", lambda e: e.dma_start(out=idx_d, in_=res_i[:]), reads=[t_res], is_out=True)
    P.dma("pool", lambda e: e.dma_start(out=gate_d, in_=res_g[:]), reads=[t_res], is_out=True)
    xs = [P.sb([128, D], name="xs") for _ in range(4)]
    t_xs = [Tok() for _ in range(4)]
    ixs = 0
    for el in range(2):
        for b in range(B):
            pas = el * 2 + b
            for (r, sc, s0, m) in [(el * 2 + b, sc, sc * 128, 128) for sc in range(4)] + [(4 + el * 2 + b, 0, 512, 32)]:
                xb = ixs % 4
                ixs += 1
                P.dma("pool", lambda e, xb=xb, r=r, sc=sc, m=m: e.indirect_dma_start(
                    out=xs[xb][0:m, :], out_offset=None, in_=h_d[:, :], in_offset=bass.IndirectOffsetOnAxis(ap=res_i[0:m, r, sc:sc + 1], axis=0)),
                    reads=[t_res], writes=[t_xs[xb]])
                P.dma("sp" if xb % 2 == 0 else "act", lambda e, xb=xb, pas=pas, s0=s0, m=m: e.dma_start(out=xsc_d[pas, s0:s0 + m, :], in_=xs[xb][0:m, :]),
                      reads=[t_xs[xb]], writes=[t_xs[xb]], is_out=True)
    return P


TVC = (np.arange(32)[None, :] * 128 + np.arange(128)[:, None]).astype(np.float32)
IOTA512 = np.ascontiguousarray(np.broadcast_to(np.arange(512, dtype=np.float32)[None, :], (128, 512)))
CCOL = np.array([512] * 4 + [32] * 4, np.float32)[:, None]


def stage_select(a_lat, a_ctx, h_lat, h_ctx):
    P = build_select()
    h_all = np.ascontiguousarray(np.concatenate([h_lat.reshape(B * SEQ, D), h_ctx.reshape(B * CTX, D)], axis=0))
    in_maps = []
    for c in range(NCORES):
        A = np.full((8, SEQ), -1.0, np.float32)
        for el in range(2):
            for b in range(B):
                A[el * 2 + b] = a_lat[b, :, 2 * c + el]
                A[4 + el * 2 + b, :CTX] = a_ctx[b, :, 2 * c + el]
        in_maps.append({"A": A, "cc": CCOL, "tvc": TVC, "iota": IOTA512, "ident": IDENT, "h": h_all})
    res = run_prog(P, in_maps)
    return [(r["idx"], r["gate"], r["xsc"]) for r in res]


def build_expert(els=(0, 1), bs=(0, 1)):
    P = Prog()
    KC = D // 128
    h_d = P.din("xsc", [4, 544, D])
    idx_d = P.din("idx", [128, 8, 4], I32)
    gate_d = P.din("gate", [128, 8, 4])
    w1_d = P.din("w1", [len(els), D, FF])
    w3_d = P.din("w3", [len(els), D, FF])
    w2_d = P.din("w2", [len(els), FF, D])
    id_d = P.din("ident", [128, 128])
    f_d = [P.dout("f%d" % dc, [H_ROWS, 512]) for dc in range(4)]
    ident = P.sb([128, 128], name="ident")
    idx = P.sb([128, 8, 4], I32, name="idx")
    gate = P.sb([128, 8, 4], name="gate")
    t_id, t_idx, t_gate = Tok(), Tok(), Tok()
    P.dma("sp", lambda e: e.dma_start(out=ident[:], in_=id_d), writes=[t_id])
    P.dma("sp", lambda e: e.dma_start(out=idx[:], in_=idx_d), writes=[t_idx])
    P.dma("sp", lambda e: e.dma_start(out=gate[:], in_=gate_d), writes=[t_gate])
    zt = P.sb([128, 1024], name="zt")
    t_z = Tok()
    t_f = [Tok() for _ in range(4)]
    P.op("dve", lambda e: e.memset(zt[:], 0.0), writes=[t_z])
    for dc in range(4):
        for r0 in range(0, H_ROWS, 256):
            P.dma("act", lambda e, dc=dc, r0=r0: e.dma_start(out=f_d[dc][r0:r0 + 256, :].rearrange("(p n) c -> p n c", p=128), in_=zt[:].rearrange("p (n c) -> p n c", n=2)),
                  reads=[t_z], writes=[t_f[dc]], is_out=True)
    NSL = 544 * len(bs)
    NST = 3
    wst = [P.sb([128, 4096], name="wst") for _ in range(NST)]
    t_wst = [Tok() for _ in range(NST)]
    xsT = P.sb([128, KC, NSL], BF16, name="xsT")
    t_xsT = Tok()
    hT = P.sb([128, KC, NSL], BF16, name="hT")
    t_hT = Tok()
    FW = 256
    NFG = FF // FW
    wa = [P.sb([128, KC, FW], BF16, name="w1c") for _ in range(2)]
    wu = [P.sb([128, KC, FW], BF16, name="w3c") for _ in range(2)]
    t_wa = [Tok(), Tok()]
    t_wu = [Tok(), Tok()]
    w2c = [P.sb([128, KC, 512], BF16, name="w2c") for _ in range(2)]
    t_w2 = [Tok(), Tok()]
    tmp = [P.sb([128, 512], name="tmp") for _ in range(2)]
    t_tmp = [Tok(), Tok()]
    yb = [P.sb([128, 512], name="yb") for _ in range(2)]
    t_yb = [Tok(), Tok()]
    pT = [P.ps([128, 512], name="pT") for _ in range(2)]
    t_pT = [Tok(True), Tok(True)]
    pa = [P.ps([128, 512], name="pa") for _ in range(2)]
    t_pa = [Tok(True), Tok(True)]
    pu = [P.ps([128, 512], name="pu") for _ in range(2)]
    t_pu = [Tok(True), Tok(True)]
    py = [P.ps([128, 512], name="py") for _ in range(2)]
    t_py = [Tok(True), Tok(True)]
    cnt = {"st": 0, "pt": 0, "au": 0, "y": 0}

    def stage_load(src_ap, dst_ap):
        si = cnt["st"] % NST
        cnt["st"] += 1
        return si

    def load_up(eli, fg, bi):
        for (w_d_, wt_, tk_) in ((w1_d, wa[bi], t_wa[bi]), (w3_d, wu[bi], t_wu[bi])):
            si = cnt["st"] % NST
            cnt["st"] += 1
            P.dma("sp", lambda e, si=si, w_d_=w_d_, eli=eli, fg=fg: e.dma_start(
                out=wst[si][:].rearrange("p (k c) -> p k c", k=KC), in_=w_d_[eli, :, fg * FW:(fg + 1) * FW].rearrange("(k p) c -> p k c", p=128)), writes=[t_wst[si]])
            P.op("pool", lambda e, si=si, wt_=wt_: e.tensor_copy(out=wt_[:], in_=wst[si][:].rearrange("p (k c) -> p k c", k=KC)), reads=[t_wst[si]], writes=[tk_])

    def load_down(eli, dc, bi):
        for half in range(2):
            si = cnt["st"] % NST
            cnt["st"] += 1
            ks = slice(half * 8, half * 8 + 8)
            P.dma("sp", lambda e, si=si, eli=eli, dc=dc, ks=ks: e.dma_start(
                out=wst[si][:].rearrange("p (k c) -> p k c", k=8), in_=w2_d[eli, :, dc * 512:(dc + 1) * 512].rearrange("(k p) c -> p k c", p=128)[:, ks, :]), writes=[t_wst[si]])
            P.op("pool", lambda e, si=si, bi=bi, ks=ks: e.tensor_copy(out=w2c[bi][:, ks, :], in_=wst[si][:].rearrange("p (k c) -> p k c", k=8)), reads=[t_wst[si]], writes=[t_w2[bi]])

    for eli, el in enumerate(els):
        chunks = []
        groups = []
        for bi_, b in enumerate(bs):
            o = bi_ * 544
            chunks += [(el * 2 + b, sc, o + sc * 128, 128, sc * 128) for sc in range(4)] + [(4 + el * 2 + b, 0, o + 512, 32, 512)]
            groups += [(o, 512), (o + 512, 32)]
        for (r, sc, s0, m, src0) in chunks:
            si = cnt["st"] % NST
            cnt["st"] += 1
            b = r % 2
            P.dma("sp", lambda e, si=si, el=el, b=b, src0=src0, m=m: e.dma_start(out=wst[si][0:m, 0:D], in_=h_d[el * 2 + b, src0:src0 + m, :]),
                  writes=[t_wst[si]])
            for k4 in range(KC // 4):
                pb = cnt["pt"] % 2
                cnt["pt"] += 1
                for kk in range(4):
                    k = k4 * 4 + kk
                    P.op("pe", lambda e, pb=pb, kk=kk, si=si, k=k, m=m: e.transpose(out=pT[pb][:, kk * 128:kk * 128 + m], in_=wst[si][0:m, k * 128:(k + 1) * 128],
                                                                               identity=ident[0:m, 0:m]),
                         reads=[t_wst[si], t_id], writes=[t_pT[pb]])
                if pb == 0:
                    P.op("act", lambda e, pb=pb, k4=k4, s0=s0, m=m: e.copy(out=xsT[:, k4 * 4:k4 * 4 + 4, s0:s0 + m],
                                                                         in_=pT[pb][:].rearrange("p (a c) -> p a c", a=4)[:, :, 0:m]),
                         reads=[t_pT[pb]], writes=[t_xsT])
                else:
                    P.op("dve", lambda e, pb=pb, k4=k4, s0=s0, m=m: e.tensor_copy(out=xsT[:, k4 * 4:k4 * 4 + 4, s0:s0 + m],
                                                                                in_=pT[pb][:].rearrange("p (a c) -> p a c", a=4)[:, :, 0:m]),
                         reads=[t_pT[pb]], writes=[t_xsT])
        load_up(eli, 0, 0)
        for fg in range(NFG):
            wb = fg % 2
            if fg + 1 < NFG:
                load_up(eli, fg + 1, (fg + 1) % 2)
            else:
                load_down(eli, 0, 0)
            for fl in range(FW // 128):
                fc = fg * (FW // 128) + fl
                fsl = slice(fl * 128, (fl + 1) * 128)
                for (g0, gw) in groups:
                    ab = cnt["au"] % 2
                    cnt["au"] += 1
                    for k in range(KC):
                        P.op("pe", lambda e, ab=ab, wb=wb, k=k, g0=g0, gw=gw, fsl=fsl: e.matmul(pa[ab][:, 0:gw], lhsT=wa[wb][:, k, fsl], rhs=xsT[:, k, g0:g0 + gw],
                                                                                             start=(k == 0), stop=(k == KC - 1)),
                             reads=[t_wa[wb], t_xsT], writes=[t_pa[ab]])
                    for k in range(KC):
                        P.op("pe", lambda e, ab=ab, wb=wb, k=k, g0=g0, gw=gw, fsl=fsl: e.matmul(pu[ab][:, 0:gw], lhsT=wu[wb][:, k, fsl], rhs=xsT[:, k, g0:g0 + gw],
                                                                                             start=(k == 0), stop=(k == KC - 1)),
                             reads=[t_wu[wb], t_xsT], writes=[t_pu[ab]])
                    P.op("act", lambda e, ab=ab, gw=gw: e.activation(out=tmp[ab][:, 0:gw], in_=pa[ab][:, 0:gw], func=AF.Silu), reads=[t_pa[ab]], writes=[t_tmp[ab]])
                    P.op("dve", lambda e, ab=ab, fc=fc, g0=g0, gw=gw: e.tensor_mul(out=hT[:, fc, g0:g0 + gw], in0=tmp[ab][:, 0:gw], in1=pu[ab][:, 0:gw]),
                         reads=[t_tmp[ab], t_pu[ab]], writes=[t_hT])
        for dc in range(4):
            w2b = dc % 2
            if dc + 1 < 4:
                load_down(eli, dc + 1, (dc + 1) % 2)
            for (r, sc, s0, m, src0) in chunks:
                yi = cnt["y"] % 2
                cnt["y"] += 1
                for fc in range(KC):
                    P.op("pe", lambda e, yi=yi, fc=fc, s0=s0, m=m, w2b=w2b: e.matmul(py[yi][0:m, :], lhsT=hT[:, fc, s0:s0 + m], rhs=w2c[w2b][:, fc, :],
                                                                                  start=(fc == 0), stop=(fc == KC - 1)),
                         reads=[t_hT, t_w2[w2b]], writes=[t_py[yi]])
                P.op("dve", lambda e, yi=yi, r=r, sc=sc, m=m: e.tensor_scalar(out=yb[yi][0:m, :], in0=py[yi][0:m, :], scalar1=gate[0:m, r, sc:sc + 1], scalar2=None, op0=ALU.mult),
                     reads=[t_py[yi], t_gate], writes=[t_yb[yi]])
                P.dma("pool", lambda e, yi=yi, dc=dc, r=r, sc=sc, m=m: e.indirect_dma_start(
                    out=f_d[dc][:, :], out_offset=bass.IndirectOffsetOnAxis(ap=idx[0:m, r, sc:sc + 1], axis=0), in_=yb[yi][0:m, :], in_offset=None, compute_op=ALU.add),
                    reads=[t_idx, t_yb[yi]], writes=[t_f[dc], t_yb[yi]], is_out=True)
    return P


def stage_expert(sel, w1_l, w3_l, w2_l, els=(0, 1), bs=(0, 1)):
    P = build_expert(els, bs)
    in_maps = []
    for c in range(NCORES):
        in_maps.append({"xsc": sel[c][2], "idx": sel[c][0], "gate": sel[c][1], "w1": np.ascontiguousarray(w1_l[[2 * c + e for e in els]]),
                        "w3": np.ascontiguousarray(w3_l[[2 * c + e for e in els]]), "w2": np.ascontiguousarray(w2_l[[2 * c + e for e in els]]), "ident": IDENT})
    res = run_prog(P, in_maps)
    return [np.stack([r["f%d" % dc] for dc in range(4)]) for r in res]


def stage_final(fparts, x_lat, x_ctx, gate_lat, gate_ctx, gain, bias):
    P = build_outproj(False, NCORES)
    in_maps = []
    for c in range(NCORES):
        b, q = divmod(c, 4)
        ys = []
        for fp in fparts:
            lat = fp[:, b * SEQ + q * 1024:b * SEQ + (q + 1) * 1024, :]
            ctx = fp[:, B * SEQ + b * CTX + q * 64:B * SEQ + b * CTX + (q + 1) * 64, :]
            y = np.concatenate([lat, ctx], axis=1)
            ys.append(y.transpose(1, 0, 2).reshape(NT_B, D))
        cst = np.stack([bc(gate_lat[b]), bc(gate_ctx), bc(gain), bc(bias)], axis=0)
        in_maps.append({"x": tok_shard(x_lat, x_ctx, c), "cst": cst, "y": np.ascontiguousarray(np.stack(ys))})
    res = run_prog(P, in_maps)
    return tok_unshard(res, "o", D)


def kernel(x, c, ctx, c_ctx, w_ada, b_ada, w_in, w_out, gla_w_up, gla_b_up, gla_norm,
           gdn_conv, gdn_a_log, gdn_dt_bias, gdn_norm, attn_qk_norm, ln_gain, ln_bias,
           router, w1, w3, w2):
    f = lambda a: np.asarray(a, dtype=np.float32)
    x, c, ctx, c_ctx = f(x), f(c), f(ctx), f(c_ctx)
    w_ada, b_ada, w_in, w_out = f(w_ada), f(b_ada), f(w_in), f(w_out)
    gla_w_up, gla_b_up, gla_norm = f(gla_w_up), f(gla_b_up), f(gla_norm)
    gdn_conv, gdn_a_log, gdn_dt_bias, gdn_norm = f(gdn_conv), f(gdn_a_log), f(gdn_dt_bias), f(gdn_norm)
    attn_qk_norm, ln_gain, ln_bias, router = f(attn_qk_norm), f(ln_gain), f(ln_bias), f(router)
    w1, w3, w2 = f(w1), f(w3), f(w2)
    mod_lat, mod_ctx = stage_mod(c, c_ctx, w_ada, b_ada)
    x_lat, x_ctx = x, ctx
    for l in range(DEPTH):
        p_lat, p_ctx = stage_inproj(x_lat, x_ctx, mod_lat[l], mod_ctx[l], w_in[l])
        gla_l, gla_c = stage_gla(p_lat, p_ctx, gla_w_up[l], gla_b_up[l], gla_norm[l])
        gdn_l, gdn_c = stage_gdn(p_lat, p_ctx, gdn_conv[l], gdn_a_log[l], gdn_dt_bias[l], gdn_norm[l])
        att_l, att_c = stage_attn(p_lat, p_ctx, attn_qk_norm[l])
        mix_l = np.concatenate([gla_l, gdn_l, att_l], axis=-1)
        mix_c = np.concatenate([gla_c, gdn_c, att_c], axis=-1)
        x_lat, x_ctx = stage_outproj(mix_l, mix_c, x_lat, x_ctx, mod_lat[l][:, 2], mod_ctx[l][2], ln_gain[l, 0], ln_bias[l, 0], w_out[l])
        h_lat, h_ctx, a_lat, a_ctx = stage_router(x_lat, x_ctx, mod_lat[l], mod_ctx[l], router[l])
        sel = stage_select(a_lat, a_ctx, h_lat, h_ctx)
        fparts = stage_expert(sel, w1[l], w3[l], w2[l])
        x_lat, x_ctx = stage_final(fparts, x_lat, x_ctx, mod_lat[l][:, 5], mod_ctx[l][5], ln_gain[l, 1], ln_bias[l, 1])
    return np.ascontiguousarray(x_lat, dtype=np.float32)
```

```python
import os
import time
import numpy as np
import concourse.bass as bass
import concourse.mybir as mybir
from concourse.bass_utils import run_bass_kernel_spmd

F32 = mybir.dt.float32
BF16 = mybir.dt.bfloat16
U32 = mybir.dt.uint32
I32 = mybir.dt.int32
AF = mybir.ActivationFunctionType
ALU = mybir.AluOpType
AX = mybir.AxisListType

NCORES = 8
D = 2048
B = 2
SEQ = 4096
CTX = 256
DEPTH = 2
NE = 16
FF = 2048
IN_W = 5168
ALPHA = (2 * DEPTH) ** 0.25


class Tok:
    __slots__ = ("w", "r", "excl")

    def __init__(self, excl=False):
        self.w = None
        self.r = []
        self.excl = excl


class Prog:
    CE = ("act", "pe", "dve", "pool")
    DQ = ("sp", "act", "pool")
    ND = 6

    def __init__(self):
        self.nc = bass.Bass("TRN2", target_bir_lowering=False)
        nc = self.nc
        self.q = {e: [] for e in ("sp", "act", "pe", "dve", "pool")}
        self.csem = {e: nc.alloc_semaphore("c_" + e) for e in self.CE}
        self.ccnt = {e: 0 for e in self.CE}
        self.dsem = {e: [nc.alloc_semaphore("d_%s%d" % (e, i)) for i in range(self.ND)] for e in self.DQ}
        self.dcnt = {e: [0] * self.ND for e in self.DQ}
        self.drr = {e: 0 for e in self.DQ}
        self.waited = {e: {} for e in self.q}
        self.out_deps = []
        self.nm = 0

    def name(self, p):
        self.nm += 1
        return "%s_%d" % (p, self.nm)

    def sb(self, shape, dt=F32, name="sb"):
        return self.nc.alloc_sbuf_tensor(self.name(name), list(shape), dt)

    def ps(self, shape, dt=F32, name="ps"):
        return self.nc.alloc_psum_tensor(self.name(name), list(shape), dt)

    def din(self, name, shape, dt=F32):
        return self.nc.dram_tensor(name, list(shape), dt, kind="ExternalInput").ap()

    def dout(self, name, shape, dt=F32):
        return self.nc.dram_tensor(name, list(shape), dt, kind="ExternalOutput").ap()

    def dscratch(self, name, shape, dt=F32):
        return self.nc.dram_tensor(name, list(shape), dt, kind="Internal").ap()

    def _deps(self, eng, reads, writes, extra=()):
        need = {}

        def add(dep):
            if dep is None:
                return
            s, v = dep
            if need.get(s, 0) < v:
                need[s] = v

        own = self.csem.get(eng)
        for t in reads:
            add(t.w)
            if t.excl:
                for r in t.r:
                    if r[0] is not own:
                        add(r)
        for t in writes:
            add(t.w)
            for r in t.r:
                add(r)
        for d in extra:
            add(d)
        if eng == "pe":
            need.pop(self.csem["pe"], None)
        out = []
        wd = self.waited[eng]
        for s, v in need.items():
            if wd.get(s, 0) >= v:
                continue
            wd[s] = v
            out.append((s, v))
        return out

    def _mark(self, reads, writes, done):
        for t in reads:
            t.r.append(done)
        for t in writes:
            t.w = done
            t.r = []

    def op(self, eng, fn, reads=(), writes=()):
        waits = self._deps(eng, reads, writes)
        self.ccnt[eng] += 1
        done = (self.csem[eng], self.ccnt[eng])
        self.q[eng].append((waits, fn, self.csem[eng], 1))
        self._mark(reads, writes, done)
        return done

    def dma(self, eng, fn, reads=(), writes=(), is_out=False):
        k = self.drr[eng]
        self.drr[eng] = (k + 1) % self.ND
        sem = self.dsem[eng][k]
        prev = (sem, self.dcnt[eng][k]) if self.dcnt[eng][k] else None
        waits = self._deps(eng, reads, writes, extra=(prev,) if prev else ())
        self.dcnt[eng][k] += 16
        done = (sem, self.dcnt[eng][k])
        self.q[eng].append((waits, fn, sem, 16))
        self._mark(reads, writes, done)
        if is_out:
            self.out_deps.append(done)
        return done

    def finish(self):
        nc = self.nc
        fin = {}
        for s, v in self.out_deps:
            fin[s] = max(fin.get(s, 0), v)
        q = self.q
        engmap = {"sp": "sync", "act": "scalar", "pe": "tensor", "dve": "vector", "pool": "gpsimd"}
        with nc.Block() as block:
            for e, bn in engmap.items():
                def body(eng, e=e):
                    for waits, fn, sem, inc in q[e]:
                        for s, v in waits:
                            eng.wait_ge(s, v)
                        fn(eng).then_inc(sem, inc)
                    if e == "pool":
                        for s, v in fin.items():
                            eng.wait_ge(s, v)
                getattr(block, bn)(body)
        return nc


def run_prog(P, in_maps):
    t0 = time.time()
    nc = P.finish()
    t1 = time.time()
    res = run_bass_kernel_spmd(nc, in_maps, core_ids=list(range(NCORES)))
    if os.environ.get("KDBG"):
        nb = sum(v.nbytes for m in in_maps for v in m.values())
        print("[run_prog] build %.1fs run %.1fs in %.0fMB" % (t1 - t0, time.time() - t1, nb / 1e6), flush=True)
    return res.results


def fm(a):
    T, C = a.shape
    return np.ascontiguousarray(a.T.reshape(C // 128, 128, T).transpose(1, 0, 2))


NT_B = 1088


def build_inproj():
    P = Prog()
    nc = P.nc
    KC = D // 128
    xT_d = P.din("xT", [128, KC, NT_B])
    mod_d = P.din("modv", [128, KC, 4])
    w_d = P.din("w", [D, IN_W]).rearrange("(k p) c -> p k c", p=128)
    p_d = P.dout("p", [NT_B, IN_W])

    xT = P.sb([128, KC, NT_B], name="xT")
    xb = P.sb([128, KC, NT_B], BF16, name="xb")
    modv = P.sb([128, KC, 4], name="modv")
    t_x = [Tok() for _ in range(KC)]
    t_mod = Tok()
    for k in range(KC):
        P.dma("sp" if k % 2 == 0 else "act", lambda e, k=k: e.dma_start(out=xT[:, k, :], in_=xT_d[:, k, :]), writes=[t_x[k]])
    P.dma("sp", lambda e: e.dma_start(out=modv[:], in_=mod_d), writes=[t_mod])
    P.op("dve", lambda e: e.tensor_scalar_add(out=modv[:, :, 1:2], in0=modv[:, :, 1:2], scalar1=1.0), reads=[t_mod], writes=[t_mod])
    P.op("dve", lambda e: e.tensor_scalar_add(out=modv[:, :, 3:4], in0=modv[:, :, 3:4], scalar1=1.0), reads=[t_mod], writes=[t_mod])
    for k in range(KC):
        P.op("dve", lambda e, k=k: e.tensor_scalar(out=xb[:, k, 0:1024], in0=xT[:, k, 0:1024], scalar1=modv[:, k, 1:2],
                                                   scalar2=modv[:, k, 0:1], op0=ALU.mult, op1=ALU.add),
             reads=[t_mod, t_x[k]], writes=[t_x[k]])
        P.op("dve", lambda e, k=k: e.tensor_scalar(out=xb[:, k, 1024:NT_B], in0=xT[:, k, 1024:NT_B], scalar1=modv[:, k, 3:4],
                                                   scalar2=modv[:, k, 2:3], op0=ALU.mult, op1=ALU.add),
             reads=[t_mod, t_x[k]], writes=[t_x[k]])
    NW = 3
    wt = [P.sb([128, KC, 512], BF16, name="wt") for _ in range(NW)]
    t_w = [Tok() for _ in range(NW)]
    NPS = 4
    pst = [P.ps([128, 512], name="pp") for _ in range(NPS)]
    t_ps = [Tok() for _ in range(NPS)]
    ot = [P.sb([128, 512], name="ot") for _ in range(NPS)]
    t_ot = [Tok() for _ in range(NPS)]
    tiles = [(i * 128, 128) for i in range(8)] + [(1024, 64)]
    cgs = [(c, min(512, IN_W - c)) for c in range(0, IN_W, 512)]
    it = 0
    for ci, (c0, cw) in enumerate(cgs):
        wb = ci % NW
        for half in range(2):
            ks = slice(half * 8, half * 8 + 8)
            P.dma("pool",
                  lambda e, wb=wb, ks=ks, c0=c0, cw=cw: e.dma_start(out=wt[wb][:, ks, 0:cw], in_=w_d[:, ks, c0:c0 + cw]),
                  writes=[t_w[wb]])
        for (t0, m) in tiles:
            pb = it % NPS
            it += 1
            for k in range(KC):
                P.op("pe", lambda e, pb=pb, k=k, t0=t0, m=m, wb=wb, cw=cw: e.matmul(
                    pst[pb][0:m, 0:cw], lhsT=xb[:, k, t0:t0 + m], rhs=wt[wb][:, k, 0:cw], start=(k == 0), stop=(k == KC - 1)),
                    reads=[t_x[k], t_w[wb]], writes=[t_ps[pb]])
            ev = "act" if pb % 2 == 0 else "dve"
            if ev == "act":
                P.op("act", lambda e, pb=pb, m=m, cw=cw: e.copy(out=ot[pb][0:m, 0:cw], in_=pst[pb][0:m, 0:cw]),
                     reads=[t_ps[pb]], writes=[t_ot[pb]])
            else:
                P.op("dve", lambda e, pb=pb, m=m, cw=cw: e.tensor_copy(out=ot[pb][0:m, 0:cw], in_=pst[pb][0:m, 0:cw]),
                     reads=[t_ps[pb]], writes=[t_ot[pb]])
            P.dma("sp" if pb % 2 == 0 else "act", lambda e, pb=pb, t0=t0, m=m, c0=c0, cw=cw: e.dma_start(out=p_d[t0:t0 + m, c0:c0 + cw], in_=ot[pb][0:m, 0:cw]),
                  reads=[t_ot[pb]], is_out=True)
    return P


def stage_inproj(x_lat, x_ctx, mod_lat, mod_ctx, w_in_l):
    P = build_inproj()
    in_maps = []
    for c in range(NCORES):
        b, q = divmod(c, 4)
        xs = np.concatenate([x_lat[b, q * 1024:(q + 1) * 1024], x_ctx[b, q * 64:(q + 1) * 64]], axis=0)
        mv = np.stack([mod_lat[b, 0], mod_lat[b, 1], mod_ctx[0], mod_ctx[1]], axis=-1)
        mv = np.ascontiguousarray(mv.reshape(D // 128, 128, 4).transpose(1, 0, 2))
        in_maps.append({"xT": fm(xs), "modv": mv, "w": np.ascontiguousarray(w_in_l)})
    res = run_prog(P, in_maps)
    p_lat = np.empty((B, SEQ, IN_W), np.float32)
    p_ctx = np.empty((B, CTX, IN_W), np.float32)
    for c in range(NCORES):
        b, q = divmod(c, 4)
        p_lat[b, q * 1024:(q + 1) * 1024] = res[c]["p"][:1024]
        p_ctx[b, q * 64:(q + 1) * 64] = res[c]["p"][1024:]
    return p_lat, p_ctx


MODW = 6 * D // NCORES


def build_mod():
    P = Prog()
    KC = D // 128
    cv_d = P.din("cv", [128, KC, 3])
    wa_d = P.din("wa", [DEPTH, D, MODW]).rearrange("l (k p) c -> l p k c", p=128)
    ba_d = P.din("ba", [DEPTH, 1, MODW])
    mod_d = P.dout("mod", [DEPTH, 3, MODW])
    cv = P.sb([128, KC, 3], name="cv")
    ones = P.sb([1, 4], name="ones")
    ba = P.sb([1, DEPTH, MODW], name="ba")
    t_cv, t_ones, t_ba = Tok(), Tok(), Tok()
    P.dma("sp", lambda e: e.dma_start(out=cv[:], in_=cv_d), writes=[t_cv])
    for l in range(DEPTH):
        P.dma("sp", lambda e, l=l: e.dma_start(out=ba[:, l, :], in_=ba_d[l]), writes=[t_ba])
    P.op("dve", lambda e: e.memset(ones[:], 1.0), writes=[t_ones])
    P.op("act", lambda e: e.activation(out=cv[:], in_=cv[:], func=AF.Silu), reads=[t_cv], writes=[t_cv])
    wt = [P.sb([128, KC, 512], name="wa") for _ in range(2)]
    t_w = [Tok(), Tok()]
    pst = [P.ps([128, 512], name="pm") for _ in range(2)]
    t_ps = [Tok(), Tok()]
    ot = [P.sb([4, 512], name="om") for _ in range(2)]
    t_ot = [Tok(), Tok()]
    it = 0
    for l in range(DEPTH):
        for c0 in range(0, MODW, 512):
            bi = it % 2
            it += 1
            for half in range(2):
                ks = slice(half * 8, half * 8 + 8)
                P.dma("sp" if half == 0 else "act",
                      lambda e, bi=bi, ks=ks, c0=c0, l=l: e.dma_start(out=wt[bi][:, ks, :], in_=wa_d[l, :, ks, c0:c0 + 512]),
                      writes=[t_w[bi]])
            for k in range(KC):
                P.op("pe", lambda e, bi=bi, k=k: e.matmul(pst[bi][0:3, :], lhsT=cv[:, k, :], rhs=wt[bi][:, k, :], start=(k == 0), stop=False),
                     reads=[t_cv, t_w[bi]], writes=[t_ps[bi]])
            P.op("pe", lambda e, bi=bi, l=l, c0=c0: e.matmul(pst[bi][0:3, :], lhsT=ones[0:1, 0:3], rhs=ba[0:1, l, c0:c0 + 512], start=False, stop=True),
                 reads=[t_ones, t_ba], writes=[t_ps[bi]])
            P.op("dve", lambda e, bi=bi: e.tensor_copy(out=ot[bi][0:3, :], in_=pst[bi][0:3, :]), reads=[t_ps[bi]], writes=[t_ot[bi]])
            P.dma("pool", lambda e, bi=bi, l=l, c0=c0: e.dma_start(out=mod_d[l, :, c0:c0 + 512], in_=ot[bi][0:3, :]), reads=[t_ot[bi]], is_out=True)
    return P


def stage_mod(c, c_ctx, w_ada, b_ada):
    P = build_mod()
    vec = np.concatenate([c, c_ctx[None]], axis=0)
    cv = np.ascontiguousarray(vec.T.reshape(D // 128, 128, 3).transpose(1, 0, 2))
    in_maps = []
    for ci in range(NCORES):
        cs = slice(ci * MODW, (ci + 1) * MODW)
        in_maps.append({"cv": cv, "wa": np.ascontiguousarray(w_ada[:, :, cs]), "ba": np.ascontiguousarray(b_ada[:, None, cs])})
    res = run_prog(P, in_maps)
    mod = np.concatenate([res[ci]["mod"] for ci in range(NCORES)], axis=-1)
    mod = mod.reshape(DEPTH, 3, 6, D)
    return np.ascontiguousarray(mod[:, 0:2]), np.ascontiguousarray(mod[:, 2])


def ln_tile(P, z, t_z, m, gain, bias, t_c, st, t_st, outt, t_out):
    s1, mu, ss, rstd = st[:, 0:1], st[:, 1:2], st[:, 2:3], st[:, 3:4]
    P.op("dve", lambda e: e.reduce_sum(out=s1[0:m], in_=z[0:m, :], axis=AX.X), reads=[t_z], writes=[t_st])
    P.op("dve", lambda e: e.tensor_scalar_mul(out=mu[0:m], in0=s1[0:m], scalar1=1.0 / D), reads=[t_st], writes=[t_st])
    P.op("dve", lambda e: e.tensor_scalar(out=z[0:m, :], in0=z[0:m, :], scalar1=mu[0:m], scalar2=None, op0=ALU.subtract),
         reads=[t_st, t_z], writes=[t_z])
    P.op("act", lambda e: e.activation(out=outt[0:m, :], in_=z[0:m, :], func=AF.Square, accum_out=ss[0:m]),
         reads=[t_z], writes=[t_out, t_st])
    P.op("dve", lambda e: e.tensor_scalar(out=ss[0:m], in0=ss[0:m], scalar1=1.0 / D, scalar2=1e-5, op0=ALU.mult, op1=ALU.add),
         reads=[t_st], writes=[t_st])
    P.op("act", lambda e: e.activation(out=ss[0:m], in_=ss[0:m], func=AF.Sqrt), reads=[t_st], writes=[t_st])
    P.op("dve", lambda e: e.reciprocal(out=rstd[0:m], in_=ss[0:m]), reads=[t_st], writes=[t_st])
    P.op("dve", lambda e: e.scalar_tensor_tensor(out=outt[0:m, :], in0=z[0:m, :], scalar=rstd[0:m], in1=gain[0:m, :],
                                                 op0=ALU.mult, op1=ALU.mult), reads=[t_st, t_z, t_c], writes=[t_out])
    P.op("dve", lambda e: e.tensor_add(out=outt[0:m, :], in0=outt[0:m, :], in1=bias[0:m, :]), reads=[t_c, t_out], writes=[t_out])


def build_outproj(with_proj=True, nparts=1):
    P = Prog()
    KC = D // 128
    NT = NT_B
    x_d = P.din("x", [NT, D])
    cst_d = P.din("cst", [4, 128, D])
    if with_proj:
        mT_d = P.din("mT", [128, KC, NT])
        w_d = P.din("w", [D, D]).rearrange("(k p) c -> p k c", p=128)
    else:
        y_d = P.din("y", [nparts, NT, D])
    o_d = P.dout("o", [NT, D])
    cst = P.sb([128, 4, D], name="cst")
    t_c = Tok()
    for i in range(4):
        P.dma("sp", lambda e, i=i: e.dma_start(out=cst[:, i, :], in_=cst_d[i]), writes=[t_c])
    if with_proj:
        w = P.sb([128, KC, D], BF16, name="w")
        t_w = Tok()
        for k in range(KC):
            P.dma("pool", lambda e, k=k: e.dma_start(out=w[:, k, :], in_=w_d[:, k, :]), writes=[t_w])
        mT = [P.sb([128, KC, 128], BF16, name="mT") for _ in range(2)]
        t_m = [Tok(), Tok()]
        pst = [P.ps([128, 512], name="po") for _ in range(4)]
        t_ps = [Tok() for _ in range(4)]
    else:
        NY = 6
        yt = [P.sb([128, D], name="yt") for _ in range(NY)]
        t_y = [Tok() for _ in range(NY)]
        iy_ = [0]
    xt = [P.sb([128, D], name="xt") for _ in range(2)]
    t_x = [Tok(), Tok()]
    zt = P.sb([128, D], name="zt")
    t_z = Tok()
    st = P.sb([128, 4], name="st")
    t_st = Tok()
    tiles = [(i * 128, 128) for i in range(8)] + [(1024, 64)]
    for ti, (t0, m) in enumerate(tiles):
        bi = ti % 2
        gi = 0 if ti < 8 else 1
        P.dma("sp", lambda e, bi=bi, t0=t0, m=m: e.dma_start(out=xt[bi][0:m, :], in_=x_d[t0:t0 + m, :]), writes=[t_x[bi]])
        if with_proj:
            P.dma("pool", lambda e, bi=bi, t0=t0, m=m: e.dma_start(out=mT[bi][:, :, 0:m], in_=mT_d[:, :, t0:t0 + m]), writes=[t_m[bi]])
            for cg in range(4):
                for k in range(KC):
                    P.op("pe", lambda e, bi=bi, cg=cg, k=k, m=m: e.matmul(pst[cg][0:m, :], lhsT=mT[bi][:, k, 0:m], rhs=w[:, k, cg * 512:(cg + 1) * 512],
                                                                      start=(k == 0), stop=(k == KC - 1)),
                         reads=[t_m[bi], t_w], writes=[t_ps[cg]])
                P.op("dve", lambda e, cg=cg, m=m, gi=gi: e.tensor_mul(out=zt[0:m, cg * 512:(cg + 1) * 512], in0=pst[cg][0:m, :],
                                                                     in1=cst[0:m, gi, cg * 512:(cg + 1) * 512]),
                     reads=[t_ps[cg], t_c], writes=[t_z])
        else:
            for pi in range(nparts):
                yb_ = iy_[0] % NY
                iy_[0] += 1
                P.dma("act" if yb_ % 2 == 0 else "sp", lambda e, yb_=yb_, t0=t0, m=m, pi=pi: e.dma_start(out=yt[yb_][0:m, :], in_=y_d[pi, t0:t0 + m, :]), writes=[t_y[yb_]])
                if pi == 0:
                    P.op("dve", lambda e, yb_=yb_, m=m: e.tensor_copy(out=zt[0:m, :], in_=yt[yb_][0:m, :]), reads=[t_y[yb_]], writes=[t_z])
                else:
                    P.op("dve", lambda e, yb_=yb_, m=m: e.tensor_add(out=zt[0:m, :], in0=zt[0:m, :], in1=yt[yb_][0:m, :]), reads=[t_y[yb_], t_z], writes=[t_z])
            P.op("dve", lambda e, m=m, gi=gi: e.tensor_mul(out=zt[0:m, :], in0=zt[0:m, :], in1=cst[0:m, gi, :]), reads=[t_c, t_z], writes=[t_z])
        P.op("dve", lambda e, bi=bi, m=m: e.scalar_tensor_tensor(out=zt[0:m, :], in0=xt[bi][0:m, :], scalar=float(ALPHA), in1=zt[0:m, :],
                                                                 op0=ALU.mult, op1=ALU.add), reads=[t_x[bi], t_z], writes=[t_z])
        ln_tile(P, zt, t_z, m, cst[:, 2, :], cst[:, 3, :], t_c, st, t_st, xt[bi], t_x[bi])
        P.dma("act", lambda e, bi=bi, t0=t0, m=m: e.dma_start(out=o_d[t0:t0 + m, :], in_=xt[bi][0:m, :]), reads=[t_x[bi]], writes=[t_x[bi]], is_out=True)
    return P


def tok_shard(lat, ctx, c):
    b, q = divmod(c, 4)
    return np.concatenate([lat[b, q * 1024:(q + 1) * 1024], ctx[b, q * 64:(q + 1) * 64]], axis=0)


def tok_unshard(res, key, width):
    lat = np.empty((B, SEQ, width), np.float32)
    ctx = np.empty((B, CTX, width), np.float32)
    for c in range(NCORES):
        b, q = divmod(c, 4)
        lat[b, q * 1024:(q + 1) * 1024] = res[c][key][:1024]
        ctx[b, q * 64:(q + 1) * 64] = res[c][key][1024:]
    return lat, ctx


def bc(v):
    return np.ascontiguousarray(np.broadcast_to(v[None, :], (128, v.shape[0])))


def stage_outproj(mix_lat, mix_ctx, x_lat, x_ctx, gate_lat, gate_ctx, gain, bias, w_out_l):
    P = build_outproj(True)
    in_maps = []
    for c in range(NCORES):
        b = c // 4
        cst = np.stack([bc(gate_lat[b]), bc(gate_ctx), bc(gain), bc(bias)], axis=0)
        in_maps.append({"x": tok_shard(x_lat, x_ctx, c), "cst": cst, "mT": fm(tok_shard(mix_lat, mix_ctx, c)),
                        "w": np.ascontiguousarray(w_out_l)})
    res = run_prog(P, in_maps)
    return tok_unshard(res, "o", D)


NTOK = CTX + SEQ
NTT = NTOK // 128
IDENT = np.eye(128, dtype=np.float32)


def rms_rstd(P, x_ap, m, width, eps, junk, t_junk, st, t_st, reads):
    P.op("act", lambda e: e.activation(out=junk[0:m, 0:width], in_=x_ap, func=AF.Square, accum_out=st[0:m, 0:1]),
         reads=reads, writes=[t_junk, t_st])
    P.op("dve", lambda e: e.tensor_scalar(out=st[0:m, 0:1], in0=st[0:m, 0:1], scalar1=1.0 / width, scalar2=eps, op0=ALU.mult, op1=ALU.add),
         reads=[t_st], writes=[t_st])
    P.op("act", lambda e: e.activation(out=st[0:m, 0:1], in_=st[0:m, 0:1], func=AF.Sqrt), reads=[t_st], writes=[t_st])
    P.op("dve", lambda e: e.reciprocal(out=st[0:m, 1:2], in_=st[0:m, 0:1]), reads=[t_st], writes=[t_st])


def build_attn():
    P = Prog()
    q_d = P.din("q", [NTOK, 256])
    k_d = P.din("k", [NTOK, 128])
    v_d = P.din("v", [NTOK, 128])
    g_d = P.din("g", [2, 128, 128])
    cs_d = P.din("cs", [2, SEQ, 128])
    id_d = P.din("ident", [128, 128])
    o_d = P.dout("o", [NTOK, 256])

    ident = P.sb([128, 128], name="ident")
    gq = P.sb([128, 2, 128], name="gq")
    t_id, t_g = Tok(), Tok()
    P.dma("sp", lambda e: e.dma_start(out=ident[:], in_=id_d), writes=[t_id])
    for i in range(2):
        P.dma("sp", lambda e, i=i: e.dma_start(out=gq[:, i, :], in_=g_d[i]), writes=[t_g])
    qT = P.sb([128, 2, NTOK], BF16, name="qT")
    kT = P.sb([128, NTOK], BF16, name="kT")
    va = P.sb([128, NTT, 129], BF16, name="va")
    t_qT, t_kT, t_va = Tok(), Tok(), Tok()
    P.op("pool", lambda e: e.memset(va[:, :, 128:129], 1.0), writes=[t_va])
    for half in range(2):
        hs = slice(half * 17, half * 17 + 17)
        P.dma("pool", lambda e, hs=hs, half=half: e.dma_start(out=va[:, hs, 0:128],
              in_=v_d[half * 17 * 128:(half + 1) * 17 * 128, :].rearrange("(n p) d -> p n d", p=128)), writes=[t_va])
    psT = [P.ps([128, 512], name="psT") for _ in range(2)]
    t_psT = [Tok(True), Tok(True)]

    def prep_lane(li):
        xin = P.sb([128, 3, 128], name="xin")
        cst = P.sb([128, 2, 128], name="cs")
        sq = P.sb([128, 3, 128], name="sq")
        xn = P.sb([128, 3, 128], name="xn")
        t1 = P.sb([128, 3, 128], name="t1")
        t2 = P.sb([128, 3, 128], name="t2")
        st = P.sb([128, 2, 3], name="st")
        t_xin, t_cs, t_sq, t_xn, t_t1, t_t2, t_st = [Tok() for _ in range(7)]
        for tt in range(li, NTT, 4):
            t0 = tt * 128
            lat = tt >= 2
            P.dma("sp", lambda e, t0=t0: e.dma_start(out=xin[:, 0:2, :], in_=q_d[t0:t0 + 128, :].rearrange("p (h d) -> p h d", h=2)), writes=[t_xin])
            P.dma("sp", lambda e, t0=t0: e.dma_start(out=xin[:, 2, :], in_=k_d[t0:t0 + 128, :]), writes=[t_xin])
            if lat:
                for i in range(2):
                    P.dma("act", lambda e, t0=t0, i=i: e.dma_start(out=cst[:, i, :], in_=cs_d[i, t0 - CTX:t0 - CTX + 128, :]), writes=[t_cs])
            yield
            P.op("act", lambda e: e.activation(out=sq[:], in_=xin[:], func=AF.Square), reads=[t_xin], writes=[t_sq])
            yield
            P.op("dve", lambda e: e.reduce_sum(out=st[:, 0, :], in_=sq[:], axis=AX.X), reads=[t_sq], writes=[t_st])
            yield
            P.op("dve", lambda e: e.tensor_scalar(out=st[:, 0, :], in0=st[:, 0, :], scalar1=1.0 / 128, scalar2=1e-6, op0=ALU.mult, op1=ALU.add), reads=[t_st], writes=[t_st])
            yield
            P.op("act", lambda e: e.activation(out=st[:, 0, :], in_=st[:, 0, :], func=AF.Sqrt), reads=[t_st], writes=[t_st])
            yield
            P.op("dve", lambda e: e.reciprocal(out=st[:, 1, :], in_=st[:, 0, :]), reads=[t_st], writes=[t_st])
            yield
            for h in range(3):
                gi = 0 if h < 2 else 1
                P.op("dve", lambda e, h=h, gi=gi: e.scalar_tensor_tensor(out=xn[:, h, :], in0=xin[:, h, :], scalar=st[:, 1, h:h + 1], in1=gq[:, gi, :], op0=ALU.mult, op1=ALU.mult),
                     reads=[t_xin, t_st, t_g], writes=[t_xn])
            yield
            src, t_src = xn, t_xn
            if lat:
                for h in range(3):
                    P.op("pool", lambda e, h=h: e.tensor_mul(out=t1[:, h, :], in0=xn[:, h, :], in1=cst[:, 0, :]), reads=[t_xn, t_cs], writes=[t_t1])
                x5 = xn[:].rearrange("p h (a b f) -> p h a b f", a=2, b=2)
                o5 = t2[:].rearrange("p h (a b f) -> p h a b f", a=2, b=2)
                s4 = cst[:, 1, :].rearrange("p (a b f) -> p a b f", a=2, b=2)
                for h in range(3):
                    for hb in range(2):
                        P.op("dve", lambda e, h=h, hb=hb: e.tensor_mul(out=o5[:, h, :, hb, :], in0=x5[:, h, :, 1 - hb, :], in1=s4[:, :, hb, :]),
                             reads=[t_xn, t_cs], writes=[t_t2])
                yield
                P.op("dve", lambda e: e.tensor_add(out=t1[:], in0=t1[:], in1=t2[:]), reads=[t_t2, t_t1], writes=[t_t1])
                yield
                src, t_src = t1, t_t1
            for h in range(3):
                P.op("pe", lambda e, h=h, src=src: e.transpose(out=psT[li % 2][:, h * 128:(h + 1) * 128], in_=src[:, h, :], identity=ident[:]),
                     reads=[t_id, t_src], writes=[t_psT[li % 2]])
            P.op("act", lambda e, t0=t0: e.copy(out=qT[:, :, t0:t0 + 128], in_=psT[li % 2][:, 0:256].rearrange("p (h t) -> p h t", h=2)), reads=[t_psT[li % 2]], writes=[t_qT])
            P.op("act", lambda e, t0=t0: e.copy(out=kT[:, t0:t0 + 128], in_=psT[li % 2][:, 256:384]), reads=[t_psT[li % 2]], writes=[t_kT])
            yield

    gens = [prep_lane(i) for i in range(4)]
    while gens:
        for g in list(gens):
            try:
                next(g)
            except StopIteration:
                gens.remove(g)
    NS = 2
    pss = [P.ps([128, 512], name="pss") for _ in range(NS)]
    t_pss = [Tok() for _ in range(NS)]
    NPT = 3
    pt = [P.sb([128, 512], BF16, name="pt") for _ in range(NPT)]
    t_pt = [Tok() for _ in range(NPT)]
    acc = [P.ps([128, 512], name="acc") for _ in range(4)]
    t_acc = [Tok() for _ in range(4)]
    ob = [P.sb([128, 128], name="ob") for _ in range(2)]
    t_ob = [Tok(), Tok()]
    rs = P.sb([128, 1], name="rs")
    t_rs = Tok()
    blocks = [(0, 256, 0, 2)] + [(CTX + i * 512, 512, 0, NTT) for i in range(SEQ // 512)]
    scale = 128 ** -0.5
    iters = [(h, q0, qw, kt0, kt1, kt) for h in range(2) for (q0, qw, kt0, kt1) in blocks for kt in range(kt0, kt1)]
    iob = 0

    def emit_qk(i):
        h, q0, qw, kt0, kt1, kt = iters[i]
        sb_ = i % NS
        P.op("pe", lambda e: e.matmul(pss[sb_][:, 0:qw], lhsT=kT[:, kt * 128:(kt + 1) * 128], rhs=qT[:, h, q0:q0 + qw], start=True, stop=True),
             reads=[t_kT, t_qT], writes=[t_pss[sb_]])

    emit_qk(0)
    for i, (h, q0, qw, kt0, kt1, kt) in enumerate(iters):
        nq = qw // 128
        sb_ = i % NS
        pb = i % NPT
        if i + 1 < len(iters):
            emit_qk(i + 1)
        P.op("act", lambda e, sb_=sb_, pb=pb, qw=qw: e.activation(out=pt[pb][:, 0:qw], in_=pss[sb_][:, 0:qw], func=AF.Exp, scale=scale),
             reads=[t_pss[sb_]], writes=[t_pt[pb]])
        for qi in range(nq):
            P.op("pe", lambda e, pb=pb, qi=qi, kt=kt, kt0=kt0, kt1=kt1: e.matmul(acc[qi][:, 0:129], lhsT=pt[pb][:, qi * 128:(qi + 1) * 128],
                                                                              rhs=va[:, kt, :], start=(kt == kt0), stop=(kt == kt1 - 1)),
                 reads=[t_pt[pb], t_va], writes=[t_acc[qi]])
        if kt == kt1 - 1:
            for qi in range(nq):
                P.op("dve", lambda e, qi=qi: e.reciprocal(out=rs[:], in_=acc[qi][:, 128:129]), reads=[t_acc[qi]], writes=[t_rs])
                oi = iob % 2
                iob += 1
                P.op("dve", lambda e, qi=qi, oi=oi: e.tensor_scalar(out=ob[oi][:], in0=acc[qi][:, 0:128], scalar1=rs[:], scalar2=None, op0=ALU.mult),
                     reads=[t_acc[qi], t_rs], writes=[t_ob[oi]])
                P.dma("pool", lambda e, oi=oi, q0=q0, qi=qi, h=h: e.dma_start(out=o_d[q0 + qi * 128:q0 + (qi + 1) * 128, h * 128:(h + 1) * 128], in_=ob[oi][:]),
                      reads=[t_ob[oi]], writes=[t_ob[oi]], is_out=True)
    return P


def rope_tables():
    rows = SEQ // 64
    row = np.repeat(np.arange(rows), 64).astype(np.float32)
    col = np.tile(np.arange(64), rows).astype(np.float32)
    inv = (10000.0 ** (-np.arange(0, 64, 2, dtype=np.float32) / 64)).astype(np.float32)
    ang = np.concatenate([row[:, None] * inv, col[:, None] * inv], axis=-1)
    cos, sin = np.cos(ang).astype(np.float32), np.sin(ang).astype(np.float32)
    C = np.empty((SEQ, 128), np.float32)
    S = np.empty((SEQ, 128), np.float32)
    for a in range(2):
        c_, s_ = cos[:, a * 32:(a + 1) * 32], sin[:, a * 32:(a + 1) * 32]
        C[:, a * 64:a * 64 + 32] = c_
        C[:, a * 64 + 32:a * 64 + 64] = c_
        S[:, a * 64:a * 64 + 32] = -s_
        S[:, a * 64 + 32:a * 64 + 64] = s_
    return np.stack([C, S], axis=0)


def stage_attn(p_lat, p_ctx, qk_gain_l):
    P = build_attn()
    cs = rope_tables()
    g = np.stack([bc(qk_gain_l[0]), bc(qk_gain_l[1])], axis=0)
    in_maps = []
    for c in range(NCORES):
        b, j = divmod(c, 4)
        pa = np.concatenate([p_ctx[b], p_lat[b]], axis=0)
        kv = j // 2
        in_maps.append({"q": np.ascontiguousarray(pa[:, 3632 + 256 * j:3632 + 256 * (j + 1)]),
                        "k": np.ascontiguousarray(pa[:, 4656 + 128 * kv:4656 + 128 * (kv + 1)]),
                        "v": np.ascontiguousarray(pa[:, 4912 + 128 * kv:4912 + 128 * (kv + 1)]),
                        "g": g, "cs": cs, "ident": IDENT})
    res = run_prog(P, in_maps)
    att_lat = np.empty((B, SEQ, 1024), np.float32)
    att_ctx = np.empty((B, CTX, 1024), np.float32)
    for c in range(NCORES):
        b, j = divmod(c, 4)
        att_ctx[b, :, 256 * j:256 * (j + 1)] = res[c]["o"][:CTX]
        att_lat[b, :, 256 * j:256 * (j + 1)] = res[c]["o"][CTX:]
    return att_lat, att_ctx


TRI_INC = np.triu(np.ones((128, 128), np.float32))
TRI_SUFEX = np.tril(np.ones((128, 128), np.float32), -1)
ANTI = np.ascontiguousarray(np.eye(128, dtype=np.float32)[::-1])


def orig_tile(tt):
    return (1 - tt) if tt < 2 else (35 - tt)


def build_gla():
    P = Prog()
    qT_d = P.din("qT", [64, NTOK])
    kT_d = P.din("kT", [64, NTOK])
    k_d = P.din("k", [NTOK, 64])
    v_d = P.din("v", [NTOK, 128])
    rT_d = P.din("rT", [2, 16, NTOK])
    w_d = P.din("w", [2, 17, 64])
    g_d = P.din("g", [NTOK, 128])
    gn_d = P.din("gn", [128, 128])
    c_d = P.din("cm", [4, 128, 128])
    o_d = P.dout("o", [NTOK, 128])

    cm = P.sb([128, 4, 128], name="cm")
    gn = P.sb([128, 128], name="gn")
    t_cm, t_gn = Tok(), Tok()
    for i in range(4):
        P.dma("sp", lambda e, i=i: e.dma_start(out=cm[:, i, :], in_=c_d[i]), writes=[t_cm])
    P.dma("sp", lambda e: e.dma_start(out=gn[:], in_=gn_d), writes=[t_gn])
    g = P.sb([128, NTT, 128], name="g")
    t_g = Tok()
    P.dma("act", lambda e: e.dma_start(out=g[:], in_=g_d.rearrange("(n p) d -> p n d", p=128)), writes=[t_g])
    P.op("act", lambda e: e.activation(out=g[:], in_=g[:], func=AF.Silu), reads=[t_g], writes=[t_g])
    qT = P.sb([64, NTOK], name="qT")
    kT = P.sb([64, NTOK], name="kT")
    kk = P.sb([128, NTT, 64], name="kk")
    vv = P.sb([128, NTT, 128], name="vv")
    t_in = Tok()
    P.dma("sp", lambda e: e.dma_start(out=qT[:], in_=qT_d), writes=[t_in])
    P.dma("act", lambda e: e.dma_start(out=kT[:], in_=kT_d), writes=[t_in])
    P.dma("act", lambda e: e.dma_start(out=kk[:], in_=k_d.rearrange("(n p) d -> p n d", p=128)), writes=[t_in])
    P.dma("sp", lambda e: e.dma_start(out=vv[:], in_=v_d.rearrange("(n p) d -> p n d", p=128)), writes=[t_in])
    odir = [P.sb([128, NTT, 128], name="odir") for _ in range(2)]
    t_odir = [Tok(), Tok()]

    def lane(z):
        C, sufx = (cm[:, 0, :], cm[:, 3, :]) if z == 0 else (cm[:, 1, :], cm[:, 2, :])
        mid, last = (63, 127) if z == 0 else (64, 0)
        rT = P.sb([17, NTOK], name="rT")
        wa = P.sb([17, 64], name="wa")
        t_r = Tok()
        P.op("dve", lambda e: e.memset(rT[:], 1.0), writes=[t_r])
        P.dma("sp", lambda e: e.dma_start(out=rT[0:16, :], in_=rT_d[z]), writes=[t_r])
        P.dma("act", lambda e: e.dma_start(out=wa[:], in_=w_d[z]), writes=[t_r])
        S = P.sb([64, 128], name="S")
        t_S = Tok()
        P.op("dve", lambda e: e.memset(S[:], 0.0), writes=[t_S])
        la = P.sb([128, 64], name="la")
        cs = P.sb([64, 128], name="cs")
        sc = P.sb([64, 4], name="sc")
        e1 = P.sb([64, 128], name="e1")
        e2 = P.sb([64, 128], name="e2")
        e3 = P.sb([64, 128], name="e3")
        k4 = P.sb([128, 64], name="k4")
        AT = P.sb([128, 128], name="AT")
        t_la, t_cs, t_sc, t_e1, t_e2, t_e3, t_k4, t_AT = [Tok() for _ in range(8)]
        b1 = P.ps([128, 512], name="b1")
        b2 = P.ps([128, 512], name="b2")
        b3 = P.ps([128, 512], name="b3")
        t1, t2, t3 = Tok(True), Tok(True), Tok(True)
        ps_la, ps_sf, ps_cT = b1[:, 0:64], b1[:, 64:128], b1[0:64, 128:256]
        ps_A = b2[:, 0:128]
        ps_o, ps_S = b3[:, 0:128], b3[0:64, 128:256]
        yield
        order = list(range(NTT)) if z == 0 else [1, 0] + list(range(NTT - 1, 1, -1))
        for tt in order:
            ts_ = slice(tt * 128, (tt + 1) * 128)
            P.op("pe", lambda e, ts_=ts_: e.matmul(ps_la, lhsT=rT[0:17, ts_], rhs=wa[0:17, :], start=True, stop=True), reads=[t_r], writes=[t1])
            yield
            P.op("act", lambda e: e.activation(out=la[:], in_=ps_la, func=AF.Exp, scale=-1.0), reads=[t1], writes=[t_la])
            yield
            P.op("dve", lambda e: e.tensor_scalar_add(out=la[:], in0=la[:], scalar1=1.0), reads=[t_la], writes=[t_la])
            yield
            P.op("act", lambda e: e.activation(out=la[:], in_=la[:], func=AF.Ln), reads=[t_la], writes=[t_la])
            yield
            P.op("dve", lambda e: e.tensor_scalar_mul(out=la[:], in0=la[:], scalar1=-1.0 / 16.0), reads=[t_la], writes=[t_la])
            yield
            P.op("pe", lambda e: e.matmul(ps_cT, lhsT=la[:], rhs=C, start=True, stop=True), reads=[t_la, t_cm], writes=[t1])
            P.op("pe", lambda e: e.matmul(ps_sf, lhsT=sufx, rhs=la[:], start=True, stop=True), reads=[t_la, t_cm], writes=[t1])
            yield
            P.op("act", lambda e: e.copy(out=cs[:], in_=ps_cT), reads=[t1], writes=[t_cs])
            P.op("act", lambda e: e.activation(out=k4[:], in_=ps_sf, func=AF.Exp), reads=[t1], writes=[t_k4])
            yield
            P.op("dve", lambda e: e.tensor_scalar_mul(out=sc[:, 0:1], in0=cs[:, mid:mid + 1], scalar1=-1.0), reads=[t_cs], writes=[t_sc])
            P.op("dve", lambda e, tt=tt: e.tensor_mul(out=k4[:], in0=kk[:, tt, :], in1=k4[:]), reads=[t_in, t_k4], writes=[t_k4])
            yield
            P.op("act", lambda e: e.activation(out=e1[:], in_=cs[:], func=AF.Exp, bias=sc[:, 0:1], scale=1.0), reads=[t_cs, t_sc], writes=[t_e1])
            P.op("act", lambda e: e.activation(out=e2[:], in_=cs[:], func=AF.Exp, bias=cs[:, mid:mid + 1], scale=-1.0), reads=[t_cs], writes=[t_e2])
            P.op("act", lambda e: e.activation(out=e3[:], in_=cs[:], func=AF.Exp), reads=[t_cs], writes=[t_e3])
            P.op("act", lambda e: e.activation(out=sc[:, 1:2], in_=cs[:, last:last + 1], func=AF.Exp), reads=[t_cs], writes=[t_sc])
            yield
            P.op("dve", lambda e, ts_=ts_: e.scalar_tensor_tensor(out=e1[:], in0=qT[:, ts_], scalar=0.125, in1=e1[:], op0=ALU.mult, op1=ALU.mult), reads=[t_in, t_e1], writes=[t_e1])
            P.op("dve", lambda e, ts_=ts_: e.tensor_mul(out=e2[:], in0=kT[:, ts_], in1=e2[:]), reads=[t_in, t_e2], writes=[t_e2])
            P.op("dve", lambda e, ts_=ts_: e.scalar_tensor_tensor(out=e3[:], in0=qT[:, ts_], scalar=0.125, in1=e3[:], op0=ALU.mult, op1=ALU.mult), reads=[t_in, t_e3], writes=[t_e3])
            yield
            P.op("pe", lambda e: e.matmul(ps_A, lhsT=e2[:], rhs=e1[:], start=True, stop=True), reads=[t_e1, t_e2], writes=[t2])
            yield
            P.op("dve", lambda e: e.tensor_mul(out=AT[:], in0=ps_A, in1=C), reads=[t2, t_cm], writes=[t_AT])
            yield
            P.op("pe", lambda e, tt=tt: e.matmul(ps_o, lhsT=AT[:], rhs=vv[:, tt, :], start=True, stop=False), reads=[t_AT, t_in], writes=[t3])
            P.op("pe", lambda e: e.matmul(ps_o, lhsT=e3[:], rhs=S[:], start=False, stop=True), reads=[t_e3, t_S], writes=[t3])
            P.op("pe", lambda e, tt=tt: e.matmul(ps_S, lhsT=k4[:], rhs=vv[:, tt, :], start=True, stop=True), reads=[t_k4, t_in], writes=[t3])
            yield
            P.op("act", lambda e, tt=tt: e.copy(out=odir[z][:, tt, :], in_=ps_o), reads=[t3], writes=[t_odir[z]])
            yield
            P.op("dve", lambda e: e.scalar_tensor_tensor(out=S[:], in0=S[:], scalar=sc[:, 1:2], in1=ps_S, op0=ALU.mult, op1=ALU.add), reads=[t_sc, t3, t_S], writes=[t_S])
            yield

    gens = [lane(0), lane(1)]
    while gens:
        for gg in list(gens):
            try:
                next(gg)
            except StopIteration:
                gens.remove(gg)
    rs = P.sb([128, 2, NTT], name="rs")
    sq = P.sb([128, NTT, 128], name="sq")
    t_rs, t_sq = Tok(), Tok()
    osum = odir[0]
    P.op("dve", lambda e: e.tensor_add(out=osum[:], in0=odir[0][:], in1=odir[1][:]), reads=[t_odir[1], t_odir[0]], writes=[t_odir[0]])
    P.op("act", lambda e: e.activation(out=sq[:], in_=osum[:], func=AF.Square), reads=[t_odir[0]], writes=[t_sq])
    P.op("dve", lambda e: e.reduce_sum(out=rs[:, 0, :], in_=sq[:], axis=AX.X), reads=[t_sq], writes=[t_rs])
    P.op("dve", lambda e: e.tensor_scalar(out=rs[:, 0, :], in0=rs[:, 0, :], scalar1=1.0 / 128, scalar2=1e-6, op0=ALU.mult, op1=ALU.add), reads=[t_rs], writes=[t_rs])
    P.op("act", lambda e: e.activation(out=rs[:, 0, :], in_=rs[:, 0, :], func=AF.Sqrt), reads=[t_rs], writes=[t_rs])
    P.op("dve", lambda e: e.reciprocal(out=rs[:, 1, :], in_=rs[:, 0, :]), reads=[t_rs], writes=[t_rs])
    for tt in range(NTT):
        P.op("dve", lambda e, tt=tt: e.scalar_tensor_tensor(out=osum[:, tt, :], in0=osum[:, tt, :], scalar=rs[:, 1, tt:tt + 1], in1=gn[:], op0=ALU.mult, op1=ALU.mult),
             reads=[t_rs, t_gn, t_odir[0]], writes=[t_odir[0]])
    P.op("dve", lambda e: e.tensor_mul(out=osum[:], in0=osum[:], in1=g[:]), reads=[t_g, t_odir[0]], writes=[t_odir[0]])
    for half in range(2):
        hs = slice(half * 17, half * 17 + 17)
        P.dma("sp" if half == 0 else "act", lambda e, hs=hs, half=half: e.dma_start(
            out=o_d[half * 17 * 128:(half + 1) * 17 * 128, :].rearrange("(n p) d -> p n d", p=128), in_=osum[:, hs, :]), reads=[t_odir[0]], is_out=True)
    return P


def flipseg(a):
    return np.concatenate([a[:CTX][::-1], a[CTX:][::-1]], axis=0)


CMATS = np.stack([TRI_INC, TRI_SUFEX, ANTI, IDENT], axis=0)


def stage_gla(p_lat, p_ctx, w_up_l, b_up_l, norm_l):
    P = build_gla()
    in_maps = []
    for c in range(NCORES):
        b, h = divmod(c, 4)
        pa = np.concatenate([p_ctx[b], p_lat[b]], axis=0)
        q = pa[:, h * 64:(h + 1) * 64]
        k = pa[:, 256 + h * 64:256 + (h + 1) * 64]
        v = pa[:, 512 + h * 128:512 + (h + 1) * 128]
        g = pa[:, 1024 + h * 128:1024 + (h + 1) * 128]
        rs = [pa[:, 1536 + z * 16:1536 + (z + 1) * 16].T for z in range(2)]
        ws = [np.concatenate([w_up_l[z][:, h * 64:(h + 1) * 64], b_up_l[z][None, h * 64:(h + 1) * 64]], axis=0) for z in range(2)]
        in_maps.append({"g": np.ascontiguousarray(g), "gn": bc(norm_l), "cm": np.ascontiguousarray(GDN_CM[0:4]),
                        "qT": np.ascontiguousarray(q.T), "kT": np.ascontiguousarray(k.T), "k": np.ascontiguousarray(k),
                        "v": np.ascontiguousarray(v), "rT": np.ascontiguousarray(np.stack(rs)), "w": np.ascontiguousarray(np.stack(ws))})
    res = run_prog(P, in_maps)
    o_lat = np.empty((B, SEQ, 512), np.float32)
    o_ctx = np.empty((B, CTX, 512), np.float32)
    for c in range(NCORES):
        b, h = divmod(c, 4)
        o_ctx[b, :, 128 * h:128 * (h + 1)] = res[c]["o"][:CTX]
        o_lat[b, :, 128 * h:128 * (h + 1)] = res[c]["o"][CTX:]
    return o_lat, o_ctx


UPP = TRI_INC
LOW = np.ascontiguousarray(TRI_INC.T)
GDN_CM = np.stack([UPP, LOW, UPP - IDENT, LOW - IDENT, IDENT, np.ones((128, 128), np.float32)], axis=0)


def build_gdn(dbg=None):
    P = Prog()
    x_d = P.din("xT", [3, 128, NTOK])
    cw_d = P.din("cw", [128, 3, 5])
    zg_d = P.din("zg", [NTOK, 128])
    ba_d = P.din("ba", [2, 2, 128, NTT])
    sc_d = P.din("sc", [128, 2, 2])
    gn_d = P.din("gn", [128, 128])
    c_d = P.din("cm", [6, 128, 128])
    o_d = P.dout("o", [NTOK, 128])

    cm = P.sb([128, 6, 128], name="cm")
    t_cm = Tok()
    for i in range(6):
        P.dma("sp", lambda e, i=i: e.dma_start(out=cm[:, i, :], in_=c_d[i]), writes=[t_cm])
    ident, ones = cm[:, 4, :], cm[:, 5, :]
    gn = P.sb([128, 128], name="gn")
    cw = P.sb([128, 3, 5], name="cw")
    scal = P.sb([128, 2, 2], name="scal")
    t_gn, t_cw, t_scal = Tok(), Tok(), Tok()
    P.dma("sp", lambda e: e.dma_start(out=gn[:], in_=gn_d), writes=[t_gn])
    P.dma("sp", lambda e: e.dma_start(out=cw[:], in_=cw_d), writes=[t_cw])
    P.dma("sp", lambda e: e.dma_start(out=scal[:], in_=sc_d), writes=[t_scal])
    zg = P.sb([128, NTT, 128], name="zg")
    t_zg = Tok()
    P.dma("act", lambda e: e.dma_start(out=zg[:], in_=zg_d.rearrange("(n p) d -> p n d", p=128)), writes=[t_zg])
    P.op("act", lambda e: e.activation(out=zg[:], in_=zg[:], func=AF.Silu), reads=[t_zg], writes=[t_zg])

    raw = P.sb([128, NTOK], name="raw")
    t_raw = Tok()
    fmx = [P.sb([128, NTOK], name="fmx") for _ in range(3)]
    t_fm = [Tok() for _ in range(3)]
    segs = [(0, CTX), (CTX, NTOK)]
    for s in range(3):
        P.dma("sp", lambda e, s=s: e.dma_start(out=raw[:], in_=x_d[s]), writes=[t_raw])
        acc = fmx[s]
        for (s0, s1) in segs:
            P.op("dve", lambda e, s=s, s0=s0, s1=s1, acc=acc: e.tensor_scalar(out=acc[:, s0:s1], in0=raw[:, s0:s1], scalar1=cw[:, s, 2:3], scalar2=None, op0=ALU.mult),
                 reads=[t_raw, t_cw], writes=[t_fm[s]])
            for j in (0, 1, 3, 4):
                sh = j - 2
                lo = max(s0, s0 - sh)
                hi = min(s1, s1 - sh)
                P.op("dve", lambda e, s=s, j=j, lo=lo, hi=hi, sh=sh, acc=acc: e.scalar_tensor_tensor(
                    out=acc[:, lo:hi], in0=raw[:, lo + sh:hi + sh], scalar=cw[:, s, j:j + 1], in1=acc[:, lo:hi], op0=ALU.mult, op1=ALU.add),
                    reads=[t_raw, t_cw, t_fm[s]], writes=[t_fm[s]])
        P.op("act", lambda e, acc=acc: e.activation(out=acc[:], in_=acc[:], func=AF.Silu), reads=[t_fm[s]], writes=[t_fm[s]])
    bk1 = P.ps([128, 512], name="bk1")
    t_bk1 = Tok(True)
    sq = P.sb([128, 512], name="sq")
    t_sq = Tok()
    blocks = [(0, 256)] + [(CTX + i * 512, 512) for i in range(SEQ // 512)]
    for s in range(2):
        mul = (128 ** -0.5) if s == 0 else 1.0
        for (b0, bw) in blocks:
            P.op("act", lambda e, s=s, b0=b0, bw=bw: e.activation(out=sq[:, 0:bw], in_=fmx[s][:, b0:b0 + bw], func=AF.Square), reads=[t_fm[s]], writes=[t_sq])
            P.op("pe", lambda e, bw=bw: e.matmul(bk1[:, 0:bw], lhsT=ones, rhs=sq[:, 0:bw], start=True, stop=True), reads=[t_sq, t_cm], writes=[t_bk1])
            P.op("dve", lambda e, bw=bw: e.tensor_scalar_add(out=sq[:, 0:bw], in0=bk1[:, 0:bw], scalar1=1e-6), reads=[t_bk1], writes=[t_sq])
            P.op("act", lambda e, bw=bw: e.activation(out=sq[:, 0:bw], in_=sq[:, 0:bw], func=AF.Sqrt), reads=[t_sq], writes=[t_sq])
            P.op("dve", lambda e, bw=bw: e.reciprocal(out=sq[:, 0:bw], in_=sq[:, 0:bw]), reads=[t_sq], writes=[t_sq])
            P.op("dve", lambda e, s=s, b0=b0, bw=bw, mul=mul: e.scalar_tensor_tensor(out=fmx[s][:, b0:b0 + bw], in0=fmx[s][:, b0:b0 + bw], scalar=float(mul), in1=sq[:, 0:bw],
                                                                                   op0=ALU.mult, op1=ALU.mult), reads=[t_sq, t_fm[s]], writes=[t_fm[s]])
    qnT, knT, vT = fmx
    t_qnT, t_knT, t_vT = t_fm
    kn = P.sb([128, NTT, 128], name="kn")
    vt = P.sb([128, NTT, 128], name="vt")
    t_kn, t_vt = Tok(), Tok()
    for tt in range(NTT):
        ts_ = slice(tt * 128, (tt + 1) * 128)
        P.op("pe", lambda e, ts_=ts_: e.transpose(out=bk1[:, 0:128], in_=knT[:, ts_], identity=ident), reads=[t_knT, t_cm], writes=[t_bk1])
        P.op("pe", lambda e, ts_=ts_: e.transpose(out=bk1[:, 128:256], in_=vT[:, ts_], identity=ident), reads=[t_vT, t_cm], writes=[t_bk1])
        P.op("act", lambda e, tt=tt: e.copy(out=kn[:, tt, :], in_=bk1[:, 0:128]), reads=[t_bk1], writes=[t_kn])
        P.op("dve", lambda e, tt=tt: e.tensor_copy(out=vt[:, tt, :], in_=bk1[:, 128:256]), reads=[t_bk1], writes=[t_vt])

    if dbg == "prep":
        for tt in range(NTT):
            P.dma("pool", lambda e, tt=tt: e.dma_start(out=o_d[tt * 128:(tt + 1) * 128, :], in_=kn[:, tt, :]), reads=[t_kn], is_out=True)
        return P
    odir = [P.sb([128, NTT, 128], name="odir") for _ in range(2)]
    t_odir = [Tok(), Tok()]

    def sbt(n, w=128):
        return P.sb([128, w], name=n), Tok()

    def make_lane(z):
        L = {}
        L["bl"] = P.sb([128, 2, NTT], name="bl")
        L["t_bl"] = Tok()
        L["nea"], L["t_nea"] = sbt("nea", 1)
        L["S"], L["t_S"] = sbt("S")
        bA = bk1 if z == 0 else P.ps([128, 512], name="bA")
        bB = P.ps([128, 512], name="bB")
        bC = P.ps([128, 512], name="bC")
        bD = P.ps([128, 512], name="bD")
        L["tA"] = t_bk1 if z == 0 else Tok(True)
        L["tB"], L["tC"], L["tD"] = Tok(True), Tok(True), Tok(True)
        L["pG"], L["pbR"], L["pKK"], L["pQK"] = bA[:, 0:128], bA[:, 128:256], bA[:, 256:384], bA[:, 384:512]
        L["psqL"], L["psqLT"], L["pg"] = bB[:, 0:128], bB[:, 128:256], bB[:, 256:258]
        L["pX"], L["pwT"], L["pvn"] = bC[:, 0:256], bC[:, 256:384], bC[:, 384:512]
        L["po"], L["pS"] = bD[:, 0:128], bD[:, 128:256]
        for n in ("lgB", "btB", "GT", "Gm", "EgR", "L0", "LT1", "AqT", "wT", "vn", "qd", "kd"):
            L[n], L["t_" + n] = sbt(n)
        L["col"], L["t_col"] = sbt("col", 8)
        L["X"], L["t_X"] = sbt("X", 256)
        L["pow"] = [sbt("pw%d" % i) for i in range(12)]
        return L

    def lane_gen(z, L):
        C, CT, Cs, CTs = (cm[:, 0, :], cm[:, 1, :], cm[:, 2, :], cm[:, 3, :]) if z == 0 else (cm[:, 1, :], cm[:, 0, :], cm[:, 3, :], cm[:, 2, :])
        last = 127 if z == 0 else 0
        bl, t_bl, nea, t_nea, S, t_S = L["bl"], L["t_bl"], L["nea"], L["t_nea"], L["S"], L["t_S"]
        tA, tB, tC, tD = L["tA"], L["tB"], L["tC"], L["tD"]
        pG, pbR, pKK, pQK, psqL, psqLT, pg = L["pG"], L["pbR"], L["pKK"], L["pQK"], L["psqL"], L["psqLT"], L["pg"]
        pX, pwT, pvn, po, pS = L["pX"], L["pwT"], L["pvn"], L["po"], L["pS"]
        lgB, btB, GT, Gm, EgR, L0, LT1, AqT, wT, vn, qd, kd, col, X = [L[n] for n in ("lgB", "btB", "GT", "Gm", "EgR", "L0", "LT1", "AqT", "wT", "vn", "qd", "kd", "col", "X")]
        t_lgB, t_btB, t_GT, t_Gm, t_EgR, t_L0, t_LT1, t_AqT, t_wT, t_vn, t_qd, t_kd, t_col, t_X = [L["t_" + n] for n in ("lgB", "btB", "GT", "Gm", "EgR", "L0", "LT1", "AqT", "wT", "vn", "qd", "kd", "col", "X")]
        for i in range(2):
            P.dma("sp", lambda e, i=i: e.dma_start(out=bl[:, i, :], in_=ba_d[z, i]), writes=[t_bl])
        P.op("act", lambda e: e.activation(out=bl[:, 0, :], in_=bl[:, 0, :], func=AF.Sigmoid), reads=[t_bl], writes=[t_bl])
        P.op("act", lambda e: e.activation(out=bl[:, 1, :], in_=bl[:, 1, :], func=AF.Exp, bias=scal[:, z, 1:2], scale=1.0), reads=[t_bl, t_scal], writes=[t_bl])
        P.op("dve", lambda e: e.tensor_scalar_add(out=bl[:, 1, :], in0=bl[:, 1, :], scalar1=1.0), reads=[t_bl], writes=[t_bl])
        P.op("act", lambda e: e.activation(out=bl[:, 1, :], in_=bl[:, 1, :], func=AF.Ln), reads=[t_bl], writes=[t_bl])
        P.op("act", lambda e: e.activation(out=nea[:], in_=scal[:, z, 0:1], func=AF.Exp), reads=[t_scal], writes=[t_nea])
        P.op("dve", lambda e: e.tensor_scalar_mul(out=nea[:], in0=nea[:], scalar1=-1.0), reads=[t_nea], writes=[t_nea])
        P.op("dve", lambda e: e.tensor_scalar(out=bl[:, 1, :], in0=bl[:, 1, :], scalar1=nea[:], scalar2=None, op0=ALU.mult), reads=[t_bl, t_nea], writes=[t_bl])
        P.op("dve", lambda e: e.memset(S[:], 0.0), writes=[t_S])
        yield
        order = list(range(NTT)) if z == 0 else [1, 0] + list(range(NTT - 1, 1, -1))
        if dbg is not None and dbg.startswith("main"):
            order = order[:int(dbg[4:])]
        for tt in order:
            ts_ = slice(tt * 128, (tt + 1) * 128)
            beta = bl[:, 0, tt:tt + 1]
            lg = bl[:, 1, tt:tt + 1]
            P.op("dve", lambda e, lg=lg: e.tensor_scalar(out=lgB[:], in0=ones, scalar1=lg, scalar2=None, op0=ALU.mult), reads=[t_bl, t_cm], writes=[t_lgB])
            yield
            P.op("dve", lambda e, beta=beta: e.tensor_scalar(out=btB[:], in0=ones, scalar1=beta, scalar2=None, op0=ALU.mult), reads=[t_bl, t_cm], writes=[t_btB])
            yield
            P.op("pe", lambda e: e.matmul(pg, lhsT=C, rhs=lgB[:, 0:2], start=True, stop=True), reads=[t_lgB, t_cm], writes=[tB])
            P.op("pe", lambda e: e.matmul(pG, lhsT=lgB[:], rhs=C, start=True, stop=True), reads=[t_lgB, t_cm], writes=[tA])
            P.op("pe", lambda e: e.matmul(pbR, lhsT=btB[:], rhs=ident, start=True, stop=True), reads=[t_btB, t_cm], writes=[tA])
            P.op("pe", lambda e, ts_=ts_: e.matmul(pKK, lhsT=knT[:, ts_], rhs=knT[:, ts_], start=True, stop=True), reads=[t_knT], writes=[tA])
            P.op("pe", lambda e, ts_=ts_: e.matmul(pQK, lhsT=knT[:, ts_], rhs=qnT[:, ts_], start=True, stop=True), reads=[t_knT, t_qnT], writes=[tA])
            yield
            P.op("act", lambda e: e.copy(out=col[:, 0:1], in_=pg[:, 0:1]), reads=[tB], writes=[t_col])
            yield
            P.op("dve", lambda e: e.tensor_scalar(out=GT[:], in0=pG, scalar1=col[:, 0:1], scalar2=0.0, op0=ALU.subtract, op1=ALU.min), reads=[tA, t_col], writes=[t_GT])
            yield
            P.op("act", lambda e: e.activation(out=GT[:], in_=GT[:], func=AF.Exp), reads=[t_GT], writes=[t_GT])
            yield
            P.op("dve", lambda e: e.tensor_scalar(out=Gm[:], in0=pG, scalar1=col[:, 0:1], scalar2=0.0, op0=ALU.subtract, op1=ALU.max), reads=[tA, t_col], writes=[t_Gm])
            yield
            P.op("act", lambda e: e.activation(out=Gm[:], in_=Gm[:], func=AF.Exp, scale=-1.0), reads=[t_Gm], writes=[t_Gm])
            yield
            P.op("act", lambda e: e.activation(out=EgR[:], in_=pG, func=AF.Exp), reads=[tA], writes=[t_EgR])
            yield
            P.op("act", lambda e: e.activation(out=col[:, 4:5], in_=col[:, 0:1], func=AF.Exp), reads=[t_col], writes=[t_col])
            yield
            P.op("dve", lambda e, beta=beta: e.tensor_mul(out=col[:, 1:2], in0=col[:, 4:5], in1=beta), reads=[t_col, t_bl], writes=[t_col])
            yield
            P.op("dve", lambda e: e.tensor_sub(out=col[:, 5:6], in0=pG[:, last:last + 1], in1=col[:, 0:1]), reads=[tA, t_col], writes=[t_col])
            yield
            P.op("act", lambda e: e.activation(out=col[:, 2:3], in_=col[:, 5:6], func=AF.Exp), reads=[t_col], writes=[t_col])
            yield
            P.op("act", lambda e: e.activation(out=col[:, 3:4], in_=pG[:, last:last + 1], func=AF.Exp), reads=[tA], writes=[t_col])
            yield
            P.op("dve", lambda e: e.tensor_mul(out=LT1[:], in0=GT[:], in1=Cs), reads=[t_GT, t_cm], writes=[t_LT1])
            yield
            P.op("dve", lambda e: e.tensor_mul(out=LT1[:], in0=LT1[:], in1=pKK), reads=[tA, t_LT1], writes=[t_LT1])
            yield
            P.op("dve", lambda e: e.tensor_mul(out=LT1[:], in0=LT1[:], in1=pbR), reads=[tA, t_LT1], writes=[t_LT1])
            yield
            P.op("dve", lambda e: e.tensor_mul(out=L0[:], in0=Gm[:], in1=CTs), reads=[t_Gm, t_cm], writes=[t_L0])
            yield
            P.op("dve", lambda e, beta=beta: e.scalar_tensor_tensor(out=L0[:], in0=L0[:], scalar=beta, in1=pKK, op0=ALU.mult, op1=ALU.mult), reads=[tA, t_bl, t_L0], writes=[t_L0])
            yield
            P.op("dve", lambda e: e.tensor_mul(out=AqT[:], in0=GT[:], in1=C), reads=[t_GT, t_cm], writes=[t_AqT])
            yield
            P.op("dve", lambda e: e.tensor_mul(out=AqT[:], in0=AqT[:], in1=pQK), reads=[tA, t_AqT], writes=[t_AqT])
            yield
            P.op("dve", lambda e, tt=tt, beta=beta: e.tensor_scalar(out=X[:, 0:128], in0=vt[:, tt, :], scalar1=beta, scalar2=None, op0=ALU.mult), reads=[t_vt, t_bl], writes=[t_X])
            yield
            P.op("dve", lambda e, tt=tt: e.tensor_scalar(out=X[:, 128:256], in0=kn[:, tt, :], scalar1=col[:, 1:2], scalar2=None, op0=ALU.mult), reads=[t_kn, t_col], writes=[t_X])
            yield
            cur_L, cur_tL, cur_LT, cur_tLT = L0, t_L0, LT1, t_LT1
            for pi in range(7):
                if pi < 6:
                    nL, t_nL = L["pow"][2 * pi]
                    nLT, t_nLT = L["pow"][2 * pi + 1]
                    P.op("pe", lambda e, cur_L=cur_L, cur_LT=cur_LT: e.matmul(psqL, lhsT=cur_LT[:], rhs=cur_L[:], start=True, stop=True), reads=[cur_tL, cur_tLT], writes=[tB])
                    P.op("pe", lambda e, cur_L=cur_L, cur_LT=cur_LT: e.matmul(psqLT, lhsT=cur_L[:], rhs=cur_LT[:], start=True, stop=True), reads=[cur_tL, cur_tLT], writes=[tB])
                P.op("pe", lambda e, cur_LT=cur_LT: e.matmul(pX, lhsT=cur_LT[:], rhs=X[:], start=True, stop=True), reads=[cur_tLT, t_X], writes=[tC])
                yield
                if pi < 6:
                    P.op("act", lambda e, nL=nL: e.copy(out=nL[:], in_=psqL), reads=[tB], writes=[t_nL])
                    P.op("act", lambda e, nLT=nLT: e.copy(out=nLT[:], in_=psqLT), reads=[tB], writes=[t_nLT])
                if pi > 0:
                    P.op("dve", lambda e: e.tensor_add(out=X[:], in0=X[:], in1=pX), reads=[tC, t_X], writes=[t_X])
                else:
                    P.op("dve", lambda e: e.tensor_sub(out=X[:], in0=X[:], in1=pX), reads=[tC, t_X], writes=[t_X])
                yield
                if pi < 6:
                    cur_L, cur_tL, cur_LT, cur_tLT = nL, t_nL, nLT, t_nLT
            P.op("pe", lambda e: e.transpose(out=pwT, in_=X[:, 128:256], identity=ident), reads=[t_X, t_cm], writes=[tC])
            yield
            P.op("act", lambda e: e.copy(out=wT[:], in_=pwT), reads=[tC], writes=[t_wT])
            yield
            P.op("pe", lambda e: e.matmul(pvn, lhsT=wT[:], rhs=S[:], start=True, stop=True), reads=[t_wT, t_S], writes=[tC])
            yield
            P.op("dve", lambda e: e.tensor_sub(out=vn[:], in0=X[:, 0:128], in1=pvn), reads=[tC, t_X], writes=[t_vn])
            yield
            P.op("dve", lambda e, ts_=ts_: e.tensor_mul(out=qd[:], in0=qnT[:, ts_], in1=EgR[:]), reads=[t_qnT, t_EgR], writes=[t_qd])
            yield
            P.op("pe", lambda e: e.matmul(po, lhsT=qd[:], rhs=S[:], start=True, stop=False), reads=[t_qd, t_S], writes=[tD])
            P.op("pe", lambda e: e.matmul(po, lhsT=AqT[:], rhs=vn[:], start=False, stop=True), reads=[t_AqT, t_vn], writes=[tD])
            yield
            P.op("dve", lambda e, tt=tt: e.tensor_scalar(out=kd[:], in0=kn[:, tt, :], scalar1=col[:, 2:3], scalar2=None, op0=ALU.mult), reads=[t_kn, t_col], writes=[t_kd])
            yield
            P.op("pe", lambda e: e.matmul(pS, lhsT=kd[:], rhs=vn[:], start=True, stop=True), reads=[t_kd, t_vn], writes=[tD])
            yield
            P.op("act", lambda e, tt=tt: e.copy(out=odir[z][:, tt, :], in_=po), reads=[tD], writes=[t_odir[z]])
            yield
            P.op("dve", lambda e: e.scalar_tensor_tensor(out=S[:], in0=S[:], scalar=col[:, 3:4], in1=pS, op0=ALU.mult, op1=ALU.add), reads=[t_col, tD, t_S], writes=[t_S])
            yield

    gens = [lane_gen(z, make_lane(z)) for z in range(2)]
    while gens:
        for g in list(gens):
            try:
                next(g)
            except StopIteration:
                gens.remove(g)
    rs = P.sb([128, 2, NTT], name="rs")
    t_rs = Tok()
    osum = odir[0]
    P.op("dve", lambda e: e.tensor_add(out=osum[:], in0=odir[0][:], in1=odir[1][:]), reads=[t_odir[1], t_odir[0]], writes=[t_odir[0]])
    sqv = raw[:].rearrange("p (n d) -> p n d", d=128)
    P.op("act", lambda e: e.activation(out=sqv, in_=osum[:], func=AF.Square), reads=[t_odir[0]], writes=[t_raw])
    P.op("dve", lambda e: e.reduce_sum(out=rs[:, 0, :], in_=sqv, axis=AX.X), reads=[t_raw], writes=[t_rs])
    P.op("dve", lambda e: e.tensor_scalar(out=rs[:, 0, :], in0=rs[:, 0, :], scalar1=1.0 / 128, scalar2=1e-6, op0=ALU.mult, op1=ALU.add), reads=[t_rs], writes=[t_rs])
    P.op("act", lambda e: e.activation(out=rs[:, 0, :], in_=rs[:, 0, :], func=AF.Sqrt), reads=[t_rs], writes=[t_rs])
    P.op("dve", lambda e: e.reciprocal(out=rs[:, 1, :], in_=rs[:, 0, :]), reads=[t_rs], writes=[t_rs])
    for tt in range(NTT):
        P.op("dve", lambda e, tt=tt: e.scalar_tensor_tensor(out=osum[:, tt, :], in0=osum[:, tt, :], scalar=rs[:, 1, tt:tt + 1], in1=gn[:], op0=ALU.mult, op1=ALU.mult),
             reads=[t_rs, t_gn, t_odir[0]], writes=[t_odir[0]])
    P.op("dve", lambda e: e.tensor_mul(out=osum[:], in0=osum[:], in1=zg[:]), reads=[t_zg, t_odir[0]], writes=[t_odir[0]])
    for half in range(2):
        hs = slice(half * 17, half * 17 + 17)
        P.dma("sp" if half == 0 else "act", lambda e, hs=hs, half=half: e.dma_start(
            out=o_d[half * 17 * 128:(half + 1) * 17 * 128, :].rearrange("(n p) d -> p n d", p=128), in_=osum[:, hs, :]), reads=[t_odir[0]], is_out=True)
    return P


def stage_gdn(p_lat, p_ctx, conv_l, a_log_l, dt_bias_l, norm_l, dbg=None):
    P = build_gdn(dbg)
    in_maps = []
    for c in range(NCORES):
        b, h = divmod(c, 4)
        pa = np.concatenate([p_ctx[b], p_lat[b]], axis=0)
        hs = slice(h * 128, (h + 1) * 128)
        xT = np.stack([pa[:, 1568:2080][:, hs].T, pa[:, 2080:2592][:, hs].T, pa[:, 2592:3104][:, hs].T], axis=0)
        cw = np.stack([conv_l[:, s * 512 + h * 128:s * 512 + (h + 1) * 128].T for s in range(3)], axis=1)
        ba = np.empty((2, 2, 128, NTT), np.float32)
        sc = np.empty((128, 2, 2), np.float32)
        for z in range(2):
            ba[z, 0] = pa[:, 3616 + z * 4 + h].reshape(NTT, 128).T
            ba[z, 1] = pa[:, 3624 + z * 4 + h].reshape(NTT, 128).T
            sc[:, z, 0] = a_log_l[z, h]
            sc[:, z, 1] = dt_bias_l[z, h]
        in_maps.append({"xT": np.ascontiguousarray(xT), "cw": np.ascontiguousarray(cw), "zg": np.ascontiguousarray(pa[:, 3104:3616][:, hs]),
                        "ba": ba, "sc": sc, "gn": bc(norm_l), "cm": GDN_CM})
    res = run_prog(P, in_maps)
    o_lat = np.empty((B, SEQ, 512), np.float32)
    o_ctx = np.empty((B, CTX, 512), np.float32)
    for c in range(NCORES):
        b, h = divmod(c, 4)
        o_ctx[b, :, 128 * h:128 * (h + 1)] = res[c]["o"][:CTX]
        o_lat[b, :, 128 * h:128 * (h + 1)] = res[c]["o"][CTX:]
    return o_lat, o_ctx


def build_router():
    P = Prog()
    KC = D // 128
    xT_d = P.din("xT", [128, KC, NT_B])
    mod_d = P.din("modv", [128, KC, 4])
    r_d = P.din("rw", [D, NE]).rearrange("(k p) c -> p k c", p=128)
    hT_d = P.dout("hT", [128, KC, NT_B])
    a_d = P.dout("aff", [NT_B, NE])
    xT = P.sb([128, KC, NT_B], name="xT")
    modv = P.sb([128, KC, 4], name="modv")
    rw = P.sb([128, KC, NE], name="rw")
    t_x = [Tok() for _ in range(KC)]
    t_mod, t_rw = Tok(), Tok()
    for k in range(KC):
        P.dma("sp" if k % 2 == 0 else "act", lambda e, k=k: e.dma_start(out=xT[:, k, :], in_=xT_d[:, k, :]), writes=[t_x[k]])
    P.dma("sp", lambda e: e.dma_start(out=modv[:], in_=mod_d), writes=[t_mod])
    P.dma("sp", lambda e: e.dma_start(out=rw[:], in_=r_d), writes=[t_rw])
    P.op("dve", lambda e: e.tensor_scalar_add(out=modv[:, :, 1:2], in0=modv[:, :, 1:2], scalar1=1.0), reads=[t_mod], writes=[t_mod])
    P.op("dve", lambda e: e.tensor_scalar_add(out=modv[:, :, 3:4], in0=modv[:, :, 3:4], scalar1=1.0), reads=[t_mod], writes=[t_mod])
    for k in range(KC):
        P.op("dve", lambda e, k=k: e.tensor_scalar(out=xT[:, k, 0:1024], in0=xT[:, k, 0:1024], scalar1=modv[:, k, 1:2],
                                                   scalar2=modv[:, k, 0:1], op0=ALU.mult, op1=ALU.add),
             reads=[t_mod, t_x[k]], writes=[t_x[k]])
        P.op("dve", lambda e, k=k: e.tensor_scalar(out=xT[:, k, 1024:NT_B], in0=xT[:, k, 1024:NT_B], scalar1=modv[:, k, 3:4],
                                                   scalar2=modv[:, k, 2:3], op0=ALU.mult, op1=ALU.add),
             reads=[t_mod, t_x[k]], writes=[t_x[k]])
        P.dma("pool", lambda e, k=k: e.dma_start(out=hT_d[:, k, :], in_=xT[:, k, :]), reads=[t_x[k]], is_out=True)
    pl = [P.ps([128, NE], name="pl") for _ in range(2)]
    t_pl = [Tok(True), Tok(True)]
    ex = [P.sb([128, NE], name="ex") for _ in range(2)]
    t_ex = [Tok(), Tok()]
    st = P.sb([128, 4], name="st")
    t_st = Tok()
    tiles = [(i * 128, 128) for i in range(8)] + [(1024, 64)]
    for ti, (t0, m) in enumerate(tiles):
        bi = ti % 2
        for k in range(KC):
            P.op("pe", lambda e, bi=bi, k=k, t0=t0, m=m: e.matmul(pl[bi][0:m, :], lhsT=xT[:, k, t0:t0 + m], rhs=rw[:, k, :], start=(k == 0), stop=(k == KC - 1)),
                 reads=[t_x[k], t_rw], writes=[t_pl[bi]])
        P.op("dve", lambda e, bi=bi, m=m: e.reduce_max(out=st[0:m, 0:1], in_=pl[bi][0:m, :], axis=AX.X), reads=[t_pl[bi]], writes=[t_st])
        P.op("dve", lambda e, m=m: e.tensor_scalar_mul(out=st[0:m, 1:2], in0=st[0:m, 0:1], scalar1=-1.0), reads=[t_st], writes=[t_st])
        P.op("act", lambda e, bi=bi, m=m: e.activation(out=ex[bi][0:m, :], in_=pl[bi][0:m, :], func=AF.Exp, bias=st[0:m, 1:2], scale=1.0, accum_out=st[0:m, 2:3]),
             reads=[t_pl[bi], t_st], writes=[t_ex[bi], t_st])
        P.op("dve", lambda e, m=m: e.reciprocal(out=st[0:m, 3:4], in_=st[0:m, 2:3]), reads=[t_st], writes=[t_st])
        P.op("dve", lambda e, bi=bi, m=m: e.tensor_scalar(out=ex[bi][0:m, :], in0=ex[bi][0:m, :], scalar1=st[0:m, 3:4], scalar2=None, op0=ALU.mult),
             reads=[t_st, t_ex[bi]], writes=[t_ex[bi]])
        P.dma("pool", lambda e, bi=bi, t0=t0, m=m: e.dma_start(out=a_d[t0:t0 + m, :], in_=ex[bi][0:m, :]), reads=[t_ex[bi]], writes=[t_ex[bi]], is_out=True)
    return P


def unfm(hT):
    p, kc, T = hT.shape
    return np.ascontiguousarray(hT.transpose(2, 1, 0).reshape(T, kc * p))


def stage_router(x_lat, x_ctx, mod_lat, mod_ctx, router_l):
    P = build_router()
    in_maps = []
    for c in range(NCORES):
        b = c // 4
        mv = np.stack([mod_lat[b, 3], mod_lat[b, 4], mod_ctx[3], mod_ctx[4]], axis=-1)
        mv = np.ascontiguousarray(mv.reshape(D // 128, 128, 4).transpose(1, 0, 2))
        in_maps.append({"xT": fm(tok_shard(x_lat, x_ctx, c)), "modv": mv, "rw": np.ascontiguousarray(router_l)})
    res = run_prog(P, in_maps)
    res2 = [{"h": unfm(r["hT"]), "aff": r["aff"]} for r in res]
    h_lat, h_ctx = tok_unshard(res2, "h", D)
    a_lat, a_ctx = tok_unshard(res2, "aff", NE)
    return h_lat, h_ctx, a_lat, a_ctx


NBIS = 30
H_ROWS = B * SEQ + B * CTX


def build_select():
    P = Prog()
    a_d = P.din("A", [128, 256])
    cc_d = P.din("cc", [128, 1])
    bd_d = P.din("bd", [2, 128, 128])
    tvc_d = P.din("tvc", [128, 32])
    io_d = P.din("iota", [128, 512])
    id_d = P.din("ident", [128, 128])
    idx_d = P.dout("idx", [128, 8, 4], I32)
    gate_d = P.dout("gate", [128, 8, 4])
    h_d = P.din("h", [H_ROWS, D])
    xsc_d = P.dout("xsc", [4, 544, D])
    A = P.sb([128, 256], name="A")
    M = P.sb([128, 256], name="M")
    Cm = P.sb([128, 256], name="Cm")
    onesr = P.sb([128, 256], name="onesr")
    cc = P.sb([128, 1], name="cc")
    bd = P.sb([128, 2, 128], name="bd")
    tvc = P.sb([128, 32], name="tvc")
    iota = P.sb([128, 512], name="iota")
    ident = P.sb([128, 128], name="ident")
    t_A, t_M, t_Cm, t_on, t_cc, t_bd, t_tvc, t_io, t_id = [Tok() for _ in range(9)]
    P.dma("sp", lambda e: e.dma_start(out=A[:], in_=a_d), writes=[t_A])
    P.dma("sp", lambda e: e.dma_start(out=cc[:], in_=cc_d), writes=[t_cc])
    for i in range(2):
        P.dma("sp", lambda e, i=i: e.dma_start(out=bd[:, i, :], in_=bd_d[i]), writes=[t_bd])
    P.dma("act", lambda e: e.dma_start(out=tvc[:], in_=tvc_d), writes=[t_tvc])
    P.dma("act", lambda e: e.dma_start(out=iota[:], in_=io_d), writes=[t_io])
    P.dma("act", lambda e: e.dma_start(out=ident[:], in_=id_d), writes=[t_id])
    P.op("pool", lambda e: e.memset(onesr[:], 1.0), writes=[t_on])
    bs = P.sb([128, 8], name="bs")
    t_bs = Tok()
    P.op("dve", lambda e: e.memset(bs[:], 0.0), writes=[t_bs])
    pc = P.ps([128, 8], name="pc")
    t_pc = Tok(True)
    for k in range(1, NBIS + 1):
        w = 2.0 ** (-k)
        P.op("dve", lambda e, w=w: e.tensor_scalar_add(out=bs[:, 1:2], in0=bs[:, 0:1], scalar1=w), reads=[t_bs], writes=[t_bs])
        P.op("dve", lambda e: e.tensor_scalar(out=M[:], in0=A[:], scalar1=bs[:, 1:2], scalar2=None, op0=ALU.is_ge, op1=ALU.add, accum_out=bs[:, 2:3]),
             reads=[t_A, t_bs], writes=[t_M, t_bs])
        P.op("pe", lambda e: e.matmul(pc[:, 0:2], lhsT=bd[:, 0, :], rhs=bs[:, 2:4], start=True, stop=True), reads=[t_bd, t_bs], writes=[t_pc])
        P.op("dve", lambda e: e.tensor_tensor(out=bs[:, 4:5], in0=pc[:, 0:1], in1=cc[:], op=ALU.is_ge), reads=[t_pc, t_cc], writes=[t_bs])
        P.op("dve", lambda e, w=w: e.scalar_tensor_tensor(out=bs[:, 0:1], in0=bs[:, 4:5], scalar=w, in1=bs[:, 0:1], op0=ALU.mult, op1=ALU.add),
             reads=[t_bs], writes=[t_bs])
    P.op("dve", lambda e: e.tensor_scalar(out=M[:], in0=A[:], scalar1=bs[:, 0:1], scalar2=None, op0=ALU.is_ge, op1=ALU.add, accum_out=bs[:, 2:3]),
         reads=[t_A, t_bs], writes=[t_M, t_bs])
    P.op("pe", lambda e: e.matmul(pc[:, 2:4], lhsT=bd[:, 1, :], rhs=bs[:, 2:4], start=True, stop=True), reads=[t_bd, t_bs], writes=[t_pc])
    P.op("dve", lambda e: e.tensor_tensor_scan(out=Cm[:], data0=onesr[:], data1=M[:], initial=0.0, op0=ALU.mult, op1=ALU.add),
         reads=[t_on, t_M], writes=[t_Cm])
    P.op("dve", lambda e: e.tensor_sub(out=Cm[:], in0=Cm[:], in1=M[:]), reads=[t_M, t_Cm], writes=[t_Cm])
    P.op("dve", lambda e: e.tensor_scalar(out=Cm[:], in0=Cm[:], scalar1=pc[:, 2:3], scalar2=None, op0=ALU.add), reads=[t_pc, t_Cm], writes=[t_Cm])
    TT = P.sb([128, 3, 2, 128], name="TT")
    t_TT = Tok()
    pT = [P.ps([128, 512], name="pT") for _ in range(2)]
    t_pT = [Tok(True), Tok(True)]
    for half in range(2):
        for i, (src, tk) in enumerate(((A, t_A), (M, t_M), (Cm, t_Cm))):
            P.op("pe", lambda e, half=half, i=i, src=src: e.transpose(out=pT[half][:, i * 128:(i + 1) * 128], in_=src[:, half * 128:(half + 1) * 128], identity=ident[:]),
                 reads=[tk, t_id], writes=[t_pT[half]])
        P.op("dve", lambda e, half=half: e.tensor_copy(out=TT[:, :, half, :], in_=pT[half][:, 0:384].rearrange("p (q c) -> p q c", q=3)), reads=[t_pT[half]], writes=[t_TT])

    class _T3:
        def __getitem__(self, key):
            _, j, cs = key
            c = cs.start
            q, r = divmod(c, 8)
            s_, half = divmod(j, 2)
            col = r * 16 + s_
            return TT[:, q, half, col:col + 1]
    T3 = _T3()
    t_T3 = t_TT
    TV = P.sb([128, 32, 8, 2], name="TV")
    t_TV = Tok()
    for r in range(8):
        P.op("dve", lambda e, r=r: e.tensor_copy(out=TV[:, :, r, 0], in_=tvc[:]), reads=[t_tvc], writes=[t_TV])
    TV5 = TV[:].rearrange("p (s h) r c -> p s h r c", h=2)
    for half in range(2):
        P.op("dve", lambda e, half=half: e.tensor_copy(out=TV5[:, :, half, :, 1], in_=TT[:, 0, half, :].rearrange("p (r s) -> p s r", r=8)), reads=[t_TT], writes=[t_TV])
    Pm = P.sb([128, 32, 512], name="Pm")
    t_Pm = Tok()
    pi_ = P.ps([128, 64], name="pi")
    t_pi = Tok(True)
    res_i = P.sb([128, 8, 4], I32, name="res_i")
    res_f = P.sb([128, 8, 4], name="res_f")
    res_g = P.sb([128, 8, 4], name="res_g")
    t_res = Tok()
    P.op("dve", lambda e: e.memset(res_f[:], 0.0), writes=[t_res])
    P.op("dve", lambda e: e.memset(res_g[:], 0.0), writes=[t_res])
    for r in range(8):
        lat = r < 4
        C = 512 if lat else 32
        nj = 32 if lat else 2
        b = r % 2
        base = float(b * SEQ) if lat else float(B * SEQ + b * CTX)
        for j in range(nj):
            P.op("dve", lambda e, j=j, r=r, C=C: e.tensor_scalar(out=Pm[:, j, 0:C], in0=iota[:, 0:C], scalar1=T3[:, j, 16 + r:17 + r],
                                                                 scalar2=T3[:, j, 8 + r:9 + r], op0=ALU.is_equal, op1=ALU.mult),
                 reads=[t_io, t_T3], writes=[t_Pm])
        for sc in range(4 if lat else 1):
            msz = 128 if lat else 32
            for j in range(nj):
                P.op("pe", lambda e, r=r, sc=sc, j=j, msz=msz, nj=nj: e.matmul(pi_[0:msz, (r * 4 + sc) * 2:(r * 4 + sc) * 2 + 2],
                                                                            lhsT=Pm[:, j, sc * 128:sc * 128 + msz], rhs=TV[:, j, r, :],
                                                                            start=(j == 0), stop=(j == nj - 1)),
                     reads=[t_Pm, t_TV], writes=[t_pi])
            P.op("dve", lambda e, r=r, sc=sc, msz=msz, base=base: e.tensor_scalar_add(out=res_f[0:msz, r, sc:sc + 1],
                                                                                   in0=pi_[0:msz, (r * 4 + sc) * 2:(r * 4 + sc) * 2 + 1], scalar1=base),
                 reads=[t_pi], writes=[t_res])
            P.op("dve", lambda e, r=r, sc=sc, msz=msz: e.tensor_copy(out=res_g[0:msz, r, sc:sc + 1], in_=pi_[0:msz, (r * 4 + sc) * 2 + 1:(r * 4 + sc) * 2 + 2]),
                 reads=[t_pi], writes=[t_res])
    P.op("dve", lambda e: e.tensor_copy(out=res_i[:], in_=res_f[:]), reads=[t_res], writes=[t_res])
    P.dma("pool", lambda e: e.dma_start(out=idx_d, in_=res_i[:]), reads=[t_res], is_out=True)
    P.dma("pool", lambda e: e.dma_start(out=gate_d, in_=res_g[:]), reads=[t_res], is_out=True)
    xs = [P.sb([128, D], name="xs") for _ in range(4)]
    t_xs = [Tok() for _ in range(4)]
    ixs = 0
    for el in range(2):
        for b in range(B):
            pas = el * 2 + b
            for (r, sc, s0, m) in [(el * 2 + b, sc, sc * 128, 128) for sc in range(4)] + [(4 + el * 2 + b, 0, 512, 32)]:
                xb = ixs % 4
                ixs += 1
                P.dma("pool", lambda e, xb=xb, r=r, sc=sc, m=m: e.indirect_dma_start(
                    out=xs[xb][0:m, :], out_offset=None, in_=h_d[:, :], in_offset=bass.IndirectOffsetOnAxis(ap=res_i[0:m, r, sc:sc + 1], axis=0)),
                    reads=[t_res], writes=[t_xs[xb]])
                P.dma("sp" if xb % 2 == 0 else "act", lambda e, xb=xb, pas=pas, s0=s0, m=m: e.dma_start(out=xsc_d[pas, s0:s0 + m, :], in_=xs[xb][0:m, :]),
                      reads=[t_xs[xb]], writes=[t_xs[xb]], is_out=True)
    return P


TVC = (np.arange(32)[None, :] * 128 + np.arange(128)[:, None]).astype(np.float32)
IOTA512 = np.ascontiguousarray(np.broadcast_to(np.arange(512, dtype=np.float32)[None, :], (128, 512)))
CCOL = np.array([512] * 4 + [32] * 4, np.float32)[:, None]
CC128 = np.ascontiguousarray(np.repeat(CCOL, 16, axis=0))
_grp = np.arange(128) // 16
BDMATS = np.stack([(_grp[:, None] == _grp[None, :]).astype(np.float32),
                   ((_grp[:, None] == _grp[None, :]) & (np.arange(128)[:, None] < np.arange(128)[None, :])).astype(np.float32)], axis=0)


def stage_select(a_lat, a_ctx, h_lat, h_ctx):
    P = build_select()
    h_all = np.ascontiguousarray(np.concatenate([h_lat.reshape(B * SEQ, D), h_ctx.reshape(B * CTX, D)], axis=0))
    in_maps = []
    for c in range(NCORES):
        A = np.full((8, SEQ), -1.0, np.float32)
        for el in range(2):
            for b in range(B):
                A[el * 2 + b] = a_lat[b, :, 2 * c + el]
                A[4 + el * 2 + b, :CTX] = a_ctx[b, :, 2 * c + el]
        in_maps.append({"A": np.ascontiguousarray(A.reshape(128, 256)), "cc": CC128, "bd": BDMATS, "tvc": TVC, "iota": IOTA512, "ident": IDENT, "h": h_all})
    res = run_prog(P, in_maps)
    return [(r["idx"], r["gate"], r["xsc"]) for r in res]


def build_expert(els=(0, 1), bs=(0, 1)):
    P = Prog()
    KC = D // 128
    h_d = P.din("xsc", [4, 544, D])
    idx_d = P.din("idx", [128, 8, 4], I32)
    gate_d = P.din("gate", [128, 8, 4])
    w1_d = P.din("w1", [len(els), D, FF])
    w3_d = P.din("w3", [len(els), D, FF])
    w2_d = P.din("w2", [len(els), FF, D])
    id_d = P.din("ident", [128, 128])
    f_d = [P.dout("f%d" % dc, [H_ROWS, 512]) for dc in range(4)]
    ident = P.sb([128, 128], name="ident")
    idx = P.sb([128, 8, 4], I32, name="idx")
    gate = P.sb([128, 8, 4], name="gate")
    t_id, t_idx, t_gate = Tok(), Tok(), Tok()
    P.dma("sp", lambda e: e.dma_start(out=ident[:], in_=id_d), writes=[t_id])
    P.dma("sp", lambda e: e.dma_start(out=idx[:], in_=idx_d), writes=[t_idx])
    P.dma("sp", lambda e: e.dma_start(out=gate[:], in_=gate_d), writes=[t_gate])
    zt = P.sb([128, 2048], name="zt")
    t_z = Tok()
    t_f = [Tok() for _ in range(4)]
    P.op("dve", lambda e: e.memset(zt[:], 0.0), writes=[t_z])
    for dc in range(4):
        for r0 in range(0, H_ROWS, 512):
            P.dma("sp" if (r0 // 512) % 2 == 0 else "act",
                  lambda e, dc=dc, r0=r0: e.dma_start(out=f_d[dc][r0:r0 + 512, :].rearrange("(p n) c -> p n c", p=128), in_=zt[:].rearrange("p (n c) -> p n c", n=4)),
                  reads=[t_z], writes=[t_f[dc]], is_out=True)
    NSL = 544 * len(bs)
    xs = [P.sb([128, D], name="xs") for _ in range(2)]
    t_xs = [Tok(), Tok()]
    xsT = P.sb([128, KC, NSL], BF16, name="xsT")
    t_xsT = Tok()
    hT = P.sb([128, KC, NSL], BF16, name="hT")
    t_hT = Tok()
    NWB = 3
    wa = [P.sb([128, KC, 128], BF16, name="w1c") for _ in range(NWB)]
    wu = [P.sb([128, KC, 128], BF16, name="w3c") for _ in range(NWB)]
    t_wa = [Tok() for _ in range(NWB)]
    t_wu = [Tok() for _ in range(NWB)]
    w2c = [P.sb([128, KC, 512], BF16, name="w2c") for _ in range(2)]
    t_w2 = [Tok(), Tok()]
    tmp = [P.sb([128, 512], name="tmp") for _ in range(2)]
    t_tmp = [Tok(), Tok()]
    yb = [P.sb([128, 512], name="yb") for _ in range(2)]
    t_yb = [Tok(), Tok()]
    pT = [P.ps([128, 512], name="pT") for _ in range(2)]
    t_pT = [Tok(True), Tok(True)]
    pa = [P.ps([128, 512], name="pa") for _ in range(2)]
    t_pa = [Tok(True), Tok(True)]
    pu = [P.ps([128, 512], name="pu") for _ in range(2)]
    t_pu = [Tok(True), Tok(True)]
    py = [P.ps([128, 512], name="py") for _ in range(2)]
    t_py = [Tok(True), Tok(True)]
    ixs = ipt = iw = iw2 = iau = iy = 0
    for eli, el in enumerate(els):
        chunks = []
        groups = []
        for bi_, b in enumerate(bs):
            o = bi_ * 544
            chunks += [(el * 2 + b, sc, o + sc * 128, 128, sc * 128) for sc in range(4)] + [(4 + el * 2 + b, 0, o + 512, 32, 512)]
            groups += [(o, 512), (o + 512, 32)]
        for (r, sc, s0, m, src0) in chunks:
            xb = ixs % 2
            ixs += 1
            b = r % 2
            P.dma("sp", lambda e, xb=xb, el=el, b=b, src0=src0, m=m: e.dma_start(out=xs[xb][0:m, :], in_=h_d[el * 2 + b, src0:src0 + m, :]),
                  writes=[t_xs[xb]])
            for k4 in range(KC // 4):
                pb = ipt % 2
                ipt += 1
                for kk in range(4):
                    k = k4 * 4 + kk
                    P.op("pe", lambda e, pb=pb, kk=kk, xb=xb, k=k, m=m: e.transpose(out=pT[pb][:, kk * 128:kk * 128 + m], in_=xs[xb][0:m, k * 128:(k + 1) * 128],
                                                                               identity=ident[0:m, 0:m]),
                         reads=[t_xs[xb], t_id], writes=[t_pT[pb]])
                if pb == 0:
                    P.op("act", lambda e, pb=pb, k4=k4, s0=s0, m=m: e.copy(out=xsT[:, k4 * 4:k4 * 4 + 4, s0:s0 + m],
                                                                         in_=pT[pb][:].rearrange("p (a c) -> p a c", a=4)[:, :, 0:m]),
                         reads=[t_pT[pb]], writes=[t_xsT])
                else:
                    P.op("dve", lambda e, pb=pb, k4=k4, s0=s0, m=m: e.tensor_copy(out=xsT[:, k4 * 4:k4 * 4 + 4, s0:s0 + m],
                                                                                in_=pT[pb][:].rearrange("p (a c) -> p a c", a=4)[:, :, 0:m]),
                         reads=[t_pT[pb]], writes=[t_xsT])
        for fc in range(KC):
            wb = iw % NWB
            iw += 1
            P.dma("pool", lambda e, wb=wb, eli=eli, fc=fc: e.dma_start(out=wa[wb][:], in_=w1_d[eli, :, fc * 128:(fc + 1) * 128].rearrange("(k p) c -> p k c", p=128)),
                  writes=[t_wa[wb]])
            P.dma("pool", lambda e, wb=wb, eli=eli, fc=fc: e.dma_start(out=wu[wb][:], in_=w3_d[eli, :, fc * 128:(fc + 1) * 128].rearrange("(k p) c -> p k c", p=128)),
                  writes=[t_wu[wb]])
            for (g0, gw) in groups:
                ab = iau % 2
                iau += 1
                for k in range(KC):
                    P.op("pe", lambda e, ab=ab, wb=wb, k=k, g0=g0, gw=gw: e.matmul(pa[ab][:, 0:gw], lhsT=wa[wb][:, k, :], rhs=xsT[:, k, g0:g0 + gw],
                                                                                start=(k == 0), stop=(k == KC - 1)),
                         reads=[t_wa[wb], t_xsT], writes=[t_pa[ab]])
                for k in range(KC):
                    P.op("pe", lambda e, ab=ab, wb=wb, k=k, g0=g0, gw=gw: e.matmul(pu[ab][:, 0:gw], lhsT=wu[wb][:, k, :], rhs=xsT[:, k, g0:g0 + gw],
                                                                                start=(k == 0), stop=(k == KC - 1)),
                         reads=[t_wu[wb], t_xsT], writes=[t_pu[ab]])
                P.op("act", lambda e, ab=ab, gw=gw: e.activation(out=tmp[ab][:, 0:gw], in_=pa[ab][:, 0:gw], func=AF.Silu), reads=[t_pa[ab]], writes=[t_tmp[ab]])
                P.op("dve", lambda e, ab=ab, fc=fc, g0=g0, gw=gw: e.tensor_mul(out=hT[:, fc, g0:g0 + gw], in0=tmp[ab][:, 0:gw], in1=pu[ab][:, 0:gw]),
                     reads=[t_tmp[ab], t_pu[ab]], writes=[t_hT])
        for dc in range(4):
            w2b = iw2 % 2
            iw2 += 1
            for half in range(2):
                ks = slice(half * 8, half * 8 + 8)
                P.dma("pool", lambda e, w2b=w2b, eli=eli, dc=dc, ks=ks: e.dma_start(
                    out=w2c[w2b][:, ks, :], in_=w2_d[eli, :, dc * 512:(dc + 1) * 512].rearrange("(k p) c -> p k c", p=128)[:, ks, :]), writes=[t_w2[w2b]])
            for (r, sc, s0, m, src0) in chunks:
                yi = iy % 2
                iy += 1
                for fc in range(KC):
                    P.op("pe", lambda e, yi=yi, fc=fc, s0=s0, m=m, w2b=w2b: e.matmul(py[yi][0:m, :], lhsT=hT[:, fc, s0:s0 + m], rhs=w2c[w2b][:, fc, :],
                                                                                  start=(fc == 0), stop=(fc == KC - 1)),
                         reads=[t_hT, t_w2[w2b]], writes=[t_py[yi]])
                P.op("dve", lambda e, yi=yi, r=r, sc=sc, m=m: e.tensor_scalar(out=yb[yi][0:m, :], in0=py[yi][0:m, :], scalar1=gate[0:m, r, sc:sc + 1], scalar2=None, op0=ALU.mult),
                     reads=[t_py[yi], t_gate], writes=[t_yb[yi]])
                P.dma("pool", lambda e, yi=yi, dc=dc, r=r, sc=sc, m=m: e.indirect_dma_start(
                    out=f_d[dc][:, :], out_offset=bass.IndirectOffsetOnAxis(ap=idx[0:m, r, sc:sc + 1], axis=0), in_=yb[yi][0:m, :], in_offset=None, compute_op=ALU.add),
                    reads=[t_idx, t_yb[yi]], writes=[t_f[dc], t_yb[yi]], is_out=True)
    return P


def stage_expert(sel, w1_l, w3_l, w2_l, els=(0, 1), bs=(0, 1)):
    P = build_expert(els, bs)
    in_maps = []
    for c in range(NCORES):
        in_maps.append({"xsc": sel[c][2], "idx": sel[c][0], "gate": sel[c][1], "w1": np.ascontiguousarray(w1_l[[2 * c + e for e in els]]),
                        "w3": np.ascontiguousarray(w3_l[[2 * c + e for e in els]]), "w2": np.ascontiguousarray(w2_l[[2 * c + e for e in els]]), "ident": IDENT})
    res = run_prog(P, in_maps)
    return [np.stack([r["f%d" % dc] for dc in range(4)]) for r in res]


def stage_final(fparts, x_lat, x_ctx, gate_lat, gate_ctx, gain, bias):
    P = build_outproj(False, NCORES)
    in_maps = []
    for c in range(NCORES):
        b, q = divmod(c, 4)
        ys = []
        for fp in fparts:
            lat = fp[:, b * SEQ + q * 1024:b * SEQ + (q + 1) * 1024, :]
            ctx = fp[:, B * SEQ + b * CTX + q * 64:B * SEQ + b * CTX + (q + 1) * 64, :]
            y = np.concatenate([lat, ctx], axis=1)
            ys.append(y.transpose(1, 0, 2).reshape(NT_B, D))
        cst = np.stack([bc(gate_lat[b]), bc(gate_ctx), bc(gain), bc(bias)], axis=0)
        in_maps.append({"x": tok_shard(x_lat, x_ctx, c), "cst": cst, "y": np.ascontiguousarray(np.stack(ys))})
    res = run_prog(P, in_maps)
    return tok_unshard(res, "o", D)


def kernel(x, c, ctx, c_ctx, w_ada, b_ada, w_in, w_out, gla_w_up, gla_b_up, gla_norm,
           gdn_conv, gdn_a_log, gdn_dt_bias, gdn_norm, attn_qk_norm, ln_gain, ln_bias,
           router, w1, w3, w2):
    f = lambda a: np.asarray(a, dtype=np.float32)
    x, c, ctx, c_ctx = f(x), f(c), f(ctx), f(c_ctx)
    w_ada, b_ada, w_in, w_out = f(w_ada), f(b_ada), f(w_in), f(w_out)
    gla_w_up, gla_b_up, gla_norm = f(gla_w_up), f(gla_b_up), f(gla_norm)
    gdn_conv, gdn_a_log, gdn_dt_bias, gdn_norm = f(gdn_conv), f(gdn_a_log), f(gdn_dt_bias), f(gdn_norm)
    attn_qk_norm, ln_gain, ln_bias, router = f(attn_qk_norm), f(ln_gain), f(ln_bias), f(router)
    w1, w3, w2 = f(w1), f(w3), f(w2)
    mod_lat, mod_ctx = stage_mod(c, c_ctx, w_ada, b_ada)
    x_lat, x_ctx = x, ctx
    for l in range(DEPTH):
        p_lat, p_ctx = stage_inproj(x_lat, x_ctx, mod_lat[l], mod_ctx[l], w_in[l])
        gla_l, gla_c = stage_gla(p_lat, p_ctx, gla_w_up[l], gla_b_up[l], gla_norm[l])
        gdn_l, gdn_c = stage_gdn(p_lat, p_ctx, gdn_conv[l], gdn_a_log[l], gdn_dt_bias[l], gdn_norm[l])
        att_l, att_c = stage_attn(p_lat, p_ctx, attn_qk_norm[l])
        mix_l = np.concatenate([gla_l, gdn_l, att_l], axis=-1)
        mix_c = np.concatenate([gla_c, gdn_c, att_c], axis=-1)
        x_lat, x_ctx = stage_outproj(mix_l, mix_c, x_lat, x_ctx, mod_lat[l][:, 2], mod_ctx[l][2], ln_gain[l, 0], ln_bias[l, 0], w_out[l])
        h_lat, h_ctx, a_lat, a_ctx = stage_router(x_lat, x_ctx, mod_lat[l], mod_ctx[l], router[l])
        sel = stage_select(a_lat, a_ctx, h_lat, h_ctx)
        fparts = stage_expert(sel, w1[l], w3[l], w2[l])
        x_lat, x_ctx = stage_final(fparts, x_lat, x_ctx, mod_lat[l][:, 5], mod_ctx[l][5], ln_gain[l, 1], ln_bias[l, 1])
    return np.ascontiguousarray(x_lat, dtype=np.float32)
```

```python
import os
import time
import numpy as np
import concourse.bass as bass
import concourse.mybir as mybir
from concourse.bass_utils import run_bass_kernel_spmd

F32 = mybir.dt.float32
BF16 = mybir.dt.bfloat16
U32 = mybir.dt.uint32
I32 = mybir.dt.int32
AF = mybir.ActivationFunctionType
ALU = mybir.AluOpType
AX = mybir.AxisListType

NCORES = 8
D = 2048
B = 2
SEQ = 4096
CTX = 256
DEPTH = 2
NE = 16
FF = 2048
IN_W = 5168
ALPHA = (2 * DEPTH) ** 0.25


class Tok:
    __slots__ = ("w", "r", "excl")

    def __init__(self, excl=False):
        self.w = None
        self.r = []
        self.excl = excl


class Prog:
    CE = ("act", "pe", "dve", "pool")
    DQ = ("sp", "act", "pool")
    ND = 6

    def __init__(self):
        self.nc = bass.Bass("TRN2", target_bir_lowering=False)
        nc = self.nc
        self.q = {e: [] for e in ("sp", "act", "pe", "dve", "pool")}
        self.csem = {e: nc.alloc_semaphore("c_" + e) for e in self.CE}
        self.ccnt = {e: 0 for e in self.CE}
        self.dsem = {e: [nc.alloc_semaphore("d_%s%d" % (e, i)) for i in range(self.ND)] for e in self.DQ}
        self.dcnt = {e: [0] * self.ND for e in self.DQ}
        self.drr = {e: 0 for e in self.DQ}
        self.waited = {e: {} for e in self.q}
        self.out_deps = []
        self.nm = 0

    def name(self, p):
        self.nm += 1
        return "%s_%d" % (p, self.nm)

    def sb(self, shape, dt=F32, name="sb"):
        return self.nc.alloc_sbuf_tensor(self.name(name), list(shape), dt)

    def ps(self, shape, dt=F32, name="ps"):
        return self.nc.alloc_psum_tensor(self.name(name), list(shape), dt)

    def din(self, name, shape, dt=F32):
        return self.nc.dram_tensor(name, list(shape), dt, kind="ExternalInput").ap()

    def dout(self, name, shape, dt=F32):
        return self.nc.dram_tensor(name, list(shape), dt, kind="ExternalOutput").ap()

    def dscratch(self, name, shape, dt=F32):
        return self.nc.dram_tensor(name, list(shape), dt, kind="Internal").ap()

    def _deps(self, eng, reads, writes, extra=()):
        need = {}

        def add(dep):
            if dep is None:
                return
            s, v = dep
            if need.get(s, 0) < v:
                need[s] = v

        own = self.csem.get(eng)
        for t in reads:
            add(t.w)
            if t.excl:
                for r in t.r:
                    if r[0] is not own:
                        add(r)
        for t in writes:
            add(t.w)
            for r in t.r:
                add(r)
        for d in extra:
            add(d)
        if eng == "pe":
            need.pop(self.csem["pe"], None)
        out = []
        wd = self.waited[eng]
        for s, v in need.items():
            if wd.get(s, 0) >= v:
                continue
            wd[s] = v
            out.append((s, v))
        return out

    def _mark(self, reads, writes, done):
        for t in reads:
            t.r.append(done)
        for t in writes:
            t.w = done
            t.r = []

    def op(self, eng, fn, reads=(), writes=()):
        waits = self._deps(eng, reads, writes)
        self.ccnt[eng] += 1
        done = (self.csem[eng], self.ccnt[eng])
        self.q[eng].append((waits, fn, self.csem[eng], 1))
        self._mark(reads, writes, done)
        return done

    def dma(self, eng, fn, reads=(), writes=(), is_out=False):
        k = self.drr[eng]
        self.drr[eng] = (k + 1) % self.ND
        sem = self.dsem[eng][k]
        prev = (sem, self.dcnt[eng][k]) if self.dcnt[eng][k] else None
        waits = self._deps(eng, reads, writes, extra=(prev,) if prev else ())
        self.dcnt[eng][k] += 16
        done = (sem, self.dcnt[eng][k])
        self.q[eng].append((waits, fn, sem, 16))
        self._mark(reads, writes, done)
        if is_out:
            self.out_deps.append(done)
        return done

    def finish(self):
        nc = self.nc
        fin = {}
        for s, v in self.out_deps:
            fin[s] = max(fin.get(s, 0), v)
        q = self.q
        engmap = {"sp": "sync", "act": "scalar", "pe": "tensor", "dve": "vector", "pool": "gpsimd"}
        with nc.Block() as block:
            for e, bn in engmap.items():
                def body(eng, e=e):
                    for waits, fn, sem, inc in q[e]:
                        for s, v in waits:
                            eng.wait_ge(s, v)
                        fn(eng).then_inc(sem, inc)
                    if e == "pool":
                        for s, v in fin.items():
                            eng.wait_ge(s, v)
                getattr(block, bn)(body)
        return nc


def run_prog(P, in_maps):
    t0 = time.time()
    nc = P.finish()
    t1 = time.time()
    res = run_bass_kernel_spmd(nc, in_maps, core_ids=list(range(NCORES)))
    if os.environ.get("KDBG"):
        nb = sum(v.nbytes for m in in_maps for v in m.values())
        print("[run_prog] build %.1fs run %.1fs in %.0fMB" % (t1 - t0, time.time() - t1, nb / 1e6), flush=True)
    return res.results


def fm(a):
    T, C = a.shape
    return np.ascontiguousarray(a.T.reshape(C // 128, 128, T).transpose(1, 0, 2))


NT_B = 1088


def build_inproj():
    P = Prog()
    nc = P.nc
    KC = D // 128
    xT_d = P.din("xT", [128, KC, NT_B])
    mod_d = P.din("modv", [128, KC, 4])
    w_d = P.din("w", [D, IN_W]).rearrange("(k p) c -> p k c", p=128)
    p_d = P.dout("p", [NT_B, IN_W])

    xT = P.sb([128, KC, NT_B], name="xT")
    xb = P.sb([128, KC, NT_B], BF16, name="xb")
    modv = P.sb([128, KC, 4], name="modv")
    t_x = [Tok() for _ in range(KC)]
    t_mod = Tok()
    for k in range(KC):
        P.dma("sp" if k % 2 == 0 else "act", lambda e, k=k: e.dma_start(out=xT[:, k, :], in_=xT_d[:, k, :]), writes=[t_x[k]])
    P.dma("sp", lambda e: e.dma_start(out=modv[:], in_=mod_d), writes=[t_mod])
    P.op("dve", lambda e: e.tensor_scalar_add(out=modv[:, :, 1:2], in0=modv[:, :, 1:2], scalar1=1.0), reads=[t_mod], writes=[t_mod])
    P.op("dve", lambda e: e.tensor_scalar_add(out=modv[:, :, 3:4], in0=modv[:, :, 3:4], scalar1=1.0), reads=[t_mod], writes=[t_mod])
    for k in range(KC):
        P.op("dve", lambda e, k=k: e.tensor_scalar(out=xb[:, k, 0:1024], in0=xT[:, k, 0:1024], scalar1=modv[:, k, 1:2],
                                                   scalar2=modv[:, k, 0:1], op0=ALU.mult, op1=ALU.add),
             reads=[t_mod, t_x[k]], writes=[t_x[k]])
        P.op("dve", lambda e, k=k: e.tensor_scalar(out=xb[:, k, 1024:NT_B], in0=xT[:, k, 1024:NT_B], scalar1=modv[:, k, 3:4],
                                                   scalar2=modv[:, k, 2:3], op0=ALU.mult, op1=ALU.add),
             reads=[t_mod, t_x[k]], writes=[t_x[k]])
    NW = 3
    wt = [P.sb([128, KC, 512], BF16, name="wt") for _ in range(NW)]
    t_w = [Tok() for _ in range(NW)]
    NPS = 4
    pst = [P.ps([128, 512], name="pp") for _ in range(NPS)]
    t_ps = [Tok() for _ in range(NPS)]
    ot = [P.sb([128, 512], name="ot") for _ in range(NPS)]
    t_ot = [Tok() for _ in range(NPS)]
    tiles = [(i * 128, 128) for i in range(8)] + [(1024, 64)]
    cgs = [(c, min(512, IN_W - c)) for c in range(0, IN_W, 512)]
    it = 0
    for ci, (c0, cw) in enumerate(cgs):
        wb = ci % NW
        for half in range(2):
            ks = slice(half * 8, half * 8 + 8)
            P.dma("pool",
                  lambda e, wb=wb, ks=ks, c0=c0, cw=cw: e.dma_start(out=wt[wb][:, ks, 0:cw], in_=w_d[:, ks, c0:c0 + cw]),
                  writes=[t_w[wb]])
        for (t0, m) in tiles:
            pb = it % NPS
            it += 1
            for k in range(KC):
                P.op("pe", lambda e, pb=pb, k=k, t0=t0, m=m, wb=wb, cw=cw: e.matmul(
                    pst[pb][0:m, 0:cw], lhsT=xb[:, k, t0:t0 + m], rhs=wt[wb][:, k, 0:cw], start=(k == 0), stop=(k == KC - 1)),
                    reads=[t_x[k], t_w[wb]], writes=[t_ps[pb]])
            ev = "act" if pb % 2 == 0 else "dve"
            if ev == "act":
                P.op("act", lambda e, pb=pb, m=m, cw=cw: e.copy(out=ot[pb][0:m, 0:cw], in_=pst[pb][0:m, 0:cw]),
                     reads=[t_ps[pb]], writes=[t_ot[pb]])
            else:
                P.op("dve", lambda e, pb=pb, m=m, cw=cw: e.tensor_copy(out=ot[pb][0:m, 0:cw], in_=pst[pb][0:m, 0:cw]),
                     reads=[t_ps[pb]], writes=[t_ot[pb]])
            P.dma("sp" if pb % 2 == 0 else "act", lambda e, pb=pb, t0=t0, m=m, c0=c0, cw=cw: e.dma_start(out=p_d[t0:t0 + m, c0:c0 + cw], in_=ot[pb][0:m, 0:cw]),
                  reads=[t_ot[pb]], is_out=True)
    return P


def stage_inproj(x_lat, x_ctx, mod_lat, mod_ctx, w_in_l):
    P = build_inproj()
    in_maps = []
    for c in range(NCORES):
        b, q = divmod(c, 4)
        xs = np.concatenate([x_lat[b, q * 1024:(q + 1) * 1024], x_ctx[b, q * 64:(q + 1) * 64]], axis=0)
        mv = np.stack([mod_lat[b, 0], mod_lat[b, 1], mod_ctx[0], mod_ctx[1]], axis=-1)
        mv = np.ascontiguousarray(mv.reshape(D // 128, 128, 4).transpose(1, 0, 2))
        in_maps.append({"xT": fm(xs), "modv": mv, "w": np.ascontiguousarray(w_in_l)})
    res = run_prog(P, in_maps)
    p_lat = np.empty((B, SEQ, IN_W), np.float32)
    p_ctx = np.empty((B, CTX, IN_W), np.float32)
    for c in range(NCORES):
        b, q = divmod(c, 4)
        p_lat[b, q * 1024:(q + 1) * 1024] = res[c]["p"][:1024]
        p_ctx[b, q * 64:(q + 1) * 64] = res[c]["p"][1024:]
    return p_lat, p_ctx


MODW = 6 * D // NCORES


def build_mod():
    P = Prog()
    KC = D // 128
    cv_d = P.din("cv", [128, KC, 3])
    wa_d = P.din("wa", [DEPTH, D, MODW]).rearrange("l (k p) c -> l p k c", p=128)
    ba_d = P.din("ba", [DEPTH, 1, MODW])
    mod_d = P.dout("mod", [DEPTH, 3, MODW])
    cv = P.sb([128, KC, 3], name="cv")
    ones = P.sb([1, 4], name="ones")
    ba = P.sb([1, DEPTH, MODW], name="ba")
    t_cv, t_ones, t_ba = Tok(), Tok(), Tok()
    P.dma("sp", lambda e: e.dma_start(out=cv[:], in_=cv_d), writes=[t_cv])
    for l in range(DEPTH):
        P.dma("sp", lambda e, l=l: e.dma_start(out=ba[:, l, :], in_=ba_d[l]), writes=[t_ba])
    P.op("dve", lambda e: e.memset(ones[:], 1.0), writes=[t_ones])
    P.op("act", lambda e: e.activation(out=cv[:], in_=cv[:], func=AF.Silu), reads=[t_cv], writes=[t_cv])
    wt = [P.sb([128, KC, 512], name="wa") for _ in range(2)]
    t_w = [Tok(), Tok()]
    pst = [P.ps([128, 512], name="pm") for _ in range(2)]
    t_ps = [Tok(), Tok()]
    ot = [P.sb([4, 512], name="om") for _ in range(2)]
    t_ot = [Tok(), Tok()]
    it = 0
    for l in range(DEPTH):
        for c0 in range(0, MODW, 512):
            bi = it % 2
            it += 1
            for half in range(2):
                ks = slice(half * 8, half * 8 + 8)
                P.dma("sp" if half == 0 else "act",
                      lambda e, bi=bi, ks=ks, c0=c0, l=l: e.dma_start(out=wt[bi][:, ks, :], in_=wa_d[l, :, ks, c0:c0 + 512]),
                      writes=[t_w[bi]])
            for k in range(KC):
                P.op("pe", lambda e, bi=bi, k=k: e.matmul(pst[bi][0:3, :], lhsT=cv[:, k, :], rhs=wt[bi][:, k, :], start=(k == 0), stop=False),
                     reads=[t_cv, t_w[bi]], writes=[t_ps[bi]])
            P.op("pe", lambda e, bi=bi, l=l, c0=c0: e.matmul(pst[bi][0:3, :], lhsT=ones[0:1, 0:3], rhs=ba[0:1, l, c0:c0 + 512], start=False, stop=True),
                 reads=[t_ones, t_ba], writes=[t_ps[bi]])
            P.op("dve", lambda e, bi=bi: e.tensor_copy(out=ot[bi][0:3, :], in_=pst[bi][0:3, :]), reads=[t_ps[bi]], writes=[t_ot[bi]])
            P.dma("pool", lambda e, bi=bi, l=l, c0=c0: e.dma_start(out=mod_d[l, :, c0:c0 + 512], in_=ot[bi][0:3, :]), reads=[t_ot[bi]], is_out=True)
    return P


def stage_mod(c, c_ctx, w_ada, b_ada):
    P = build_mod()
    vec = np.concatenate([c, c_ctx[None]], axis=0)
    cv = np.ascontiguousarray(vec.T.reshape(D // 128, 128, 3).transpose(1, 0, 2))
    in_maps = []
    for ci in range(NCORES):
        cs = slice(ci * MODW, (ci + 1) * MODW)
        in_maps.append({"cv": cv, "wa": np.ascontiguousarray(w_ada[:, :, cs]), "ba": np.ascontiguousarray(b_ada[:, None, cs])})
    res = run_prog(P, in_maps)
    mod = np.concatenate([res[ci]["mod"] for ci in range(NCORES)], axis=-1)
    mod = mod.reshape(DEPTH, 3, 6, D)
    return np.ascontiguousarray(mod[:, 0:2]), np.ascontiguousarray(mod[:, 2])


def ln_tile(P, z, t_z, m, gain, bias, t_c, st, t_st, outt, t_out):
    s1, mu, ss, rstd = st[:, 0:1], st[:, 1:2], st[:, 2:3], st[:, 3:4]
    P.op("dve", lambda e: e.reduce_sum(out=s1[0:m], in_=z[0:m, :], axis=AX.X), reads=[t_z], writes=[t_st])
    P.op("dve", lambda e: e.tensor_scalar_mul(out=mu[0:m], in0=s1[0:m], scalar1=1.0 / D), reads=[t_st], writes=[t_st])
    P.op("dve", lambda e: e.tensor_scalar(out=z[0:m, :], in0=z[0:m, :], scalar1=mu[0:m], scalar2=None, op0=ALU.subtract),
         reads=[t_st, t_z], writes=[t_z])
    P.op("act", lambda e: e.activation(out=outt[0:m, :], in_=z[0:m, :], func=AF.Square, accum_out=ss[0:m]),
         reads=[t_z], writes=[t_out, t_st])
    P.op("dve", lambda e: e.tensor_scalar(out=ss[0:m], in0=ss[0:m], scalar1=1.0 / D, scalar2=1e-5, op0=ALU.mult, op1=ALU.add),
         reads=[t_st], writes=[t_st])
    P.op("act", lambda e: e.activation(out=ss[0:m], in_=ss[0:m], func=AF.Sqrt), reads=[t_st], writes=[t_st])
    P.op("dve", lambda e: e.reciprocal(out=rstd[0:m], in_=ss[0:m]), reads=[t_st], writes=[t_st])
    P.op("dve", lambda e: e.scalar_tensor_tensor(out=outt[0:m, :], in0=z[0:m, :], scalar=rstd[0:m], in1=gain[0:m, :],
                                                 op0=ALU.mult, op1=ALU.mult), reads=[t_st, t_z, t_c], writes=[t_out])
    P.op("dve", lambda e: e.tensor_add(out=outt[0:m, :], in0=outt[0:m, :], in1=bias[0:m, :]), reads=[t_c, t_out], writes=[t_out])


def build_outproj(with_proj=True, nparts=1):
    P = Prog()
    KC = D // 128
    NT = NT_B
    x_d = P.din("x", [NT, D])
    cst_d = P.din("cst", [4, 128, D])
    if with_proj:
        mT_d = P.din("mT", [128, KC, NT])
        w_d = P.din("w", [D, D]).rearrange("(k p) c -> p k c", p=128)
    else:
        y_d = P.din("y", [nparts, NT, D])
    o_d = P.dout("o", [NT, D])
    cst = P.sb([128, 4, D], name="cst")
    t_c = Tok()
    for i in range(4):
        P.dma("sp", lambda e, i=i: e.dma_start(out=cst[:, i, :], in_=cst_d[i]), writes=[t_c])
    if with_proj:
        w = P.sb([128, KC, D], BF16, name="w")
        t_wc = [Tok() for _ in range(4)]
        for cg in range(4):
            for half in range(2):
                ks = slice(half * 8, half * 8 + 8)
                P.dma("pool", lambda e, cg=cg, ks=ks: e.dma_start(out=w[:, ks, cg * 512:(cg + 1) * 512], in_=w_d[:, ks, cg * 512:(cg + 1) * 512]), writes=[t_wc[cg]])
        mT = [P.sb([128, KC, 128], BF16, name="mT") for _ in range(2)]
        t_m = [Tok(), Tok()]
        pst = [P.ps([128, 512], name="po") for _ in range(4)]
        t_ps = [Tok() for _ in range(4)]
    else:
        NY = 6
        yt = [P.sb([128, D], name="yt") for _ in range(NY)]
        t_y = [Tok() for _ in range(NY)]
        iy_ = [0]
    xt = [P.sb([128, D], name="xt") for _ in range(2)]
    t_x = [Tok(), Tok()]
    zt = P.sb([128, D], name="zt")
    t_z = Tok()
    st = P.sb([128, 4], name="st")
    t_st = Tok()
    tiles = [(i * 128, 128) for i in range(8)] + [(1024, 64)]
    for ti, (t0, m) in enumerate(tiles):
        bi = ti % 2
        gi = 0 if ti < 8 else 1
        P.dma("sp", lambda e, bi=bi, t0=t0, m=m: e.dma_start(out=xt[bi][0:m, :], in_=x_d[t0:t0 + m, :]), writes=[t_x[bi]])
        if with_proj:
            P.dma("pool", lambda e, bi=bi, t0=t0, m=m: e.dma_start(out=mT[bi][:, :, 0:m], in_=mT_d[:, :, t0:t0 + m]), writes=[t_m[bi]])
            for cg in range(4):
                for k in range(KC):
                    P.op("pe", lambda e, bi=bi, cg=cg, k=k, m=m: e.matmul(pst[cg][0:m, :], lhsT=mT[bi][:, k, 0:m], rhs=w[:, k, cg * 512:(cg + 1) * 512],
                                                                      start=(k == 0), stop=(k == KC - 1)),
                         reads=[t_m[bi], t_wc[cg]], writes=[t_ps[cg]])
                P.op("dve", lambda e, cg=cg, m=m, gi=gi: e.tensor_mul(out=zt[0:m, cg * 512:(cg + 1) * 512], in0=pst[cg][0:m, :],
                                                                     in1=cst[0:m, gi, cg * 512:(cg + 1) * 512]),
                     reads=[t_ps[cg], t_c], writes=[t_z])
        else:
            for pi in range(nparts):
                yb_ = iy_[0] % NY
                iy_[0] += 1
                P.dma("act" if yb_ % 2 == 0 else "sp", lambda e, yb_=yb_, t0=t0, m=m, pi=pi: e.dma_start(out=yt[yb_][0:m, :], in_=y_d[pi, t0:t0 + m, :]), writes=[t_y[yb_]])
                if pi == 0:
                    P.op("dve", lambda e, yb_=yb_, m=m: e.tensor_copy(out=zt[0:m, :], in_=yt[yb_][0:m, :]), reads=[t_y[yb_]], writes=[t_z])
                else:
                    P.op("dve", lambda e, yb_=yb_, m=m: e.tensor_add(out=zt[0:m, :], in0=zt[0:m, :], in1=yt[yb_][0:m, :]), reads=[t_y[yb_], t_z], writes=[t_z])
            P.op("dve", lambda e, m=m, gi=gi: e.tensor_mul(out=zt[0:m, :], in0=zt[0:m, :], in1=cst[0:m, gi, :]), reads=[t_c, t_z], writes=[t_z])
        P.op("dve", lambda e, bi=bi, m=m: e.scalar_tensor_tensor(out=zt[0:m, :], in0=xt[bi][0:m, :], scalar=float(ALPHA), in1=zt[0:m, :],
                                                                 op0=ALU.mult, op1=ALU.add), reads=[t_x[bi], t_z], writes=[t_z])
        ln_tile(P, zt, t_z, m, cst[:, 2, :], cst[:, 3, :], t_c, st, t_st, xt[bi], t_x[bi])
        P.dma("act", lambda e, bi=bi, t0=t0, m=m: e.dma_start(out=o_d[t0:t0 + m, :], in_=xt[bi][0:m, :]), reads=[t_x[bi]], writes=[t_x[bi]], is_out=True)
    return P


def tok_shard(lat, ctx, c):
    b, q = divmod(c, 4)
    return np.concatenate([lat[b, q * 1024:(q + 1) * 1024], ctx[b, q * 64:(q + 1) * 64]], axis=0)


def tok_unshard(res, key, width):
    lat = np.empty((B, SEQ, width), np.float32)
    ctx = np.empty((B, CTX, width), np.float32)
    for c in range(NCORES):
        b, q = divmod(c, 4)
        lat[b, q * 1024:(q + 1) * 1024] = res[c][key][:1024]
        ctx[b, q * 64:(q + 1) * 64] = res[c][key][1024:]
    return lat, ctx


def bc(v):
    return np.ascontiguousarray(np.broadcast_to(v[None, :], (128, v.shape[0])))


def stage_outproj(mix_lat, mix_ctx, x_lat, x_ctx, gate_lat, gate_ctx, gain, bias, w_out_l):
    P = build_outproj(True)
    in_maps = []
    for c in range(NCORES):
        b = c // 4
        cst = np.stack([bc(gate_lat[b]), bc(gate_ctx), bc(gain), bc(bias)], axis=0)
        in_maps.append({"x": tok_shard(x_lat, x_ctx, c), "cst": cst, "mT": fm(tok_shard(mix_lat, mix_ctx, c)),
                        "w": np.ascontiguousarray(w_out_l)})
    res = run_prog(P, in_maps)
    return tok_unshard(res, "o", D)


NTOK = CTX + SEQ
NTT = NTOK // 128
IDENT = np.eye(128, dtype=np.float32)


def rms_rstd(P, x_ap, m, width, eps, junk, t_junk, st, t_st, reads):
    P.op("act", lambda e: e.activation(out=junk[0:m, 0:width], in_=x_ap, func=AF.Square, accum_out=st[0:m, 0:1]),
         reads=reads, writes=[t_junk, t_st])
    P.op("dve", lambda e: e.tensor_scalar(out=st[0:m, 0:1], in0=st[0:m, 0:1], scalar1=1.0 / width, scalar2=eps, op0=ALU.mult, op1=ALU.add),
         reads=[t_st], writes=[t_st])
    P.op("act", lambda e: e.activation(out=st[0:m, 0:1], in_=st[0:m, 0:1], func=AF.Sqrt), reads=[t_st], writes=[t_st])
    P.op("dve", lambda e: e.reciprocal(out=st[0:m, 1:2], in_=st[0:m, 0:1]), reads=[t_st], writes=[t_st])


def build_attn():
    P = Prog()
    q_d = P.din("q", [NTOK, 256])
    k_d = P.din("k", [NTOK, 128])
    v_d = P.din("v", [NTOK, 128])
    g_d = P.din("g", [2, 128, 128])
    cs_d = P.din("cs", [2, SEQ, 128])
    id_d = P.din("ident", [128, 128])
    o_d = P.dout("o", [NTOK, 256])

    ident = P.sb([128, 128], name="ident")
    gq = P.sb([128, 2, 128], name="gq")
    t_id, t_g = Tok(), Tok()
    P.dma("sp", lambda e: e.dma_start(out=ident[:], in_=id_d), writes=[t_id])
    for i in range(2):
        P.dma("sp", lambda e, i=i: e.dma_start(out=gq[:, i, :], in_=g_d[i]), writes=[t_g])
    qT = P.sb([128, 2, NTOK], BF16, name="qT")
    kT = P.sb([128, NTOK], BF16, name="kT")
    va = P.sb([128, NTT, 129], BF16, name="va")
    t_qT, t_kT, t_va = Tok(), Tok(), Tok()
    P.op("pool", lambda e: e.memset(va[:, :, 128:129], 1.0), writes=[t_va])
    for half in range(2):
        hs = slice(half * 17, half * 17 + 17)
        P.dma("pool", lambda e, hs=hs, half=half: e.dma_start(out=va[:, hs, 0:128],
              in_=v_d[half * 17 * 128:(half + 1) * 17 * 128, :].rearrange("(n p) d -> p n d", p=128)), writes=[t_va])
    psT = [P.ps([128, 512], name="psT") for _ in range(2)]
    t_psT = [Tok(True), Tok(True)]

    def prep_lane(li):
        xin = P.sb([128, 3, 128], name="xin")
        cst = P.sb([128, 2, 128], name="cs")
        sq = P.sb([128, 3, 128], name="sq")
        xn = P.sb([128, 3, 128], name="xn")
        t1 = P.sb([128, 3, 128], name="t1")
        t2 = P.sb([128, 3, 128], name="t2")
        st = P.sb([128, 2, 3], name="st")
        t_xin, t_cs, t_sq, t_xn, t_t1, t_t2, t_st = [Tok() for _ in range(7)]
        for tt in range(li, NTT, 4):
            t0 = tt * 128
            lat = tt >= 2
            P.dma("sp", lambda e, t0=t0: e.dma_start(out=xin[:, 0:2, :], in_=q_d[t0:t0 + 128, :].rearrange("p (h d) -> p h d", h=2)), writes=[t_xin])
            P.dma("sp", lambda e, t0=t0: e.dma_start(out=xin[:, 2, :], in_=k_d[t0:t0 + 128, :]), writes=[t_xin])
            if lat:
                for i in range(2):
                    P.dma("act", lambda e, t0=t0, i=i: e.dma_start(out=cst[:, i, :], in_=cs_d[i, t0 - CTX:t0 - CTX + 128, :]), writes=[t_cs])
            yield
            P.op("act", lambda e: e.activation(out=sq[:], in_=xin[:], func=AF.Square), reads=[t_xin], writes=[t_sq])
            yield
            P.op("dve", lambda e: e.reduce_sum(out=st[:, 0, :], in_=sq[:], axis=AX.X), reads=[t_sq], writes=[t_st])
            yield
            P.op("dve", lambda e: e.tensor_scalar(out=st[:, 0, :], in0=st[:, 0, :], scalar1=1.0 / 128, scalar2=1e-6, op0=ALU.mult, op1=ALU.add), reads=[t_st], writes=[t_st])
            yield
            P.op("act", lambda e: e.activation(out=st[:, 0, :], in_=st[:, 0, :], func=AF.Sqrt), reads=[t_st], writes=[t_st])
            yield
            P.op("dve", lambda e: e.reciprocal(out=st[:, 1, :], in_=st[:, 0, :]), reads=[t_st], writes=[t_st])
            yield
            for h in range(3):
                gi = 0 if h < 2 else 1
                P.op("dve", lambda e, h=h, gi=gi: e.scalar_tensor_tensor(out=xn[:, h, :], in0=xin[:, h, :], scalar=st[:, 1, h:h + 1], in1=gq[:, gi, :], op0=ALU.mult, op1=ALU.mult),
                     reads=[t_xin, t_st, t_g], writes=[t_xn])
            yield
            src, t_src = xn, t_xn
            if lat:
                for h in range(3):
                    P.op("pool", lambda e, h=h: e.tensor_mul(out=t1[:, h, :], in0=xn[:, h, :], in1=cst[:, 0, :]), reads=[t_xn, t_cs], writes=[t_t1])
                x5 = xn[:].rearrange("p h (a b f) -> p h a b f", a=2, b=2)
                o5 = t2[:].rearrange("p h (a b f) -> p h a b f", a=2, b=2)
                s4 = cst[:, 1, :].rearrange("p (a b f) -> p a b f", a=2, b=2)
                for h in range(3):
                    for hb in range(2):
                        P.op("dve", lambda e, h=h, hb=hb: e.tensor_mul(out=o5[:, h, :, hb, :], in0=x5[:, h, :, 1 - hb, :], in1=s4[:, :, hb, :]),
                             reads=[t_xn, t_cs], writes=[t_t2])
                yield
                P.op("dve", lambda e: e.tensor_add(out=t1[:], in0=t1[:], in1=t2[:]), reads=[t_t2, t_t1], writes=[t_t1])
                yield
                src, t_src = t1, t_t1
            for h in range(3):
                P.op("pe", lambda e, h=h, src=src: e.transpose(out=psT[li % 2][:, h * 128:(h + 1) * 128], in_=src[:, h, :], identity=ident[:]),
                     reads=[t_id, t_src], writes=[t_psT[li % 2]])
            P.op("act", lambda e, t0=t0: e.copy(out=qT[:, :, t0:t0 + 128], in_=psT[li % 2][:, 0:256].rearrange("p (h t) -> p h t", h=2)), reads=[t_psT[li % 2]], writes=[t_qT])
            P.op("act", lambda e, t0=t0: e.copy(out=kT[:, t0:t0 + 128], in_=psT[li % 2][:, 256:384]), reads=[t_psT[li % 2]], writes=[t_kT])
            yield

    gens = [prep_lane(i) for i in range(4)]
    while gens:
        for g in list(gens):
            try:
                next(g)
            except StopIteration:
                gens.remove(g)
    NS = 2
    pss = [P.ps([128, 512], name="pss") for _ in range(NS)]
    t_pss = [Tok() for _ in range(NS)]
    NPT = 3
    pt = [P.sb([128, 512], BF16, name="pt") for _ in range(NPT)]
    t_pt = [Tok() for _ in range(NPT)]
    acc = [P.ps([128, 512], name="acc") for _ in range(4)]
    t_acc = [Tok() for _ in range(4)]
    ob = [P.sb([128, 128], name="ob") for _ in range(2)]
    t_ob = [Tok(), Tok()]
    rs = P.sb([128, 1], name="rs")
    t_rs = Tok()
    blocks = [(0, 256, 0, 2)] + [(CTX + i * 512, 512, 0, NTT) for i in range(SEQ // 512)]
    scale = 128 ** -0.5
    iters = [(h, q0, qw, kt0, kt1, kt) for h in range(2) for (q0, qw, kt0, kt1) in blocks for kt in range(kt0, kt1)]
    iob = 0

    def emit_qk(i):
        h, q0, qw, kt0, kt1, kt = iters[i]
        sb_ = i % NS
        P.op("pe", lambda e: e.matmul(pss[sb_][:, 0:qw], lhsT=kT[:, kt * 128:(kt + 1) * 128], rhs=qT[:, h, q0:q0 + qw], start=True, stop=True),
             reads=[t_kT, t_qT], writes=[t_pss[sb_]])

    emit_qk(0)
    for i, (h, q0, qw, kt0, kt1, kt) in enumerate(iters):
        nq = qw // 128
        sb_ = i % NS
        pb = i % NPT
        if i + 1 < len(iters):
            emit_qk(i + 1)
        P.op("act", lambda e, sb_=sb_, pb=pb, qw=qw: e.activation(out=pt[pb][:, 0:qw], in_=pss[sb_][:, 0:qw], func=AF.Exp, scale=scale),
             reads=[t_pss[sb_]], writes=[t_pt[pb]])
        for qi in range(nq):
            P.op("pe", lambda e, pb=pb, qi=qi, kt=kt, kt0=kt0, kt1=kt1: e.matmul(acc[qi][:, 0:129], lhsT=pt[pb][:, qi * 128:(qi + 1) * 128],
                                                                              rhs=va[:, kt, :], start=(kt == kt0), stop=(kt == kt1 - 1)),
                 reads=[t_pt[pb], t_va], writes=[t_acc[qi]])
        if kt == kt1 - 1:
            for qi in range(nq):
                P.op("dve", lambda e, qi=qi: e.reciprocal(out=rs[:], in_=acc[qi][:, 128:129]), reads=[t_acc[qi]], writes=[t_rs])
                oi = iob % 2
                iob += 1
                P.op("dve", lambda e, qi=qi, oi=oi: e.tensor_scalar(out=ob[oi][:], in0=acc[qi][:, 0:128], scalar1=rs[:], scalar2=None, op0=ALU.mult),
                     reads=[t_acc[qi], t_rs], writes=[t_ob[oi]])
                P.dma("pool", lambda e, oi=oi, q0=q0, qi=qi, h=h: e.dma_start(out=o_d[q0 + qi * 128:q0 + (qi + 1) * 128, h * 128:(h + 1) * 128], in_=ob[oi][:]),
                      reads=[t_ob[oi]], writes=[t_ob[oi]], is_out=True)
    return P


def rope_tables():
    rows = SEQ // 64
    row = np.repeat(np.arange(rows), 64).astype(np.float32)
    col = np.tile(np.arange(64), rows).astype(np.float32)
    inv = (10000.0 ** (-np.arange(0, 64, 2, dtype=np.float32) / 64)).astype(np.float32)
    ang = np.concatenate([row[:, None] * inv, col[:, None] * inv], axis=-1)
    cos, sin = np.cos(ang).astype(np.float32), np.sin(ang).astype(np.float32)
    C = np.empty((SEQ, 128), np.float32)
    S = np.empty((SEQ, 128), np.float32)
    for a in range(2):
        c_, s_ = cos[:, a * 32:(a + 1) * 32], sin[:, a * 32:(a + 1) * 32]
        C[:, a * 64:a * 64 + 32] = c_
        C[:, a * 64 + 32:a * 64 + 64] = c_
        S[:, a * 64:a * 64 + 32] = -s_
        S[:, a * 64 + 32:a * 64 + 64] = s_
    return np.stack([C, S], axis=0)


def stage_attn(p_lat, p_ctx, qk_gain_l):
    P = build_attn()
    cs = rope_tables()
    g = np.stack([bc(qk_gain_l[0]), bc(qk_gain_l[1])], axis=0)
    in_maps = []
    for c in range(NCORES):
        b, j = divmod(c, 4)
        pa = np.concatenate([p_ctx[b], p_lat[b]], axis=0)
        kv = j // 2
        in_maps.append({"q": np.ascontiguousarray(pa[:, 3632 + 256 * j:3632 + 256 * (j + 1)]),
                        "k": np.ascontiguousarray(pa[:, 4656 + 128 * kv:4656 + 128 * (kv + 1)]),
                        "v": np.ascontiguousarray(pa[:, 4912 + 128 * kv:4912 + 128 * (kv + 1)]),
                        "g": g, "cs": cs, "ident": IDENT})
    res = run_prog(P, in_maps)
    att_lat = np.empty((B, SEQ, 1024), np.float32)
    att_ctx = np.empty((B, CTX, 1024), np.float32)
    for c in range(NCORES):
        b, j = divmod(c, 4)
        att_ctx[b, :, 256 * j:256 * (j + 1)] = res[c]["o"][:CTX]
        att_lat[b, :, 256 * j:256 * (j + 1)] = res[c]["o"][CTX:]
    return att_lat, att_ctx


TRI_INC = np.triu(np.ones((128, 128), np.float32))
TRI_SUFEX = np.tril(np.ones((128, 128), np.float32), -1)
ANTI = np.ascontiguousarray(np.eye(128, dtype=np.float32)[::-1])


def orig_tile(tt):
    return (1 - tt) if tt < 2 else (35 - tt)


def build_gla():
    P = Prog()
    qT_d = P.din("qT", [64, NTOK])
    kT_d = P.din("kT", [64, NTOK])
    k_d = P.din("k", [NTOK, 64])
    v_d = P.din("v", [NTOK, 128])
    rT_d = P.din("rT", [2, 16, NTOK])
    w_d = P.din("w", [2, 17, 64])
    g_d = P.din("g", [NTOK, 128])
    gn_d = P.din("gn", [128, 128])
    c_d = P.din("cm", [4, 128, 128])
    o_d = P.dout("o", [NTOK, 128])

    cm = P.sb([128, 4, 128], name="cm")
    gn = P.sb([128, 128], name="gn")
    t_cm, t_gn = Tok(), Tok()
    for i in range(4):
        P.dma("sp", lambda e, i=i: e.dma_start(out=cm[:, i, :], in_=c_d[i]), writes=[t_cm])
    P.dma("sp", lambda e: e.dma_start(out=gn[:], in_=gn_d), writes=[t_gn])
    g = P.sb([128, NTT, 128], name="g")
    t_g = Tok()
    P.dma("act", lambda e: e.dma_start(out=g[:], in_=g_d.rearrange("(n p) d -> p n d", p=128)), writes=[t_g])
    P.op("act", lambda e: e.activation(out=g[:], in_=g[:], func=AF.Silu), reads=[t_g], writes=[t_g])
    qT = P.sb([64, NTOK], name="qT")
    kT = P.sb([64, NTOK], name="kT")
    kk = P.sb([128, NTT, 64], name="kk")
    vv = P.sb([128, NTT, 128], name="vv")
    t_in = Tok()
    P.dma("sp", lambda e: e.dma_start(out=qT[:], in_=qT_d), writes=[t_in])
    P.dma("act", lambda e: e.dma_start(out=kT[:], in_=kT_d), writes=[t_in])
    P.dma("act", lambda e: e.dma_start(out=kk[:], in_=k_d.rearrange("(n p) d -> p n d", p=128)), writes=[t_in])
    P.dma("sp", lambda e: e.dma_start(out=vv[:], in_=v_d.rearrange("(n p) d -> p n d", p=128)), writes=[t_in])
    odir = [P.sb([128, NTT, 128], name="odir") for _ in range(2)]
    t_odir = [Tok(), Tok()]

    def lane(z):
        C, sufx = (cm[:, 0, :], cm[:, 3, :]) if z == 0 else (cm[:, 1, :], cm[:, 2, :])
        mid, last = (63, 127) if z == 0 else (64, 0)
        rT = P.sb([17, NTOK], name="rT")
        wa = P.sb([17, 64], name="wa")
        t_r = Tok()
        P.op("dve", lambda e: e.memset(rT[:], 1.0), writes=[t_r])
        P.dma("sp", lambda e: e.dma_start(out=rT[0:16, :], in_=rT_d[z]), writes=[t_r])
        P.dma("act", lambda e: e.dma_start(out=wa[:], in_=w_d[z]), writes=[t_r])
        S = P.sb([64, 128], name="S")
        t_S = Tok()
        P.op("dve", lambda e: e.memset(S[:], 0.0), writes=[t_S])
        la = P.sb([128, 64], name="la")
        cs = P.sb([64, 128], name="cs")
        sc = P.sb([64, 4], name="sc")
        e1 = P.sb([64, 128], name="e1")
        e2 = P.sb([64, 128], name="e2")
        e3 = P.sb([64, 128], name="e3")
        k4 = P.sb([128, 64], name="k4")
        AT = P.sb([128, 128], name="AT")
        t_la, t_cs, t_sc, t_e1, t_e2, t_e3, t_k4, t_AT = [Tok() for _ in range(8)]
        b1 = P.ps([128, 512], name="b1")
        b2 = P.ps([128, 512], name="b2")
        b3 = P.ps([128, 512], name="b3")
        t1, t2, t3 = Tok(True), Tok(True), Tok(True)
        ps_la, ps_sf, ps_cT = b1[:, 0:64], b1[:, 64:128], b1[0:64, 128:256]
        ps_A = b2[:, 0:128]
        ps_o, ps_S = b3[:, 0:128], b3[0:64, 128:256]
        yield
        la_all = P.sb([128, NTT, 64], name="la_all")
        for g0_ in range(0, NTT, 8):
            n = min(8, NTT - g0_)
            for i in range(n):
                tsl = slice((g0_ + i) * 128, (g0_ + i + 1) * 128)
                P.op("pe", lambda e, i=i, tsl=tsl: e.matmul(b1[:, i * 64:(i + 1) * 64], lhsT=rT[0:17, tsl], rhs=wa[0:17, :], start=True, stop=True), reads=[t_r], writes=[t1])
            yield
            lav = la_all[:, g0_:g0_ + n, :]
            P.op("act", lambda e, lav=lav, n=n: e.activation(out=lav, in_=b1[:, 0:n * 64].rearrange("p (n d) -> p n d", d=64), func=AF.Exp, scale=-1.0), reads=[t1], writes=[t_la])
            yield
            P.op("dve", lambda e, lav=lav: e.tensor_scalar_add(out=lav, in0=lav, scalar1=1.0), reads=[t_la], writes=[t_la])
            yield
            P.op("act", lambda e, lav=lav: e.activation(out=lav, in_=lav, func=AF.Ln), reads=[t_la], writes=[t_la])
            yield
            P.op("dve", lambda e, lav=lav: e.tensor_scalar_mul(out=lav, in0=lav, scalar1=-1.0 / 16.0), reads=[t_la], writes=[t_la])
            yield
        order = list(range(NTT)) if z == 0 else [1, 0] + list(range(NTT - 1, 1, -1))
        for tt in order:
            ts_ = slice(tt * 128, (tt + 1) * 128)
            la = la_all[:, tt, :]
            P.op("pe", lambda e, la=la: e.matmul(ps_cT, lhsT=la, rhs=C, start=True, stop=True), reads=[t_la, t_cm], writes=[t1])
            P.op("pe", lambda e, la=la: e.matmul(ps_sf, lhsT=sufx, rhs=la, start=True, stop=True), reads=[t_la, t_cm], writes=[t1])
            yield
            P.op("act", lambda e: e.copy(out=cs[:], in_=ps_cT), reads=[t1], writes=[t_cs])
            P.op("act", lambda e: e.activation(out=k4[:], in_=ps_sf, func=AF.Exp), reads=[t1], writes=[t_k4])
            yield
            P.op("dve", lambda e: e.tensor_scalar_mul(out=sc[:, 0:1], in0=cs[:, mid:mid + 1], scalar1=-1.0), reads=[t_cs], writes=[t_sc])
            P.op("dve", lambda e, tt=tt: e.tensor_mul(out=k4[:], in0=kk[:, tt, :], in1=k4[:]), reads=[t_in, t_k4], writes=[t_k4])
            yield
            P.op("act", lambda e: e.activation(out=e1[:], in_=cs[:], func=AF.Exp, bias=sc[:, 0:1], scale=1.0), reads=[t_cs, t_sc], writes=[t_e1])
            P.op("act", lambda e: e.activation(out=e2[:], in_=cs[:], func=AF.Exp, bias=cs[:, mid:mid + 1], scale=-1.0), reads=[t_cs], writes=[t_e2])
            P.op("act", lambda e: e.activation(out=e3[:], in_=cs[:], func=AF.Exp), reads=[t_cs], writes=[t_e3])
            P.op("act", lambda e: e.activation(out=sc[:, 1:2], in_=cs[:, last:last + 1], func=AF.Exp), reads=[t_cs], writes=[t_sc])
            yield
            P.op("dve", lambda e, ts_=ts_: e.scalar_tensor_tensor(out=e1[:], in0=qT[:, ts_], scalar=0.125, in1=e1[:], op0=ALU.mult, op1=ALU.mult), reads=[t_in, t_e1], writes=[t_e1])
            P.op("dve", lambda e, ts_=ts_: e.tensor_mul(out=e2[:], in0=kT[:, ts_], in1=e2[:]), reads=[t_in, t_e2], writes=[t_e2])
            P.op("dve", lambda e, ts_=ts_: e.scalar_tensor_tensor(out=e3[:], in0=qT[:, ts_], scalar=0.125, in1=e3[:], op0=ALU.mult, op1=ALU.mult), reads=[t_in, t_e3], writes=[t_e3])
            yield
            P.op("pe", lambda e: e.matmul(ps_A, lhsT=e2[:], rhs=e1[:], start=True, stop=True), reads=[t_e1, t_e2], writes=[t2])
            yield
            P.op("dve", lambda e: e.tensor_mul(out=AT[:], in0=ps_A, in1=C), reads=[t2, t_cm], writes=[t_AT])
            yield
            P.op("pe", lambda e, tt=tt: e.matmul(ps_o, lhsT=AT[:], rhs=vv[:, tt, :], start=True, stop=False), reads=[t_AT, t_in], writes=[t3])
            P.op("pe", lambda e: e.matmul(ps_o, lhsT=e3[:], rhs=S[:], start=False, stop=True), reads=[t_e3, t_S], writes=[t3])
            P.op("pe", lambda e, tt=tt: e.matmul(ps_S, lhsT=k4[:], rhs=vv[:, tt, :], start=True, stop=True), reads=[t_k4, t_in], writes=[t3])
            yield
            P.op("act", lambda e, tt=tt: e.copy(out=odir[z][:, tt, :], in_=ps_o), reads=[t3], writes=[t_odir[z]])
            yield
            P.op("dve", lambda e: e.scalar_tensor_tensor(out=S[:], in0=S[:], scalar=sc[:, 1:2], in1=ps_S, op0=ALU.mult, op1=ALU.add), reads=[t_sc, t3, t_S], writes=[t_S])
            yield

    gens = [lane(0), lane(1)]
    while gens:
        for gg in list(gens):
            try:
                next(gg)
            except StopIteration:
                gens.remove(gg)
    rs = P.sb([128, 2, NTT], name="rs")
    sq = P.sb([128, NTT, 128], name="sq")
    t_rs, t_sq = Tok(), Tok()
    osum = odir[0]
    P.op("dve", lambda e: e.tensor_add(out=osum[:], in0=odir[0][:], in1=odir[1][:]), reads=[t_odir[1], t_odir[0]], writes=[t_odir[0]])
    P.op("act", lambda e: e.activation(out=sq[:], in_=osum[:], func=AF.Square), reads=[t_odir[0]], writes=[t_sq])
    P.op("dve", lambda e: e.reduce_sum(out=rs[:, 0, :], in_=sq[:], axis=AX.X), reads=[t_sq], writes=[t_rs])
    P.op("dve", lambda e: e.tensor_scalar(out=rs[:, 0, :], in0=rs[:, 0, :], scalar1=1.0 / 128, scalar2=1e-6, op0=ALU.mult, op1=ALU.add), reads=[t_rs], writes=[t_rs])
    P.op("act", lambda e: e.activation(out=rs[:, 0, :], in_=rs[:, 0, :], func=AF.Sqrt), reads=[t_rs], writes=[t_rs])
    P.op("dve", lambda e: e.reciprocal(out=rs[:, 1, :], in_=rs[:, 0, :]), reads=[t_rs], writes=[t_rs])
    for tt in range(NTT):
        P.op("dve", lambda e, tt=tt: e.scalar_tensor_tensor(out=osum[:, tt, :], in0=osum[:, tt, :], scalar=rs[:, 1, tt:tt + 1], in1=gn[:], op0=ALU.mult, op1=ALU.mult),
             reads=[t_rs, t_gn, t_odir[0]], writes=[t_odir[0]])
    P.op("dve", lambda e: e.tensor_mul(out=osum[:], in0=osum[:], in1=g[:]), reads=[t_g, t_odir[0]], writes=[t_odir[0]])
    for half in range(2):
        hs = slice(half * 17, half * 17 + 17)
        P.dma("sp" if half == 0 else "act", lambda e, hs=hs, half=half: e.dma_start(
            out=o_d[half * 17 * 128:(half + 1) * 17 * 128, :].rearrange("(n p) d -> p n d", p=128), in_=osum[:, hs, :]), reads=[t_odir[0]], is_out=True)
    return P


def flipseg(a):
    return np.concatenate([a[:CTX][::-1], a[CTX:][::-1]], axis=0)


CMATS = np.stack([TRI_INC, TRI_SUFEX, ANTI, IDENT], axis=0)


def stage_gla(p_lat, p_ctx, w_up_l, b_up_l, norm_l):
    P = build_gla()
    in_maps = []
    for c in range(NCORES):
        b, h = divmod(c, 4)
        pa = np.concatenate([p_ctx[b], p_lat[b]], axis=0)
        q = pa[:, h * 64:(h + 1) * 64]
        k = pa[:, 256 + h * 64:256 + (h + 1) * 64]
        v = pa[:, 512 + h * 128:512 + (h + 1) * 128]
        g = pa[:, 1024 + h * 128:1024 + (h + 1) * 128]
        rs = [pa[:, 1536 + z * 16:1536 + (z + 1) * 16].T for z in range(2)]
        ws = [np.concatenate([w_up_l[z][:, h * 64:(h + 1) * 64], b_up_l[z][None, h * 64:(h + 1) * 64]], axis=0) for z in range(2)]
        in_maps.append({"g": np.ascontiguousarray(g), "gn": bc(norm_l), "cm": np.ascontiguousarray(GDN_CM[0:4]),
                        "qT": np.ascontiguousarray(q.T), "kT": np.ascontiguousarray(k.T), "k": np.ascontiguousarray(k),
                        "v": np.ascontiguousarray(v), "rT": np.ascontiguousarray(np.stack(rs)), "w": np.ascontiguousarray(np.stack(ws))})
    res = run_prog(P, in_maps)
    o_lat = np.empty((B, SEQ, 512), np.float32)
    o_ctx = np.empty((B, CTX, 512), np.float32)
    for c in range(NCORES):
        b, h = divmod(c, 4)
        o_ctx[b, :, 128 * h:128 * (h + 1)] = res[c]["o"][:CTX]
        o_lat[b, :, 128 * h:128 * (h + 1)] = res[c]["o"][CTX:]
    return o_lat, o_ctx


UPP = TRI_INC
LOW = np.ascontiguousarray(TRI_INC.T)
GDN_CM = np.stack([UPP, LOW, UPP - IDENT, LOW - IDENT, IDENT, np.ones((128, 128), np.float32)], axis=0)


def build_gdn(dbg=None):
    P = Prog()
    x_d = P.din("xT", [3, 128, NTOK])
    cw_d = P.din("cw", [128, 3, 5])
    zg_d = P.din("zg", [NTOK, 128])
    ba_d = P.din("ba", [2, 2, 128, NTT])
    sc_d = P.din("sc", [128, 2, 2])
    gn_d = P.din("gn", [128, 128])
    c_d = P.din("cm", [6, 128, 128])
    o_d = P.dout("o", [NTOK, 128])

    cm = P.sb([128, 6, 128], name="cm")
    t_cm = Tok()
    for i in range(6):
        P.dma("sp", lambda e, i=i: e.dma_start(out=cm[:, i, :], in_=c_d[i]), writes=[t_cm])
    ident, ones = cm[:, 4, :], cm[:, 5, :]
    gn = P.sb([128, 128], name="gn")
    cw = P.sb([128, 3, 5], name="cw")
    scal = P.sb([128, 2, 2], name="scal")
    t_gn, t_cw, t_scal = Tok(), Tok(), Tok()
    P.dma("sp", lambda e: e.dma_start(out=gn[:], in_=gn_d), writes=[t_gn])
    P.dma("sp", lambda e: e.dma_start(out=cw[:], in_=cw_d), writes=[t_cw])
    P.dma("sp", lambda e: e.dma_start(out=scal[:], in_=sc_d), writes=[t_scal])
    zg = P.sb([128, NTT, 128], name="zg")
    t_zg = Tok()
    P.dma("act", lambda e: e.dma_start(out=zg[:], in_=zg_d.rearrange("(n p) d -> p n d", p=128)), writes=[t_zg])
    P.op("act", lambda e: e.activation(out=zg[:], in_=zg[:], func=AF.Silu), reads=[t_zg], writes=[t_zg])

    raw = P.sb([128, NTOK], name="raw")
    t_raw = Tok()
    fmx = [P.sb([128, NTOK], name="fmx") for _ in range(3)]
    t_fm = [Tok() for _ in range(3)]
    segs = [(0, CTX), (CTX, NTOK)]
    for s in range(3):
        P.dma("sp", lambda e, s=s: e.dma_start(out=raw[:], in_=x_d[s]), writes=[t_raw])
        acc = fmx[s]
        for (s0, s1) in segs:
            P.op("dve", lambda e, s=s, s0=s0, s1=s1, acc=acc: e.tensor_scalar(out=acc[:, s0:s1], in0=raw[:, s0:s1], scalar1=cw[:, s, 2:3], scalar2=None, op0=ALU.mult),
                 reads=[t_raw, t_cw], writes=[t_fm[s]])
            for j in (0, 1, 3, 4):
                sh = j - 2
                lo = max(s0, s0 - sh)
                hi = min(s1, s1 - sh)
                P.op("dve", lambda e, s=s, j=j, lo=lo, hi=hi, sh=sh, acc=acc: e.scalar_tensor_tensor(
                    out=acc[:, lo:hi], in0=raw[:, lo + sh:hi + sh], scalar=cw[:, s, j:j + 1], in1=acc[:, lo:hi], op0=ALU.mult, op1=ALU.add),
                    reads=[t_raw, t_cw, t_fm[s]], writes=[t_fm[s]])
        P.op("act", lambda e, acc=acc: e.activation(out=acc[:], in_=acc[:], func=AF.Silu), reads=[t_fm[s]], writes=[t_fm[s]])
    banks = [P.ps([128, 512], name="bk%d" % i) for i in range(8)]
    btok = [Tok(True) for _ in range(8)]
    bk1, t_bk1 = banks[0], btok[0]
    blocks = [(0, 256)] + [(CTX + i * 512, 512) for i in range(SEQ // 512)]
    items = [(s, b0, bw) for s in range(2) for (b0, bw) in blocks]
    t_blk = [Tok() for _ in items]

    def norm_lane(li):
        sq = P.sb([128, 512], name="sq")
        t_sq = Tok()
        bk, tb = banks[li], btok[li]
        for ii, (s, b0, bw) in enumerate(items):
            if ii % 3 != li:
                continue
            mul = (128 ** -0.5) if s == 0 else 1.0
            P.op("act", lambda e, s=s, b0=b0, bw=bw: e.activation(out=sq[:, 0:bw], in_=fmx[s][:, b0:b0 + bw], func=AF.Square), reads=[t_fm[s]], writes=[t_sq])
            yield
            P.op("pe", lambda e, bw=bw: e.matmul(bk[:, 0:bw], lhsT=ones, rhs=sq[:, 0:bw], start=True, stop=True), reads=[t_sq, t_cm], writes=[tb])
            yield
            P.op("dve", lambda e, bw=bw: e.tensor_scalar_add(out=sq[:, 0:bw], in0=bk[:, 0:bw], scalar1=1e-6), reads=[tb], writes=[t_sq])
            yield
            P.op("act", lambda e, bw=bw: e.activation(out=sq[:, 0:bw], in_=sq[:, 0:bw], func=AF.Sqrt), reads=[t_sq], writes=[t_sq])
            yield
            P.op("dve", lambda e, bw=bw: e.reciprocal(out=sq[:, 0:bw], in_=sq[:, 0:bw]), reads=[t_sq], writes=[t_sq])
            P.op("dve", lambda e, s=s, b0=b0, bw=bw, mul=mul: e.scalar_tensor_tensor(out=fmx[s][:, b0:b0 + bw], in0=fmx[s][:, b0:b0 + bw], scalar=float(mul), in1=sq[:, 0:bw],
                                                                                   op0=ALU.mult, op1=ALU.mult), reads=[t_sq, t_fm[s]], writes=[t_blk[ii]])
            yield

    gens = [norm_lane(i) for i in range(3)]
    while gens:
        for g_ in list(gens):
            try:
                next(g_)
            except StopIteration:
                gens.remove(g_)
    for s in range(2):
        P.op("dve", lambda e, s=s: e.tensor_copy(out=fmx[s][0:1, 0:1], in_=fmx[s][0:1, 0:1]),
             reads=[t_blk[ii] for ii, it in enumerate(items) if it[0] == s], writes=[t_fm[s]])
    qnT, knT, vT = fmx
    t_qnT, t_knT, t_vT = t_fm
    kn = P.sb([128, NTT, 128], name="kn")
    vt = P.sb([128, NTT, 128], name="vt")
    t_kn, t_vt = Tok(), Tok()
    for tt in range(NTT):
        ts_ = slice(tt * 128, (tt + 1) * 128)
        bk, tb = banks[tt % 2], btok[tt % 2]
        P.op("pe", lambda e, ts_=ts_, bk=bk: e.transpose(out=bk[:, 0:128], in_=knT[:, ts_], identity=ident), reads=[t_knT, t_cm], writes=[tb])
        P.op("pe", lambda e, ts_=ts_, bk=bk: e.transpose(out=bk[:, 128:256], in_=vT[:, ts_], identity=ident), reads=[t_vT, t_cm], writes=[tb])
        P.op("act", lambda e, tt=tt, bk=bk: e.copy(out=kn[:, tt, :], in_=bk[:, 0:128]), reads=[tb], writes=[t_kn])
        P.op("dve", lambda e, tt=tt, bk=bk: e.tensor_copy(out=vt[:, tt, :], in_=bk[:, 128:256]), reads=[tb], writes=[t_vt])

    if dbg == "prep":
        for tt in range(NTT):
            P.dma("pool", lambda e, tt=tt: e.dma_start(out=o_d[tt * 128:(tt + 1) * 128, :], in_=kn[:, tt, :]), reads=[t_kn], is_out=True)
        return P
    odir = [P.sb([128, NTT, 128], name="odir") for _ in range(2)]
    t_odir = [Tok(), Tok()]

    def sbt(n, w=128):
        return P.sb([128, w], name=n), Tok()

    def make_lane(z):
        L = {}
        L["bl"] = P.sb([128, 2, NTT], name="bl")
        L["t_bl"] = Tok()
        L["nea"], L["t_nea"] = sbt("nea", 1)
        L["S"], L["t_S"] = sbt("S")
        bA, bB, bC, bD = banks[4 * z:4 * z + 4]
        L["tA"], L["tB"], L["tC"], L["tD"] = btok[4 * z:4 * z + 4]
        L["pG"], L["pbR"], L["pKK"], L["pQK"] = bA[:, 0:128], bA[:, 128:256], bA[:, 256:384], bA[:, 384:512]
        L["psqL"], L["psqLT"], L["pg"] = bB[:, 0:128], bB[:, 128:256], bB[:, 256:258]
        L["pX"], L["pwT"], L["pvn"] = bC[:, 0:256], bC[:, 256:384], bC[:, 384:512]
        L["po"], L["pS"] = bD[:, 0:128], bD[:, 128:256]
        for n in ("lgB", "btB", "GT", "Gm", "EgR", "L0", "LT1", "AqT", "wT", "vn", "qd", "kd"):
            L[n], L["t_" + n] = sbt(n)
        L["col"], L["t_col"] = sbt("col", 8)
        L["X"], L["t_X"] = sbt("X", 256)
        L["pow"] = [sbt("pw%d" % i) for i in range(12)]
        return L

    def lane_gen(z, L):
        C, CT, Cs, CTs = (cm[:, 0, :], cm[:, 1, :], cm[:, 2, :], cm[:, 3, :]) if z == 0 else (cm[:, 1, :], cm[:, 0, :], cm[:, 3, :], cm[:, 2, :])
        last = 127 if z == 0 else 0
        bl, t_bl, nea, t_nea, S, t_S = L["bl"], L["t_bl"], L["nea"], L["t_nea"], L["S"], L["t_S"]
        tA, tB, tC, tD = L["tA"], L["tB"], L["tC"], L["tD"]
        pG, pbR, pKK, pQK, psqL, psqLT, pg = L["pG"], L["pbR"], L["pKK"], L["pQK"], L["psqL"], L["psqLT"], L["pg"]
        pX, pwT, pvn, po, pS = L["pX"], L["pwT"], L["pvn"], L["po"], L["pS"]
        lgB, btB, GT, Gm, EgR, L0, LT1, AqT, wT, vn, qd, kd, col, X = [L[n] for n in ("lgB", "btB", "GT", "Gm", "EgR", "L0", "LT1", "AqT", "wT", "vn", "qd", "kd", "col", "X")]
        t_lgB, t_btB, t_GT, t_Gm, t_EgR, t_L0, t_LT1, t_AqT, t_wT, t_vn, t_qd, t_kd, t_col, t_X = [L["t_" + n] for n in ("lgB", "btB", "GT", "Gm", "EgR", "L0", "LT1", "AqT", "wT", "vn", "qd", "kd", "col", "X")]
        for i in range(2):
            P.dma("sp", lambda e, i=i: e.dma_start(out=bl[:, i, :], in_=ba_d[z, i]), writes=[t_bl])
        P.op("act", lambda e: e.activation(out=bl[:, 0, :], in_=bl[:, 0, :], func=AF.Sigmoid), reads=[t_bl], writes=[t_bl])
        P.op("act", lambda e: e.activation(out=bl[:, 1, :], in_=bl[:, 1, :], func=AF.Exp, bias=scal[:, z, 1:2], scale=1.0), reads=[t_bl, t_scal], writes=[t_bl])
        P.op("dve", lambda e: e.tensor_scalar_add(out=bl[:, 1, :], in0=bl[:, 1, :], scalar1=1.0), reads=[t_bl], writes=[t_bl])
        P.op("act", lambda e: e.activation(out=bl[:, 1, :], in_=bl[:, 1, :], func=AF.Ln), reads=[t_bl], writes=[t_bl])
        P.op("act", lambda e: e.activation(out=nea[:], in_=scal[:, z, 0:1], func=AF.Exp), reads=[t_scal], writes=[t_nea])
        P.op("dve", lambda e: e.tensor_scalar_mul(out=nea[:], in0=nea[:], scalar1=-1.0), reads=[t_nea], writes=[t_nea])
        P.op("dve", lambda e: e.tensor_scalar(out=bl[:, 1, :], in0=bl[:, 1, :], scalar1=nea[:], scalar2=None, op0=ALU.mult), reads=[t_bl, t_nea], writes=[t_bl])
        P.op("dve", lambda e: e.memset(S[:], 0.0), writes=[t_S])
        yield
        order = list(range(NTT)) if z == 0 else [1, 0] + list(range(NTT - 1, 1, -1))
        if dbg is not None and dbg.startswith("main"):
            order = order[:int(dbg[4:])]
        for tt in order:
            ts_ = slice(tt * 128, (tt + 1) * 128)
            beta = bl[:, 0, tt:tt + 1]
            lg = bl[:, 1, tt:tt + 1]
            P.op("dve", lambda e, lg=lg: e.tensor_scalar(out=lgB[:], in0=ones, scalar1=lg, scalar2=None, op0=ALU.mult), reads=[t_bl, t_cm], writes=[t_lgB])
            yield
            P.op("dve", lambda e, beta=beta: e.tensor_scalar(out=btB[:], in0=ones, scalar1=beta, scalar2=None, op0=ALU.mult), reads=[t_bl, t_cm], writes=[t_btB])
            yield
            P.op("pe", lambda e: e.matmul(pg, lhsT=C, rhs=lgB[:, 0:2], start=True, stop=True), reads=[t_lgB, t_cm], writes=[tB])
            P.op("pe", lambda e: e.matmul(pG, lhsT=lgB[:], rhs=C, start=True, stop=True), reads=[t_lgB, t_cm], writes=[tA])
            P.op("pe", lambda e: e.matmul(pbR, lhsT=btB[:], rhs=ident, start=True, stop=True), reads=[t_btB, t_cm], writes=[tA])
            P.op("pe", lambda e, ts_=ts_: e.matmul(pKK, lhsT=knT[:, ts_], rhs=knT[:, ts_], start=True, stop=True), reads=[t_knT], writes=[tA])
            P.op("pe", lambda e, ts_=ts_: e.matmul(pQK, lhsT=knT[:, ts_], rhs=qnT[:, ts_], start=True, stop=True), reads=[t_knT, t_qnT], writes=[tA])
            yield
            P.op("act", lambda e: e.copy(out=col[:, 0:1], in_=pg[:, 0:1]), reads=[tB], writes=[t_col])
            yield
            P.op("dve", lambda e: e.tensor_scalar(out=GT[:], in0=pG, scalar1=col[:, 0:1], scalar2=0.0, op0=ALU.subtract, op1=ALU.min), reads=[tA, t_col], writes=[t_GT])
            yield
            P.op("act", lambda e: e.activation(out=GT[:], in_=GT[:], func=AF.Exp), reads=[t_GT], writes=[t_GT])
            yield
            P.op("dve", lambda e: e.tensor_scalar(out=Gm[:], in0=pG, scalar1=col[:, 0:1], scalar2=0.0, op0=ALU.subtract, op1=ALU.max), reads=[tA, t_col], writes=[t_Gm])
            yield
            P.op("act", lambda e: e.activation(out=Gm[:], in_=Gm[:], func=AF.Exp, scale=-1.0), reads=[t_Gm], writes=[t_Gm])
            yield
            P.op("act", lambda e: e.activation(out=EgR[:], in_=pG, func=AF.Exp), reads=[tA], writes=[t_EgR])
            yield
            P.op("act", lambda e: e.activation(out=col[:, 4:5], in_=col[:, 0:1], func=AF.Exp), reads=[t_col], writes=[t_col])
            yield
            P.op("dve", lambda e, beta=beta: e.tensor_mul(out=col[:, 1:2], in0=col[:, 4:5], in1=beta), reads=[t_col, t_bl], writes=[t_col])
            yield
            P.op("dve", lambda e: e.tensor_sub(out=col[:, 5:6], in0=pG[:, last:last + 1], in1=col[:, 0:1]), reads=[tA, t_col], writes=[t_col])
            yield
            P.op("act", lambda e: e.activation(out=col[:, 2:3], in_=col[:, 5:6], func=AF.Exp), reads=[t_col], writes=[t_col])
            yield
            P.op("act", lambda e: e.activation(out=col[:, 3:4], in_=pG[:, last:last + 1], func=AF.Exp), reads=[tA], writes=[t_col])
            yield
            P.op("dve", lambda e: e.tensor_mul(out=LT1[:], in0=GT[:], in1=Cs), reads=[t_GT, t_cm], writes=[t_LT1])
            yield
            P.op("dve", lambda e: e.tensor_mul(out=LT1[:], in0=LT1[:], in1=pKK), reads=[tA, t_LT1], writes=[t_LT1])
            yield
            P.op("dve", lambda e: e.tensor_mul(out=LT1[:], in0=LT1[:], in1=pbR), reads=[tA, t_LT1], writes=[t_LT1])
            yield
            P.op("dve", lambda e: e.tensor_mul(out=L0[:], in0=Gm[:], in1=CTs), reads=[t_Gm, t_cm], writes=[t_L0])
            yield
            P.op("dve", lambda e, beta=beta: e.scalar_tensor_tensor(out=L0[:], in0=L0[:], scalar=beta, in1=pKK, op0=ALU.mult, op1=ALU.mult), reads=[tA, t_bl, t_L0], writes=[t_L0])
            yield
            P.op("dve", lambda e: e.tensor_mul(out=AqT[:], in0=GT[:], in1=C), reads=[t_GT, t_cm], writes=[t_AqT])
            yield
            P.op("dve", lambda e: e.tensor_mul(out=AqT[:], in0=AqT[:], in1=pQK), reads=[tA, t_AqT], writes=[t_AqT])
            yield
            P.op("dve", lambda e, tt=tt, beta=beta: e.tensor_scalar(out=X[:, 0:128], in0=vt[:, tt, :], scalar1=beta, scalar2=None, op0=ALU.mult), reads=[t_vt, t_bl], writes=[t_X])
            yield
            P.op("dve", lambda e, tt=tt: e.tensor_scalar(out=X[:, 128:256], in0=kn[:, tt, :], scalar1=col[:, 1:2], scalar2=None, op0=ALU.mult), reads=[t_kn, t_col], writes=[t_X])
            yield
            cur_L, cur_tL, cur_LT, cur_tLT = L0, t_L0, LT1, t_LT1
            for pi in range(7):
                if pi < 6:
                    nL, t_nL = L["pow"][2 * pi]
                    nLT, t_nLT = L["pow"][2 * pi + 1]
                    P.op("pe", lambda e, cur_L=cur_L, cur_LT=cur_LT: e.matmul(psqL, lhsT=cur_LT[:], rhs=cur_L[:], start=True, stop=True), reads=[cur_tL, cur_tLT], writes=[tB])
                    P.op("pe", lambda e, cur_L=cur_L, cur_LT=cur_LT: e.matmul(psqLT, lhsT=cur_L[:], rhs=cur_LT[:], start=True, stop=True), reads=[cur_tL, cur_tLT], writes=[tB])
                P.op("pe", lambda e, cur_LT=cur_LT: e.matmul(pX, lhsT=cur_LT[:], rhs=X[:], start=True, stop=True), reads=[cur_tLT, t_X], writes=[tC])
                yield
                if pi < 6:
                    P.op("act", lambda e, nL=nL: e.copy(out=nL[:], in_=psqL), reads=[tB], writes=[t_nL])
                    P.op("act", lambda e, nLT=nLT: e.copy(out=nLT[:], in_=psqLT), reads=[tB], writes=[t_nLT])
                if pi > 0:
                    P.op("dve", lambda e: e.tensor_add(out=X[:], in0=X[:], in1=pX), reads=[tC, t_X], writes=[t_X])
                else:
                    P.op("dve", lambda e: e.tensor_sub(out=X[:], in0=X[:], in1=pX), reads=[tC, t_X], writes=[t_X])
                yield
                if pi < 6:
                    cur_L, cur_tL, cur_LT, cur_tLT = nL, t_nL, nLT, t_nLT
            P.op("pe", lambda e: e.transpose(out=pwT, in_=X[:, 128:256], identity=ident), reads=[t_X, t_cm], writes=[tC])
            yield
            P.op("act", lambda e: e.copy(out=wT[:], in_=pwT), reads=[tC], writes=[t_wT])
            yield
            P.op("pe", lambda e: e.matmul(pvn, lhsT=wT[:], rhs=S[:], start=True, stop=True), reads=[t_wT, t_S], writes=[tC])
            yield
            P.op("dve", lambda e: e.tensor_sub(out=vn[:], in0=X[:, 0:128], in1=pvn), reads=[tC, t_X], writes=[t_vn])
            yield
            P.op("dve", lambda e, ts_=ts_: e.tensor_mul(out=qd[:], in0=qnT[:, ts_], in1=EgR[:]), reads=[t_qnT, t_EgR], writes=[t_qd])
            yield
            P.op("pe", lambda e: e.matmul(po, lhsT=qd[:], rhs=S[:], start=True, stop=False), reads=[t_qd, t_S], writes=[tD])
            P.op("pe", lambda e: e.matmul(po, lhsT=AqT[:], rhs=vn[:], start=False, stop=True), reads=[t_AqT, t_vn], writes=[tD])
            yield
            P.op("dve", lambda e, tt=tt: e.tensor_scalar(out=kd[:], in0=kn[:, tt, :], scalar1=col[:, 2:3], scalar2=None, op0=ALU.mult), reads=[t_kn, t_col], writes=[t_kd])
            yield
            P.op("pe", lambda e: e.matmul(pS, lhsT=kd[:], rhs=vn[:], start=True, stop=True), reads=[t_kd, t_vn], writes=[tD])
            yield
            P.op("act", lambda e, tt=tt: e.copy(out=odir[z][:, tt, :], in_=po), reads=[tD], writes=[t_odir[z]])
            yield
            P.op("dve", lambda e: e.scalar_tensor_tensor(out=S[:], in0=S[:], scalar=col[:, 3:4], in1=pS, op0=ALU.mult, op1=ALU.add), reads=[t_col, tD, t_S], writes=[t_S])
            yield

    gens = [lane_gen(z, make_lane(z)) for z in range(2)]
    while gens:
        for g in list(gens):
            try:
                next(g)
            except StopIteration:
                gens.remove(g)
    rs = P.sb([128, 2, NTT], name="rs")
    t_rs = Tok()
    osum = odir[0]
    P.op("dve", lambda e: e.tensor_add(out=osum[:], in0=odir[0][:], in1=odir[1][:]), reads=[t_odir[1], t_odir[0]], writes=[t_odir[0]])
    sqv = raw[:].rearrange("p (n d) -> p n d", d=128)
    P.op("act", lambda e: e.activation(out=sqv, in_=osum[:], func=AF.Square), reads=[t_odir[0]], writes=[t_raw])
    P.op("dve", lambda e: e.reduce_sum(out=rs[:, 0, :], in_=sqv, axis=AX.X), reads=[t_raw], writes=[t_rs])
    P.op("dve", lambda e: e.tensor_scalar(out=rs[:, 0, :], in0=rs[:, 0, :], scalar1=1.0 / 128, scalar2=1e-6, op0=ALU.mult, op1=ALU.add), reads=[t_rs], writes=[t_rs])
    P.op("act", lambda e: e.activation(out=rs[:, 0, :], in_=rs[:, 0, :], func=AF.Sqrt), reads=[t_rs], writes=[t_rs])
    P.op("dve", lambda e: e.reciprocal(out=rs[:, 1, :], in_=rs[:, 0, :]), reads=[t_rs], writes=[t_rs])
    for tt in range(NTT):
        P.op("dve", lambda e, tt=tt: e.scalar_tensor_tensor(out=osum[:, tt, :], in0=osum[:, tt, :], scalar=rs[:, 1, tt:tt + 1], in1=gn[:], op0=ALU.mult, op1=ALU.mult),
             reads=[t_rs, t_gn, t_odir[0]], writes=[t_odir[0]])
    P.op("dve", lambda e: e.tensor_mul(out=osum[:], in0=osum[:], in1=zg[:]), reads=[t_zg, t_odir[0]], writes=[t_odir[0]])
    for half in range(2):
        hs = slice(half * 17, half * 17 + 17)
        P.dma("sp" if half == 0 else "act", lambda e, hs=hs, half=half: e.dma_start(
            out=o_d[half * 17 * 128:(half + 1) * 17 * 128, :].rearrange("(n p) d -> p n d", p=128), in_=osum[:, hs, :]), reads=[t_odir[0]], is_out=True)
    return P


def stage_gdn(p_lat, p_ctx, conv_l, a_log_l, dt_bias_l, norm_l, dbg=None):
    P = build_gdn(dbg)
    in_maps = []
    for c in range(NCORES):
        b, h = divmod(c, 4)
        pa = np.concatenate([p_ctx[b], p_lat[b]], axis=0)
        hs = slice(h * 128, (h + 1) * 128)
        xT = np.stack([pa[:, 1568:2080][:, hs].T, pa[:, 2080:2592][:, hs].T, pa[:, 2592:3104][:, hs].T], axis=0)
        cw = np.stack([conv_l[:, s * 512 + h * 128:s * 512 + (h + 1) * 128].T for s in range(3)], axis=1)
        ba = np.empty((2, 2, 128, NTT), np.float32)
        sc = np.empty((128, 2, 2), np.float32)
        for z in range(2):
            ba[z, 0] = pa[:, 3616 + z * 4 + h].reshape(NTT, 128).T
            ba[z, 1] = pa[:, 3624 + z * 4 + h].reshape(NTT, 128).T
            sc[:, z, 0] = a_log_l[z, h]
            sc[:, z, 1] = dt_bias_l[z, h]
        in_maps.append({"xT": np.ascontiguousarray(xT), "cw": np.ascontiguousarray(cw), "zg": np.ascontiguousarray(pa[:, 3104:3616][:, hs]),
                        "ba": ba, "sc": sc, "gn": bc(norm_l), "cm": GDN_CM})
    res = run_prog(P, in_maps)
    o_lat = np.empty((B, SEQ, 512), np.float32)
    o_ctx = np.empty((B, CTX, 512), np.float32)
    for c in range(NCORES):
        b, h = divmod(c, 4)
        o_ctx[b, :, 128 * h:128 * (h + 1)] = res[c]["o"][:CTX]
        o_lat[b, :, 128 * h:128 * (h + 1)] = res[c]["o"][CTX:]
    return o_lat, o_ctx


def build_router():
    P = Prog()
    KC = D // 128
    xT_d = P.din("xT", [128, KC, NT_B])
    mod_d = P.din("modv", [128, KC, 4])
    r_d = P.din("rw", [D, NE]).rearrange("(k p) c -> p k c", p=128)
    hT_d = P.dout("hT", [128, KC, NT_B])
    a_d = P.dout("aff", [NT_B, NE])
    xT = P.sb([128, KC, NT_B], name="xT")
    modv = P.sb([128, KC, 4], name="modv")
    rw = P.sb([128, KC, NE], name="rw")
    t_x = [Tok() for _ in range(KC)]
    t_mod, t_rw = Tok(), Tok()
    for k in range(KC):
        P.dma("sp" if k % 2 == 0 else "act", lambda e, k=k: e.dma_start(out=xT[:, k, :], in_=xT_d[:, k, :]), writes=[t_x[k]])
    P.dma("sp", lambda e: e.dma_start(out=modv[:], in_=mod_d), writes=[t_mod])
    P.dma("sp", lambda e: e.dma_start(out=rw[:], in_=r_d), writes=[t_rw])
    P.op("dve", lambda e: e.tensor_scalar_add(out=modv[:, :, 1:2], in0=modv[:, :, 1:2], scalar1=1.0), reads=[t_mod], writes=[t_mod])
    P.op("dve", lambda e: e.tensor_scalar_add(out=modv[:, :, 3:4], in0=modv[:, :, 3:4], scalar1=1.0), reads=[t_mod], writes=[t_mod])
    for k in range(KC):
        P.op("dve", lambda e, k=k: e.tensor_scalar(out=xT[:, k, 0:1024], in0=xT[:, k, 0:1024], scalar1=modv[:, k, 1:2],
                                                   scalar2=modv[:, k, 0:1], op0=ALU.mult, op1=ALU.add),
             reads=[t_mod, t_x[k]], writes=[t_x[k]])
        P.op("dve", lambda e, k=k: e.tensor_scalar(out=xT[:, k, 1024:NT_B], in0=xT[:, k, 1024:NT_B], scalar1=modv[:, k, 3:4],
                                                   scalar2=modv[:, k, 2:3], op0=ALU.mult, op1=ALU.add),
             reads=[t_mod, t_x[k]], writes=[t_x[k]])
        P.dma("pool", lambda e, k=k: e.dma_start(out=hT_d[:, k, :], in_=xT[:, k, :]), reads=[t_x[k]], is_out=True)
    pl = [P.ps([128, NE], name="pl") for _ in range(2)]
    t_pl = [Tok(True), Tok(True)]
    ex = [P.sb([128, NE], name="ex") for _ in range(2)]
    t_ex = [Tok(), Tok()]
    st = P.sb([128, 4], name="st")
    t_st = Tok()
    tiles = [(i * 128, 128) for i in range(8)] + [(1024, 64)]
    for ti, (t0, m) in enumerate(tiles):
        bi = ti % 2
        for k in range(KC):
            P.op("pe", lambda e, bi=bi, k=k, t0=t0, m=m: e.matmul(pl[bi][0:m, :], lhsT=xT[:, k, t0:t0 + m], rhs=rw[:, k, :], start=(k == 0), stop=(k == KC - 1)),
                 reads=[t_x[k], t_rw], writes=[t_pl[bi]])
        P.op("dve", lambda e, bi=bi, m=m: e.reduce_max(out=st[0:m, 0:1], in_=pl[bi][0:m, :], axis=AX.X), reads=[t_pl[bi]], writes=[t_st])
        P.op("dve", lambda e, m=m: e.tensor_scalar_mul(out=st[0:m, 1:2], in0=st[0:m, 0:1], scalar1=-1.0), reads=[t_st], writes=[t_st])
        P.op("act", lambda e, bi=bi, m=m: e.activation(out=ex[bi][0:m, :], in_=pl[bi][0:m, :], func=AF.Exp, bias=st[0:m, 1:2], scale=1.0, accum_out=st[0:m, 2:3]),
             reads=[t_pl[bi], t_st], writes=[t_ex[bi], t_st])
        P.op("dve", lambda e, m=m: e.reciprocal(out=st[0:m, 3:4], in_=st[0:m, 2:3]), reads=[t_st], writes=[t_st])
        P.op("dve", lambda e, bi=bi, m=m: e.tensor_scalar(out=ex[bi][0:m, :], in0=ex[bi][0:m, :], scalar1=st[0:m, 3:4], scalar2=None, op0=ALU.mult),
             reads=[t_st, t_ex[bi]], writes=[t_ex[bi]])
        P.dma("pool", lambda e, bi=bi, t0=t0, m=m: e.dma_start(out=a_d[t0:t0 + m, :], in_=ex[bi][0:m, :]), reads=[t_ex[bi]], writes=[t_ex[bi]], is_out=True)
    return P


def unfm(hT):
    p, kc, T = hT.shape
    return np.ascontiguousarray(hT.transpose(2, 1, 0).reshape(T, kc * p))


def stage_router(x_lat, x_ctx, mod_lat, mod_ctx, router_l):
    P = build_router()
    in_maps = []
    for c in range(NCORES):
        b = c // 4
        mv = np.stack([mod_lat[b, 3], mod_lat[b, 4], mod_ctx[3], mod_ctx[4]], axis=-1)
        mv = np.ascontiguousarray(mv.reshape(D // 128, 128, 4).transpose(1, 0, 2))
        in_maps.append({"xT": fm(tok_shard(x_lat, x_ctx, c)), "modv": mv, "rw": np.ascontiguousarray(router_l)})
    res = run_prog(P, in_maps)
    res2 = [{"h": unfm(r["hT"]), "aff": r["aff"]} for r in res]
    h_lat, h_ctx = tok_unshard(res2, "h", D)
    a_lat, a_ctx = tok_unshard(res2, "aff", NE)
    return h_lat, h_ctx, a_lat, a_ctx


NBIS = 30
H_ROWS = B * SEQ + B * CTX


def build_select():
    P = Prog()
    a_d = P.din("A", [128, 256])
    cc_d = P.din("cc", [128, 1])
    bd_d = P.din("bd", [2, 128, 128])
    tvc_d = P.din("tvc", [128, 32])
    io_d = P.din("iota", [128, 512])
    id_d = P.din("ident", [128, 128])
    idx_d = P.dout("idx", [128, 8, 4], I32)
    gate_d = P.dout("gate", [128, 8, 4])
    h_d = P.din("h", [H_ROWS, D])
    xsc_d = P.dout("xsc", [4, 544, D])
    A = P.sb([128, 256], name="A")
    M = P.sb([128, 256], name="M")
    Cm = P.sb([128, 256], name="Cm")
    onesr = P.sb([128, 256], name="onesr")
    cc = P.sb([128, 1], name="cc")
    bd = P.sb([128, 2, 128], name="bd")
    tvc = P.sb([128, 32], name="tvc")
    iota = P.sb([128, 512], name="iota")
    ident = P.sb([128, 128], name="ident")
    t_A, t_M, t_Cm, t_on, t_cc, t_bd, t_tvc, t_io, t_id = [Tok() for _ in range(9)]
    P.dma("sp", lambda e: e.dma_start(out=A[:], in_=a_d), writes=[t_A])
    P.dma("sp", lambda e: e.dma_start(out=cc[:], in_=cc_d), writes=[t_cc])
    for i in range(2):
        P.dma("sp", lambda e, i=i: e.dma_start(out=bd[:, i, :], in_=bd_d[i]), writes=[t_bd])
    P.dma("act", lambda e: e.dma_start(out=tvc[:], in_=tvc_d), writes=[t_tvc])
    P.dma("act", lambda e: e.dma_start(out=iota[:], in_=io_d), writes=[t_io])
    P.dma("act", lambda e: e.dma_start(out=ident[:], in_=id_d), writes=[t_id])
    P.op("pool", lambda e: e.memset(onesr[:], 1.0), writes=[t_on])
    bs = P.sb([128, 8], name="bs")
    t_bs = Tok()
    P.op("dve", lambda e: e.memset(bs[:], 0.0), writes=[t_bs])
    pc = P.ps([128, 8], name="pc")
    t_pc = Tok(True)
    for k in range(1, NBIS + 1):
        w = 2.0 ** (-k)
        P.op("dve", lambda e, w=w: e.tensor_scalar_add(out=bs[:, 1:2], in0=bs[:, 0:1], scalar1=w), reads=[t_bs], writes=[t_bs])
        P.op("dve", lambda e: e.tensor_scalar(out=M[:], in0=A[:], scalar1=bs[:, 1:2], scalar2=None, op0=ALU.is_ge, op1=ALU.add, accum_out=bs[:, 2:3]),
             reads=[t_A, t_bs], writes=[t_M, t_bs])
        P.op("pe", lambda e: e.matmul(pc[:, 0:2], lhsT=bd[:, 0, :], rhs=bs[:, 2:4], start=True, stop=True), reads=[t_bd, t_bs], writes=[t_pc])
        P.op("dve", lambda e: e.tensor_tensor(out=bs[:, 4:5], in0=pc[:, 0:1], in1=cc[:], op=ALU.is_ge), reads=[t_pc, t_cc], writes=[t_bs])
        P.op("dve", lambda e, w=w: e.scalar_tensor_tensor(out=bs[:, 0:1], in0=bs[:, 4:5], scalar=w, in1=bs[:, 0:1], op0=ALU.mult, op1=ALU.add),
             reads=[t_bs], writes=[t_bs])
    P.op("dve", lambda e: e.tensor_scalar(out=M[:], in0=A[:], scalar1=bs[:, 0:1], scalar2=None, op0=ALU.is_ge, op1=ALU.add, accum_out=bs[:, 2:3]),
         reads=[t_A, t_bs], writes=[t_M, t_bs])
    P.op("pe", lambda e: e.matmul(pc[:, 2:4], lhsT=bd[:, 1, :], rhs=bs[:, 2:4], start=True, stop=True), reads=[t_bd, t_bs], writes=[t_pc])
    P.op("dve", lambda e: e.tensor_tensor_scan(out=Cm[:], data0=onesr[:], data1=M[:], initial=0.0, op0=ALU.mult, op1=ALU.add),
         reads=[t_on, t_M], writes=[t_Cm])
    P.op("dve", lambda e: e.tensor_sub(out=Cm[:], in0=Cm[:], in1=M[:]), reads=[t_M, t_Cm], writes=[t_Cm])
    P.op("dve", lambda e: e.tensor_scalar(out=Cm[:], in0=Cm[:], scalar1=pc[:, 2:3], scalar2=None, op0=ALU.add), reads=[t_pc, t_Cm], writes=[t_Cm])
    TT = P.sb([128, 3, 2, 128], name="TT")
    t_TT = Tok()
    pT = [P.ps([128, 512], name="pT") for _ in range(2)]
    t_pT = [Tok(True), Tok(True)]
    for half in range(2):
        for i, (src, tk) in enumerate(((A, t_A), (M, t_M), (Cm, t_Cm))):
            P.op("pe", lambda e, half=half, i=i, src=src: e.transpose(out=pT[half][:, i * 128:(i + 1) * 128], in_=src[:, half * 128:(half + 1) * 128], identity=ident[:]),
                 reads=[tk, t_id], writes=[t_pT[half]])
        P.op("dve", lambda e, half=half: e.tensor_copy(out=TT[:, :, half, :], in_=pT[half][:, 0:384].rearrange("p (q c) -> p q c", q=3)), reads=[t_pT[half]], writes=[t_TT])

    class _T3:
        def __getitem__(self, key):
            _, j, cs = key
            c = cs.start
            q, r = divmod(c, 8)
            s_, half = divmod(j, 2)
            col = r * 16 + s_
            return TT[:, q, half, col:col + 1]
    T3 = _T3()
    t_T3 = t_TT
    TV = P.sb([128, 32, 8, 2], name="TV")
    t_TV = Tok()
    for r in range(8):
        P.op("dve", lambda e, r=r: e.tensor_copy(out=TV[:, :, r, 0], in_=tvc[:]), reads=[t_tvc], writes=[t_TV])
    TV5 = TV[:].rearrange("p (s h) r c -> p s h r c", h=2)
    for half in range(2):
        P.op("dve", lambda e, half=half: e.tensor_copy(out=TV5[:, :, half, :, 1], in_=TT[:, 0, half, :].rearrange("p (r s) -> p s r", r=8)), reads=[t_TT], writes=[t_TV])
    Pm = P.sb([128, 32, 512], name="Pm")
    t_Pm = Tok()
    pi_ = P.ps([128, 64], name="pi")
    t_pi = Tok(True)
    res_i = P.sb([128, 8, 4], I32, name="res_i")
    res_f = P.sb([128, 8, 4], name="res_f")
    res_g = P.sb([128, 8, 4], name="res_g")
    t_res = Tok()
    P.op("dve", lambda e: e.memset(res_f[:], 0.0), writes=[t_res])
    P.op("dve", lambda e: e.memset(res_g[:], 0.0), writes=[t_res])
    for r in range(8):
        lat = r < 4
        C = 512 if lat else 32
        nj = 32 if lat else 2
        b = r % 2
        base = float(b * SEQ) if lat else float(B * SEQ + b * CTX)
        for j in range(nj):
            P.op("dve", lambda e, j=j, r=r, C=C: e.tensor_scalar(out=Pm[:, j, 0:C], in0=iota[:, 0:C], scalar1=T3[:, j, 16 + r:17 + r],
                                                                 scalar2=T3[:, j, 8 + r:9 + r], op0=ALU.is_equal, op1=ALU.mult),
                 reads=[t_io, t_T3], writes=[t_Pm])
        for sc in range(4 if lat else 1):
            msz = 128 if lat else 32
            for j in range(nj):
                P.op("pe", lambda e, r=r, sc=sc, j=j, msz=msz, nj=nj: e.matmul(pi_[0:msz, (r * 4 + sc) * 2:(r * 4 + sc) * 2 + 2],
                                                                            lhsT=Pm[:, j, sc * 128:sc * 128 + msz], rhs=TV[:, j, r, :],
                                                                            start=(j == 0), stop=(j == nj - 1)),
                     reads=[t_Pm, t_TV], writes=[t_pi])
            P.op("dve", lambda e, r=r, sc=sc, msz=msz, base=base: e.tensor_scalar_add(out=res_f[0:msz, r, sc:sc + 1],
                                                                                   in0=pi_[0:msz, (r * 4 + sc) * 2:(r * 4 + sc) * 2 + 1], scalar1=base),
                 reads=[t_pi], writes=[t_res])
            P.op("dve", lambda e, r=r, sc=sc, msz=msz: e.tensor_copy(out=res_g[0:msz, r, sc:sc + 1], in_=pi_[0:msz, (r * 4 + sc) * 2 + 1:(r * 4 + sc) * 2 + 2]),
                 reads=[t_pi], writes=[t_res])
    P.op("dve", lambda e: e.tensor_copy(out=res_i[:], in_=res_f[:]), reads=[t_res], writes=[t_res])
    P.dma("pool", lambda e: e.dma_start(out=idx_d, in_=res_i[:]), reads=[t_res], is_out=True)
    P.dma("pool", lambda e: e.dma_start(out=gate_d, in_=res_g[:]), reads=[t_res], is_out=True)
    xs = [P.sb([128, D], name="xs") for _ in range(4)]
    t_xs = [Tok() for _ in range(4)]
    ixs = 0
    for el in range(2):
        for b in range(B):
            pas = el * 2 + b
            for (r, sc, s0, m) in [(el * 2 + b, sc, sc * 128, 128) for sc in range(4)] + [(4 + el * 2 + b, 0, 512, 32)]:
                xb = ixs % 4
                ixs += 1
                P.dma("pool", lambda e, xb=xb, r=r, sc=sc, m=m: e.indirect_dma_start(
                    out=xs[xb][0:m, :], out_offset=None, in_=h_d[:, :], in_offset=bass.IndirectOffsetOnAxis(ap=res_i[0:m, r, sc:sc + 1], axis=0)),
                    reads=[t_res], writes=[t_xs[xb]])
                P.dma("sp" if xb % 2 == 0 else "act", lambda e, xb=xb, pas=pas, s0=s0, m=m: e.dma_start(out=xsc_d[pas, s0:s0 + m, :], in_=xs[xb][0:m, :]),
                      reads=[t_xs[xb]], writes=[t_xs[xb]], is_out=True)
    return P


TVC = (np.arange(32)[None, :] * 128 + np.arange(128)[:, None]).astype(np.float32)
IOTA512 = np.ascontiguousarray(np.broadcast_to(np.arange(512, dtype=np.float32)[None, :], (128, 512)))
CCOL = np.array([512] * 4 + [32] * 4, np.float32)[:, None]
CC128 = np.ascontiguousarray(np.repeat(CCOL, 16, axis=0))
_grp = np.arange(128) // 16
BDMATS = np.stack([(_grp[:, None] == _grp[None, :]).astype(np.float32),
                   ((_grp[:, None] == _grp[None, :]) & (np.arange(128)[:, None] < np.arange(128)[None, :])).astype(np.float32)], axis=0)


def stage_select(a_lat, a_ctx, h_lat, h_ctx):
    P = build_select()
    h_all = np.ascontiguousarray(np.concatenate([h_lat.reshape(B * SEQ, D), h_ctx.reshape(B * CTX, D)], axis=0))
    in_maps = []
    for c in range(NCORES):
        A = np.full((8, SEQ), -1.0, np.float32)
        for el in range(2):
            for b in range(B):
                A[el * 2 + b] = a_lat[b, :, 2 * c + el]
                A[4 + el * 2 + b, :CTX] = a_ctx[b, :, 2 * c + el]
        in_maps.append({"A": np.ascontiguousarray(A.reshape(128, 256)), "cc": CC128, "bd": BDMATS, "tvc": TVC, "iota": IOTA512, "ident": IDENT, "h": h_all})
    res = run_prog(P, in_maps)
    return [(r["idx"], r["gate"], r["xsc"]) for r in res]


def build_expert(els=(0, 1), bs=(0, 1)):
    P = Prog()
    KC = D // 128
    h_d = P.din("xsc", [4, 544, D])
    idx_d = P.din("idx", [128, 8, 4], I32)
    gate_d = P.din("gate", [128, 8, 4])
    w1_d = P.din("w1", [len(els), D, FF])
    w3_d = P.din("w3", [len(els), D, FF])
    w2_d = P.din("w2", [len(els), FF, D])
    id_d = P.din("ident", [128, 128])
    f_d = [P.dout("f%d" % dc, [H_ROWS, 512]) for dc in range(4)]
    ident = P.sb([128, 128], name="ident")
    idx = P.sb([128, 8, 4], I32, name="idx")
    gate = P.sb([128, 8, 4], name="gate")
    t_id, t_idx, t_gate = Tok(), Tok(), Tok()
    P.dma("sp", lambda e: e.dma_start(out=ident[:], in_=id_d), writes=[t_id])
    P.dma("sp", lambda e: e.dma_start(out=idx[:], in_=idx_d), writes=[t_idx])
    P.dma("sp", lambda e: e.dma_start(out=gate[:], in_=gate_d), writes=[t_gate])
    zt = P.sb([128, 2048], name="zt")
    t_z = Tok()
    t_f = [Tok() for _ in range(4)]
    P.op("dve", lambda e: e.memset(zt[:], 0.0), writes=[t_z])
    for dc in range(4):
        for r0 in range(0, H_ROWS, 512):
            P.dma("sp" if (r0 // 512) % 2 == 0 else "act",
                  lambda e, dc=dc, r0=r0: e.dma_start(out=f_d[dc][r0:r0 + 512, :].rearrange("(p n) c -> p n c", p=128), in_=zt[:].rearrange("p (n c) -> p n c", n=4)),
                  reads=[t_z], writes=[t_f[dc]], is_out=True)
    NSL = 544 * len(bs)
    xs = [P.sb([128, D], name="xs") for _ in range(2)]
    t_xs = [Tok(), Tok()]
    xsT = P.sb([128, KC, NSL], BF16, name="xsT")
    t_xsT = Tok()
    hT = P.sb([128, KC, NSL], BF16, name="hT")
    t_hT = Tok()
    NWB = 3
    wa = [P.sb([128, KC, 128], BF16, name="w1c") for _ in range(NWB)]
    wu = [P.sb([128, KC, 128], BF16, name="w3c") for _ in range(NWB)]
    t_wa = [Tok() for _ in range(NWB)]
    t_wu = [Tok() for _ in range(NWB)]
    w2c = [P.sb([128, KC, 512], BF16, name="w2c") for _ in range(2)]
    t_w2 = [Tok(), Tok()]
    tmp = [P.sb([128, 512], name="tmp") for _ in range(2)]
    t_tmp = [Tok(), Tok()]
    yb = [P.sb([128, 512], name="yb") for _ in range(2)]
    t_yb = [Tok(), Tok()]
    pT = [P.ps([128, 512], name="pT") for _ in range(2)]
    t_pT = [Tok(True), Tok(True)]
    pa = [P.ps([128, 512], name="pa") for _ in range(2)]
    t_pa = [Tok(True), Tok(True)]
    pu = [P.ps([128, 512], name="pu") for _ in range(2)]
    t_pu = [Tok(True), Tok(True)]
    py = [P.ps([128, 512], name="py") for _ in range(2)]
    t_py = [Tok(True), Tok(True)]
    ixs = ipt = iw = iw2 = iau = iy = 0
    for eli, el in enumerate(els):
        chunks = []
        groups = []
        for bi_, b in enumerate(bs):
            o = bi_ * 544
            chunks += [(el * 2 + b, sc, o + sc * 128, 128, sc * 128) for sc in range(4)] + [(4 + el * 2 + b, 0, o + 512, 32, 512)]
            groups += [(o, 512), (o + 512, 32)]
        for (r, sc, s0, m, src0) in chunks:
            xb = ixs % 2
            ixs += 1
            b = r % 2
            P.dma("sp", lambda e, xb=xb, el=el, b=b, src0=src0, m=m: e.dma_start(out=xs[xb][0:m, :], in_=h_d[el * 2 + b, src0:src0 + m, :]),
                  writes=[t_xs[xb]])
            for k4 in range(KC // 4):
                pb = ipt % 2
                ipt += 1
                for kk in range(4):
                    k = k4 * 4 + kk
                    P.op("pe", lambda e, pb=pb, kk=kk, xb=xb, k=k, m=m: e.transpose(out=pT[pb][:, kk * 128:kk * 128 + m], in_=xs[xb][0:m, k * 128:(k + 1) * 128],
                                                                               identity=ident[0:m, 0:m]),
                         reads=[t_xs[xb], t_id], writes=[t_pT[pb]])
                if pb == 0:
                    P.op("act", lambda e, pb=pb, k4=k4, s0=s0, m=m: e.copy(out=xsT[:, k4 * 4:k4 * 4 + 4, s0:s0 + m],
                                                                         in_=pT[pb][:].rearrange("p (a c) -> p a c", a=4)[:, :, 0:m]),
                         reads=[t_pT[pb]], writes=[t_xsT])
                else:
                    P.op("dve", lambda e, pb=pb, k4=k4, s0=s0, m=m: e.tensor_copy(out=xsT[:, k4 * 4:k4 * 4 + 4, s0:s0 + m],
                                                                                in_=pT[pb][:].rearrange("p (a c) -> p a c", a=4)[:, :, 0:m]),
                         reads=[t_pT[pb]], writes=[t_xsT])
        for fc in range(KC):
            wb = iw % NWB
            iw += 1
            P.dma("pool", lambda e, wb=wb, eli=eli, fc=fc: e.dma_start(out=wa[wb][:], in_=w1_d[eli, :, fc * 128:(fc + 1) * 128].rearrange("(k p) c -> p k c", p=128)),
                  writes=[t_wa[wb]])
            P.dma("pool", lambda e, wb=wb, eli=eli, fc=fc: e.dma_start(out=wu[wb][:], in_=w3_d[eli, :, fc * 128:(fc + 1) * 128].rearrange("(k p) c -> p k c", p=128)),
                  writes=[t_wu[wb]])
            for (g0, gw) in groups:
                ab = iau % 2
                iau += 1
                for k in range(KC):
                    P.op("pe", lambda e, ab=ab, wb=wb, k=k, g0=g0, gw=gw: e.matmul(pa[ab][:, 0:gw], lhsT=wa[wb][:, k, :], rhs=xsT[:, k, g0:g0 + gw],
                                                                                start=(k == 0), stop=(k == KC - 1)),
                         reads=[t_wa[wb], t_xsT], writes=[t_pa[ab]])
                for k in range(KC):
                    P.op("pe", lambda e, ab=ab, wb=wb, k=k, g0=g0, gw=gw: e.matmul(pu[ab][:, 0:gw], lhsT=wu[wb][:, k, :], rhs=xsT[:, k, g0:g0 + gw],
                                                                                start=(k == 0), stop=(k == KC - 1)),
                         reads=[t_wu[wb], t_xsT], writes=[t_pu[ab]])
                P.op("act", lambda e, ab=ab, gw=gw: e.activation(out=tmp[ab][:, 0:gw], in_=pa[ab][:, 0:gw], func=AF.Silu), reads=[t_pa[ab]], writes=[t_tmp[ab]])
                P.op("dve", lambda e, ab=ab, fc=fc, g0=g0, gw=gw: e.tensor_mul(out=hT[:, fc, g0:g0 + gw], in0=tmp[ab][:, 0:gw], in1=pu[ab][:, 0:gw]),
                     reads=[t_tmp[ab], t_pu[ab]], writes=[t_hT])
        for dc in range(4):
            w2b = iw2 % 2
            iw2 += 1
            for half in range(2):
                ks = slice(half * 8, half * 8 + 8)
                P.dma("pool", lambda e, w2b=w2b, eli=eli, dc=dc, ks=ks: e.dma_start(
                    out=w2c[w2b][:, ks, :], in_=w2_d[eli, :, dc * 512:(dc + 1) * 512].rearrange("(k p) c -> p k c", p=128)[:, ks, :]), writes=[t_w2[w2b]])
            for (r, sc, s0, m, src0) in chunks:
                yi = iy % 2
                iy += 1
                for fc in range(KC):
                    P.op("pe", lambda e, yi=yi, fc=fc, s0=s0, m=m, w2b=w2b: e.matmul(py[yi][0:m, :], lhsT=hT[:, fc, s0:s0 + m], rhs=w2c[w2b][:, fc, :],
                                                                                  start=(fc == 0), stop=(fc == KC - 1)),
                         reads=[t_hT, t_w2[w2b]], writes=[t_py[yi]])
                P.op("dve", lambda e, yi=yi, r=r, sc=sc, m=m: e.tensor_scalar(out=yb[yi][0:m, :], in0=py[yi][0:m, :], scalar1=gate[0:m, r, sc:sc + 1], scalar2=None, op0=ALU.mult),
                     reads=[t_py[yi], t_gate], writes=[t_yb[yi]])
                P.dma("pool", lambda e, yi=yi, dc=dc, r=r, sc=sc, m=m: e.indirect_dma_start(
                    out=f_d[dc][:, :], out_offset=bass.IndirectOffsetOnAxis(ap=idx[0:m, r, sc:sc + 1], axis=0), in_=yb[yi][0:m, :], in_offset=None, compute_op=ALU.add),
                    reads=[t_idx, t_yb[yi]], writes=[t_f[dc], t_yb[yi]], is_out=True)
    return P


def stage_expert(sel, w1_l, w3_l, w2_l, els=(0, 1), bs=(0, 1)):
    P = build_expert(els, bs)
    in_maps = []
    for c in range(NCORES):
        in_maps.append({"xsc": sel[c][2], "idx": sel[c][0], "gate": sel[c][1], "w1": np.ascontiguousarray(w1_l[[2 * c + e for e in els]]),
                        "w3": np.ascontiguousarray(w3_l[[2 * c + e for e in els]]), "w2": np.ascontiguousarray(w2_l[[2 * c + e for e in els]]), "ident": IDENT})
    res = run_prog(P, in_maps)
    return [np.stack([r["f%d" % dc] for dc in range(4)]) for r in res]


def stage_final(fparts, x_lat, x_ctx, gate_lat, gate_ctx, gain, bias):
    P = build_outproj(False, NCORES)
    in_maps = []
    for c in range(NCORES):
        b, q = divmod(c, 4)
        ys = []
        for fp in fparts:
            lat = fp[:, b * SEQ + q * 1024:b * SEQ + (q + 1) * 1024, :]
            ctx = fp[:, B * SEQ + b * CTX + q * 64:B * SEQ + b * CTX + (q + 1) * 64, :]
            y = np.concatenate([lat, ctx], axis=1)
            ys.append(y.transpose(1, 0, 2).reshape(NT_B, D))
        cst = np.stack([bc(gate_lat[b]), bc(gate_ctx), bc(gain), bc(bias)], axis=0)
        in_maps.append({"x": tok_shard(x_lat, x_ctx, c), "cst": cst, "y": np.ascontiguousarray(np.stack(ys))})
    res = run_prog(P, in_maps)
    return tok_unshard(res, "o", D)


def kernel(x, c, ctx, c_ctx, w_ada, b_ada, w_in, w_out, gla_w_up, gla_b_up, gla_norm,
           gdn_conv, gdn_a_log, gdn_dt_bias, gdn_norm, attn_qk_norm, ln_gain, ln_bias,
           router, w1, w3, w2):
    f = lambda a: np.asarray(a, dtype=np.float32)
    x, c, ctx, c_ctx = f(x), f(c), f(ctx), f(c_ctx)
    w_ada, b_ada, w_in, w_out = f(w_ada), f(b_ada), f(w_in), f(w_out)
    gla_w_up, gla_b_up, gla_norm = f(gla_w_up), f(gla_b_up), f(gla_norm)
    gdn_conv, gdn_a_log, gdn_dt_bias, gdn_norm = f(gdn_conv), f(gdn_a_log), f(gdn_dt_bias), f(gdn_norm)
    attn_qk_norm, ln_gain, ln_bias, router = f(attn_qk_norm), f(ln_gain), f(ln_bias), f(router)
    w1, w3, w2 = f(w1), f(w3), f(w2)
    mod_lat, mod_ctx = stage_mod(c, c_ctx, w_ada, b_ada)
    x_lat, x_ctx = x, ctx
    for l in range(DEPTH):
        p_lat, p_ctx = stage_inproj(x_lat, x_ctx, mod_lat[l], mod_ctx[l], w_in[l])
        gla_l, gla_c = stage_gla(p_lat, p_ctx, gla_w_up[l], gla_b_up[l], gla_norm[l])
        gdn_l, gdn_c = stage_gdn(p_lat, p_ctx, gdn_conv[l], gdn_a_log[l], gdn_dt_bias[l], gdn_norm[l])
        att_l, att_c = stage_attn(p_lat, p_ctx, attn_qk_norm[l])
        mix_l = np.concatenate([gla_l, gdn_l, att_l], axis=-1)
        mix_c = np.concatenate([gla_c, gdn_c, att_c], axis=-1)
        x_lat, x_ctx = stage_outproj(mix_l, mix_c, x_lat, x_ctx, mod_lat[l][:, 2], mod_ctx[l][2], ln_gain[l, 0], ln_bias[l, 0], w_out[l])
        h_lat, h_ctx, a_lat, a_ctx = stage_router(x_lat, x_ctx, mod_lat[l], mod_ctx[l], router[l])
        sel = stage_select(a_lat, a_ctx, h_lat, h_ctx)
        fparts = stage_expert(sel, w1[l], w3[l], w2[l])
        x_lat, x_ctx = stage_final(fparts, x_lat, x_ctx, mod_lat[l][:, 5], mod_ctx[l][5], ln_gain[l, 1], ln_bias[l, 1])
    return np.ascontiguousarray(x_lat, dtype=np.float32)
```
